# Optimizing a Trainium2 kernel written in Bass

```python
import jax, jax.numpy as jnp
from jax import lax
import numpy as np

D_MODEL = 1024
BATCH = 32
SEQ = 256
DEPTH = 4
DEC_BATCH = 8
DEC_SEQ = 2048
PAST_LEN = 256

GRID_W = 64
N_EVEN = (DEPTH + 1) // 2
N_ODD = DEPTH // 2
ROPE_BASE = 10000.0
EPS = 1e-6
Q_BLOCK = 128
NEG_INF = -1e30

MLA_HEADS = 8
MLA_NOPE = 64
MLA_ROPE = 32
MLA_V = 64
MLA_Q_RANK = 384
MLA_KV_RANK = 256
MLA_WIDTH = MLA_HEADS * MLA_V
MLA_SCALE = (MLA_NOPE + MLA_ROPE) ** -0.5
POOL_WINDOWS = (2, 4, 8, 16)
POOL_GROUPS = 4
POOL_GROUP_W = D_MODEL // 8
POOL_WIDTH = POOL_GROUPS * POOL_GROUP_W
A_IN = MLA_Q_RANK + MLA_KV_RANK + MLA_ROPE + MLA_WIDTH + 2 * POOL_WIDTH
A_MIX = MLA_WIDTH + POOL_WIDTH
SWA_HEADS = 16
SWA_KV_HEADS = 4
SWA_HEAD_DIM = 64
SWA_WINDOW = 128
SWA_WIDTH = SWA_HEADS * SWA_HEAD_DIM
SWA_KV_W = SWA_KV_HEADS * SWA_HEAD_DIM
SWA_SCALE = SWA_HEAD_DIM ** -0.5
C_IN = 2 * SWA_WIDTH + 2 * SWA_KV_W

kernel_name = "hybrid_diffusion_mla_pool_swa_step"


def rmsnorm(x, g):
    xf = x.astype(jnp.float32)
    y = xf * lax.rsqrt(jnp.mean(xf * xf, axis=-1, keepdims=True) + EPS)
    return (y * g.astype(jnp.float32)).astype(x.dtype)


def ada_mod(cvec, w, b):
    m = jax.nn.silu(cvec) @ w + b
    shift, scale, gate = jnp.split(m, 3, axis=-1)
    return shift[:, None], scale[:, None], gate[:, None]


def modulate(x, mod, g):
    shift, scale, _ = mod
    return rmsnorm(x, g) * (1 + scale) + shift


def grid_positions(T):
    rows = T // GRID_W
    row = jnp.repeat(jnp.arange(rows, dtype=jnp.float32), GRID_W)
    col = jnp.tile(jnp.arange(GRID_W, dtype=jnp.float32), rows)
    return row, col


def _rope_1d(x, pos):
    n = x.shape[-1]
    inv = ROPE_BASE ** (-jnp.arange(0, n, 2, dtype=jnp.float32) / n)
    ang = pos[:, None] * inv[None]
    cos = jnp.cos(ang)[None, :, None, :]
    sin = jnp.sin(ang)[None, :, None, :]
    x1, x2 = x[..., : n // 2], x[..., n // 2:]
    return jnp.concatenate([x1 * cos - x2 * sin, x1 * sin + x2 * cos], axis=-1)


def rope_2d(x):
    T, R = x.shape[1], x.shape[-1]
    row, col = grid_positions(T)
    xf = x.astype(jnp.float32)
    out = jnp.concatenate([_rope_1d(xf[..., : R // 2], row), _rope_1d(xf[..., R // 2:], col)], axis=-1)
    return out.astype(x.dtype)


def dense_attention(q, k, v, scale, sink=None):
    B, T, H, d = q.shape
    KH = k.shape[2]
    G = H // KH
    dv = v.shape[-1]
    nb = T // Q_BLOCK
    qb = q.reshape(B, nb, Q_BLOCK, KH, G, d).transpose(1, 0, 2, 3, 4, 5)

    def block(qi):
        s = jnp.einsum('bqkgd,bskd->bkgqs', qi, k).astype(jnp.float32) * scale
        if sink is not None:
            sk = jnp.broadcast_to(sink.astype(jnp.float32).reshape(1, KH, G, 1, 1), s.shape[:-1] + (1,))
            p = jax.nn.softmax(jnp.concatenate([s, sk], axis=-1), axis=-1)[..., :-1]
        else:
            p = jax.nn.softmax(s, axis=-1)
        return jnp.einsum('bkgqs,bskd->bqkgd', p.astype(v.dtype), v)

    o = lax.map(block, qb)
    return o.transpose(1, 0, 2, 3, 4, 5).reshape(B, T, H, dv)


def banded_attention(q, k, v, ck, cv, sink, scale):
    B, T, H, d = q.shape
    KH = k.shape[2]
    G = H // KH
    W = SWA_WINDOW
    nb = T // W
    P = ck.shape[1]
    qb = q.reshape(B, nb, W, KH, G, d)

    def neighbours(t):
        tp = jnp.pad(t, ((0, 0), (W, W), (0, 0), (0, 0))).reshape(B, nb + 2, W, KH, t.shape[-1])
        return jnp.concatenate([tp[:, :-2], tp[:, 1:-1], tp[:, 2:]], axis=2)

    kw, vw = neighbours(k), neighbours(v)
    s_loc = jnp.einsum('bnqkgd,bnskd->bnkgqs', qb, kw).astype(jnp.float32) * scale
    qpos = jnp.arange(nb)[:, None] * W + jnp.arange(W)[None]
    kpos = (jnp.arange(nb)[:, None] - 1) * W + jnp.arange(3 * W)[None]
    rel = kpos[:, None, :] - qpos[:, :, None]
    valid = (jnp.abs(rel) <= SWA_WINDOW) & (kpos[:, None, :] >= 0) & (kpos[:, None, :] < T)
    s_loc = jnp.where(valid[None, :, None, None], s_loc, NEG_INF)
    s_ctx = jnp.einsum('bnqkgd,bpkd->bnkgqp', qb, ck).astype(jnp.float32) * scale
    sk = jnp.broadcast_to(sink.astype(jnp.float32).reshape(1, 1, KH, G, 1, 1), s_loc.shape[:-1] + (1,))
    p = jax.nn.softmax(jnp.concatenate([s_loc, s_ctx, sk], axis=-1), axis=-1)
    L = 3 * W
    p_loc = p[..., :L].astype(v.dtype)
    p_ctx = p[..., L:L + P].astype(v.dtype)
    o = jnp.einsum('bnkgqs,bnskd->bnqkgd', p_loc, vw) + jnp.einsum('bnkgqp,bpkd->bnqkgd', p_ctx, cv)
    return o.reshape(B, T, H, v.shape[-1])


def multiscale_pool(v, w_pool, s_pool):
    B, T, _ = v.shape
    vf = v.astype(jnp.float32).reshape(B, T, POOL_GROUPS, POOL_GROUP_W)
    S = jnp.concatenate([jnp.zeros((B, 1, POOL_GROUPS, POOL_GROUP_W), jnp.float32), lax.cumsum(vf, axis=1)], axis=1)
    t = jnp.arange(T)
    outs = []
    for g, w in enumerate(POOL_WINDOWS):
        lo = jnp.clip(t - w // 2, 0, T)
        hi = jnp.clip(t + w // 2, 0, T)
        cnt = (hi - lo).astype(jnp.float32)[None, :, None]
        outs.append((S[:, hi, g] - S[:, lo, g]) / cnt)
    pooled = jnp.stack(outs, axis=2) - vf
    y = jnp.einsum('btgc,gce->btge', pooled, w_pool.astype(jnp.float32)).reshape(B, T, POOL_WIDTH)
    return (y * s_pool.astype(jnp.float32)).astype(v.dtype)


def mla_pool_mixer(h, w_in, g_qn, g_kvn, w_uq, w_ukv, w_pool, s_pool, w_out, ctx_ckv=None, ctx_krope=None):
    B, T, _ = h.shape
    i1 = MLA_Q_RANK
    i2 = i1 + MLA_KV_RANK
    i3 = i2 + MLA_ROPE
    i4 = i3 + MLA_WIDTH
    i5 = i4 + POOL_WIDTH
    cq, ckv, krope, gate_a, v_pool, gate_b = jnp.split(h @ w_in, [i1, i2, i3, i4, i5], axis=-1)
    ckv = rmsnorm(ckv, g_kvn)
    q = (rmsnorm(cq, g_qn) @ w_uq).reshape(B, T, MLA_HEADS, MLA_NOPE + MLA_ROPE)
    if ctx_ckv is not None:
        q = jnp.concatenate([q[..., :MLA_NOPE], rope_2d(q[..., MLA_NOPE:])], axis=-1)
        kr_lat = rope_2d(krope[:, :, None, :])
        keys_ckv = jnp.concatenate([ckv, ctx_ckv], axis=1)
        keys_kr = jnp.concatenate([kr_lat, ctx_krope[:, :, None, :]], axis=1)
    else:
        keys_ckv, keys_kr = ckv, krope[:, :, None, :]
    S = keys_ckv.shape[1]
    kv = (keys_ckv @ w_ukv).reshape(B, S, MLA_HEADS, MLA_NOPE + MLA_V)
    k = jnp.concatenate([kv[..., :MLA_NOPE], jnp.broadcast_to(keys_kr, (B, S, MLA_HEADS, MLA_ROPE))], axis=-1)
    v = kv[..., MLA_NOPE:]
    o = dense_attention(q, k, v, MLA_SCALE)
    a = o.reshape(B, T, MLA_WIDTH) * jax.nn.silu(gate_a)
    b = multiscale_pool(v_pool, w_pool, s_pool) * jax.nn.silu(gate_b)
    out = jnp.concatenate([a, b], axis=-1) @ w_out
    return out, ckv, krope


def swa_mixer(h, w_in, sink, w_out, ctx_k=None, ctx_v=None):
    B, T, _ = h.shape
    q, k, v, g = jnp.split(h @ w_in, [SWA_WIDTH, SWA_WIDTH + SWA_KV_W, SWA_WIDTH + 2 * SWA_KV_W], axis=-1)
    q = q.reshape(B, T, SWA_HEADS, SWA_HEAD_DIM)
    k = k.reshape(B, T, SWA_KV_HEADS, SWA_HEAD_DIM)
    v = v.reshape(B, T, SWA_KV_HEADS, SWA_HEAD_DIM)
    if ctx_k is None:
        o = dense_attention(q, k, v, SWA_SCALE, sink)
    else:
        o = banded_attention(rope_2d(q), rope_2d(k), v, ctx_k, ctx_v, sink, SWA_SCALE)
    out = (o.reshape(B, T, SWA_WIDTH) * jax.nn.silu(g)) @ w_out
    return out, k, v


def setup_inputs(seed: int = 0) -> dict:
    key = jax.random.key(seed)
    ks = jax.random.split(key, 23)
    f = jnp.float32
    D = D_MODEL

    def nrm(k, shape, scale=1.0):
        return jax.random.normal(k, shape, f) * scale

    return {
        "x_prompt": nrm(ks[0], (BATCH, SEQ, D)),
        "x_sample": nrm(ks[1], (DEC_BATCH, DEC_SEQ, D)),
        "cache_ckv": nrm(ks[2], (DEC_BATCH, N_EVEN, PAST_LEN, MLA_KV_RANK)),
        "cache_krope": nrm(ks[3], (DEC_BATCH, N_EVEN, PAST_LEN, MLA_ROPE)),
        "cache_k": nrm(ks[4], (DEC_BATCH, N_ODD, PAST_LEN, SWA_KV_HEADS, SWA_HEAD_DIM)),
        "cache_v": nrm(ks[5], (DEC_BATCH, N_ODD, PAST_LEN, SWA_KV_HEADS, SWA_HEAD_DIM)),
        "c": nrm(ks[6], (DEC_BATCH, D)),
        "c_ctx": nrm(ks[7], (D,)),
        "ada_w": nrm(ks[8], (DEPTH, D, 3 * D), D ** -0.5),
        "ada_b": nrm(ks[9], (DEPTH, 3 * D), 0.02),
        "norm_pre": 1.0 + nrm(ks[10], (DEPTH, D), 0.1),
        "norm_post": 1.0 + nrm(ks[11], (DEPTH, D), 0.1),
        "mla_w_in": nrm(ks[12], (N_EVEN, D, A_IN), D ** -0.5),
        "mla_g_qn": 1.0 + nrm(ks[13], (N_EVEN, MLA_Q_RANK), 0.1),
        "mla_g_kvn": 1.0 + nrm(ks[14], (N_EVEN, MLA_KV_RANK), 0.1),
        "mla_w_uq": nrm(ks[15], (N_EVEN, MLA_Q_RANK, MLA_HEADS * (MLA_NOPE + MLA_ROPE)), MLA_Q_RANK ** -0.5),
        "mla_w_ukv": nrm(ks[16], (N_EVEN, MLA_KV_RANK, MLA_HEADS * (MLA_NOPE + MLA_V)), MLA_KV_RANK ** -0.5),
        "pool_w": nrm(ks[17], (N_EVEN, POOL_GROUPS, POOL_GROUP_W, POOL_GROUP_W), POOL_GROUP_W ** -0.5),
        "pool_scale": 1.0 + nrm(ks[18], (N_EVEN, POOL_WIDTH), 0.1),
        "mixa_w_out": nrm(ks[19], (N_EVEN, A_MIX, D), A_MIX ** -0.5),
        "swa_w_in": nrm(ks[20], (N_ODD, D, C_IN), D ** -0.5),
        "swa_sink": nrm(ks[21], (N_ODD, SWA_HEADS)),
        "swa_w_out": nrm(ks[22], (N_ODD, SWA_WIDTH, D), SWA_WIDTH ** -0.5),
    }


def reference(x_prompt, x_sample, cache_ckv, cache_krope, cache_k, cache_v, c, c_ctx,
              ada_w, ada_b, norm_pre, norm_post,
              mla_w_in, mla_g_qn, mla_g_kvn, mla_w_uq, mla_w_ukv, pool_w, pool_scale, mixa_w_out,
              swa_w_in, swa_sink, swa_w_out):
    xp, xs = x_prompt, x_sample
    st_ckv, st_krope, st_k, st_v = [], [], [], []
    for l in range(DEPTH):
        mp = ada_mod(c_ctx[None], ada_w[l], ada_b[l])
        ms = ada_mod(c, ada_w[l], ada_b[l])
        hp = modulate(xp, mp, norm_pre[l])
        hs = modulate(xs, ms, norm_pre[l])
        if l % 2 == 0:
            i = l // 2
            wts = (mla_w_in[i], mla_g_qn[i], mla_g_kvn[i], mla_w_uq[i], mla_w_ukv[i], pool_w[i], pool_scale[i], mixa_w_out[i])
            op, ckv, krope = mla_pool_mixer(hp, *wts)
            os_, _, _ = mla_pool_mixer(hs, *wts, ctx_ckv=cache_ckv[:, i], ctx_krope=cache_krope[:, i])
            st_ckv.append(ckv)
            st_krope.append(krope)
        else:
            i = l // 2
            op, kc, vc = swa_mixer(hp, swa_w_in[i], swa_sink[i], swa_w_out[i])
            os_, _, _ = swa_mixer(hs, swa_w_in[i], swa_sink[i], swa_w_out[i], ctx_k=cache_k[:, i], ctx_v=cache_v[:, i])
            st_k.append(kc)
            st_v.append(vc)
        xp = xp + mp[2] * rmsnorm(op, norm_post[l])
        xs = xs + ms[2] * rmsnorm(os_, norm_post[l])
    state_ckv = jnp.stack(st_ckv, axis=1)
    state_krope = jnp.stack(st_krope, axis=1)
    state_k = jnp.stack(st_k, axis=1)
    state_v = jnp.stack(st_v, axis=1)
    return (xp, xs, state_ckv, state_krope, state_k, state_v)
```

```python
import numpy as np
from contextlib import ExitStack
import concourse.bass as bass
import concourse.mybir as mybir
from concourse.bass_utils import run_bass_kernel_spmd

F32 = mybir.dt.float32
BF16 = mybir.dt.bfloat16
AF = mybir.ActivationFunctionType
ALU = mybir.AluOpType

ENGS = ("pe", "act", "dve", "pool", "sp")
SAME_ENGINE_RAW = True
N_DMA_SEMS = 24
BUCKET = 4096

D = 1024
EPS = 1e-6
MLA_SCALE = 96 ** -0.5
SWA_SCALE = 64 ** -0.5


class Prog:
    def __init__(self, nc, stack):
        self.nc = nc
        self.stack = stack
        self.ops = {e: [] for e in ENGS}
        self.recs = {}
        self.known = {e: {} for e in ENGS}
        self.eng_sem = {e: stack.enter_context(nc.semaphore("s_" + e)) for e in ENGS}
        self.dma_sems = [stack.enter_context(nc.semaphore("s_dma%d" % i)) for i in range(N_DMA_SEMS)]
        self.dma_cnt = [0] * N_DMA_SEMS
        self.dma_rr = 0
        self.dma_rr_pool = 0
        self.out_dma_tokens = []

    def sb(self, name, shape, dtype):
        return self.stack.enter_context(self.nc.sbuf_tensor("sb_" + name, list(shape), dtype))

    def ps(self, name, shape, dtype=F32):
        return self.stack.enter_context(self.nc.psum_tensor("pp_" + name, list(shape), dtype))

    @staticmethod
    def _box(ap):
        shp = list(ap.tensor.shape)
        row = 1
        for s in shp[1:]:
            row *= s
        isz = mybir.dt.size(ap.dtype)
        off = int(ap.offset)
        dims = ap.ap
        p0 = off // row
        f0 = off % row
        pstep, pcnt = dims[0]
        if pstep == row or (pcnt == 1 and len(dims) > 1):
            p1 = p0 + pcnt
            rest = dims[1:]
        elif pstep == 0:
            p1 = p0 + 1
            rest = dims[1:]
        else:
            p1 = p0 + 1
            rest = dims
        ext = 0
        for st, cn in rest:
            ext += abs(st) * (cn - 1)
        return (p0, p1, f0 * isz, (f0 + ext + 1) * isz)

    @staticmethod
    def _tracked(ap):
        return str(ap.space).upper() in ("SB", "PSUM")

    def op(self, eng, fn, reads=(), writes=(), dma=False, out_dma=False):
        idx = len(self.ops[eng])
        waits = []
        kn = self.known[eng]

        def need(tok, raw=False):
            if tok[0] == 'e':
                if tok[1] == eng and not (raw and SAME_ENGINE_RAW and eng != "pe"):
                    return
                key = tok[1]
            else:
                key = ('d', tok[1])
            if kn.get(key, -1) >= tok[2]:
                return
            kn[key] = tok[2]
            waits.append(tok)

        if dma:
            if eng == "pool":
                k = 8 + self.dma_rr_pool
                self.dma_rr_pool = (self.dma_rr_pool + 1) % (N_DMA_SEMS - 8)
            else:
                k = self.dma_rr
                self.dma_rr = (self.dma_rr + 1) % 8
            if self.dma_cnt[k] > 0:
                need(('d', k, self.dma_cnt[k] * 16))
            self.dma_cnt[k] += 1
            mytok = ('d', k, self.dma_cnt[k] * 16)
        else:
            mytok = ('e', eng, idx)

        rb = [(ap.name, self._box(ap)) for ap in reads if self._tracked(ap) and str(ap.space).upper() != "PSUM"]
        wb = [(ap.name, self._box(ap)) for ap in writes if self._tracked(ap) and str(ap.space).upper() != "PSUM"]
        for ap in list(reads) + list(writes):
            if str(ap.space).upper() == "PSUM":
                ent = (ap.name, (0, 128, 0, 1 << 20))
                if ent not in wb:
                    wb.append(ent)
        for name, box in rb:
            tr = self.recs.get(name)
            if tr is None:
                continue
            for b in range(box[2] // BUCKET, (box[3] - 1) // BUCKET + 1):
                for r in tr.get(b, ()):
                    if r[3] and r[2]:
                        rbx = r[0]
                        if rbx[0] < box[1] and box[0] < rbx[1] and rbx[2] < box[3] and box[2] < rbx[3]:
                            need(r[1], True)
        for name, box in wb:
            tr = self.recs.get(name)
            if tr is None:
                continue
            for b in range(box[2] // BUCKET, (box[3] - 1) // BUCKET + 1):
                lst = tr.get(b)
                if not lst:
                    continue
                for r in lst:
                    if r[3]:
                        rbx = r[0]
                        if rbx[0] < box[1] and box[0] < rbx[1] and rbx[2] < box[3] and box[2] < rbx[3]:
                            need(r[1])
                            if box[0] <= rbx[0] and box[1] >= rbx[1] and box[2] <= rbx[2] and box[3] >= rbx[3]:
                                r[3] = False
                tr[b] = [r for r in lst if r[3]]
        for name, box in rb:
            tr = self.recs.setdefault(name, {})
            rec = [box, mytok, False, True]
            for b in range(box[2] // BUCKET, (box[3] - 1) // BUCKET + 1):
                lst = tr.setdefault(b, [])
                if not dma:
                    for r in lst:
                        if r[3] and (not r[2]) and r[1][0] == 'e' and r[1][1] == eng and r[0] == box:
                            r[3] = False
                lst.append(rec)
        for name, box in wb:
            tr = self.recs.setdefault(name, {})
            rec = [box, mytok, True, True]
            for b in range(box[2] // BUCKET, (box[3] - 1) // BUCKET + 1):
                tr.setdefault(b, []).append(rec)
        o = dict(fn=fn, waits=waits, tok=mytok, marked=False)
        self.ops[eng].append(o)
        if out_dma:
            self.out_dma_tokens.append(mytok)
        return o

    def dma(self, out, in_, eng="sp", out_dma=False):
        return self.op(eng, lambda e: e.dma_start(out=out, in_=in_), reads=[in_], writes=[out],
                       dma=True, out_dma=out_dma)

    def mm(self, out, lhsT, rhs, start=True, stop=True):
        return self.op("pe", lambda e: e.matmul(out, lhsT, rhs, start=start, stop=stop),
                       reads=[lhsT, rhs] + ([] if start else [out]), writes=[out])

    def tr(self, out, in_, ident):
        return self.op("pe", lambda e: e.transpose(out, in_, ident), reads=[in_, ident], writes=[out])

    def act(self, out, in_, func, bias=None, scale=None, accum=None):
        kw = {}
        rd = [in_]
        wr = [out]
        if bias is not None:
            kw["bias"] = bias
            if not isinstance(bias, (int, float)):
                rd.append(bias)
        if scale is not None:
            kw["scale"] = scale
            if not isinstance(scale, (int, float)):
                rd.append(scale)
        if accum is not None:
            kw["accum_out"] = accum
            wr.append(accum)
        return self.op("act", lambda e: e.activation(out=out, in_=in_, func=func, **kw), reads=rd, writes=wr)

    def cp(self, eng, out, in_):
        if eng == "act":
            return self.op("act", lambda e: e.copy(out=out, in_=in_), reads=[in_], writes=[out])
        return self.op(eng, lambda e: e.tensor_copy(out=out, in_=in_), reads=[in_], writes=[out])

    def tt(self, eng, out, in0, in1, op):
        return self.op(eng, lambda e: e.tensor_tensor(out=out, in0=in0, in1=in1, op=op), reads=[in0, in1], writes=[out])

    def ts(self, eng, out, in0, s1, op0, s2=None, op1=None):
        rd = [in0]
        if not isinstance(s1, (int, float)):
            rd.append(s1)
        if s2 is not None and not isinstance(s2, (int, float)):
            rd.append(s2)
        if op1 is None:
            return self.op(eng, lambda e: e.tensor_scalar(out=out, in0=in0, scalar1=s1, scalar2=None, op0=op0),
                           reads=rd, writes=[out])
        return self.op(eng, lambda e: e.tensor_scalar(out=out, in0=in0, scalar1=s1, scalar2=s2, op0=op0, op1=op1),
                       reads=rd, writes=[out])

    def stt(self, eng, out, in0, scalar, in1, op0, op1):
        rd = [in0, in1]
        if not isinstance(scalar, (int, float)):
            rd.append(scalar)
        return self.op(eng, lambda e: e.scalar_tensor_tensor(out=out, in0=in0, scalar=scalar, in1=in1, op0=op0, op1=op1),
                       reads=rd, writes=[out])

    def memset(self, eng, out, val):
        return self.op(eng, lambda e: e.memset(out, val), writes=[out])

    def recip(self, out, in_):
        return self.op("dve", lambda e: e.reciprocal(out=out, in_=in_), reads=[in_], writes=[out])

    def finish(self):
        nc = self.nc
        for e in ENGS:
            for o in self.ops[e]:
                for tok in o["waits"]:
                    if tok[0] == 'e':
                        self.ops[tok[1]][tok[2]]["marked"] = True
        cnt_at = {}
        for e in ENGS:
            c = 0
            arr = []
            for o in self.ops[e]:
                if o["marked"]:
                    c += 1
                arr.append(c)
            cnt_at[e] = arr
        fw = {}
        for tok in self.out_dma_tokens:
            fw[tok[1]] = max(fw.get(tok[1], 0), tok[2])

        with nc.Block() as block:
            def emit(ename, engobj):
                for o in self.ops[ename]:
                    for tok in o["waits"]:
                        if tok[0] == 'e':
                            engobj.wait_ge(self.eng_sem[tok[1]], cnt_at[tok[1]][tok[2]])
                        else:
                            engobj.wait_ge(self.dma_sems[tok[1]], tok[2])
                    ins = o["fn"](engobj)
                    if o["tok"][0] == 'd':
                        ins.then_inc(self.dma_sems[o["tok"][1]], 16)
                    elif o["marked"]:
                        ins.then_inc(self.eng_sem[ename], 1)
                if ename == "sp":
                    for k, v in fw.items():
                        engobj.wait_ge(self.dma_sems[k], v)

            @block.sync
            def _(sync):
                emit("sp", sync)

            @block.tensor
            def _(tensor):
                emit("pe", tensor)

            @block.scalar
            def _(scalar):
                emit("act", scalar)

            @block.vector
            def _(vector):
                emit("dve", vector)

            @block.gpsimd
            def _(gpsimd):
                emit("pool", gpsimd)


class _Trunc(Exception):
    pass


class Arena:
    def __init__(self, tensor, nelem):
        self.t = tensor
        self.n = nelem
        self.free = [(0, nelem)]
        self.live = {}
        self.peak = 0

    def alloc(self, name, nelem_bf16):
        nelem_bf16 = (nelem_bf16 + 15) // 16 * 16
        for i, (o, s) in enumerate(self.free):
            if s >= nelem_bf16:
                if s == nelem_bf16:
                    self.free.pop(i)
                else:
                    self.free[i] = (o + nelem_bf16, s - nelem_bf16)
                self.live[name] = (o, nelem_bf16)
                used = self.n - sum(s for _, s in self.free)
                self.peak = max(self.peak, used)
                return o
        raise RuntimeError("arena OOM allocating %s (%d); live=%s free=%s" % (name, nelem_bf16, self.live, self.free))

    def release(self, name):
        o, s = self.live.pop(name)
        self.free.append((o, s))
        self.free.sort()
        merged = []
        for o, s in self.free:
            if merged and merged[-1][0] + merged[-1][1] == o:
                merged[-1] = (merged[-1][0], merged[-1][1] + s)
            else:
                merged.append((o, s))
        self.free = merged

    def bf(self, name, shape):
        n = 1
        for s in shape[1:]:
            n *= s
        o = self.alloc(name, n)
        v = self.t[0:shape[0], o:o + n]
        if len(shape) == 3:
            v = v.rearrange("p (a b) -> p a b", a=shape[1])
        elif len(shape) == 4:
            v = v.rearrange("p (a b c) -> p a b c", a=shape[1], b=shape[2])
        return v

    def f32(self, name, shape):
        n = 1
        for s in shape[1:]:
            n *= s
        o = self.alloc(name, 2 * n)
        v = self.t[0:shape[0], o:o + 2 * n].bitcast(F32)
        if len(shape) == 3:
            v = v.rearrange("p (a b) -> p a b", a=shape[1])
        elif len(shape) == 4:
            v = v.rearrange("p (a b c) -> p a b c", a=shape[1], b=shape[2])
        return v


def _rope_tables():
    t = np.arange(2048)
    row = (t // 64).astype(np.float64)
    col = (t % 64).astype(np.float64)
    mla = np.zeros((2, 96, 2048), np.float32)
    mla[0, 0:64] = 1.0
    for r in range(32):
        pos = row if r < 16 else col
        i = r % 16
        f = i % 8
        inv = 10000.0 ** (-(2.0 * f) / 16.0)
        ang = pos * inv
        mla[0, 64 + r] = np.cos(ang)
        mla[1, 64 + r] = np.sin(ang) if i < 8 else -np.sin(ang)
    swa = np.zeros((2, 128, 2048), np.float32)
    for p in range(128):
        d = p % 64
        pos = row if d < 32 else col
        i = d % 32
        f = i % 16
        inv = 10000.0 ** (-(2.0 * f) / 32.0)
        ang = pos * inv
        swa[0, p] = np.cos(ang)
        swa[1, p] = np.sin(ang) if i < 16 else -np.sin(ang)
    return mla, swa


def _perm_mats():
    pm96 = np.zeros((96, 96), np.float32)
    for r in range(32):
        i = r % 16
        partner = r + 8 if i < 8 else r - 8
        pm96[64 + partner, 64 + r] = 1.0
    pm128 = np.zeros((128, 128), np.float32)
    for m in range(128):
        i = m % 32
        partner = m + 16 if i < 16 else m - 16
        pm128[partner, m] = 1.0
    return pm96, pm128


def _pool_mats():
    T = 384
    out = np.zeros((4, 5, 128, 128), np.float32)
    for g, w in enumerate((2, 4, 8, 16)):
        A = np.zeros((T, T), np.float64)
        for t in range(T):
            lo = min(max(t - w // 2, 0), T)
            hi = min(max(t + w // 2, 0), T)
            A[lo:hi, t] = 1.0 / (hi - lo)
            A[t, t] -= 1.0
        out[g, 0] = A[0:128, 0:128]
        out[g, 1] = A[128:256, 128:256]
        out[g, 2] = A[256:384, 256:384]
        out[g, 3] = A[0:128, 128:256]
        out[g, 4] = A[128:256, 0:128]
    return out


def _masks():
    k = np.arange(128)[:, None]
    q = np.arange(128)[None, :]
    m = np.zeros((2, 128, 128), np.float32)
    m[0] = np.where(k <= q, 0.0, -30000.0)
    m[1] = np.where(q <= k, 0.0, -30000.0)
    return m


def build_program(nl_p=4, nl_s=4, stages=3, dbg=None):
    nc = bass.Bass("TRN2", target_bir_lowering=False)

    def din(name, shape):
        return nc.dram_tensor(name, list(shape), F32, kind="ExternalInput").ap()

    def dout(name, shape):
        return nc.dram_tensor(name, list(shape), F32, kind="ExternalOutput").ap()

    xp_d = din("xp", [1024, 1024])
    xs_d = din("xs", [2048, 1024])
    cckv_d = din("cckv", [2, 256, 256])
    ckr_d = din("ckr", [2, 256, 32])
    ck_d = din("ck", [2, 256, 256])
    cv_d = din("cv", [2, 256, 256])
    cc_d = din("cc", [2, 1024])
    adaw_d = din("ada_w", [4, 1024, 3072])
    adab_d = din("ada_b", [4, 3072])
    vecs_d = din("vecs", [14, 1024])
    gkvn_d = din("gkvn", [2, 256])
    sink_d = din("sink", [32])
    wine_d = din("w_in_e", [2, 1024, 2208])
    wuq_d = din("w_uq", [2, 384, 768])
    wukv_d = din("w_ukv", [2, 256, 1024])
    wpool_d = din("w_pool", [2, 4, 128, 128])
    woe_d = din("w_out_e", [2, 1024, 1024])
    wino_d = din("w_in_o", [2, 1024, 2560])
    woo_d = din("w_out_o", [2, 1024, 1024])
    ident_d = din("ident", [128, 128])
    pm96_d = din("pm96", [96, 96])
    pm128_d = din("pm128", [128, 128])
    masks_d = din("masks", [2, 128, 128])
    amats_d = din("amats", [20, 128, 128])
    mlacs_d = din("mla_cs", [2, 96, 2048])
    swacs_d = din("swa_cs", [2, 128, 2048])

    yp_d = dout("yp", [1024, 1024])
    ys_d = dout("ys", [2048, 1024])
    sckv_d = dout("st_ckv", [4, 2, 256, 256])
    skr_d = dout("st_kr", [4, 2, 256, 32])
    sk_d = dout("st_k", [4, 2, 256, 256])
    sv_d = dout("st_v", [4, 2, 256, 256])

    with ExitStack() as st:
        P = Prog(nc, st)
        xT = P.sb("xT", [128, 8, 2048], F32)
        ident = P.sb("ident", [128, 128], F32)
        ones_bf = P.sb("ones_bf", [128, 128], BF16)
        identb = P.sb("identb", [128, 128], BF16)
        scb = P.sb("scb", [128, 8, 2], BF16)
        pm96 = P.sb("pm96", [96, 96], BF16)
        pm128 = P.sb("pm128", [128, 128], BF16)
        masks = P.sb("masks", [128, 2, 128], BF16)
        epsT = P.sb("epsT", [128, 1], F32)
        mod = P.sb("mod", [128, 4, 48], F32)
        vecT = P.sb("vecT", [128, 8, 32], F32)
        coefA = P.sb("coefA", [128, 4, 2, 8], F32)
        coefB = P.sb("coefB", [128, 4, 2, 8], F32)
        coefG = P.sb("coefG", [128, 4, 2, 8], F32)
        gkvb = P.sb("gkvb", [128, 2, 256], F32)
        esink = P.sb("esink", [128, 32], F32)
        rstd = P.sb("rstd", [128, 512], F32)
        rstd2 = P.sb("rstd2", [128, 512], F32)
        tmpf = [P.sb("tmpf%d" % i, [128, 512], F32) for i in range(2)]
        ARENA_N = 64 * 1024
        arena_t = P.sb("arena", [128, ARENA_N], BF16)
        AR = Arena(arena_t, ARENA_N)
        banks = [P.ps("ps%d" % i, [128, 512], F32) for i in range(8)]
        rr = {"all": 0, "s": 0, "o": 0, "g": 0}
        pools = {"all": list(range(8)), "s": [0, 1, 2, 3], "o": [4, 5], "g": [6, 7]}

        def bank(pool="all"):
            lst = pools[pool]
            b = banks[lst[rr[pool] % len(lst)]]
            rr[pool] += 1
            return b

        evac_rr = [0]

        def evac_eng():
            evac_rr[0] += 1
            return "dve" if evac_rr[0] % 2 else "act"

        P.dma(ident[:], ident_d)
        P.dma(identb[:], ident_d, eng="pool")
        P.dma(pm96[:], pm96_d, eng="pool")
        P.dma(pm128[:], pm128_d, eng="pool")
        P.dma(masks[:], masks_d.rearrange("a k q -> k a q"), eng="pool")
        P.dma(gkvb[:].rearrange("p a n -> p (a n)"), gkvn_d.rearrange("a n -> (a n)").partition_broadcast(128))
        P.dma(esink[:], sink_d.partition_broadcast(128))
        P.memset("dve", ones_bf[:], 1.0)
        P.memset("dve", epsT[:], EPS)
        P.act(esink[:], esink[:], AF.Exp)

        if dbg == "consts":
            P.dma(yp_d[0:128, 0:32], esink[:], out_dma=True)
            P.finish()
            return nc
        vst = AR.f32("vst", [32, 1024])
        P.memset("dve", vst[:], 0.0)
        P.dma(vst[0:14, :], vecs_d)
        pv = bank()
        for c in range(8):
            P.tr(pv[:, c * 32:(c + 1) * 32], vst[0:32, c * 128:(c + 1) * 128], ident[0:32, 0:32])
        for c in range(8):
            P.cp("dve", vecT[:, c, :], pv[:, c * 32:(c + 1) * 32])
        AR.release("vst")

        if dbg == "vec":
            P.dma(yp_d[0:128, 0:256], vecT[:].rearrange("p a b -> p (a b)"), out_dma=True)
            P.finish()
            return nc
        ccT = AR.f32("ccT", [128, 2, 8])
        for m in range(2):
            P.dma(ccT[:, m, :], cc_d[m].rearrange("(p c) -> p c", c=8))
        for m in range(2):
            P.act(scb[:, :, m], ccT[:, m, :], AF.Silu)
        AR.release("ccT")
        modv = mod[:].rearrange("p l (j c m) -> p l j c m", j=3, c=8)
        ada_state = {}

        def ada_alloc(l, nbuf):
            ada_state["l"] = l
            ada_state["adab"] = AR.bf("adab", [1, 3072])
            ada_state["bufs"] = [AR.bf("adw%d" % i, [128, 8, 512]) for i in range(nbuf)]
            ada_state["nbuf"] = nbuf
            P.dma(ada_state["adab"][:], adab_d[l:l + 1, :], eng="pool")

        def ada_issue(blks):
            l = ada_state["l"]
            wv = adaw_d[l].rearrange("(p c) n -> p c n", c=8)
            for blk in blks:
                wt = ada_state["bufs"][blk % ada_state["nbuf"]]
                P.dma(wt[:], wv[:, :, blk * 512:(blk + 1) * 512], eng="pool")

        def ada_compute(blks):
            l = ada_state["l"]
            adab = ada_state["adab"]
            for blk in blks:
                wt = ada_state["bufs"][blk % ada_state["nbuf"]]
                pm = bank()
                for nci in range(4):
                    nch = blk * 4 + nci
                    for c in range(8):
                        P.mm(pm[:, 2 * nci:2 * nci + 2], wt[:, c, nci * 128:(nci + 1) * 128], scb[:, c, :],
                             start=(c == 0), stop=False)
                    P.mm(pm[:, 2 * nci:2 * nci + 2], adab[0:1, nch * 128:(nch + 1) * 128], ones_bf[0:1, 0:2],
                         start=False, stop=True)
                P.cp("dve", mod[:, l, blk * 8:(blk + 1) * 8], pm[:, 0:8])

        def ada_finish():
            l = ada_state["l"]
            for m in range(2):
                P.stt("dve", coefA[:, l, m, :], modv[:, l, 1, :, m], 1.0, vecT[:, :, l], ALU.add, ALU.mult)
                P.cp("dve", coefB[:, l, m, :], modv[:, l, 0, :, m])
                P.tt("dve", coefG[:, l, m, :], modv[:, l, 2, :, m], vecT[:, :, 4 + l], ALU.mult)
            for i in range(ada_state["nbuf"]):
                AR.release("adw%d" % i)
            AR.release("adab")
            ada_state.clear()

        def ada_all(l):
            ada_alloc(l, 3)
            for blk in range(6):
                ada_issue([blk])
                ada_compute([blk])
            ada_finish()

        ADA_INTERLEAVE = nl_p >= 4
        ada0_done = [False]
        if not ADA_INTERLEAVE:
            for l in range(1, 4):
                ada_all(l)
        if dbg == "ada":
            P.dma(yp_d[0:128, 0:192], mod[:].rearrange("p a b -> p (a b)"), out_dma=True)
            P.dma(yp_d[128:256, 0:64], coefA[:].rearrange("p a b c -> p (a b c)"), out_dma=True)
            P.dma(yp_d[256:384, 0:64], coefG[:].rearrange("p a b c -> p (a b c)"), out_dma=True)
            P.finish()
            return nc
        def rstd_from_ssq(ps_ssq, n, dim, out):
            P.act(out, ps_ssq, AF.Ln, bias=epsT[:, 0:1], scale=1.0 / dim)
            P.act(out, out, AF.Exp, scale=-0.5)

        def modulate_parts(hT, sq, l, m, t0, n):
            parts = []

            ns_ = sq.shape[1]

            def stats_a():
                for c in range(8):
                    P.act(sq[:, c, :n], xT[:, c, t0:t0 + n], AF.Square)

            def stats_b():
                pb = bank()
                for c in range(8):
                    P.mm(pb[:, :n], ones_bf[:, :], sq[:, c, :n], start=(c == 0), stop=(c == 7))
                rstd_from_ssq(pb[:, :n], n, 1024, rstd[:, :n])

            def stats():
                pb = bank()
                for c0 in range(0, 8, ns_):
                    for c in range(c0, c0 + ns_):
                        P.act(sq[:, c % ns_, :n], xT[:, c, t0:t0 + n], AF.Square)
                    for c in range(c0, c0 + ns_):
                        P.mm(pb[:, :n], ones_bf[:, :], sq[:, c % ns_, :n], start=(c == 0), stop=(c == 7))
                rstd_from_ssq(pb[:, :n], n, 1024, rstd[:, :n])
            if ns_ >= 8:
                parts.extend([stats_a, (lambda: None), stats_b])
            else:
                parts.append(stats)
            for c in range(8):
                def ap(c=c):
                    tf = tmpf[c % 2]
                    P.stt("dve", tf[:, :n], xT[:, c, t0:t0 + n], coefA[:, l, m, c:c + 1], rstd[:, :n], ALU.mult, ALU.mult)
                    P.act(hT[:, c, :n], tf[:, :n], AF.Identity, bias=coefB[:, l, m, c:c + 1], scale=1.0)
                parts.append(ap)
            return parts

        def modulate(hT, sq, l, m, t0, n):
            for f in modulate_parts(hT, sq, l, m, t0, n):
                f()

        def run_part(parts, k=1):
            for _ in range(k):
                if parts:
                    parts.pop(0)()

        def load_w(dst, src_rows_view, col0, ncols):
            for c in range(dst.shape[1]):
                P.dma(dst[:, c, :], src_rows_view[:, c, col0:col0 + ncols], eng="pool")

        def alloc_hs():
            hs = [(AR.bf("hT", [128, 8, 512]), AR.bf("sq", [128, 8, 512]))]
            try:
                a = AR.bf("hT1", [128, 8, 512])
                try:
                    b = AR.bf("sq1", [128, 8, 512])
                    hs.append((a, b))
                except RuntimeError:
                    AR.release("hT1")
            except RuntimeError:
                pass
            return hs

        def free_hs(hs):
            AR.release("hT")
            AR.release("sq")
            if len(hs) > 1:
                AR.release("hT1")
                AR.release("sq1")

        def stage3(l, m, NT, mbuf, wg, wout, hooks=None, nxt=None):
            oT = AR.f32("oT", [128, 8, 512])
            hs3 = alloc_hs()
            sg = [AR.bf("sg%d" % i, [128, 512]) for i in range(2)]
            if nxt is not None:
                prefetch_w1(*nxt)
            tiles = list(range(0, NT, 512))
            n = 512
            pipe = len(hs3) > 1
            if pipe:
                modulate(hs3[0][0], hs3[0][1], l, m, tiles[0], n)
            for ti, t0 in enumerate(tiles):
                if hooks and ti in hooks:
                    hooks[ti]()
                hT, sq = hs3[ti % len(hs3)]
                if not pipe:
                    modulate(hT, sq, l, m, t0, n)
                nparts = []
                if pipe and ti + 1 < len(tiles):
                    nh, nsq = hs3[(ti + 1) % 2]
                    nparts = modulate_parts(nh, nsq, l, m, tiles[ti + 1], n)
                for mc in range(8):
                    pb = bank()
                    for k in range(8):
                        P.mm(pb[:, :n], wg[:, k, mc * 128:(mc + 1) * 128], hT[:, k, :n], start=(k == 0), stop=(k == 7))
                    s_ = sg[mc % 2]
                    P.act(s_[:, :n], pb[:, :n], AF.Silu)
                    P.tt("dve" if mc % 2 else "pool", mbuf[:, mc, t0:t0 + n], mbuf[:, mc, t0:t0 + n], s_[:, :n], ALU.mult)
                    if mc == 1:
                        run_part(nparts)
                for dc in range(8):
                    pb = bank()
                    for k in range(8):
                        P.mm(pb[:, :n], wout[:, k, dc * 128:(dc + 1) * 128], mbuf[:, k, t0:t0 + n],
                             start=(k == 0), stop=(k == 7))
                    P.cp("dve", oT[:, dc, :n], pb[:, :n])
                    P.act(sq[:, dc, :n], pb[:, :n], AF.Square)
                    run_part(nparts)
                run_part(nparts, 16)
                pb = bank()
                for dc in range(8):
                    P.mm(pb[:, :n], ones_bf[:, :], sq[:, dc, :n], start=(dc == 0), stop=(dc == 7))
                rstd_from_ssq(pb[:, :n], n, 1024, rstd2[:, :n])
                for dc in range(8):
                    tf = tmpf[dc % 2]
                    P.stt("dve", tf[:, :n], oT[:, dc, :n], coefG[:, l, m, dc:dc + 1], rstd2[:, :n], ALU.mult, ALU.mult)
                    P.tt("pool", xT[:, dc, t0:t0 + n], xT[:, dc, t0:t0 + n], tf[:, :n], ALU.add)
            if hooks and "post" in hooks:
                hooks["post"]()
            free_hs(hs3)
            for nm in ("oT", "sg0", "sg1"):
                AR.release(nm)

        rope_rr = [0]

        def rope_a(src_ps, pr, n, scale, pm, cs, t0, rb):
            qc, qsn = rb[rope_rr[0] % len(rb)]
            rope_rr[0] += 1
            P.stt("dve", qc[0:pr, :n], src_ps[0:pr, :n], float(scale), cs[0:pr, 0, t0:t0 + n], ALU.mult, ALU.mult)
            P.stt("dve", qsn[0:pr, :n], src_ps[0:pr, :n], float(scale), cs[0:pr, 1, t0:t0 + n], ALU.mult, ALU.mult)

            def phase_b():
                p2 = bank("g")
                P.mm(p2[0:pr, :n], identb[0:pr, 0:pr], qc[0:pr, :n], start=True, stop=False)
                P.mm(p2[0:pr, :n], pm[:, :], qsn[0:pr, :n], start=False, stop=True)
                return p2
            return phase_b

        def rope_apply(src_ps, pr, n, scale, pm, cs, t0, rb):
            return rope_a(src_ps, pr, n, scale, pm, cs, t0, rb)()

        def alloc_rb():
            return [(AR.bf("rqc%d" % i, [128, 512]), AR.bf("rqs%d" % i, [128, 512])) for i in range(2)]

        def free_rb():
            for i in range(2):
                AR.release("rqc%d" % i)
                AR.release("rqs%d" % i)

        pref = {}

        def prefetch_w1(lnext, ctx_next, full):
            i2 = lnext // 2
            if (not full) and lnext % 2 == 1:
                return
            try:
                if lnext % 2 == 0:
                    wv = wine_d[i2].rearrange("(c p) n -> p c n", p=128)
                    a = AR.bf("w1a", [128, 8, 672])
                    load_w(a, wv, 0, 672)
                    pref["w1a"] = a
                    if full:
                        b_ = AR.bf("w1b", [128, 8, 512])
                        load_w(b_, wv, 1184, 512)
                        pref["w1b"] = b_
                else:
                    wv = wino_d[i2].rearrange("(c p) n -> p c n", p=128)
                    a = AR.bf("wq0", [128, 8, 512])
                    load_w(a, wv, 0, 512)
                    pref["wq0"] = a
                    if full:
                        b_ = AR.bf("wq1", [128, 8, 512])
                        load_w(b_, wv, 512, 512)
                        pref["wq1"] = b_
                        k_ = AR.bf("wk", [128, 8, 256])
                        load_w(k_, wv, 1024, 256)
                        pref["wk"] = k_
            except RuntimeError:
                pass

        def even_layer(l, m, NT, nseq, T, ctx, nxt=None):
            i = l // 2
            S = T + (256 if ctx else 0)
            KT = nseq * S
            nkc = S // 128
            wv_in = wine_d[i].rearrange("(c p) n -> p c n", p=128)
            w1a = pref.pop("w1a", None)
            if w1a is None:
                w1a = AR.bf("w1a", [128, 8, 672])
                load_w(w1a, wv_in, 0, 672)
            w1b = pref.pop("w1b", None)
            if w1b is None:
                w1b = AR.bf("w1b", [128, 8, 512])
                load_w(w1b, wv_in, 1184, 512)
            wp = AR.bf("wp", [128, 4, 128])
            amats = AR.bf("amats", [128, 20, 128])
            P.dma(amats[:], amats_d.rearrange("a k q -> k a q"), eng="pool")
            P.dma(wp[:], wpool_d[i].rearrange("g c e -> c g e"), eng="pool")
            mbuf = AR.bf("mbuf", [128, 8, NT])
            mo = AR.live["mbuf"][0]
            vtok = arena_t[:, mo:mo + (NT // 128) * 512].rearrange("p (j q) -> p j q", q=512)
            cqn = AR.bf("cqn", [128, 3, NT])
            ckvn = AR.bf("ckvn", [128, 2, KT])
            krT = AR.bf("krT", [96, KT])
            cs = None
            if ctx:
                cs = AR.bf("cs", [96, 2, 2048])
                P.dma(cs[:, 0, :], mlacs_d[0], eng="pool")
                P.dma(cs[:, 1, :], mlacs_d[1], eng="pool")
                rb = alloc_rb()
                cst = AR.f32("cst", [128, 2, 256 + 96])
                P.memset("pool", cst[:, :, 256:320], 0.0)
                for jj in range(2):
                    P.dma(cst[:, jj, 0:256], cckv_d[i, jj * 128:(jj + 1) * 128, :])
                    P.dma(cst[:, jj, 320:352], ckr_d[i, jj * 128:(jj + 1) * 128, :])
                for jj in range(2):
                    pb = bank()
                    for c in range(2):
                        P.tr(pb[:, c * 128:(c + 1) * 128], cst[:, jj, c * 128:(c + 1) * 128], ident[:, :])
                    P.tr(pb[0:96, 256:384], cst[:, jj, 256:352], ident[:, :])
                    for c in range(2):
                        P.cp("dve", ckvn[:, c, T + jj * 128:T + (jj + 1) * 128], pb[:, c * 128:(c + 1) * 128])
                    P.cp("dve", krT[64:96, T + jj * 128:T + (jj + 1) * 128], pb[64:96, 256:384])
                AR.release("cst")
            stg = None
            if not ctx:
                stg = [AR.f32("stg%d" % j, [128, 288]) for j in range(2)]
            junk = AR.f32("junk", [128, 256])
            ssq1 = AR.f32("ssq1", [128, 2])
            hs1 = alloc_hs()

            def kidx(t):
                return (t // T) * S + (t % T)

            pipe1 = len(hs1) > 1
            if pipe1:
                modulate(hs1[0][0], hs1[0][1], l, m, 0, 512)
            for t0 in range(0, NT, 512):
                n = 512
                hT, sq = hs1[(t0 // 512) % len(hs1)]
                if not pipe1:
                    modulate(hT, sq, l, m, t0, n)
                nparts = []
                if pipe1 and t0 + 512 < NT:
                    nh, nsq = hs1[((t0 // 512) + 1) % 2]
                    nparts = modulate_parts(nh, nsq, l, m, t0 + 512, 512)
                for oc in range(3):
                    pb = bank()
                    for k in range(8):
                        P.mm(pb[:, :n], w1a[:, k, oc * 128:(oc + 1) * 128], hT[:, k, :n], start=(k == 0), stop=(k == 7))
                    P.act(sq[:, oc, :n], pb[:, :n], AF.Square)
                    P.ts("dve", cqn[:, oc, t0:t0 + n], pb[:, :n], vecT[:, oc, 8 + i:9 + i], ALU.mult)
                    run_part(nparts)
                pb = bank()
                for oc in range(3):
                    P.mm(pb[:, :n], ones_bf[:, :], sq[:, oc, :n], start=(oc == 0), stop=(oc == 2))
                rstd_from_ssq(pb[:, :n], n, 384, rstd2[:, :n])
                for oc in range(3):
                    P.tt("pool" if oc == 1 else "dve", cqn[:, oc, t0:t0 + n], cqn[:, oc, t0:t0 + n], rstd2[:, :n], ALU.mult)
                pieces = [(t0, n)] if T >= 512 else [(t0 + a, T) for a in range(0, n, T)]
                for oc in range(2):
                    pb = bank()
                    for k in range(8):
                        P.mm(pb[:, :n], w1a[:, k, 384 + oc * 128:384 + (oc + 1) * 128], hT[:, k, :n],
                             start=(k == 0), stop=(k == 7))
                    P.act(sq[:, oc, :n], pb[:, :n], AF.Square)
                    for (ta, tn) in pieces:
                        P.ts("dve", ckvn[:, oc, kidx(ta):kidx(ta) + tn], pb[:, ta - t0:ta - t0 + tn],
                             vecT[:, oc, 10 + i:11 + i], ALU.mult)
                    run_part(nparts)
                pb = bank()
                for oc in range(2):
                    P.mm(pb[:, :n], ones_bf[:, :], sq[:, oc, :n], start=(oc == 0), stop=(oc == 1))
                rstd_from_ssq(pb[:, :n], n, 256, rstd2[:, :n])
                for oc in range(2):
                    for (ta, tn) in pieces:
                        P.tt("pool" if oc == 1 else "dve", ckvn[:, oc, kidx(ta):kidx(ta) + tn],
                             ckvn[:, oc, kidx(ta):kidx(ta) + tn], rstd2[:, ta - t0:ta - t0 + tn], ALU.mult)
                pb = bank()
                for k in range(8):
                    P.mm(pb[0:96, :n], w1a[:, k, 576:672], hT[:, k, :n], start=(k == 0), stop=(k == 7))
                if ctx:
                    p2 = rope_apply(pb, 96, n, 1.0, pm96, cs, t0, rb)
                    P.cp("act", krT[64:96, kidx(t0):kidx(t0) + n], p2[64:96, :n])
                else:
                    for (ta, tn) in pieces:
                        P.cp("dve", krT[64:96, kidx(ta):kidx(ta) + tn], pb[64:96, ta - t0:ta - t0 + tn])
                for j in range(n // 128):
                    pb = bank()
                    for k in range(8):
                        P.mm(pb[:, :], hT[:, k, j * 128:(j + 1) * 128], w1b[:, k, :], start=(k == 0), stop=(k == 7))
                    P.cp(evac_eng(), vtok[:, (t0 // 128) + j, :], pb[:, :])
                    run_part(nparts)
                run_part(nparts, 16)
                if not ctx:
                    for j in range(n // 128):
                        tok = t0 + j * 128
                        b = tok // T
                        pos = tok % T
                        pb = bank()
                        for k in range(8):
                            P.mm(pb[:, 0:288], hT[:, k, j * 128:(j + 1) * 128], w1a[:, k, 384:672],
                                 start=(k == 0), stop=(k == 7))
                        so = stg[j % 2]
                        P.act(junk[:, :], pb[:, 0:256], AF.Square)
                        P.op("dve", lambda e: e.reduce_sum(out=ssq1[:, 0:1], in_=junk[:, :], axis=mybir.AxisListType.X),
                             reads=[junk[:, :]], writes=[ssq1[:, 0:1]])
                        P.act(ssq1[:, 1:2], ssq1[:, 0:1], AF.Ln, bias=epsT[:, 0:1], scale=1.0 / 256)
                        P.act(ssq1[:, 1:2], ssq1[:, 1:2], AF.Exp, scale=-0.5)
                        P.stt("dve", so[:, 0:256], pb[:, 0:256], ssq1[:, 1:2], gkvb[:, i, :], ALU.mult, ALU.mult)
                        P.cp("dve", so[:, 256:288], pb[:, 256:288])
                        P.dma(sckv_d[b, i, pos:pos + 128, :], so[:, 0:256], out_dma=True)
                        P.dma(skr_d[b, i, pos:pos + 128, :], so[:, 256:288], out_dma=True)
            free_hs(hs1)
            pooled = [AR.bf("pooled%d" % j, [128, 512]) for j in range(2)]
            ppend = [None]
            ncs = T // 128
            for t0 in range(0, NT, 512):
                for g in range(4):
                    pb = bank()
                    for j in range(4):
                        ch = t0 // 128 + j
                        cin = ch % ncs
                        contrib = []
                        if cin > 0:
                            contrib.append((ch - 1, 3))
                        contrib.append((ch, 0 if cin == 0 else (2 if cin == ncs - 1 else 1)))
                        if cin < ncs - 1:
                            contrib.append((ch + 1, 4))
                        for ci, (src, kind) in enumerate(contrib):
                            P.mm(pb[:, j * 128:(j + 1) * 128], vtok[:, src, g * 128:(g + 1) * 128],
                                 amats[:, g * 5 + kind, :], start=(ci == 0), stop=(ci == len(contrib) - 1))
                    pl = pooled[g % 2]
                    P.cp(evac_eng(), pl[:, :], pb[:, :])
                    if ppend[0] is not None:
                        ppend[0]()

                    def _pw(pl=pl, g=g, t0=t0):
                        pb2 = bank()
                        P.mm(pb2[:, :], wp[:, g, :], pl[:, :])
                        P.ts("dve", mbuf[:, 4 + g, t0:t0 + 512], pb2[:, :], vecT[:, g, 12 + i:13 + i], ALU.mult)
                    ppend[0] = _pw
            if ppend[0] is not None:
                ppend[0]()
                ppend[0] = None
            for nm in ("pooled0", "pooled1", "junk", "ssq1", "w1a", "w1b", "wp", "amats"):
                AR.release(nm)
            if not ctx:
                AR.release("stg0")
                AR.release("stg1")
            if stages == 1 and l == trunc_l[0]:
                raise _Trunc()
            ada_hooks = None
            if ADA_INTERLEAVE and (not ctx) and l + 1 < 4:
                ada_alloc(l + 1, 2)
                ada_issue([0, 1])

                def _h0():
                    ada_compute([0, 1])
                    ada_issue([2, 3])

                def _h1():
                    ada_compute([2, 3])
                    ada_issue([4, 5])

                def _hp():
                    ada_compute([4, 5])
                    ada_finish()
                ada_hooks = {0: _h0, 1: _h1, "post": _hp}
            wuq = AR.bf("wuq", [128, 3, 768])
            wuk = AR.bf("wuk", [128, 2, 8, 64])
            wuv = AR.bf("wuv", [128, 2, 8, 64])
            P.dma(wuq[:], wuq_d[i].rearrange("(c p) n -> p c n", p=128), eng="pool")
            wukv_v = wukv_d[i].rearrange("(c p) (h t d) -> p c h t d", p=128, h=8, t=2)
            for c in range(2):
                P.dma(wuk[:, c, :, :], wukv_v[:, c, :, 0, :], eng="pool")
                P.dma(wuv[:, c, :, :], wukv_v[:, c, :, 1, :], eng="pool")
            def load_wg():
                wg_ = AR.bf("wg", [128, 8, 1024])
                load_w(wg_[:, :, 0:512], wv_in, 672, 512)
                load_w(wg_[:, :, 512:1024], wv_in, 1696, 512)
                return wg_

            def load_wout():
                wout_ = AR.bf("wout", [128, 8, 1024])
                load_w(wout_, woe_d[i].rearrange("(c p) n -> p c n", p=128), 0, 1024)
                return wout_
            wg = load_wg()
            if not ctx:
                wout = load_wout()
            NB = 2
            TT = nseq * T
            nkt = KT // 128
            qh = [AR.bf("qh%d" % b, [96, TT]) for b in range(NB)]
            kh = [AR.bf("kh%d" % b, [96, KT]) for b in range(NB)]
            vh = [AR.bf("vh%d" % b, [128, nkt, 128]) for b in range(NB)]
            pT = [AR.bf("pT%d" % b, [128, 512]) for b in range(4)]
            rc = AR.f32("rc", [64, 512])
            for b in range(NB):
                P.memset("pool", vh[b][:, :, 64:128], 1.0)
                P.cp("pool", kh[b][64:96, :], krT[64:96, 0:KT])
            pend = [None]

            def flush_pend():
                if pend[0] is not None:
                    pend[0]()
                    pend[0] = None

            def build_steps(h, b):
                st_ = []
                for ka in range(0, KT, 512):
                    def f(ka=ka):
                        kn = min(512, KT - ka)
                        pb = bank("g")
                        for c in range(2):
                            P.mm(pb[0:64, :kn], wuk[:, c, h, :], ckvn[:, c, ka:ka + kn], start=(c == 0), stop=(c == 1))
                        P.cp("dve", kh[b][0:64, ka:ka + kn], pb[0:64, :kn])
                    st_.append(f)
                for ja in range(0, nkt, 8):
                    def f(ja=ja):
                        jn = min(8, nkt - ja)
                        pb = bank("g")
                        for jj in range(jn):
                            for c in range(2):
                                P.mm(pb[:, jj * 64:(jj + 1) * 64], ckvn[:, c, (ja + jj) * 128:(ja + jj + 1) * 128],
                                     wuv[:, c, h, :], start=(c == 0), stop=(c == 1))
                        P.cp("dve", vh[b][:, ja:ja + jn, 0:64], pb[:, 0:jn * 64].rearrange("p (j d) -> p j d", d=64))
                    st_.append(f)
                for qa in range(0, TT, 512):
                    hold = {}

                    def f(qa=qa, hold=hold):
                        pb = bank("g")
                        for c in range(3):
                            P.mm(pb[0:96, :512], wuq[:, c, h * 96:(h + 1) * 96], cqn[:, c, qa:qa + 512],
                                 start=(c == 0), stop=(c == 2))
                        if ctx:
                            hold["b"] = rope_a(pb, 96, 512, MLA_SCALE, pm96, cs, qa, rb)
                        else:
                            P.ts("dve", qh[b][:, qa:qa + 512], pb[0:96, :512], MLA_SCALE, ALU.mult)
                    st_.append(f)
                    if ctx:
                        def f2(qa=qa, hold=hold):
                            p2 = hold["b"]()
                            P.cp("dve", qh[b][0:96, qa:qa + 512], p2[0:96, :512])
                        st_.append(f2)
                return st_

            for f in build_steps(0, 0):
                f()
            for h in range(8):
                b = h % NB
                half = h % 2
                inj = build_steps(h + 1, (h + 1) % NB) if h + 1 < 8 else []
                if ctx:
                    total_steps = (T // 512) * nkc
                    every = max(1, total_steps // (len(inj) + 1))
                    stepc = 0
                    for qa in range(0, T, 512):
                        po = bank("o")
                        sc_ps = {}

                        def issue_s(j):
                            ps_ = bank("s")
                            P.mm(ps_[:, :512], kh[b][:, j * 128:(j + 1) * 128], qh[b][:, qa:qa + 512])
                            sc_ps[j] = ps_

                        for j in range(3):
                            issue_s(j)
                        for j in range(nkc):
                            pt = pT[j % 4]
                            P.act(pt[:, :], sc_ps.pop(j)[:, :], AF.Exp)
                            if j + 3 < nkc:
                                issue_s(j + 3)
                            P.mm(po[:, :], vh[b][:, j, :], pt[:, :], start=(j == 0), stop=(j == nkc - 1))
                            stepc += 1
                            if inj and stepc % every == 0:
                                inj.pop(0)()
                        P.recip(rc[0:64, :], po[64:128, :])
                        P.tt("dve", mbuf[half * 64:(half + 1) * 64, h // 2, qa:qa + 512],
                             po[0:64, :], rc[0:64, :], ALU.mult)
                else:
                    for p in range(nseq // 2):
                        pts = []
                        for a in range(2):
                            sq_ = 2 * p + a
                            ps_ = bank("s")
                            for j in range(2):
                                P.mm(ps_[:, j * 256:(j + 1) * 256], kh[b][:, sq_ * 256 + j * 128:sq_ * 256 + (j + 1) * 128],
                                     qh[b][:, sq_ * 256:(sq_ + 1) * 256])
                            pt = pT[(2 * (p + h * (nseq // 2)) + a) % 4]
                            P.act(pt[:, :], ps_[:, :], AF.Exp)
                            pts.append(pt)
                        flush_pend()
                        for _ in range(3):
                            if inj:
                                inj.pop(0)()

                        def fin(p=p, pts=pts, b=b, h=h, half=half):
                            po = bank("o")
                            for a in range(2):
                                sq_ = 2 * p + a
                                for j in range(2):
                                    P.mm(po[:, a * 256:(a + 1) * 256], vh[b][:, 2 * sq_ + j, :], pts[a][:, j * 256:(j + 1) * 256],
                                         start=(j == 0), stop=(j == 1))
                            P.recip(rc[0:64, :], po[64:128, :])
                            P.tt("dve", mbuf[half * 64:(half + 1) * 64, h // 2, p * 512:(p + 1) * 512],
                                 po[0:64, :], rc[0:64, :], ALU.mult)
                        pend[0] = fin
                while inj:
                    inj.pop(0)()
            flush_pend()
            for nm in ["qh%d" % b for b in range(NB)] + ["kh%d" % b for b in range(NB)] + ["vh%d" % b for b in range(NB)] + \
                      ["pT%d" % b for b in range(4)] + ["rc", "wuq", "wuk", "wuv", "cqn", "ckvn", "krT"]:
                AR.release(nm)
            if ctx:
                AR.release("cs")
                free_rb()
                wout = load_wout()
            if stages == 2 and l == trunc_l[0]:
                raise _Trunc()
            stage3(l, m, NT, mbuf, wg, wout, ada_hooks, nxt)
            for nm in ("wg", "wout", "mbuf"):
                AR.release(nm)

        def odd_layer(l, m, NT, nseq, T, ctx, nxt=None):
            i = l // 2
            S = T + (256 if ctx else 0)
            KT = nseq * S
            nkc = S // 128
            wv_in = wino_d[i].rearrange("(c p) n -> p c n", p=128)
            qT = AR.bf("mbuf", [128, 8, NT])
            kd = AR.bf("kd", [128, 4, KT])
            va = AR.bf("va", [128, KT // 128, 4, 128])
            wq0 = pref.pop("wq0", None)
            if wq0 is None:
                wq0 = AR.bf("wq0", [128, 8, 512])
                load_w(wq0, wv_in, 0, 512)
            wq1 = pref.pop("wq1", None)
            if wq1 is None:
                wq1 = AR.bf("wq1", [128, 8, 512])
                load_w(wq1, wv_in, 512, 512)
            wqs = [wq0, wq1]
            wk = pref.pop("wk", None)
            if wk is None:
                wk = AR.bf("wk", [128, 8, 256])
                load_w(wk, wv_in, 1024, 256)
            nkv = 256 if ctx else 512
            wkv = AR.bf("wkv", [128, 8, nkv])
            load_w(wkv, wv_in, 1536 - nkv, nkv)
            P.memset("pool", va[:, :, :, 64:128], 1.0)
            cs = None
            qs = None
            if ctx:
                cs = AR.bf("cs", [128, 2, 2048])
                P.dma(cs[:, 0, :], swacs_d[0], eng="pool")
                P.dma(cs[:, 1, :], swacs_d[1], eng="pool")
                rb = alloc_rb()
                cst = AR.f32("cst", [128, 2, 256])
                cdup = AR.f32("cdup", [128, 4, 128])
                for jj in range(2):
                    P.dma(cst[:, jj, :], cv_d[i, jj * 128:(jj + 1) * 128, :])
                for jj in range(2):
                    P.cp("dve", va[:, T // 128 + jj, :, 0:64], cst[:, jj, :].rearrange("p (h d) -> p h d", h=4))
                for jj in range(2):
                    P.dma(cst[:, jj, :], ck_d[i, jj * 128:(jj + 1) * 128, :])
                for jj in range(2):
                    P.cp("dve", cdup[:, :, 0:64], cst[:, jj, :].rearrange("p (h d) -> p h d", h=4))
                    P.cp("pool", cdup[:, :, 64:128], cst[:, jj, :].rearrange("p (h d) -> p h d", h=4))
                    pb = bank()
                    for kvh in range(4):
                        P.tr(pb[:, kvh * 128:(kvh + 1) * 128], cdup[:, kvh, :], ident[:, :])
                    for kvh in range(4):
                        P.cp("dve", kd[:, kvh, T + jj * 128:T + (jj + 1) * 128], pb[:, kvh * 128:(kvh + 1) * 128])
                AR.release("cst")
                AR.release("cdup")
            stg = None
            if not ctx:
                stg = [AR.f32("stg%d" % j, [128, 512]) for j in range(2)]
            if ctx:
                hs1 = [(AR.bf("hT", [128, 8, 512]), AR.bf("sq", [128, 4, 512])),
                       (AR.bf("hT1", [128, 8, 512]), AR.bf("sq1", [128, 4, 512]))]
            else:
                hs1 = alloc_hs()

            def kidx(t):
                return (t // T) * S + (t % T)

            pipe1 = len(hs1) > 1
            rpend = [None]
            if pipe1:
                modulate(hs1[0][0], hs1[0][1], l, m, 0, 512)
            for t0 in range(0, NT, 512):
                n = 512
                hT, sq = hs1[(t0 // 512) % len(hs1)]
                if not pipe1:
                    modulate(hT, sq, l, m, t0, n)
                pieces = [(t0, n)] if T >= 512 else [(t0 + a, T) for a in range(0, n, T)]
                nparts = []
                if pipe1 and t0 + 512 < NT:
                    nh, nsq = hs1[((t0 // 512) + 1) % 2]
                    nparts = modulate_parts(nh, nsq, l, m, t0 + 512, 512)
                for oc in range(8):
                    pb = bank()
                    for k in range(8):
                        P.mm(pb[:, :n], wqs[oc // 4][:, k, (oc % 4) * 128:(oc % 4 + 1) * 128], hT[:, k, :n],
                             start=(k == 0), stop=(k == 7))
                    if ctx:
                        pb_fn = rope_a(pb, 128, n, SWA_SCALE, pm128, cs, t0, rb)
                        if rpend[0] is not None:
                            rpend[0]()

                        def _fin_q(pb_fn=pb_fn, oc=oc, t0=t0, n=n):
                            p2 = pb_fn()
                            P.cp("act", qT[:, oc, t0:t0 + n], p2[:, :n])
                        rpend[0] = _fin_q
                    else:
                        P.ts("dve", qT[:, oc, t0:t0 + n], pb[:, :n], SWA_SCALE, ALU.mult)
                    run_part(nparts)
                for kc in range(2):
                    pb = bank()
                    for k in range(8):
                        P.mm(pb[:, :n], wk[:, k, kc * 128:(kc + 1) * 128], hT[:, k, :n], start=(k == 0), stop=(k == 7))
                    def _copies(srcs, kc=kc):
                        for (src, so, sn, ko) in srcs:
                            for hh in range(2):
                                for dh in range(2):
                                    P.cp("act" if dh != hh else "dve",
                                         kd[dh * 64:(dh + 1) * 64, 2 * kc + hh, ko:ko + sn],
                                         src[hh * 64:(hh + 1) * 64, so:so + sn])
                    if ctx:
                        pb_fn = rope_a(pb, 128, n, 1.0, pm128, cs, t0, rb)
                        if rpend[0] is not None:
                            rpend[0]()

                        def _fin_k(pb_fn=pb_fn, t0=t0, n=n, _copies=_copies):
                            p2 = pb_fn()
                            _copies([(p2, 0, n, kidx(t0))])
                        rpend[0] = _fin_k
                    else:
                        _copies([(pb, ta - t0, tn, kidx(ta)) for (ta, tn) in pieces])
                run_part(nparts, 16)
                for j in range(n // 128):
                    tok = t0 + j * 128
                    pb = bank()
                    for k in range(8):
                        P.mm(pb[:, 0:nkv], hT[:, k, j * 128:(j + 1) * 128], wkv[:, k, :], start=(k == 0), stop=(k == 7))
                    P.cp("dve", va[:, kidx(tok) // 128, :, 0:64], pb[:, nkv - 256:nkv].rearrange("p (h d) -> p h d", h=4))
                    if not ctx:
                        b = tok // T
                        pos = tok % T
                        so = stg[j % 2]
                        P.cp("act", so[:, :], pb[:, :])
                        P.dma(sk_d[b, i, pos:pos + 128, :], so[:, 0:256], out_dma=True)
                        P.dma(sv_d[b, i, pos:pos + 128, :], so[:, 256:512], out_dma=True)
                    if j == 0 and rpend[0] is not None:
                        rpend[0]()
                        rpend[0] = None
            free_hs(hs1)
            for nm in ("wq0", "wq1", "wk", "wkv"):
                AR.release(nm)
            if ctx:
                AR.release("cs")
                free_rb()
            else:
                AR.release("stg0")
                AR.release("stg1")
            if stages == 1 and l == trunc_l[0]:
                raise _Trunc()
            ada_hooks = None
            if ADA_INTERLEAVE and (not ctx) and l + 1 < 4:
                ada_alloc(l + 1, 2)
                ada_issue([0, 1])

                def _h0():
                    ada_compute([0, 1])
                    ada_issue([2, 3])

                def _h1():
                    ada_compute([2, 3])
                    ada_issue([4, 5])

                def _hp():
                    ada_compute([4, 5])
                    ada_finish()
                ada_hooks = {0: _h0, 1: _h1, "post": _hp}
            vaB = AR.bf("vaB", [128, KT // 128, 4, 128])
            wg = AR.bf("wg", [128, 8, 1024])
            wout = AR.bf("wout", [128, 8, 1024])
            load_w(wg, wv_in, 1536, 1024)
            load_w(wout, woo_d[i].rearrange("(c p) n -> p c n", p=128), 0, 1024)
            pT = [AR.bf("pT%d" % b, [128, 512]) for b in range(4)]
            rc = AR.f32("rc", [128, 512])
            P.memset("pool", vaB[:, :, :, 0:64], 1.0)
            nch_all = KT // 128
            for ja in range(0, nch_all, 6):
                jb = min(nch_all, ja + 6)
                P.cp("pool", vaB[:, ja:jb, :, 64:128], va[:, ja:jb, :, 0:64])
            mbuf = qT
            pools["o4"] = [4, 5, 6, 7]
            rr["o4"] = 0
            pend = []

            def flush_pend(keep=0):
                while len(pend) > keep:
                    pend.pop(0)()

            def finalize_pair(c, pos, t_lo):
                hA, hB = 2 * c, 2 * c + 1
                P.ts("dve", rc[0:64, :], pos[0][64:128, :], esink[64:128, i * 16 + hA:i * 16 + hA + 1], ALU.add)
                P.ts("dve", rc[64:128, :], pos[1][0:64, :], esink[0:64, i * 16 + hB:i * 16 + hB + 1], ALU.add)
                P.recip(rc[:, :], rc[:, :])
                P.tt("dve", mbuf[0:64, c, t_lo:t_lo + 512], pos[0][0:64, :], rc[0:64, :], ALU.mult)
                P.tt("dve", mbuf[64:128, c, t_lo:t_lo + 512], pos[1][64:128, :], rc[64:128, :], ALU.mult)

            ucount = 0
            for c in range(8):
                kvh = c // 2
                ntile = (nseq // 2) if not ctx else (T // 512)
                for tix in range(ntile):
                    pos = []
                    if ctx:
                        qt = tix
                        q0 = qt * 512
                        pos = [bank("o4"), bank("o4")]
                        jobs = []
                        for jj in range(2):
                            jobs.append((T // 128 + jj, 0, 512, []))
                        for j in range(4 * qt - 1, 4 * qt + 5):
                            if j < 0 or j >= T // 128:
                                continue
                            nlo = max(4 * qt, j - 1)
                            nhi = min(4 * qt + 3, j + 1)
                            mk = []
                            for nb in range(nlo, nhi + 1):
                                if nb == j - 1:
                                    mk.append((nb, 0))
                                elif nb == j + 1:
                                    mk.append((nb, 1))
                            jobs.append((j, (nlo - 4 * qt) * 128, (nhi - 4 * qt + 1) * 128, mk))
                        sc_ps = {}

                        def issue_s(ji):
                            kc, lo, hi, mk_ = jobs[ji]
                            for half in (0, 1):
                                r0, r1 = half * 64, half * 64 + 64
                                ps_ = bank("s")
                                P.mm(ps_[:, lo:hi], kd[r0:r1, kvh, kc * 128:(kc + 1) * 128], qT[r0:r1, c, q0 + lo:q0 + hi],
                                     start=True, stop=(len(mk_) == 0))
                                sc_ps[(ji, half)] = ps_
                            for half in (0, 1):
                                for mi, (nb, which) in enumerate(mk_):
                                    cl = (nb - 4 * qt) * 128
                                    P.mm(sc_ps[(ji, half)][:, cl:cl + 128], identb[:, :], masks[:, which, :],
                                         start=False, stop=(mi == len(mk_) - 1))

                        for ji in range(min(2, len(jobs))):
                            issue_s(ji)
                        for ji, (kc, lo, hi, mk) in enumerate(jobs):
                            pts = []
                            for half in (0, 1):
                                pt = pT[(2 * ji + half) % 4]
                                P.act(pt[:, lo:hi], sc_ps.pop((ji, half))[:, lo:hi], AF.Exp)
                                pts.append(pt)
                            if ji + 2 < len(jobs):
                                issue_s(ji + 2)
                            for half in (0, 1):
                                vsrc = va if half == 0 else vaB
                                P.mm(pos[half][:, lo:hi], vsrc[:, kc, kvh, :], pts[half][:, lo:hi],
                                     start=(ji == 0), stop=(ji == len(jobs) - 1))
                            if ji == 2:
                                flush_pend()
                        pend.append(lambda c=c, pos=pos, q0=q0: finalize_pair(c, pos, q0))
                        continue
                    for half in (0, 1):
                        r0, r1 = half * 64, half * 64 + 64
                        vsrc = va if half == 0 else vaB
                        p = tix
                        pts = []
                        for a in range(2):
                            sq_ = 2 * p + a
                            ps_ = bank("s")
                            for j in range(2):
                                P.mm(ps_[:, j * 256:(j + 1) * 256], kd[r0:r1, kvh, sq_ * 256 + j * 128:sq_ * 256 + (j + 1) * 128],
                                     qT[r0:r1, c, sq_ * 256:(sq_ + 1) * 256])
                            pt = pT[(2 * ucount + a) % 4]
                            P.act(pt[:, :], ps_[:, :], AF.Exp)
                            pts.append(pt)
                        ucount += 1
                        flush_pend()
                        po = bank("o4")
                        pos.append(po)

                        def pv(p=p, pts=pts, po=po, vsrc=vsrc, kvh=kvh):
                            for a in range(2):
                                sq_ = 2 * p + a
                                for j in range(2):
                                    P.mm(po[:, a * 256:(a + 1) * 256], vsrc[:, 2 * sq_ + j, kvh, :], pts[a][:, j * 256:(j + 1) * 256],
                                         start=(j == 0), stop=(j == 1))
                        pend.append(pv)
                        if half == 1:
                            pend.append(lambda c=c, pos=pos, p=p: finalize_pair(c, pos, p * 512))
            flush_pend()
            AR.release("vaB")
            for nm in ["pT%d" % b for b in range(4)] + ["rc", "kd", "va"]:
                AR.release(nm)
            if stages == 2 and l == trunc_l[0]:
                raise _Trunc()
            stage3(l, m, NT, mbuf, wg, wout, ada_hooks, nxt)
            for nm in ("wg", "wout", "mbuf"):
                AR.release(nm)

        def load_x(src, NT):
            stage = AR.f32("xstage", [128, 4, 1024])
            for t0 in range(0, NT, 512):
                for j in range(4):
                    P.dma(stage[:, j, :], src[t0 + j * 128:t0 + (j + 1) * 128, :])
                for c in range(8):
                    pb = bank()
                    for j in range(4):
                        P.tr(pb[:, j * 128:(j + 1) * 128], stage[:, j, c * 128:(c + 1) * 128], ident[:, :])
                    P.cp(evac_eng(), xT[:, c, t0:t0 + 512], pb[:, :])
            AR.release("xstage")

        def store_x(dst, NT):
            stage = AR.f32("xstage", [128, 4, 1024])
            for t0 in range(0, NT, 512):
                for j in range(4):
                    for hb in range(2):
                        pb = bank()
                        for cc in range(4):
                            c = hb * 4 + cc
                            P.tr(pb[:, cc * 128:(cc + 1) * 128], xT[:, c, t0 + j * 128:t0 + (j + 1) * 128], ident[:, :])
                        P.cp(evac_eng(), stage[:, j, hb * 512:(hb + 1) * 512], pb[:, :])
                    P.dma(dst[t0 + j * 128:t0 + (j + 1) * 128, :], stage[:, j, :], out_dma=True)
            AR.release("xstage")

        trunc_l = [-1]
        try:
          for (src, dst, m, NT, nseq, T, ctx) in ((xp_d, yp_d, 0, 1024, 4, 256, False),
                                                   (xs_d, ys_d, 1, 2048, 1, 2048, True)):
              nl = nl_p if not ctx else nl_s
              if nl < 0:
                  continue
              load_x(src, NT)
              if not ada0_done[0]:
                  ada_all(0)
                  ada0_done[0] = True
              trunc_l[0] = nl - 1 if ((ctx and nl_s >= 0) or (not ctx and nl_s < 0)) else -1
              for l in range(nl):
                  if l + 1 < nl:
                      nxt = (l + 1, ctx, not ctx)
                  elif (not ctx) and nl_s > 0:
                      nxt = (0, True, True)
                  else:
                      nxt = None
                  if l % 2 == 0:
                      even_layer(l, m, NT, nseq, T, ctx, nxt)
                  else:
                      odd_layer(l, m, NT, nseq, T, ctx, nxt)
              store_x(dst, NT)

        except _Trunc:
            P.dma(yp_d[0:128, 0:512], rstd[:, :], out_dma=True)

        P.finish()
        build_program.stats = {e: len(P.ops[e]) for e in ENGS}
        build_program.peak = AR.peak
    return nc


_CONSTS = None


def _consts():
    global _CONSTS
    if _CONSTS is None:
        mla, swa = _rope_tables()
        pm96, pm128 = _perm_mats()
        _CONSTS = dict(ident=np.eye(128, dtype=np.float32), pm96=pm96, pm128=pm128, masks=_masks(),
                       amats=_pool_mats().reshape(20, 128, 128), mla_cs=mla, swa_cs=swa)
    return _CONSTS


def kernel(x_prompt, x_sample, cache_ckv, cache_krope, cache_k, cache_v, c, c_ctx,
           ada_w, ada_b, norm_pre, norm_post,
           mla_w_in, mla_g_qn, mla_g_kvn, mla_w_uq, mla_w_ukv, pool_w, pool_scale, mixa_w_out,
           swa_w_in, swa_sink, swa_w_out):
    f = lambda a: np.ascontiguousarray(np.asarray(a, dtype=np.float32))
    x_prompt, x_sample = f(x_prompt), f(x_sample)
    cache_ckv, cache_krope, cache_k, cache_v = f(cache_ckv), f(cache_krope), f(cache_k), f(cache_v)
    c, c_ctx = f(c), f(c_ctx)
    vecs = np.zeros((14, 1024), np.float32)
    vecs[0:4] = f(norm_pre)
    vecs[4:8] = f(norm_post)
    vecs[8:10, :384] = f(mla_g_qn)
    vecs[10:12, :256] = f(mla_g_kvn)
    vecs[12:14, :512] = f(pool_scale)
    shared = dict(ada_w=f(ada_w), ada_b=f(ada_b), vecs=vecs, gkvn=f(mla_g_kvn), sink=f(swa_sink).reshape(32),
                  w_in_e=f(mla_w_in), w_uq=f(mla_w_uq), w_ukv=f(mla_w_ukv), w_pool=f(pool_w), w_out_e=f(mixa_w_out),
                  w_in_o=f(swa_w_in), w_out_o=f(swa_w_out))
    shared.update(_consts())
    in_maps = []
    for i in range(8):
        d = dict(shared)
        d["xp"] = x_prompt[4 * i:4 * i + 4].reshape(1024, 1024)
        d["xs"] = x_sample[i]
        d["cckv"] = cache_ckv[i]
        d["ckr"] = cache_krope[i]
        d["ck"] = cache_k[i].reshape(2, 256, 256)
        d["cv"] = cache_v[i].reshape(2, 256, 256)
        d["cc"] = np.stack([c_ctx, c[i]], axis=0)
        in_maps.append(d)
    nc = build_program()
    res = run_bass_kernel_spmd(nc, in_maps, core_ids=list(range(8)))
    R = res.results
    y_prompt = np.concatenate([r["yp"].reshape(4, 256, 1024) for r in R], axis=0)
    y_sample = np.stack([r["ys"] for r in R], axis=0)
    st_ckv = np.concatenate([r["st_ckv"] for r in R], axis=0)
    st_kr = np.concatenate([r["st_kr"] for r in R], axis=0)
    st_k = np.concatenate([r["st_k"].reshape(4, 2, 256, 4, 64) for r in R], axis=0)
    st_v = np.concatenate([r["st_v"].reshape(4, 2, 256, 4, 64) for r in R], axis=0)
    return (y_prompt.astype(np.float32), y_sample.astype(np.float32), st_ckv.astype(np.float32),
            st_kr.astype(np.float32), st_k.astype(np.float32), st_v.astype(np.float32))
```

```python
import numpy as np
from contextlib import ExitStack
import concourse.bass as bass
import concourse.mybir as mybir
from concourse.bass_utils import run_bass_kernel_spmd

F32 = mybir.dt.float32
BF16 = mybir.dt.bfloat16
AF = mybir.ActivationFunctionType
ALU = mybir.AluOpType

ENGS = ("pe", "act", "dve", "pool", "sp")
SAME_ENGINE_RAW = True
N_DMA_SEMS = 24
BUCKET = 4096

D = 1024
EPS = 1e-6
MLA_SCALE = 96 ** -0.5
SWA_SCALE = 64 ** -0.5


class Prog:
    def __init__(self, nc, stack):
        self.nc = nc
        self.stack = stack
        self.ops = {e: [] for e in ENGS}
        self.recs = {}
        self.known = {e: {} for e in ENGS}
        self.eng_sem = {e: stack.enter_context(nc.semaphore("s_" + e)) for e in ENGS}
        self.dma_sems = [stack.enter_context(nc.semaphore("s_dma%d" % i)) for i in range(N_DMA_SEMS)]
        self.dma_cnt = [0] * N_DMA_SEMS
        self.dma_rr = 0
        self.dma_rr_pool = 0
        self.out_dma_tokens = []

    def sb(self, name, shape, dtype):
        return self.stack.enter_context(self.nc.sbuf_tensor("sb_" + name, list(shape), dtype))

    def ps(self, name, shape, dtype=F32):
        return self.stack.enter_context(self.nc.psum_tensor("pp_" + name, list(shape), dtype))

    @staticmethod
    def _box(ap):
        shp = list(ap.tensor.shape)
        row = 1
        for s in shp[1:]:
            row *= s
        isz = mybir.dt.size(ap.dtype)
        off = int(ap.offset)
        dims = ap.ap
        p0 = off // row
        f0 = off % row
        pstep, pcnt = dims[0]
        if pstep == row or (pcnt == 1 and len(dims) > 1):
            p1 = p0 + pcnt
            rest = dims[1:]
        elif pstep == 0:
            p1 = p0 + 1
            rest = dims[1:]
        else:
            p1 = p0 + 1
            rest = dims
        ext = 0
        for st, cn in rest:
            ext += abs(st) * (cn - 1)
        return (p0, p1, f0 * isz, (f0 + ext + 1) * isz)

    @staticmethod
    def _tracked(ap):
        return str(ap.space).upper() in ("SB", "PSUM")

    def op(self, eng, fn, reads=(), writes=(), dma=False, out_dma=False):
        idx = len(self.ops[eng])
        waits = []
        kn = self.known[eng]

        def need(tok, raw=False):
            if tok[0] == 'e':
                if tok[1] == eng and not (raw and SAME_ENGINE_RAW and eng != "pe"):
                    return
                key = tok[1]
            else:
                key = ('d', tok[1])
            if kn.get(key, -1) >= tok[2]:
                return
            kn[key] = tok[2]
            waits.append(tok)

        if dma:
            if eng == "pool":
                k = 8 + self.dma_rr_pool
                self.dma_rr_pool = (self.dma_rr_pool + 1) % (N_DMA_SEMS - 8)
            else:
                k = self.dma_rr
                self.dma_rr = (self.dma_rr + 1) % 8
            if self.dma_cnt[k] > 0:
                need(('d', k, self.dma_cnt[k] * 16))
            self.dma_cnt[k] += 1
            mytok = ('d', k, self.dma_cnt[k] * 16)
        else:
            mytok = ('e', eng, idx)

        rb = [(ap.name, self._box(ap)) for ap in reads if self._tracked(ap) and str(ap.space).upper() != "PSUM"]
        wb = [(ap.name, self._box(ap)) for ap in writes if self._tracked(ap) and str(ap.space).upper() != "PSUM"]
        for ap in list(reads) + list(writes):
            if str(ap.space).upper() == "PSUM":
                ent = (ap.name, (0, 128, 0, 1 << 20))
                if ent not in wb:
                    wb.append(ent)
        for name, box in rb:
            tr = self.recs.get(name)
            if tr is None:
                continue
            for b in range(box[2] // BUCKET, (box[3] - 1) // BUCKET + 1):
                for r in tr.get(b, ()):
                    if r[3] and r[2]:
                        rbx = r[0]
                        if rbx[0] < box[1] and box[0] < rbx[1] and rbx[2] < box[3] and box[2] < rbx[3]:
                            need(r[1], True)
        for name, box in wb:
            tr = self.recs.get(name)
            if tr is None:
                continue
            for b in range(box[2] // BUCKET, (box[3] - 1) // BUCKET + 1):
                lst = tr.get(b)
                if not lst:
                    continue
                for r in lst:
                    if r[3]:
                        rbx = r[0]
                        if rbx[0] < box[1] and box[0] < rbx[1] and rbx[2] < box[3] and box[2] < rbx[3]:
                            need(r[1])
                            if box[0] <= rbx[0] and box[1] >= rbx[1] and box[2] <= rbx[2] and box[3] >= rbx[3]:
                                r[3] = False
                tr[b] = [r for r in lst if r[3]]
        for name, box in rb:
            tr = self.recs.setdefault(name, {})
            rec = [box, mytok, False, True]
            for b in range(box[2] // BUCKET, (box[3] - 1) // BUCKET + 1):
                lst = tr.setdefault(b, [])
                if not dma:
                    for r in lst:
                        if r[3] and (not r[2]) and r[1][0] == 'e' and r[1][1] == eng and r[0] == box:
                            r[3] = False
                lst.append(rec)
        for name, box in wb:
            tr = self.recs.setdefault(name, {})
            rec = [box, mytok, True, True]
            for b in range(box[2] // BUCKET, (box[3] - 1) // BUCKET + 1):
                tr.setdefault(b, []).append(rec)
        o = dict(fn=fn, waits=waits, tok=mytok, marked=False)
        self.ops[eng].append(o)
        if out_dma:
            self.out_dma_tokens.append(mytok)
        return o

    def dma(self, out, in_, eng="sp", out_dma=False):
        return self.op(eng, lambda e: e.dma_start(out=out, in_=in_), reads=[in_], writes=[out],
                       dma=True, out_dma=out_dma)

    def mm(self, out, lhsT, rhs, start=True, stop=True):
        return self.op("pe", lambda e: e.matmul(out, lhsT, rhs, start=start, stop=stop),
                       reads=[lhsT, rhs] + ([] if start else [out]), writes=[out])

    def tr(self, out, in_, ident):
        return self.op("pe", lambda e: e.transpose(out, in_, ident), reads=[in_, ident], writes=[out])

    def act(self, out, in_, func, bias=None, scale=None, accum=None):
        kw = {}
        rd = [in_]
        wr = [out]
        if bias is not None:
            kw["bias"] = bias
            if not isinstance(bias, (int, float)):
                rd.append(bias)
        if scale is not None:
            kw["scale"] = scale
            if not isinstance(scale, (int, float)):
                rd.append(scale)
        if accum is not None:
            kw["accum_out"] = accum
            wr.append(accum)
        return self.op("act", lambda e: e.activation(out=out, in_=in_, func=func, **kw), reads=rd, writes=wr)

    def cp(self, eng, out, in_):
        if eng == "act":
            return self.op("act", lambda e: e.copy(out=out, in_=in_), reads=[in_], writes=[out])
        return self.op(eng, lambda e: e.tensor_copy(out=out, in_=in_), reads=[in_], writes=[out])

    def tt(self, eng, out, in0, in1, op):
        return self.op(eng, lambda e: e.tensor_tensor(out=out, in0=in0, in1=in1, op=op), reads=[in0, in1], writes=[out])

    def ts(self, eng, out, in0, s1, op0, s2=None, op1=None):
        rd = [in0]
        if not isinstance(s1, (int, float)):
            rd.append(s1)
        if s2 is not None and not isinstance(s2, (int, float)):
            rd.append(s2)
        if op1 is None:
            return self.op(eng, lambda e: e.tensor_scalar(out=out, in0=in0, scalar1=s1, scalar2=None, op0=op0),
                           reads=rd, writes=[out])
        return self.op(eng, lambda e: e.tensor_scalar(out=out, in0=in0, scalar1=s1, scalar2=s2, op0=op0, op1=op1),
                       reads=rd, writes=[out])

    def stt(self, eng, out, in0, scalar, in1, op0, op1):
        rd = [in0, in1]
        if not isinstance(scalar, (int, float)):
            rd.append(scalar)
        return self.op(eng, lambda e: e.scalar_tensor_tensor(out=out, in0=in0, scalar=scalar, in1=in1, op0=op0, op1=op1),
                       reads=rd, writes=[out])

    def memset(self, eng, out, val):
        return self.op(eng, lambda e: e.memset(out, val), writes=[out])

    def recip(self, out, in_):
        return self.op("dve", lambda e: e.reciprocal(out=out, in_=in_), reads=[in_], writes=[out])

    def finish(self):
        nc = self.nc
        for e in ENGS:
            for o in self.ops[e]:
                for tok in o["waits"]:
                    if tok[0] == 'e':
                        self.ops[tok[1]][tok[2]]["marked"] = True
        cnt_at = {}
        for e in ENGS:
            c = 0
            arr = []
            for o in self.ops[e]:
                if o["marked"]:
                    c += 1
                arr.append(c)
            cnt_at[e] = arr
        fw = {}
        for tok in self.out_dma_tokens:
            fw[tok[1]] = max(fw.get(tok[1], 0), tok[2])

        with nc.Block() as block:
            def emit(ename, engobj):
                for o in self.ops[ename]:
                    for tok in o["waits"]:
                        if tok[0] == 'e':
                            engobj.wait_ge(self.eng_sem[tok[1]], cnt_at[tok[1]][tok[2]])
                        else:
                            engobj.wait_ge(self.dma_sems[tok[1]], tok[2])
                    ins = o["fn"](engobj)
                    if o["tok"][0] == 'd':
                        ins.then_inc(self.dma_sems[o["tok"][1]], 16)
                    elif o["marked"]:
                        ins.then_inc(self.eng_sem[ename], 1)
                if ename == "sp":
                    for k, v in fw.items():
                        engobj.wait_ge(self.dma_sems[k], v)

            @block.sync
            def _(sync):
                emit("sp", sync)

            @block.tensor
            def _(tensor):
                emit("pe", tensor)

            @block.scalar
            def _(scalar):
                emit("act", scalar)

            @block.vector
            def _(vector):
                emit("dve", vector)

            @block.gpsimd
            def _(gpsimd):
                emit("pool", gpsimd)


class _Trunc(Exception):
    pass


class Arena:
    def __init__(self, tensor, nelem):
        self.t = tensor
        self.n = nelem
        self.free = [(0, nelem)]
        self.live = {}
        self.peak = 0

    def alloc(self, name, nelem_bf16):
        nelem_bf16 = (nelem_bf16 + 15) // 16 * 16
        for i, (o, s) in enumerate(self.free):
            if s >= nelem_bf16:
                if s == nelem_bf16:
                    self.free.pop(i)
                else:
                    self.free[i] = (o + nelem_bf16, s - nelem_bf16)
                self.live[name] = (o, nelem_bf16)
                used = self.n - sum(s for _, s in self.free)
                self.peak = max(self.peak, used)
                return o
        raise RuntimeError("arena OOM allocating %s (%d); live=%s free=%s" % (name, nelem_bf16, self.live, self.free))

    def release(self, name):
        o, s = self.live.pop(name)
        self.free.append((o, s))
        self.free.sort()
        merged = []
        for o, s in self.free:
            if merged and merged[-1][0] + merged[-1][1] == o:
                merged[-1] = (merged[-1][0], merged[-1][1] + s)
            else:
                merged.append((o, s))
        self.free = merged

    def bf(self, name, shape):
        n = 1
        for s in shape[1:]:
            n *= s
        o = self.alloc(name, n)
        v = self.t[0:shape[0], o:o + n]
        if len(shape) == 3:
            v = v.rearrange("p (a b) -> p a b", a=shape[1])
        elif len(shape) == 4:
            v = v.rearrange("p (a b c) -> p a b c", a=shape[1], b=shape[2])
        return v

    def f32(self, name, shape):
        n = 1
        for s in shape[1:]:
            n *= s
        o = self.alloc(name, 2 * n)
        v = self.t[0:shape[0], o:o + 2 * n].bitcast(F32)
        if len(shape) == 3:
            v = v.rearrange("p (a b) -> p a b", a=shape[1])
        elif len(shape) == 4:
            v = v.rearrange("p (a b c) -> p a b c", a=shape[1], b=shape[2])
        return v


def _rope_tables():
    t = np.arange(2048)
    row = (t // 64).astype(np.float64)
    col = (t % 64).astype(np.float64)
    mla = np.zeros((2, 96, 2048), np.float32)
    mla[0, 0:64] = 1.0
    for r in range(32):
        pos = row if r < 16 else col
        i = r % 16
        f = i % 8
        inv = 10000.0 ** (-(2.0 * f) / 16.0)
        ang = pos * inv
        mla[0, 64 + r] = np.cos(ang)
        mla[1, 64 + r] = np.sin(ang) if i < 8 else -np.sin(ang)
    swa = np.zeros((2, 128, 2048), np.float32)
    for p in range(128):
        d = p % 64
        pos = row if d < 32 else col
        i = d % 32
        f = i % 16
        inv = 10000.0 ** (-(2.0 * f) / 32.0)
        ang = pos * inv
        swa[0, p] = np.cos(ang)
        swa[1, p] = np.sin(ang) if i < 16 else -np.sin(ang)
    return mla, swa


def _perm_mats():
    pm96 = np.zeros((96, 96), np.float32)
    for r in range(32):
        i = r % 16
        partner = r + 8 if i < 8 else r - 8
        pm96[64 + partner, 64 + r] = 1.0
    pm128 = np.zeros((128, 128), np.float32)
    for m in range(128):
        i = m % 32
        partner = m + 16 if i < 16 else m - 16
        pm128[partner, m] = 1.0
    return pm96, pm128


def _pool_mats():
    T = 384
    out = np.zeros((4, 5, 128, 128), np.float32)
    for g, w in enumerate((2, 4, 8, 16)):
        A = np.zeros((T, T), np.float64)
        for t in range(T):
            lo = min(max(t - w // 2, 0), T)
            hi = min(max(t + w // 2, 0), T)
            A[lo:hi, t] = 1.0 / (hi - lo)
            A[t, t] -= 1.0
        out[g, 0] = A[0:128, 0:128]
        out[g, 1] = A[128:256, 128:256]
        out[g, 2] = A[256:384, 256:384]
        out[g, 3] = A[0:128, 128:256]
        out[g, 4] = A[128:256, 0:128]
    return out


def _masks():
    k = np.arange(128)[:, None]
    q = np.arange(128)[None, :]
    m = np.zeros((2, 128, 128), np.float32)
    m[0] = np.where(k <= q, 0.0, -30000.0)
    m[1] = np.where(q <= k, 0.0, -30000.0)
    return m


def build_program(nl_p=4, nl_s=4, stages=3, dbg=None):
    nc = bass.Bass("TRN2", target_bir_lowering=False)

    def din(name, shape):
        return nc.dram_tensor(name, list(shape), F32, kind="ExternalInput").ap()

    def dout(name, shape):
        return nc.dram_tensor(name, list(shape), F32, kind="ExternalOutput").ap()

    xp_d = din("xp", [1024, 1024])
    xs_d = din("xs", [2048, 1024])
    cckv_d = din("cckv", [2, 256, 256])
    ckr_d = din("ckr", [2, 256, 32])
    ck_d = din("ck", [2, 256, 256])
    cv_d = din("cv", [2, 256, 256])
    cc_d = din("cc", [2, 1024])
    adaw_d = din("ada_w", [4, 1024, 3072])
    adab_d = din("ada_b", [4, 3072])
    vecs_d = din("vecs", [14, 1024])
    gkvn_d = din("gkvn", [2, 256])
    sink_d = din("sink", [32])
    wine_d = din("w_in_e", [2, 1024, 2208])
    wuq_d = din("w_uq", [2, 384, 768])
    wukv_d = din("w_ukv", [2, 256, 1024])
    wpool_d = din("w_pool", [2, 4, 128, 128])
    woe_d = din("w_out_e", [2, 1024, 1024])
    wino_d = din("w_in_o", [2, 1024, 2560])
    woo_d = din("w_out_o", [2, 1024, 1024])
    ident_d = din("ident", [128, 128])
    pm96_d = din("pm96", [96, 96])
    pm128_d = din("pm128", [128, 128])
    masks_d = din("masks", [2, 128, 128])
    amats_d = din("amats", [20, 128, 128])
    mlacs_d = din("mla_cs", [2, 96, 2048])
    swacs_d = din("swa_cs", [2, 128, 2048])

    yp_d = dout("yp", [1024, 1024])
    ys_d = dout("ys", [2048, 1024])
    sckv_d = dout("st_ckv", [4, 2, 256, 256])
    skr_d = dout("st_kr", [4, 2, 256, 32])
    sk_d = dout("st_k", [4, 2, 256, 256])
    sv_d = dout("st_v", [4, 2, 256, 256])

    with ExitStack() as st:
        P = Prog(nc, st)
        xT = P.sb("xT", [128, 8, 2048], F32)
        ident = P.sb("ident", [128, 128], F32)
        ones_bf = P.sb("ones_bf", [128, 128], BF16)
        identb = P.sb("identb", [128, 128], BF16)
        scb = P.sb("scb", [128, 8, 2], BF16)
        pm96 = P.sb("pm96", [96, 96], BF16)
        pm128 = P.sb("pm128", [128, 128], BF16)
        masks = P.sb("masks", [128, 2, 128], BF16)
        epsT = P.sb("epsT", [128, 1], F32)
        mod = P.sb("mod", [128, 4, 48], F32)
        vecT = P.sb("vecT", [128, 8, 32], F32)
        coefA = P.sb("coefA", [128, 4, 2, 8], F32)
        coefB = P.sb("coefB", [128, 4, 2, 8], F32)
        coefG = P.sb("coefG", [128, 4, 2, 8], F32)
        gkvb = P.sb("gkvb", [128, 2, 256], F32)
        esink = P.sb("esink", [128, 32], F32)
        rstd = P.sb("rstd", [128, 512], F32)
        rstd2 = P.sb("rstd2", [128, 512], F32)
        tmpf = [P.sb("tmpf%d" % i, [128, 512], F32) for i in range(2)]
        ARENA_N = 64 * 1024
        arena_t = P.sb("arena", [128, ARENA_N], BF16)
        AR = Arena(arena_t, ARENA_N)
        banks = [P.ps("ps%d" % i, [128, 512], F32) for i in range(8)]
        rr = {"all": 0, "s": 0, "o": 0, "g": 0}
        pools = {"all": list(range(8)), "s": [0, 1, 2, 3], "o": [4, 5], "g": [6, 7]}

        def bank(pool="all"):
            lst = pools[pool]
            b = banks[lst[rr[pool] % len(lst)]]
            rr[pool] += 1
            return b

        evac_rr = [0]

        def evac_eng():
            evac_rr[0] += 1
            return "dve" if evac_rr[0] % 2 else "act"

        P.dma(ident[:], ident_d)
        P.dma(identb[:], ident_d, eng="pool")
        P.dma(pm96[:], pm96_d, eng="pool")
        P.dma(pm128[:], pm128_d, eng="pool")
        P.dma(masks[:], masks_d.rearrange("a k q -> k a q"), eng="pool")
        P.dma(gkvb[:].rearrange("p a n -> p (a n)"), gkvn_d.rearrange("a n -> (a n)").partition_broadcast(128))
        P.dma(esink[:], sink_d.partition_broadcast(128))
        P.memset("dve", ones_bf[:], 1.0)
        P.memset("dve", epsT[:], EPS)
        P.act(esink[:], esink[:], AF.Exp)

        if dbg == "consts":
            P.dma(yp_d[0:128, 0:32], esink[:], out_dma=True)
            P.finish()
            return nc
        vst = AR.f32("vst", [32, 1024])
        P.memset("dve", vst[:], 0.0)
        P.dma(vst[0:14, :], vecs_d)
        pv = bank()
        for c in range(8):
            P.tr(pv[:, c * 32:(c + 1) * 32], vst[0:32, c * 128:(c + 1) * 128], ident[0:32, 0:32])
        for c in range(8):
            P.cp("dve", vecT[:, c, :], pv[:, c * 32:(c + 1) * 32])
        AR.release("vst")

        if dbg == "vec":
            P.dma(yp_d[0:128, 0:256], vecT[:].rearrange("p a b -> p (a b)"), out_dma=True)
            P.finish()
            return nc
        ccT = AR.f32("ccT", [128, 2, 8])
        for m in range(2):
            P.dma(ccT[:, m, :], cc_d[m].rearrange("(p c) -> p c", c=8))
        for m in range(2):
            P.act(scb[:, :, m], ccT[:, m, :], AF.Silu)
        AR.release("ccT")
        modv = mod[:].rearrange("p l (j c m) -> p l j c m", j=3, c=8)
        ada_state = {}

        def ada_alloc(l, nbuf):
            ada_state["l"] = l
            ada_state["adab"] = AR.bf("adab", [1, 3072])
            ada_state["bufs"] = [AR.bf("adw%d" % i, [128, 8, 512]) for i in range(nbuf)]
            ada_state["nbuf"] = nbuf
            P.dma(ada_state["adab"][:], adab_d[l:l + 1, :], eng="pool")

        def ada_issue(blks):
            l = ada_state["l"]
            wv = adaw_d[l].rearrange("(p c) n -> p c n", c=8)
            for blk in blks:
                wt = ada_state["bufs"][blk % ada_state["nbuf"]]
                P.dma(wt[:], wv[:, :, blk * 512:(blk + 1) * 512], eng="pool")

        def ada_compute(blks):
            l = ada_state["l"]
            adab = ada_state["adab"]
            for blk in blks:
                wt = ada_state["bufs"][blk % ada_state["nbuf"]]
                pm = bank()
                for nci in range(4):
                    nch = blk * 4 + nci
                    for c in range(8):
                        P.mm(pm[:, 2 * nci:2 * nci + 2], wt[:, c, nci * 128:(nci + 1) * 128], scb[:, c, :],
                             start=(c == 0), stop=False)
                    P.mm(pm[:, 2 * nci:2 * nci + 2], adab[0:1, nch * 128:(nch + 1) * 128], ones_bf[0:1, 0:2],
                         start=False, stop=True)
                P.cp("dve", mod[:, l, blk * 8:(blk + 1) * 8], pm[:, 0:8])

        def ada_finish():
            l = ada_state["l"]
            for m in range(2):
                P.stt("dve", coefA[:, l, m, :], modv[:, l, 1, :, m], 1.0, vecT[:, :, l], ALU.add, ALU.mult)
                P.cp("dve", coefB[:, l, m, :], modv[:, l, 0, :, m])
                P.tt("dve", coefG[:, l, m, :], modv[:, l, 2, :, m], vecT[:, :, 4 + l], ALU.mult)
            for i in range(ada_state["nbuf"]):
                AR.release("adw%d" % i)
            AR.release("adab")
            ada_state.clear()

        def ada_all(l):
            ada_alloc(l, 3)
            for blk in range(6):
                ada_issue([blk])
                ada_compute([blk])
            ada_finish()

        ADA_INTERLEAVE = nl_p >= 4
        ada0_done = [False]
        if not ADA_INTERLEAVE:
            for l in range(1, 4):
                ada_all(l)
        if dbg == "ada":
            P.dma(yp_d[0:128, 0:192], mod[:].rearrange("p a b -> p (a b)"), out_dma=True)
            P.dma(yp_d[128:256, 0:64], coefA[:].rearrange("p a b c -> p (a b c)"), out_dma=True)
            P.dma(yp_d[256:384, 0:64], coefG[:].rearrange("p a b c -> p (a b c)"), out_dma=True)
            P.finish()
            return nc
        def rstd_from_ssq(ps_ssq, n, dim, out):
            P.act(out, ps_ssq, AF.Ln, bias=epsT[:, 0:1], scale=1.0 / dim)
            P.act(out, out, AF.Exp, scale=-0.5)

        def modulate_parts(hT, sq, l, m, t0, n):
            parts = []

            ns_ = sq.shape[1]

            def stats_a():
                for c in range(8):
                    P.act(sq[:, c, :n], xT[:, c, t0:t0 + n], AF.Square)

            def stats_b():
                pb = bank()
                for c in range(8):
                    P.mm(pb[:, :n], ones_bf[:, :], sq[:, c, :n], start=(c == 0), stop=(c == 7))
                rstd_from_ssq(pb[:, :n], n, 1024, rstd[:, :n])

            def stats():
                pb = bank()
                for c0 in range(0, 8, ns_):
                    for c in range(c0, c0 + ns_):
                        P.act(sq[:, c % ns_, :n], xT[:, c, t0:t0 + n], AF.Square)
                    for c in range(c0, c0 + ns_):
                        P.mm(pb[:, :n], ones_bf[:, :], sq[:, c % ns_, :n], start=(c == 0), stop=(c == 7))
                rstd_from_ssq(pb[:, :n], n, 1024, rstd[:, :n])
            if ns_ >= 8:
                parts.extend([stats_a, (lambda: None), stats_b])
            else:
                parts.append(stats)
            for c in range(8):
                def ap(c=c):
                    tf = tmpf[c % 2]
                    P.stt("dve", tf[:, :n], xT[:, c, t0:t0 + n], coefA[:, l, m, c:c + 1], rstd[:, :n], ALU.mult, ALU.mult)
                    P.act(hT[:, c, :n], tf[:, :n], AF.Identity, bias=coefB[:, l, m, c:c + 1], scale=1.0)
                parts.append(ap)
            return parts

        def modulate(hT, sq, l, m, t0, n):
            for f in modulate_parts(hT, sq, l, m, t0, n):
                f()

        def run_part(parts, k=1):
            for _ in range(k):
                if parts:
                    parts.pop(0)()

        def load_w(dst, src_rows_view, col0, ncols):
            for c in range(dst.shape[1]):
                P.dma(dst[:, c, :], src_rows_view[:, c, col0:col0 + ncols], eng="pool")

        def alloc_hs():
            hs = [(AR.bf("hT", [128, 8, 512]), AR.bf("sq", [128, 8, 512]))]
            try:
                a = AR.bf("hT1", [128, 8, 512])
                try:
                    b = AR.bf("sq1", [128, 8, 512])
                    hs.append((a, b))
                except RuntimeError:
                    AR.release("hT1")
            except RuntimeError:
                pass
            return hs

        def free_hs(hs):
            AR.release("hT")
            AR.release("sq")
            if len(hs) > 1:
                AR.release("hT1")
                AR.release("sq1")

        def stage3(l, m, NT, mbuf, wg, wout, hooks=None, nxt=None):
            oT = AR.f32("oT", [128, 8, 512])
            hs3 = alloc_hs()
            sg = [AR.bf("sg%d" % i, [128, 512]) for i in range(2)]
            if nxt is not None:
                prefetch_w1(*nxt)
            tiles = list(range(0, NT, 512))
            n = 512
            pipe = len(hs3) > 1
            if pipe:
                modulate(hs3[0][0], hs3[0][1], l, m, tiles[0], n)
            for ti, t0 in enumerate(tiles):
                if hooks and ti in hooks:
                    hooks[ti]()
                hT, sq = hs3[ti % len(hs3)]
                if not pipe:
                    modulate(hT, sq, l, m, t0, n)
                nparts = []
                if pipe and ti + 1 < len(tiles):
                    nh, nsq = hs3[(ti + 1) % 2]
                    nparts = modulate_parts(nh, nsq, l, m, tiles[ti + 1], n)
                for mc in range(8):
                    pb = bank()
                    for k in range(8):
                        P.mm(pb[:, :n], wg[:, k, mc * 128:(mc + 1) * 128], hT[:, k, :n], start=(k == 0), stop=(k == 7))
                    s_ = sg[mc % 2]
                    P.act(s_[:, :n], pb[:, :n], AF.Silu)
                    P.tt("dve" if mc % 2 else "pool", mbuf[:, mc, t0:t0 + n], mbuf[:, mc, t0:t0 + n], s_[:, :n], ALU.mult)
                    if mc == 1:
                        run_part(nparts)
                for dc in range(8):
                    pb = bank()
                    for k in range(8):
                        P.mm(pb[:, :n], wout[:, k, dc * 128:(dc + 1) * 128], mbuf[:, k, t0:t0 + n],
                             start=(k == 0), stop=(k == 7))
                    P.cp("dve", oT[:, dc, :n], pb[:, :n])
                    P.act(sq[:, dc, :n], pb[:, :n], AF.Square)
                    run_part(nparts)
                run_part(nparts, 16)
                pb = bank()
                for dc in range(8):
                    P.mm(pb[:, :n], ones_bf[:, :], sq[:, dc, :n], start=(dc == 0), stop=(dc == 7))
                rstd_from_ssq(pb[:, :n], n, 1024, rstd2[:, :n])
                for dc in range(8):
                    tf = tmpf[dc % 2]
                    P.stt("dve", tf[:, :n], oT[:, dc, :n], coefG[:, l, m, dc:dc + 1], rstd2[:, :n], ALU.mult, ALU.mult)
                    P.tt("pool", xT[:, dc, t0:t0 + n], xT[:, dc, t0:t0 + n], tf[:, :n], ALU.add)
            if hooks and "post" in hooks:
                hooks["post"]()
            free_hs(hs3)
            for nm in ("oT", "sg0", "sg1"):
                AR.release(nm)

        rope_rr = [0]

        def rope_a(src_ps, pr, n, scale, pm, cs, t0, rb):
            qc, qsn = rb[rope_rr[0] % len(rb)]
            rope_rr[0] += 1
            P.stt("dve", qc[0:pr, :n], src_ps[0:pr, :n], float(scale), cs[0:pr, 0, t0:t0 + n], ALU.mult, ALU.mult)
            P.stt("dve", qsn[0:pr, :n], src_ps[0:pr, :n], float(scale), cs[0:pr, 1, t0:t0 + n], ALU.mult, ALU.mult)

            def phase_b():
                p2 = bank("g")
                P.mm(p2[0:pr, :n], identb[0:pr, 0:pr], qc[0:pr, :n], start=True, stop=False)
                P.mm(p2[0:pr, :n], pm[:, :], qsn[0:pr, :n], start=False, stop=True)
                return p2
            return phase_b

        def rope_apply(src_ps, pr, n, scale, pm, cs, t0, rb):
            return rope_a(src_ps, pr, n, scale, pm, cs, t0, rb)()

        def alloc_rb():
            return [(AR.bf("rqc%d" % i, [128, 512]), AR.bf("rqs%d" % i, [128, 512])) for i in range(2)]

        def free_rb():
            for i in range(2):
                AR.release("rqc%d" % i)
                AR.release("rqs%d" % i)

        pref = {}

        def prefetch_w1(lnext, ctx_next, full):
            i2 = lnext // 2
            if (not full) and lnext % 2 == 1:
                return
            try:
                if lnext % 2 == 0:
                    wv = wine_d[i2].rearrange("(c p) n -> p c n", p=128)
                    a = AR.bf("w1a", [128, 8, 672])
                    load_w(a, wv, 0, 672)
                    pref["w1a"] = a
                    if full:
                        b_ = AR.bf("w1b", [128, 8, 512])
                        load_w(b_, wv, 1184, 512)
                        pref["w1b"] = b_
                else:
                    wv = wino_d[i2].rearrange("(c p) n -> p c n", p=128)
                    a = AR.bf("wq0", [128, 8, 512])
                    load_w(a, wv, 0, 512)
                    pref["wq0"] = a
                    if full:
                        b_ = AR.bf("wq1", [128, 8, 512])
                        load_w(b_, wv, 512, 512)
                        pref["wq1"] = b_
                        k_ = AR.bf("wk", [128, 8, 256])
                        load_w(k_, wv, 1024, 256)
                        pref["wk"] = k_
            except RuntimeError:
                pass

        def even_layer(l, m, NT, nseq, T, ctx, nxt=None):
            i = l // 2
            S = T + (256 if ctx else 0)
            KT = nseq * S
            nkc = S // 128
            wv_in = wine_d[i].rearrange("(c p) n -> p c n", p=128)
            w1a = pref.pop("w1a", None)
            if w1a is None:
                w1a = AR.bf("w1a", [128, 8, 672])
                load_w(w1a, wv_in, 0, 672)
            w1b = pref.pop("w1b", None)
            if w1b is None:
                w1b = AR.bf("w1b", [128, 8, 512])
                load_w(w1b, wv_in, 1184, 512)
            wp = AR.bf("wp", [128, 4, 128])
            amats = AR.bf("amats", [128, 20, 128])
            P.dma(amats[:], amats_d.rearrange("a k q -> k a q"), eng="pool")
            P.dma(wp[:], wpool_d[i].rearrange("g c e -> c g e"), eng="pool")
            mbuf = AR.bf("mbuf", [128, 8, NT])
            mo = AR.live["mbuf"][0]
            vtok = arena_t[:, mo:mo + (NT // 128) * 512].rearrange("p (j q) -> p j q", q=512)
            cqn = AR.bf("cqn", [128, 3, NT])
            ckvn = AR.bf("ckvn", [128, 2, KT])
            krT = AR.bf("krT", [96, KT])
            cs = None
            if ctx:
                cs = AR.bf("cs", [96, 2, 2048])
                P.dma(cs[:, 0, :], mlacs_d[0], eng="pool")
                P.dma(cs[:, 1, :], mlacs_d[1], eng="pool")
                rb = alloc_rb()
                cst = AR.f32("cst", [128, 2, 256 + 96])
                P.memset("pool", cst[:, :, 256:320], 0.0)
                for jj in range(2):
                    P.dma(cst[:, jj, 0:256], cckv_d[i, jj * 128:(jj + 1) * 128, :])
                    P.dma(cst[:, jj, 320:352], ckr_d[i, jj * 128:(jj + 1) * 128, :])
                for jj in range(2):
                    pb = bank()
                    for c in range(2):
                        P.tr(pb[:, c * 128:(c + 1) * 128], cst[:, jj, c * 128:(c + 1) * 128], ident[:, :])
                    P.tr(pb[0:96, 256:384], cst[:, jj, 256:352], ident[:, :])
                    for c in range(2):
                        P.cp("dve", ckvn[:, c, T + jj * 128:T + (jj + 1) * 128], pb[:, c * 128:(c + 1) * 128])
                    P.cp("dve", krT[64:96, T + jj * 128:T + (jj + 1) * 128], pb[64:96, 256:384])
                AR.release("cst")
            stg = None
            if not ctx:
                stg = [AR.f32("stg%d" % j, [128, 288]) for j in range(2)]
            junk = AR.f32("junk", [128, 256])
            ssq1 = AR.f32("ssq1", [128, 2])
            hs1 = alloc_hs()

            def kidx(t):
                return (t // T) * S + (t % T)

            pipe1 = len(hs1) > 1
            if pipe1:
                modulate(hs1[0][0], hs1[0][1], l, m, 0, 512)
            for t0 in range(0, NT, 512):
                n = 512
                hT, sq = hs1[(t0 // 512) % len(hs1)]
                if not pipe1:
                    modulate(hT, sq, l, m, t0, n)
                nparts = []
                if pipe1 and t0 + 512 < NT:
                    nh, nsq = hs1[((t0 // 512) + 1) % 2]
                    nparts = modulate_parts(nh, nsq, l, m, t0 + 512, 512)
                for oc in range(3):
                    pb = bank()
                    for k in range(8):
                        P.mm(pb[:, :n], w1a[:, k, oc * 128:(oc + 1) * 128], hT[:, k, :n], start=(k == 0), stop=(k == 7))
                    P.act(sq[:, oc, :n], pb[:, :n], AF.Square)
                    P.ts("dve", cqn[:, oc, t0:t0 + n], pb[:, :n], vecT[:, oc, 8 + i:9 + i], ALU.mult)
                    run_part(nparts)
                pb = bank()
                for oc in range(3):
                    P.mm(pb[:, :n], ones_bf[:, :], sq[:, oc, :n], start=(oc == 0), stop=(oc == 2))
                rstd_from_ssq(pb[:, :n], n, 384, rstd2[:, :n])
                for oc in range(3):
                    P.tt("pool" if oc == 1 else "dve", cqn[:, oc, t0:t0 + n], cqn[:, oc, t0:t0 + n], rstd2[:, :n], ALU.mult)
                pieces = [(t0, n)] if T >= 512 else [(t0 + a, T) for a in range(0, n, T)]
                for oc in range(2):
                    pb = bank()
                    for k in range(8):
                        P.mm(pb[:, :n], w1a[:, k, 384 + oc * 128:384 + (oc + 1) * 128], hT[:, k, :n],
                             start=(k == 0), stop=(k == 7))
                    P.act(sq[:, oc, :n], pb[:, :n], AF.Square)
                    for (ta, tn) in pieces:
                        P.ts("dve", ckvn[:, oc, kidx(ta):kidx(ta) + tn], pb[:, ta - t0:ta - t0 + tn],
                             vecT[:, oc, 10 + i:11 + i], ALU.mult)
                    run_part(nparts)
                pb = bank()
                for oc in range(2):
                    P.mm(pb[:, :n], ones_bf[:, :], sq[:, oc, :n], start=(oc == 0), stop=(oc == 1))
                rstd_from_ssq(pb[:, :n], n, 256, rstd2[:, :n])
                for oc in range(2):
                    for (ta, tn) in pieces:
                        P.tt("pool" if oc == 1 else "dve", ckvn[:, oc, kidx(ta):kidx(ta) + tn],
                             ckvn[:, oc, kidx(ta):kidx(ta) + tn], rstd2[:, ta - t0:ta - t0 + tn], ALU.mult)
                pb = bank()
                for k in range(8):
                    P.mm(pb[0:96, :n], w1a[:, k, 576:672], hT[:, k, :n], start=(k == 0), stop=(k == 7))
                if ctx:
                    p2 = rope_apply(pb, 96, n, 1.0, pm96, cs, t0, rb)
                    P.cp("act", krT[64:96, kidx(t0):kidx(t0) + n], p2[64:96, :n])
                else:
                    for (ta, tn) in pieces:
                        P.cp("dve", krT[64:96, kidx(ta):kidx(ta) + tn], pb[64:96, ta - t0:ta - t0 + tn])
                for j in range(n // 128):
                    pb = bank()
                    for k in range(8):
                        P.mm(pb[:, :], hT[:, k, j * 128:(j + 1) * 128], w1b[:, k, :], start=(k == 0), stop=(k == 7))
                    P.cp(evac_eng(), vtok[:, (t0 // 128) + j, :], pb[:, :])
                    run_part(nparts)
                run_part(nparts, 16)
                if not ctx:
                    for j in range(n // 128):
                        tok = t0 + j * 128
                        b = tok // T
                        pos = tok % T
                        pb = bank()
                        for k in range(8):
                            P.mm(pb[:, 0:288], hT[:, k, j * 128:(j + 1) * 128], w1a[:, k, 384:672],
                                 start=(k == 0), stop=(k == 7))
                        so = stg[j % 2]
                        P.act(junk[:, :], pb[:, 0:256], AF.Square)
                        P.op("dve", lambda e: e.reduce_sum(out=ssq1[:, 0:1], in_=junk[:, :], axis=mybir.AxisListType.X),
                             reads=[junk[:, :]], writes=[ssq1[:, 0:1]])
                        P.act(ssq1[:, 1:2], ssq1[:, 0:1], AF.Ln, bias=epsT[:, 0:1], scale=1.0 / 256)
                        P.act(ssq1[:, 1:2], ssq1[:, 1:2], AF.Exp, scale=-0.5)
                        P.stt("dve", so[:, 0:256], pb[:, 0:256], ssq1[:, 1:2], gkvb[:, i, :], ALU.mult, ALU.mult)
                        P.cp("dve", so[:, 256:288], pb[:, 256:288])
                        P.dma(sckv_d[b, i, pos:pos + 128, :], so[:, 0:256], out_dma=True)
                        P.dma(skr_d[b, i, pos:pos + 128, :], so[:, 256:288], out_dma=True)
            free_hs(hs1)
            pooled = [AR.bf("pooled%d" % j, [128, 512]) for j in range(2)]
            ppend = [None]
            ncs = T // 128
            for t0 in range(0, NT, 512):
                for g in range(4):
                    pb = bank()
                    for j in range(4):
                        ch = t0 // 128 + j
                        cin = ch % ncs
                        contrib = []
                        if cin > 0:
                            contrib.append((ch - 1, 3))
                        contrib.append((ch, 0 if cin == 0 else (2 if cin == ncs - 1 else 1)))
                        if cin < ncs - 1:
                            contrib.append((ch + 1, 4))
                        for ci, (src, kind) in enumerate(contrib):
                            P.mm(pb[:, j * 128:(j + 1) * 128], vtok[:, src, g * 128:(g + 1) * 128],
                                 amats[:, g * 5 + kind, :], start=(ci == 0), stop=(ci == len(contrib) - 1))
                    pl = pooled[g % 2]
                    P.cp(evac_eng(), pl[:, :], pb[:, :])
                    if ppend[0] is not None:
                        ppend[0]()

                    def _pw(pl=pl, g=g, t0=t0):
                        pb2 = bank()
                        P.mm(pb2[:, :], wp[:, g, :], pl[:, :])
                        P.ts("dve", mbuf[:, 4 + g, t0:t0 + 512], pb2[:, :], vecT[:, g, 12 + i:13 + i], ALU.mult)
                    ppend[0] = _pw
            if ppend[0] is not None:
                ppend[0]()
                ppend[0] = None
            for nm in ("pooled0", "pooled1", "junk", "ssq1", "w1a", "w1b", "wp", "amats"):
                AR.release(nm)
            if not ctx:
                AR.release("stg0")
                AR.release("stg1")
            if stages == 1 and l == trunc_l[0]:
                raise _Trunc()
            ada_hooks = None
            if ADA_INTERLEAVE and (not ctx) and l + 1 < 4:
                ada_alloc(l + 1, 2)
                ada_issue([0, 1])

                def _h0():
                    ada_compute([0, 1])
                    ada_issue([2, 3])

                def _h1():
                    ada_compute([2, 3])
                    ada_issue([4, 5])

                def _hp():
                    ada_compute([4, 5])
                    ada_finish()
                ada_hooks = {0: _h0, 1: _h1, "post": _hp}
            wuq = AR.bf("wuq", [128, 3, 768])
            wuk = AR.bf("wuk", [128, 2, 8, 64])
            wuv = AR.bf("wuv", [128, 2, 8, 64])
            P.dma(wuq[:], wuq_d[i].rearrange("(c p) n -> p c n", p=128), eng="pool")
            wukv_v = wukv_d[i].rearrange("(c p) (h t d) -> p c h t d", p=128, h=8, t=2)
            for c in range(2):
                P.dma(wuk[:, c, :, :], wukv_v[:, c, :, 0, :], eng="pool")
                P.dma(wuv[:, c, :, :], wukv_v[:, c, :, 1, :], eng="pool")
            def load_wg():
                wg_ = AR.bf("wg", [128, 8, 1024])
                load_w(wg_[:, :, 0:512], wv_in, 672, 512)
                load_w(wg_[:, :, 512:1024], wv_in, 1696, 512)
                return wg_

            def load_wout():
                wout_ = AR.bf("wout", [128, 8, 1024])
                load_w(wout_, woe_d[i].rearrange("(c p) n -> p c n", p=128), 0, 1024)
                return wout_
            wg = load_wg()
            if not ctx:
                wout = load_wout()
            NB = 2
            TT = nseq * T
            nkt = KT // 128
            qh = [AR.bf("qh%d" % b, [96, TT]) for b in range(NB)]
            kh = [AR.bf("kh%d" % b, [96, KT]) for b in range(NB)]
            vh = [AR.bf("vh%d" % b, [128, nkt, 128]) for b in range(NB)]
            pT = [AR.bf("pT%d" % b, [128, 512]) for b in range(4)]
            rc = AR.f32("rc", [64, 512])
            for b in range(NB):
                P.memset("pool", vh[b][:, :, 64:128], 1.0)
                P.cp("pool", kh[b][64:96, :], krT[64:96, 0:KT])
            pend = [None]

            def flush_pend():
                if pend[0] is not None:
                    pend[0]()
                    pend[0] = None

            def build_steps(h, b):
                st_ = []
                for ka in range(0, KT, 512):
                    def f(ka=ka):
                        kn = min(512, KT - ka)
                        pb = bank("g")
                        for c in range(2):
                            P.mm(pb[0:64, :kn], wuk[:, c, h, :], ckvn[:, c, ka:ka + kn], start=(c == 0), stop=(c == 1))
                        P.cp("dve" if ctx else "act", kh[b][0:64, ka:ka + kn], pb[0:64, :kn])
                    st_.append(f)
                for ja in range(0, nkt, 8):
                    def f(ja=ja):
                        jn = min(8, nkt - ja)
                        pb = bank("g")
                        for jj in range(jn):
                            for c in range(2):
                                P.mm(pb[:, jj * 64:(jj + 1) * 64], ckvn[:, c, (ja + jj) * 128:(ja + jj + 1) * 128],
                                     wuv[:, c, h, :], start=(c == 0), stop=(c == 1))
                        P.cp("dve" if ctx else "act", vh[b][:, ja:ja + jn, 0:64], pb[:, 0:jn * 64].rearrange("p (j d) -> p j d", d=64))
                    st_.append(f)
                for qa in range(0, TT, 512):
                    hold = {}

                    def f(qa=qa, hold=hold):
                        pb = bank("g")
                        for c in range(3):
                            P.mm(pb[0:96, :512], wuq[:, c, h * 96:(h + 1) * 96], cqn[:, c, qa:qa + 512],
                                 start=(c == 0), stop=(c == 2))
                        if ctx:
                            hold["b"] = rope_a(pb, 96, 512, MLA_SCALE, pm96, cs, qa, rb)
                        else:
                            P.act(qh[b][:, qa:qa + 512], pb[0:96, :512], AF.Copy, scale=float(MLA_SCALE))
                    st_.append(f)
                    if ctx:
                        def f2(qa=qa, hold=hold):
                            p2 = hold["b"]()
                            P.cp("dve", qh[b][0:96, qa:qa + 512], p2[0:96, :512])
                        st_.append(f2)
                return st_

            for f in build_steps(0, 0):
                f()
            for h in range(8):
                b = h % NB
                half = h % 2
                inj = build_steps(h + 1, (h + 1) % NB) if h + 1 < 8 else []
                if ctx:
                    total_steps = (T // 512) * nkc
                    every = max(1, total_steps // (len(inj) + 1))
                    stepc = 0
                    for qa in range(0, T, 512):
                        po = bank("o")
                        sc_ps = {}

                        def issue_s(j):
                            ps_ = bank("s")
                            P.mm(ps_[:, :512], kh[b][:, j * 128:(j + 1) * 128], qh[b][:, qa:qa + 512])
                            sc_ps[j] = ps_

                        for j in range(3):
                            issue_s(j)
                        for j in range(nkc):
                            pt = pT[j % 4]
                            P.act(pt[:, :], sc_ps.pop(j)[:, :], AF.Exp)
                            if j + 3 < nkc:
                                issue_s(j + 3)
                            P.mm(po[:, :], vh[b][:, j, :], pt[:, :], start=(j == 0), stop=(j == nkc - 1))
                            stepc += 1
                            if inj and stepc % every == 0:
                                inj.pop(0)()
                        P.recip(rc[0:64, :], po[64:128, :])
                        P.tt("dve", mbuf[half * 64:(half + 1) * 64, h // 2, qa:qa + 512],
                             po[0:64, :], rc[0:64, :], ALU.mult)
                else:
                    for p in range(nseq // 2):
                        pts = []
                        for a in range(2):
                            sq_ = 2 * p + a
                            ps_ = bank("s")
                            for j in range(2):
                                P.mm(ps_[:, j * 256:(j + 1) * 256], kh[b][:, sq_ * 256 + j * 128:sq_ * 256 + (j + 1) * 128],
                                     qh[b][:, sq_ * 256:(sq_ + 1) * 256])
                            pt = pT[(2 * (p + h * (nseq // 2)) + a) % 4]
                            P.act(pt[:, :], ps_[:, :], AF.Exp)
                            pts.append(pt)
                        flush_pend()
                        for _ in range(3):
                            if inj:
                                inj.pop(0)()

                        def fin(p=p, pts=pts, b=b, h=h, half=half):
                            po = bank("o")
                            for a in range(2):
                                sq_ = 2 * p + a
                                for j in range(2):
                                    P.mm(po[:, a * 256:(a + 1) * 256], vh[b][:, 2 * sq_ + j, :], pts[a][:, j * 256:(j + 1) * 256],
                                         start=(j == 0), stop=(j == 1))
                            P.cp("dve", rc[0:64, :], po[64:128, :])
                            P.act(rc[0:64, :], rc[0:64, :], AF.Ln)
                            P.act(rc[0:64, :], rc[0:64, :], AF.Exp, scale=-1.0)
                            P.tt("dve", mbuf[half * 64:(half + 1) * 64, h // 2, p * 512:(p + 1) * 512],
                                 po[0:64, :], rc[0:64, :], ALU.mult)
                        pend[0] = fin
                while inj:
                    inj.pop(0)()
            flush_pend()
            for nm in ["qh%d" % b for b in range(NB)] + ["kh%d" % b for b in range(NB)] + ["vh%d" % b for b in range(NB)] + \
                      ["pT%d" % b for b in range(4)] + ["rc", "wuq", "wuk", "wuv", "cqn", "ckvn", "krT"]:
                AR.release(nm)
            if ctx:
                AR.release("cs")
                free_rb()
                wout = load_wout()
            if stages == 2 and l == trunc_l[0]:
                raise _Trunc()
            stage3(l, m, NT, mbuf, wg, wout, ada_hooks, nxt)
            for nm in ("wg", "wout", "mbuf"):
                AR.release(nm)

        def odd_layer(l, m, NT, nseq, T, ctx, nxt=None):
            i = l // 2
            S = T + (256 if ctx else 0)
            KT = nseq * S
            nkc = S // 128
            wv_in = wino_d[i].rearrange("(c p) n -> p c n", p=128)
            qT = AR.bf("mbuf", [128, 8, NT])
            kd = AR.bf("kd", [128, 4, KT])
            va = AR.bf("va", [128, KT // 128, 4, 128])
            wq0 = pref.pop("wq0", None)
            if wq0 is None:
                wq0 = AR.bf("wq0", [128, 8, 512])
                load_w(wq0, wv_in, 0, 512)
            wq1 = pref.pop("wq1", None)
            if wq1 is None:
                wq1 = AR.bf("wq1", [128, 8, 512])
                load_w(wq1, wv_in, 512, 512)
            wqs = [wq0, wq1]
            wk = pref.pop("wk", None)
            if wk is None:
                wk = AR.bf("wk", [128, 8, 256])
                load_w(wk, wv_in, 1024, 256)
            nkv = 256 if ctx else 512
            wkv = AR.bf("wkv", [128, 8, nkv])
            load_w(wkv, wv_in, 1536 - nkv, nkv)
            P.memset("pool", va[:, :, :, 64:128], 1.0)
            cs = None
            qs = None
            if ctx:
                cs = AR.bf("cs", [128, 2, 2048])
                P.dma(cs[:, 0, :], swacs_d[0], eng="pool")
                P.dma(cs[:, 1, :], swacs_d[1], eng="pool")
                rb = alloc_rb()
                cst = AR.f32("cst", [128, 2, 256])
                cdup = AR.f32("cdup", [128, 4, 128])
                for jj in range(2):
                    P.dma(cst[:, jj, :], cv_d[i, jj * 128:(jj + 1) * 128, :])
                for jj in range(2):
                    P.cp("dve", va[:, T // 128 + jj, :, 0:64], cst[:, jj, :].rearrange("p (h d) -> p h d", h=4))
                for jj in range(2):
                    P.dma(cst[:, jj, :], ck_d[i, jj * 128:(jj + 1) * 128, :])
                for jj in range(2):
                    P.cp("dve", cdup[:, :, 0:64], cst[:, jj, :].rearrange("p (h d) -> p h d", h=4))
                    P.cp("pool", cdup[:, :, 64:128], cst[:, jj, :].rearrange("p (h d) -> p h d", h=4))
                    pb = bank()
                    for kvh in range(4):
                        P.tr(pb[:, kvh * 128:(kvh + 1) * 128], cdup[:, kvh, :], ident[:, :])
                    for kvh in range(4):
                        P.cp("dve", kd[:, kvh, T + jj * 128:T + (jj + 1) * 128], pb[:, kvh * 128:(kvh + 1) * 128])
                AR.release("cst")
                AR.release("cdup")
            stg = None
            if not ctx:
                stg = [AR.f32("stg%d" % j, [128, 512]) for j in range(2)]
            if ctx:
                hs1 = [(AR.bf("hT", [128, 8, 512]), AR.bf("sq", [128, 4, 512])),
                       (AR.bf("hT1", [128, 8, 512]), AR.bf("sq1", [128, 4, 512]))]
            else:
                hs1 = alloc_hs()

            def kidx(t):
                return (t // T) * S + (t % T)

            pipe1 = len(hs1) > 1
            rpend = [None]
            if pipe1:
                modulate(hs1[0][0], hs1[0][1], l, m, 0, 512)
            for t0 in range(0, NT, 512):
                n = 512
                hT, sq = hs1[(t0 // 512) % len(hs1)]
                if not pipe1:
                    modulate(hT, sq, l, m, t0, n)
                pieces = [(t0, n)] if T >= 512 else [(t0 + a, T) for a in range(0, n, T)]
                nparts = []
                if pipe1 and t0 + 512 < NT:
                    nh, nsq = hs1[((t0 // 512) + 1) % 2]
                    nparts = modulate_parts(nh, nsq, l, m, t0 + 512, 512)
                for oc in range(8):
                    pb = bank()
                    for k in range(8):
                        P.mm(pb[:, :n], wqs[oc // 4][:, k, (oc % 4) * 128:(oc % 4 + 1) * 128], hT[:, k, :n],
                             start=(k == 0), stop=(k == 7))
                    if ctx:
                        pb_fn = rope_a(pb, 128, n, SWA_SCALE, pm128, cs, t0, rb)
                        if rpend[0] is not None:
                            rpend[0]()

                        def _fin_q(pb_fn=pb_fn, oc=oc, t0=t0, n=n):
                            p2 = pb_fn()
                            P.cp("act", qT[:, oc, t0:t0 + n], p2[:, :n])
                        rpend[0] = _fin_q
                    else:
                        P.ts("dve", qT[:, oc, t0:t0 + n], pb[:, :n], SWA_SCALE, ALU.mult)
                    run_part(nparts)
                for kc in range(2):
                    pb = bank()
                    for k in range(8):
                        P.mm(pb[:, :n], wk[:, k, kc * 128:(kc + 1) * 128], hT[:, k, :n], start=(k == 0), stop=(k == 7))
                    def _copies(srcs, kc=kc):
                        for (src, so, sn, ko) in srcs:
                            for hh in range(2):
                                for dh in range(2):
                                    P.cp("act" if dh != hh else "dve",
                                         kd[dh * 64:(dh + 1) * 64, 2 * kc + hh, ko:ko + sn],
                                         src[hh * 64:(hh + 1) * 64, so:so + sn])
                    if ctx:
                        pb_fn = rope_a(pb, 128, n, 1.0, pm128, cs, t0, rb)
                        if rpend[0] is not None:
                            rpend[0]()

                        def _fin_k(pb_fn=pb_fn, t0=t0, n=n, _copies=_copies):
                            p2 = pb_fn()
                            _copies([(p2, 0, n, kidx(t0))])
                        rpend[0] = _fin_k
                    else:
                        _copies([(pb, ta - t0, tn, kidx(ta)) for (ta, tn) in pieces])
                run_part(nparts, 16)
                for j in range(n // 128):
                    tok = t0 + j * 128
                    pb = bank()
                    for k in range(8):
                        P.mm(pb[:, 0:nkv], hT[:, k, j * 128:(j + 1) * 128], wkv[:, k, :], start=(k == 0), stop=(k == 7))
                    P.cp("dve", va[:, kidx(tok) // 128, :, 0:64], pb[:, nkv - 256:nkv].rearrange("p (h d) -> p h d", h=4))
                    if not ctx:
                        b = tok // T
                        pos = tok % T
                        so = stg[j % 2]
                        P.cp("act", so[:, :], pb[:, :])
                        P.dma(sk_d[b, i, pos:pos + 128, :], so[:, 0:256], out_dma=True)
                        P.dma(sv_d[b, i, pos:pos + 128, :], so[:, 256:512], out_dma=True)
                    if j == 0 and rpend[0] is not None:
                        rpend[0]()
                        rpend[0] = None
            free_hs(hs1)
            for nm in ("wq0", "wq1", "wk", "wkv"):
                AR.release(nm)
            if ctx:
                AR.release("cs")
                free_rb()
            else:
                AR.release("stg0")
                AR.release("stg1")
            if stages == 1 and l == trunc_l[0]:
                raise _Trunc()
            ada_hooks = None
            if ADA_INTERLEAVE and (not ctx) and l + 1 < 4:
                ada_alloc(l + 1, 2)
                ada_issue([0, 1])

                def _h0():
                    ada_compute([0, 1])
                    ada_issue([2, 3])

                def _h1():
                    ada_compute([2, 3])
                    ada_issue([4, 5])

                def _hp():
                    ada_compute([4, 5])
                    ada_finish()
                ada_hooks = {0: _h0, 1: _h1, "post": _hp}
            vaB = AR.bf("vaB", [128, KT // 128, 4, 128])
            wg = AR.bf("wg", [128, 8, 1024])
            wout = AR.bf("wout", [128, 8, 1024])
            load_w(wg, wv_in, 1536, 1024)
            load_w(wout, woo_d[i].rearrange("(c p) n -> p c n", p=128), 0, 1024)
            pT = [AR.bf("pT%d" % b, [128, 512]) for b in range(4)]
            rc = AR.f32("rc", [128, 512])
            P.memset("pool", vaB[:, :, :, 0:64], 1.0)
            nch_all = KT // 128
            for ja in range(0, nch_all, 6):
                jb = min(nch_all, ja + 6)
                P.cp("pool", vaB[:, ja:jb, :, 64:128], va[:, ja:jb, :, 0:64])
            mbuf = qT
            pools["o4"] = [4, 5, 6, 7]
            rr["o4"] = 0
            pend = []

            def flush_pend(keep=0):
                while len(pend) > keep:
                    pend.pop(0)()

            def finalize_pair(c, pos, t_lo):
                hA, hB = 2 * c, 2 * c + 1
                P.ts("dve", rc[0:64, :], pos[0][64:128, :], esink[64:128, i * 16 + hA:i * 16 + hA + 1], ALU.add)
                P.ts("dve", rc[64:128, :], pos[1][0:64, :], esink[0:64, i * 16 + hB:i * 16 + hB + 1], ALU.add)
                if ctx:
                    P.recip(rc[:, :], rc[:, :])
                else:
                    P.act(rc[:, :], rc[:, :], AF.Ln)
                    P.act(rc[:, :], rc[:, :], AF.Exp, scale=-1.0)
                P.tt("dve", mbuf[0:64, c, t_lo:t_lo + 512], pos[0][0:64, :], rc[0:64, :], ALU.mult)
                P.tt("dve", mbuf[64:128, c, t_lo:t_lo + 512], pos[1][64:128, :], rc[64:128, :], ALU.mult)

            ucount = 0
            for c in range(8):
                kvh = c // 2
                ntile = (nseq // 2) if not ctx else (T // 512)
                for tix in range(ntile):
                    pos = []
                    if ctx:
                        qt = tix
                        q0 = qt * 512
                        pos = [bank("o4"), bank("o4")]
                        jobs = []
                        for jj in range(2):
                            jobs.append((T // 128 + jj, 0, 512, []))
                        for j in range(4 * qt - 1, 4 * qt + 5):
                            if j < 0 or j >= T // 128:
                                continue
                            nlo = max(4 * qt, j - 1)
                            nhi = min(4 * qt + 3, j + 1)
                            mk = []
                            for nb in range(nlo, nhi + 1):
                                if nb == j - 1:
                                    mk.append((nb, 0))
                                elif nb == j + 1:
                                    mk.append((nb, 1))
                            jobs.append((j, (nlo - 4 * qt) * 128, (nhi - 4 * qt + 1) * 128, mk))
                        sc_ps = {}

                        def issue_s(ji):
                            kc, lo, hi, mk_ = jobs[ji]
                            for half in (0, 1):
                                r0, r1 = half * 64, half * 64 + 64
                                ps_ = bank("s")
                                P.mm(ps_[:, lo:hi], kd[r0:r1, kvh, kc * 128:(kc + 1) * 128], qT[r0:r1, c, q0 + lo:q0 + hi],
                                     start=True, stop=(len(mk_) == 0))
                                sc_ps[(ji, half)] = ps_
                            for half in (0, 1):
                                for mi, (nb, which) in enumerate(mk_):
                                    cl = (nb - 4 * qt) * 128
                                    P.mm(sc_ps[(ji, half)][:, cl:cl + 128], identb[:, :], masks[:, which, :],
                                         start=False, stop=(mi == len(mk_) - 1))

                        for ji in range(min(2, len(jobs))):
                            issue_s(ji)
                        for ji, (kc, lo, hi, mk) in enumerate(jobs):
                            pts = []
                            for half in (0, 1):
                                pt = pT[(2 * ji + half) % 4]
                                P.act(pt[:, lo:hi], sc_ps.pop((ji, half))[:, lo:hi], AF.Exp)
                                pts.append(pt)
                            if ji + 2 < len(jobs):
                                issue_s(ji + 2)
                            for half in (0, 1):
                                vsrc = va if half == 0 else vaB
                                P.mm(pos[half][:, lo:hi], vsrc[:, kc, kvh, :], pts[half][:, lo:hi],
                                     start=(ji == 0), stop=(ji == len(jobs) - 1))
                            if ji == 2:
                                flush_pend()
                        pend.append(lambda c=c, pos=pos, q0=q0: finalize_pair(c, pos, q0))
                        continue
                    for half in (0, 1):
                        r0, r1 = half * 64, half * 64 + 64
                        vsrc = va if half == 0 else vaB
                        p = tix
                        pts = []
                        for a in range(2):
                            sq_ = 2 * p + a
                            ps_ = bank("s")
                            for j in range(2):
                                P.mm(ps_[:, j * 256:(j + 1) * 256], kd[r0:r1, kvh, sq_ * 256 + j * 128:sq_ * 256 + (j + 1) * 128],
                                     qT[r0:r1, c, sq_ * 256:(sq_ + 1) * 256])
                            pt = pT[(2 * ucount + a) % 4]
                            P.act(pt[:, :], ps_[:, :], AF.Exp)
                            pts.append(pt)
                        ucount += 1
                        flush_pend()
                        po = bank("o4")
                        pos.append(po)

                        def pv(p=p, pts=pts, po=po, vsrc=vsrc, kvh=kvh):
                            for a in range(2):
                                sq_ = 2 * p + a
                                for j in range(2):
                                    P.mm(po[:, a * 256:(a + 1) * 256], vsrc[:, 2 * sq_ + j, kvh, :], pts[a][:, j * 256:(j + 1) * 256],
                                         start=(j == 0), stop=(j == 1))
                        pend.append(pv)
                        if half == 1:
                            pend.append(lambda c=c, pos=pos, p=p: finalize_pair(c, pos, p * 512))
            flush_pend()
            AR.release("vaB")
            for nm in ["pT%d" % b for b in range(4)] + ["rc", "kd", "va"]:
                AR.release(nm)
            if stages == 2 and l == trunc_l[0]:
                raise _Trunc()
            stage3(l, m, NT, mbuf, wg, wout, ada_hooks, nxt)
            for nm in ("wg", "wout", "mbuf"):
                AR.release(nm)

        def load_x(src, NT):
            stage = AR.f32("xstage", [128, 4, 1024])
            for t0 in range(0, NT, 512):
                for j in range(4):
                    P.dma(stage[:, j, :], src[t0 + j * 128:t0 + (j + 1) * 128, :])
                for c in range(8):
                    pb = bank()
                    for j in range(4):
                        P.tr(pb[:, j * 128:(j + 1) * 128], stage[:, j, c * 128:(c + 1) * 128], ident[:, :])
                    P.cp(evac_eng(), xT[:, c, t0:t0 + 512], pb[:, :])
            AR.release("xstage")

        def store_x(dst, NT):
            stage = AR.f32("xstage", [128, 4, 1024])
            for t0 in range(0, NT, 512):
                for j in range(4):
                    for hb in range(2):
                        pb = bank()
                        for cc in range(4):
                            c = hb * 4 + cc
                            P.tr(pb[:, cc * 128:(cc + 1) * 128], xT[:, c, t0 + j * 128:t0 + (j + 1) * 128], ident[:, :])
                        P.cp(evac_eng(), stage[:, j, hb * 512:(hb + 1) * 512], pb[:, :])
                    P.dma(dst[t0 + j * 128:t0 + (j + 1) * 128, :], stage[:, j, :], out_dma=True)
            AR.release("xstage")

        trunc_l = [-1]
        try:
          for (src, dst, m, NT, nseq, T, ctx) in ((xp_d, yp_d, 0, 1024, 4, 256, False),
                                                   (xs_d, ys_d, 1, 2048, 1, 2048, True)):
              nl = nl_p if not ctx else nl_s
              if nl < 0:
                  continue
              load_x(src, NT)
              if not ada0_done[0]:
                  ada_all(0)
                  ada0_done[0] = True
              trunc_l[0] = nl - 1 if ((ctx and nl_s >= 0) or (not ctx and nl_s < 0)) else -1
              for l in range(nl):
                  if l + 1 < nl:
                      nxt = (l + 1, ctx, not ctx)
                  elif (not ctx) and nl_s > 0:
                      nxt = (0, True, True)
                  else:
                      nxt = None
                  if l % 2 == 0:
                      even_layer(l, m, NT, nseq, T, ctx, nxt)
                  else:
                      odd_layer(l, m, NT, nseq, T, ctx, nxt)
              store_x(dst, NT)

        except _Trunc:
            P.dma(yp_d[0:128, 0:512], rstd[:, :], out_dma=True)

        P.finish()
        build_program.stats = {e: len(P.ops[e]) for e in ENGS}
        build_program.peak = AR.peak
    return nc


_CONSTS = None


def _consts():
    global _CONSTS
    if _CONSTS is None:
        mla, swa = _rope_tables()
        pm96, pm128 = _perm_mats()
        _CONSTS = dict(ident=np.eye(128, dtype=np.float32), pm96=pm96, pm128=pm128, masks=_masks(),
                       amats=_pool_mats().reshape(20, 128, 128), mla_cs=mla, swa_cs=swa)
    return _CONSTS


def kernel(x_prompt, x_sample, cache_ckv, cache_krope, cache_k, cache_v, c, c_ctx,
           ada_w, ada_b, norm_pre, norm_post,
           mla_w_in, mla_g_qn, mla_g_kvn, mla_w_uq, mla_w_ukv, pool_w, pool_scale, mixa_w_out,
           swa_w_in, swa_sink, swa_w_out):
    f = lambda a: np.ascontiguousarray(np.asarray(a, dtype=np.float32))
    x_prompt, x_sample = f(x_prompt), f(x_sample)
    cache_ckv, cache_krope, cache_k, cache_v = f(cache_ckv), f(cache_krope), f(cache_k), f(cache_v)
    c, c_ctx = f(c), f(c_ctx)
    vecs = np.zeros((14, 1024), np.float32)
    vecs[0:4] = f(norm_pre)
    vecs[4:8] = f(norm_post)
    vecs[8:10, :384] = f(mla_g_qn)
    vecs[10:12, :256] = f(mla_g_kvn)
    vecs[12:14, :512] = f(pool_scale)
    shared = dict(ada_w=f(ada_w), ada_b=f(ada_b), vecs=vecs, gkvn=f(mla_g_kvn), sink=f(swa_sink).reshape(32),
                  w_in_e=f(mla_w_in), w_uq=f(mla_w_uq), w_ukv=f(mla_w_ukv), w_pool=f(pool_w), w_out_e=f(mixa_w_out),
                  w_in_o=f(swa_w_in), w_out_o=f(swa_w_out))
    shared.update(_consts())
    in_maps = []
    for i in range(8):
        d = dict(shared)
        d["xp"] = x_prompt[4 * i:4 * i + 4].reshape(1024, 1024)
        d["xs"] = x_sample[i]
        d["cckv"] = cache_ckv[i]
        d["ckr"] = cache_krope[i]
        d["ck"] = cache_k[i].reshape(2, 256, 256)
        d["cv"] = cache_v[i].reshape(2, 256, 256)
        d["cc"] = np.stack([c_ctx, c[i]], axis=0)
        in_maps.append(d)
    nc = build_program()
    res = run_bass_kernel_spmd(nc, in_maps, core_ids=list(range(8)))
    R = res.results
    y_prompt = np.concatenate([r["yp"].reshape(4, 256, 1024) for r in R], axis=0)
    y_sample = np.stack([r["ys"] for r in R], axis=0)
    st_ckv = np.concatenate([r["st_ckv"] for r in R], axis=0)
    st_kr = np.concatenate([r["st_kr"] for r in R], axis=0)
    st_k = np.concatenate([r["st_k"].reshape(4, 2, 256, 4, 64) for r in R], axis=0)
    st_v = np.concatenate([r["st_v"].reshape(4, 2, 256, 4, 64) for r in R], axis=0)
    return (y_prompt.astype(np.float32), y_sample.astype(np.float32), st_ckv.astype(np.float32),
            st_kr.astype(np.float32), st_k.astype(np.float32), st_v.astype(np.float32))
```

```python
import numpy as np
from contextlib import ExitStack
import concourse.bass as bass
import concourse.mybir as mybir
from concourse.bass_utils import run_bass_kernel_spmd

F32 = mybir.dt.float32
BF16 = mybir.dt.bfloat16
AF = mybir.ActivationFunctionType
ALU = mybir.AluOpType

ENGS = ("pe", "act", "dve", "pool", "sp")
SAME_ENGINE_RAW = True
EMBED_LAST_WAIT = True
EMBED_ENGINES = ("pe", "act", "dve")
N_DMA_SEMS = 24
BUCKET = 4096

D = 1024
EPS = 1e-6
MLA_SCALE = 96 ** -0.5
SWA_SCALE = 64 ** -0.5


class Prog:
    def __init__(self, nc, stack):
        self.nc = nc
        self.stack = stack
        self.ops = {e: [] for e in ENGS}
        self.recs = {}
        self.known = {e: {} for e in ENGS}
        self.eng_sem = {e: stack.enter_context(nc.semaphore("s_" + e)) for e in ENGS}
        self.dma_sems = [stack.enter_context(nc.semaphore("s_dma%d" % i)) for i in range(N_DMA_SEMS)]
        self.dma_cnt = [0] * N_DMA_SEMS
        self.dma_rr = 0
        self.dma_rr_pool = 0
        self.out_dma_tokens = []

    def sb(self, name, shape, dtype):
        return self.stack.enter_context(self.nc.sbuf_tensor("sb_" + name, list(shape), dtype))

    def ps(self, name, shape, dtype=F32):
        return self.stack.enter_context(self.nc.psum_tensor("pp_" + name, list(shape), dtype))

    @staticmethod
    def _box(ap):
        shp = list(ap.tensor.shape)
        row = 1
        for s in shp[1:]:
            row *= s
        isz = mybir.dt.size(ap.dtype)
        off = int(ap.offset)
        dims = ap.ap
        p0 = off // row
        f0 = off % row
        pstep, pcnt = dims[0]
        if pstep == row or (pcnt == 1 and len(dims) > 1):
            p1 = p0 + pcnt
            rest = dims[1:]
        elif pstep == 0:
            p1 = p0 + 1
            rest = dims[1:]
        else:
            p1 = p0 + 1
            rest = dims
        ext = 0
        for st, cn in rest:
            ext += abs(st) * (cn - 1)
        return (p0, p1, f0 * isz, (f0 + ext + 1) * isz)

    @staticmethod
    def _tracked(ap):
        return str(ap.space).upper() in ("SB", "PSUM")

    def op(self, eng, fn, reads=(), writes=(), dma=False, out_dma=False):
        idx = len(self.ops[eng])
        waits = []
        kn = self.known[eng]

        def need(tok, raw=False):
            if tok[0] == 'e':
                if tok[1] == eng and not (raw and SAME_ENGINE_RAW and eng != "pe"):
                    return
                key = tok[1]
            else:
                key = ('d', tok[1])
            if kn.get(key, -1) >= tok[2]:
                return
            kn[key] = tok[2]
            waits.append(tok)

        if dma:
            if eng == "pool":
                k = 8 + self.dma_rr_pool
                self.dma_rr_pool = (self.dma_rr_pool + 1) % (N_DMA_SEMS - 8)
            else:
                k = self.dma_rr
                self.dma_rr = (self.dma_rr + 1) % 8
            if self.dma_cnt[k] > 0:
                need(('d', k, self.dma_cnt[k] * 16))
            self.dma_cnt[k] += 1
            mytok = ('d', k, self.dma_cnt[k] * 16)
        else:
            mytok = ('e', eng, idx)

        rb = [(ap.name, self._box(ap)) for ap in reads if self._tracked(ap) and str(ap.space).upper() != "PSUM"]
        wb = [(ap.name, self._box(ap)) for ap in writes if self._tracked(ap) and str(ap.space).upper() != "PSUM"]
        for ap in list(reads) + list(writes):
            if str(ap.space).upper() == "PSUM":
                ent = (ap.name, (0, 128, 0, 1 << 20))
                if ent not in wb:
                    wb.append(ent)
        for name, box in rb:
            tr = self.recs.get(name)
            if tr is None:
                continue
            for b in range(box[2] // BUCKET, (box[3] - 1) // BUCKET + 1):
                for r in tr.get(b, ()):
                    if r[3] and r[2]:
                        rbx = r[0]
                        if rbx[0] < box[1] and box[0] < rbx[1] and rbx[2] < box[3] and box[2] < rbx[3]:
                            need(r[1], True)
        for name, box in wb:
            tr = self.recs.get(name)
            if tr is None:
                continue
            for b in range(box[2] // BUCKET, (box[3] - 1) // BUCKET + 1):
                lst = tr.get(b)
                if not lst:
                    continue
                for r in lst:
                    if r[3]:
                        rbx = r[0]
                        if rbx[0] < box[1] and box[0] < rbx[1] and rbx[2] < box[3] and box[2] < rbx[3]:
                            need(r[1])
                            if box[0] <= rbx[0] and box[1] >= rbx[1] and box[2] <= rbx[2] and box[3] >= rbx[3]:
                                r[3] = False
                tr[b] = [r for r in lst if r[3]]
        for name, box in rb:
            tr = self.recs.setdefault(name, {})
            rec = [box, mytok, False, True]
            for b in range(box[2] // BUCKET, (box[3] - 1) // BUCKET + 1):
                lst = tr.setdefault(b, [])
                if not dma:
                    for r in lst:
                        if r[3] and (not r[2]) and r[1][0] == 'e' and r[1][1] == eng and r[0] == box:
                            r[3] = False
                lst.append(rec)
        for name, box in wb:
            tr = self.recs.setdefault(name, {})
            rec = [box, mytok, True, True]
            for b in range(box[2] // BUCKET, (box[3] - 1) // BUCKET + 1):
                tr.setdefault(b, []).append(rec)
        o = dict(fn=fn, waits=waits, tok=mytok, marked=False)
        self.ops[eng].append(o)
        if out_dma:
            self.out_dma_tokens.append(mytok)
        return o

    def dma(self, out, in_, eng="sp", out_dma=False):
        return self.op(eng, lambda e: e.dma_start(out=out, in_=in_), reads=[in_], writes=[out],
                       dma=True, out_dma=out_dma)

    def mm(self, out, lhsT, rhs, start=True, stop=True):
        return self.op("pe", lambda e: e.matmul(out, lhsT, rhs, start=start, stop=stop),
                       reads=[lhsT, rhs] + ([] if start else [out]), writes=[out])

    def tr(self, out, in_, ident):
        return self.op("pe", lambda e: e.transpose(out, in_, ident), reads=[in_, ident], writes=[out])

    def act(self, out, in_, func, bias=None, scale=None, accum=None):
        kw = {}
        rd = [in_]
        wr = [out]
        if bias is not None:
            kw["bias"] = bias
            if not isinstance(bias, (int, float)):
                rd.append(bias)
        if scale is not None:
            kw["scale"] = scale
            if not isinstance(scale, (int, float)):
                rd.append(scale)
        if accum is not None:
            kw["accum_out"] = accum
            wr.append(accum)
        return self.op("act", lambda e: e.activation(out=out, in_=in_, func=func, **kw), reads=rd, writes=wr)

    def cp(self, eng, out, in_):
        if eng == "act":
            return self.op("act", lambda e: e.copy(out=out, in_=in_), reads=[in_], writes=[out])
        return self.op(eng, lambda e: e.tensor_copy(out=out, in_=in_), reads=[in_], writes=[out])

    def tt(self, eng, out, in0, in1, op):
        return self.op(eng, lambda e: e.tensor_tensor(out=out, in0=in0, in1=in1, op=op), reads=[in0, in1], writes=[out])

    def ts(self, eng, out, in0, s1, op0, s2=None, op1=None):
        rd = [in0]
        if not isinstance(s1, (int, float)):
            rd.append(s1)
        if s2 is not None and not isinstance(s2, (int, float)):
            rd.append(s2)
        if op1 is None:
            return self.op(eng, lambda e: e.tensor_scalar(out=out, in0=in0, scalar1=s1, scalar2=None, op0=op0),
                           reads=rd, writes=[out])
        return self.op(eng, lambda e: e.tensor_scalar(out=out, in0=in0, scalar1=s1, scalar2=s2, op0=op0, op1=op1),
                       reads=rd, writes=[out])

    def stt(self, eng, out, in0, scalar, in1, op0, op1):
        rd = [in0, in1]
        if not isinstance(scalar, (int, float)):
            rd.append(scalar)
        return self.op(eng, lambda e: e.scalar_tensor_tensor(out=out, in0=in0, scalar=scalar, in1=in1, op0=op0, op1=op1),
                       reads=rd, writes=[out])

    def memset(self, eng, out, val):
        return self.op(eng, lambda e: e.memset(out, val), writes=[out])

    def recip(self, out, in_):
        return self.op("dve", lambda e: e.reciprocal(out=out, in_=in_), reads=[in_], writes=[out])

    def finish(self):
        nc = self.nc
        for e in ENGS:
            for o in self.ops[e]:
                for tok in o["waits"]:
                    if tok[0] == 'e':
                        self.ops[tok[1]][tok[2]]["marked"] = True
        cnt_at = {}
        for e in ENGS:
            c = 0
            arr = []
            for o in self.ops[e]:
                if o["marked"]:
                    c += 1
                arr.append(c)
            cnt_at[e] = arr
        fw = {}
        for tok in self.out_dma_tokens:
            fw[tok[1]] = max(fw.get(tok[1], 0), tok[2])

        with nc.Block() as block:
            def emit(ename, engobj):
                for o in self.ops[ename]:
                    ws = o["waits"]
                    emb = None
                    if EMBED_LAST_WAIT and ws and ename in EMBED_ENGINES and o["tok"][0] == 'e':
                        emb = ws[-1]
                        ws = ws[:-1]
                    for tok in ws:
                        if tok[0] == 'e':
                            engobj.wait_ge(self.eng_sem[tok[1]], cnt_at[tok[1]][tok[2]])
                        else:
                            engobj.wait_ge(self.dma_sems[tok[1]], tok[2])
                    ins = o["fn"](engobj)
                    if emb is not None:
                        if emb[0] == 'e':
                            ins._wait_ge(self.eng_sem[emb[1]], cnt_at[emb[1]][emb[2]])
                        else:
                            ins._wait_ge(self.dma_sems[emb[1]], emb[2])
                    if o["tok"][0] == 'd':
                        ins.then_inc(self.dma_sems[o["tok"][1]], 16)
                    elif o["marked"]:
                        ins.then_inc(self.eng_sem[ename], 1)
                if ename == "sp":
                    for k, v in fw.items():
                        engobj.wait_ge(self.dma_sems[k], v)

            @block.sync
            def _(sync):
                emit("sp", sync)

            @block.tensor
            def _(tensor):
                emit("pe", tensor)

            @block.scalar
            def _(scalar):
                emit("act", scalar)

            @block.vector
            def _(vector):
                emit("dve", vector)

            @block.gpsimd
            def _(gpsimd):
                emit("pool", gpsimd)


class _Trunc(Exception):
    pass


class Arena:
    def __init__(self, tensor, nelem):
        self.t = tensor
        self.n = nelem
        self.free = [(0, nelem)]
        self.live = {}
        self.peak = 0

    def alloc(self, name, nelem_bf16):
        nelem_bf16 = (nelem_bf16 + 15) // 16 * 16
        for i, (o, s) in enumerate(self.free):
            if s >= nelem_bf16:
                if s == nelem_bf16:
                    self.free.pop(i)
                else:
                    self.free[i] = (o + nelem_bf16, s - nelem_bf16)
                self.live[name] = (o, nelem_bf16)
                used = self.n - sum(s for _, s in self.free)
                self.peak = max(self.peak, used)
                return o
        raise RuntimeError("arena OOM allocating %s (%d); live=%s free=%s" % (name, nelem_bf16, self.live, self.free))

    def release(self, name):
        o, s = self.live.pop(name)
        self.free.append((o, s))
        self.free.sort()
        merged = []
        for o, s in self.free:
            if merged and merged[-1][0] + merged[-1][1] == o:
                merged[-1] = (merged[-1][0], merged[-1][1] + s)
            else:
                merged.append((o, s))
        self.free = merged

    def bf(self, name, shape):
        n = 1
        for s in shape[1:]:
            n *= s
        o = self.alloc(name, n)
        v = self.t[0:shape[0], o:o + n]
        if len(shape) == 3:
            v = v.rearrange("p (a b) -> p a b", a=shape[1])
        elif len(shape) == 4:
            v = v.rearrange("p (a b c) -> p a b c", a=shape[1], b=shape[2])
        return v

    def f32(self, name, shape):
        n = 1
        for s in shape[1:]:
            n *= s
        o = self.alloc(name, 2 * n)
        v = self.t[0:shape[0], o:o + 2 * n].bitcast(F32)
        if len(shape) == 3:
            v = v.rearrange("p (a b) -> p a b", a=shape[1])
        elif len(shape) == 4:
            v = v.rearrange("p (a b c) -> p a b c", a=shape[1], b=shape[2])
        return v


def _rope_tables():
    t = np.arange(2048)
    row = (t // 64).astype(np.float64)
    col = (t % 64).astype(np.float64)
    mla = np.zeros((2, 96, 2048), np.float32)
    mla[0, 0:64] = 1.0
    for r in range(32):
        pos = row if r < 16 else col
        i = r % 16
        f = i % 8
        inv = 10000.0 ** (-(2.0 * f) / 16.0)
        ang = pos * inv
        mla[0, 64 + r] = np.cos(ang)
        mla[1, 64 + r] = np.sin(ang) if i < 8 else -np.sin(ang)
    swa = np.zeros((2, 128, 2048), np.float32)
    for p in range(128):
        d = p % 64
        pos = row if d < 32 else col
        i = d % 32
        f = i % 16
        inv = 10000.0 ** (-(2.0 * f) / 32.0)
        ang = pos * inv
        swa[0, p] = np.cos(ang)
        swa[1, p] = np.sin(ang) if i < 16 else -np.sin(ang)
    return mla, swa


def _perm_mats():
    pm96 = np.zeros((96, 96), np.float32)
    for r in range(32):
        i = r % 16
        partner = r + 8 if i < 8 else r - 8
        pm96[64 + partner, 64 + r] = 1.0
    pm128 = np.zeros((128, 128), np.float32)
    for m in range(128):
        i = m % 32
        partner = m + 16 if i < 16 else m - 16
        pm128[partner, m] = 1.0
    return pm96, pm128


def _pool_mats():
    T = 384
    out = np.zeros((4, 5, 128, 128), np.float32)
    for g, w in enumerate((2, 4, 8, 16)):
        A = np.zeros((T, T), np.float64)
        for t in range(T):
            lo = min(max(t - w // 2, 0), T)
            hi = min(max(t + w // 2, 0), T)
            A[lo:hi, t] = 1.0 / (hi - lo)
            A[t, t] -= 1.0
        out[g, 0] = A[0:128, 0:128]
        out[g, 1] = A[128:256, 128:256]
        out[g, 2] = A[256:384, 256:384]
        out[g, 3] = A[0:128, 128:256]
        out[g, 4] = A[128:256, 0:128]
    return out


def _masks():
    k = np.arange(128)[:, None]
    q = np.arange(128)[None, :]
    m = np.zeros((2, 128, 128), np.float32)
    m[0] = np.where(k <= q, 0.0, -30000.0)
    m[1] = np.where(q <= k, 0.0, -30000.0)
    return m


def build_program(nl_p=4, nl_s=4, stages=3, dbg=None):
    nc = bass.Bass("TRN2", target_bir_lowering=False)

    def din(name, shape):
        return nc.dram_tensor(name, list(shape), F32, kind="ExternalInput").ap()

    def dout(name, shape):
        return nc.dram_tensor(name, list(shape), F32, kind="ExternalOutput").ap()

    xp_d = din("xp", [1024, 1024])
    xs_d = din("xs", [2048, 1024])
    cckv_d = din("cckv", [2, 256, 256])
    ckr_d = din("ckr", [2, 256, 32])
    ck_d = din("ck", [2, 256, 256])
    cv_d = din("cv", [2, 256, 256])
    cc_d = din("cc", [2, 1024])
    adaw_d = din("ada_w", [4, 1024, 3072])
    adab_d = din("ada_b", [4, 3072])
    vecs_d = din("vecs", [14, 1024])
    gkvn_d = din("gkvn", [2, 256])
    sink_d = din("sink", [32])
    wine_d = din("w_in_e", [2, 1024, 2208])
    wuq_d = din("w_uq", [2, 384, 768])
    wukv_d = din("w_ukv", [2, 256, 1024])
    wpool_d = din("w_pool", [2, 4, 128, 128])
    woe_d = din("w_out_e", [2, 1024, 1024])
    wino_d = din("w_in_o", [2, 1024, 2560])
    woo_d = din("w_out_o", [2, 1024, 1024])
    ident_d = din("ident", [128, 128])
    pm96_d = din("pm96", [96, 96])
    pm128_d = din("pm128", [128, 128])
    masks_d = din("masks", [2, 128, 128])
    amats_d = din("amats", [20, 128, 128])
    mlacs_d = din("mla_cs", [2, 96, 2048])
    swacs_d = din("swa_cs", [2, 128, 2048])

    yp_d = dout("yp", [1024, 1024])
    ys_d = dout("ys", [2048, 1024])
    sckv_d = dout("st_ckv", [4, 2, 256, 256])
    skr_d = dout("st_kr", [4, 2, 256, 32])
    sk_d = dout("st_k", [4, 2, 256, 256])
    sv_d = dout("st_v", [4, 2, 256, 256])

    with ExitStack() as st:
        P = Prog(nc, st)
        xT = P.sb("xT", [128, 8, 2048], F32)
        ident = P.sb("ident", [128, 128], F32)
        ones_bf = P.sb("ones_bf", [128, 128], BF16)
        identb = P.sb("identb", [128, 128], BF16)
        scb = P.sb("scb", [128, 8, 2], BF16)
        pm96 = P.sb("pm96", [96, 96], BF16)
        pm128 = P.sb("pm128", [128, 128], BF16)
        masks = P.sb("masks", [128, 2, 128], BF16)
        epsT = P.sb("epsT", [128, 1], F32)
        mod = P.sb("mod", [128, 4, 48], F32)
        vecT = P.sb("vecT", [128, 8, 32], F32)
        coefA = P.sb("coefA", [128, 4, 2, 8], F32)
        coefB = P.sb("coefB", [128, 4, 2, 8], F32)
        coefG = P.sb("coefG", [128, 4, 2, 8], F32)
        gkvb = P.sb("gkvb", [128, 2, 256], F32)
        esink = P.sb("esink", [128, 32], F32)
        rstd = P.sb("rstd", [128, 512], F32)
        rstd2 = P.sb("rstd2", [128, 512], F32)
        tmpf = [P.sb("tmpf%d" % i, [128, 512], F32) for i in range(2)]
        ARENA_N = 64 * 1024
        arena_t = P.sb("arena", [128, ARENA_N], BF16)
        AR = Arena(arena_t, ARENA_N)
        banks = [P.ps("ps%d" % i, [128, 512], F32) for i in range(8)]
        rr = {"all": 0, "s": 0, "o": 0, "g": 0}
        pools = {"all": list(range(8)), "s": [0, 1, 2, 3], "o": [4, 5], "g": [6, 7]}

        def bank(pool="all"):
            lst = pools[pool]
            b = banks[lst[rr[pool] % len(lst)]]
            rr[pool] += 1
            return b

        evac_rr = [0]

        def evac_eng():
            evac_rr[0] += 1
            return "dve" if evac_rr[0] % 2 else "act"

        P.dma(ident[:], ident_d)
        P.dma(identb[:], ident_d, eng="pool")
        P.dma(pm96[:], pm96_d, eng="pool")
        P.dma(pm128[:], pm128_d, eng="pool")
        P.dma(masks[:], masks_d.rearrange("a k q -> k a q"), eng="pool")
        P.dma(gkvb[:].rearrange("p a n -> p (a n)"), gkvn_d.rearrange("a n -> (a n)").partition_broadcast(128))
        P.dma(esink[:], sink_d.partition_broadcast(128))
        P.memset("dve", ones_bf[:], 1.0)
        P.memset("dve", epsT[:], EPS)
        P.act(esink[:], esink[:], AF.Exp)

        if dbg == "consts":
            P.dma(yp_d[0:128, 0:32], esink[:], out_dma=True)
            P.finish()
            return nc
        vst = AR.f32("vst", [32, 1024])
        P.memset("dve", vst[:], 0.0)
        P.dma(vst[0:14, :], vecs_d)
        pv = bank()
        for c in range(8):
            P.tr(pv[:, c * 32:(c + 1) * 32], vst[0:32, c * 128:(c + 1) * 128], ident[0:32, 0:32])
        for c in range(8):
            P.cp("dve", vecT[:, c, :], pv[:, c * 32:(c + 1) * 32])
        AR.release("vst")

        if dbg == "vec":
            P.dma(yp_d[0:128, 0:256], vecT[:].rearrange("p a b -> p (a b)"), out_dma=True)
            P.finish()
            return nc
        ccT = AR.f32("ccT", [128, 2, 8])
        for m in range(2):
            P.dma(ccT[:, m, :], cc_d[m].rearrange("(p c) -> p c", c=8))
        for m in range(2):
            P.act(scb[:, :, m], ccT[:, m, :], AF.Silu)
        AR.release("ccT")
        modv = mod[:].rearrange("p l (j c m) -> p l j c m", j=3, c=8)
        ada_state = {}

        def ada_alloc(l, nbuf):
            ada_state["l"] = l
            ada_state["adab"] = AR.bf("adab", [1, 3072])
            ada_state["bufs"] = [AR.bf("adw%d" % i, [128, 8, 512]) for i in range(nbuf)]
            ada_state["nbuf"] = nbuf
            P.dma(ada_state["adab"][:], adab_d[l:l + 1, :], eng="pool")

        def ada_issue(blks):
            l = ada_state["l"]
            wv = adaw_d[l].rearrange("(p c) n -> p c n", c=8)
            for blk in blks:
                wt = ada_state["bufs"][blk % ada_state["nbuf"]]
                P.dma(wt[:], wv[:, :, blk * 512:(blk + 1) * 512], eng="pool")

        def ada_compute(blks):
            l = ada_state["l"]
            adab = ada_state["adab"]
            for blk in blks:
                wt = ada_state["bufs"][blk % ada_state["nbuf"]]
                pm = bank()
                for nci in range(4):
                    nch = blk * 4 + nci
                    for c in range(8):
                        P.mm(pm[:, 2 * nci:2 * nci + 2], wt[:, c, nci * 128:(nci + 1) * 128], scb[:, c, :],
                             start=(c == 0), stop=False)
                    P.mm(pm[:, 2 * nci:2 * nci + 2], adab[0:1, nch * 128:(nch + 1) * 128], ones_bf[0:1, 0:2],
                         start=False, stop=True)
                P.cp("dve", mod[:, l, blk * 8:(blk + 1) * 8], pm[:, 0:8])

        def ada_finish():
            l = ada_state["l"]
            for m in range(2):
                P.stt("dve", coefA[:, l, m, :], modv[:, l, 1, :, m], 1.0, vecT[:, :, l], ALU.add, ALU.mult)
                P.cp("dve", coefB[:, l, m, :], modv[:, l, 0, :, m])
                P.tt("dve", coefG[:, l, m, :], modv[:, l, 2, :, m], vecT[:, :, 4 + l], ALU.mult)
            for i in range(ada_state["nbuf"]):
                AR.release("adw%d" % i)
            AR.release("adab")
            ada_state.clear()

        def ada_all(l):
            ada_alloc(l, 3)
            for blk in range(6):
                ada_issue([blk])
                ada_compute([blk])
            ada_finish()

        ADA_INTERLEAVE = nl_p >= 4
        ada0_done = [False]
        if not ADA_INTERLEAVE:
            for l in range(1, 4):
                ada_all(l)
        if dbg == "ada":
            P.dma(yp_d[0:128, 0:192], mod[:].rearrange("p a b -> p (a b)"), out_dma=True)
            P.dma(yp_d[128:256, 0:64], coefA[:].rearrange("p a b c -> p (a b c)"), out_dma=True)
            P.dma(yp_d[256:384, 0:64], coefG[:].rearrange("p a b c -> p (a b c)"), out_dma=True)
            P.finish()
            return nc
        def rstd_from_ssq(ps_ssq, n, dim, out):
            P.act(out, ps_ssq, AF.Ln, bias=epsT[:, 0:1], scale=1.0 / dim)
            P.act(out, out, AF.Exp, scale=-0.5)

        def modulate_parts(hT, sq, l, m, t0, n):
            parts = []

            ns_ = sq.shape[1]

            def stats_a():
                for c in range(8):
                    P.act(sq[:, c, :n], xT[:, c, t0:t0 + n], AF.Square)

            def stats_b():
                pb = bank()
                for c in range(8):
                    P.mm(pb[:, :n], ones_bf[:, :], sq[:, c, :n], start=(c == 0), stop=(c == 7))
                rstd_from_ssq(pb[:, :n], n, 1024, rstd[:, :n])

            def stats():
                pb = bank()
                for c0 in range(0, 8, ns_):
                    for c in range(c0, c0 + ns_):
                        P.act(sq[:, c % ns_, :n], xT[:, c, t0:t0 + n], AF.Square)
                    for c in range(c0, c0 + ns_):
                        P.mm(pb[:, :n], ones_bf[:, :], sq[:, c % ns_, :n], start=(c == 0), stop=(c == 7))
                rstd_from_ssq(pb[:, :n], n, 1024, rstd[:, :n])
            if ns_ >= 8:
                parts.extend([stats_a, (lambda: None), stats_b])
            else:
                parts.append(stats)
            for c in range(8):
                def ap(c=c):
                    tf = tmpf[c % 2]
                    P.stt("dve", tf[:, :n], xT[:, c, t0:t0 + n], coefA[:, l, m, c:c + 1], rstd[:, :n], ALU.mult, ALU.mult)
                    P.act(hT[:, c, :n], tf[:, :n], AF.Identity, bias=coefB[:, l, m, c:c + 1], scale=1.0)
                parts.append(ap)
            return parts

        def modulate(hT, sq, l, m, t0, n):
            for f in modulate_parts(hT, sq, l, m, t0, n):
                f()

        def run_part(parts, k=1):
            for _ in range(k):
                if parts:
                    parts.pop(0)()

        def load_w(dst, src_rows_view, col0, ncols):
            for c in range(dst.shape[1]):
                P.dma(dst[:, c, :], src_rows_view[:, c, col0:col0 + ncols], eng="pool")

        def alloc_hs():
            hs = [(AR.bf("hT", [128, 8, 512]), AR.bf("sq", [128, 8, 512]))]
            try:
                a = AR.bf("hT1", [128, 8, 512])
                try:
                    b = AR.bf("sq1", [128, 8, 512])
                    hs.append((a, b))
                except RuntimeError:
                    AR.release("hT1")
            except RuntimeError:
                pass
            return hs

        def free_hs(hs):
            AR.release("hT")
            AR.release("sq")
            if len(hs) > 1:
                AR.release("hT1")
                AR.release("sq1")

        def stage3(l, m, NT, mbuf, wg, wout, hooks=None, nxt=None):
            oT = AR.f32("oT", [128, 8, 512])
            hs3 = alloc_hs()
            sg = [AR.bf("sg%d" % i, [128, 512]) for i in range(2)]
            if nxt is not None:
                prefetch_w1(*nxt)
            tiles = list(range(0, NT, 512))
            n = 512
            pipe = len(hs3) > 1
            if pipe:
                modulate(hs3[0][0], hs3[0][1], l, m, tiles[0], n)
            for ti, t0 in enumerate(tiles):
                if hooks and ti in hooks:
                    hooks[ti]()
                hT, sq = hs3[ti % len(hs3)]
                if not pipe:
                    modulate(hT, sq, l, m, t0, n)
                nparts = []
                if pipe and ti + 1 < len(tiles):
                    nh, nsq = hs3[(ti + 1) % 2]
                    nparts = modulate_parts(nh, nsq, l, m, tiles[ti + 1], n)
                for mc in range(8):
                    pb = bank()
                    for k in range(8):
                        P.mm(pb[:, :n], wg[:, k, mc * 128:(mc + 1) * 128], hT[:, k, :n], start=(k == 0), stop=(k == 7))
                    s_ = sg[mc % 2]
                    P.act(s_[:, :n], pb[:, :n], AF.Silu)
                    P.tt("dve" if mc % 2 else "pool", mbuf[:, mc, t0:t0 + n], mbuf[:, mc, t0:t0 + n], s_[:, :n], ALU.mult)
                    if mc == 1:
                        run_part(nparts)
                for dc in range(8):
                    pb = bank()
                    for k in range(8):
                        P.mm(pb[:, :n], wout[:, k, dc * 128:(dc + 1) * 128], mbuf[:, k, t0:t0 + n],
                             start=(k == 0), stop=(k == 7))
                    P.cp("dve", oT[:, dc, :n], pb[:, :n])
                    P.act(sq[:, dc, :n], pb[:, :n], AF.Square)
                    run_part(nparts)
                run_part(nparts, 16)
                pb = bank()
                for dc in range(8):
                    P.mm(pb[:, :n], ones_bf[:, :], sq[:, dc, :n], start=(dc == 0), stop=(dc == 7))
                rstd_from_ssq(pb[:, :n], n, 1024, rstd2[:, :n])
                for dc in range(8):
                    tf = tmpf[dc % 2]
                    P.stt("dve", tf[:, :n], oT[:, dc, :n], coefG[:, l, m, dc:dc + 1], rstd2[:, :n], ALU.mult, ALU.mult)
                    P.tt("pool", xT[:, dc, t0:t0 + n], xT[:, dc, t0:t0 + n], tf[:, :n], ALU.add)
            if hooks and "post" in hooks:
                hooks["post"]()
            free_hs(hs3)
            for nm in ("oT", "sg0", "sg1"):
                AR.release(nm)

        rope_rr = [0]

        def rope_a(src_ps, pr, n, scale, pm, cs, t0, rb):
            qc, qsn = rb[rope_rr[0] % len(rb)]
            rope_rr[0] += 1
            P.stt("dve", qc[0:pr, :n], src_ps[0:pr, :n], float(scale), cs[0:pr, 0, t0:t0 + n], ALU.mult, ALU.mult)
            P.stt("dve", qsn[0:pr, :n], src_ps[0:pr, :n], float(scale), cs[0:pr, 1, t0:t0 + n], ALU.mult, ALU.mult)

            def phase_b():
                p2 = bank("g")
                P.mm(p2[0:pr, :n], identb[0:pr, 0:pr], qc[0:pr, :n], start=True, stop=False)
                P.mm(p2[0:pr, :n], pm[:, :], qsn[0:pr, :n], start=False, stop=True)
                return p2
            return phase_b

        def rope_apply(src_ps, pr, n, scale, pm, cs, t0, rb):
            return rope_a(src_ps, pr, n, scale, pm, cs, t0, rb)()

        def alloc_rb():
            return [(AR.bf("rqc%d" % i, [128, 512]), AR.bf("rqs%d" % i, [128, 512])) for i in range(2)]

        def free_rb():
            for i in range(2):
                AR.release("rqc%d" % i)
                AR.release("rqs%d" % i)

        pref = {}

        def prefetch_w1(lnext, ctx_next, full):
            i2 = lnext // 2
            if (not full) and lnext % 2 == 1:
                return
            try:
                if lnext % 2 == 0:
                    wv = wine_d[i2].rearrange("(c p) n -> p c n", p=128)
                    a = AR.bf("w1a", [128, 8, 672])
                    load_w(a, wv, 0, 672)
                    pref["w1a"] = a
                    if full:
                        b_ = AR.bf("w1b", [128, 8, 512])
                        load_w(b_, wv, 1184, 512)
                        pref["w1b"] = b_
                else:
                    wv = wino_d[i2].rearrange("(c p) n -> p c n", p=128)
                    a = AR.bf("wq0", [128, 8, 512])
                    load_w(a, wv, 0, 512)
                    pref["wq0"] = a
                    if full:
                        b_ = AR.bf("wq1", [128, 8, 512])
                        load_w(b_, wv, 512, 512)
                        pref["wq1"] = b_
                        k_ = AR.bf("wk", [128, 8, 256])
                        load_w(k_, wv, 1024, 256)
                        pref["wk"] = k_
            except RuntimeError:
                pass

        def even_layer(l, m, NT, nseq, T, ctx, nxt=None):
            i = l // 2
            S = T + (256 if ctx else 0)
            KT = nseq * S
            nkc = S // 128
            wv_in = wine_d[i].rearrange("(c p) n -> p c n", p=128)
            w1a = pref.pop("w1a", None)
            if w1a is None:
                w1a = AR.bf("w1a", [128, 8, 672])
                load_w(w1a, wv_in, 0, 672)
            w1b = pref.pop("w1b", None)
            if w1b is None:
                w1b = AR.bf("w1b", [128, 8, 512])
                load_w(w1b, wv_in, 1184, 512)
            wp = AR.bf("wp", [128, 4, 128])
            amats = AR.bf("amats", [128, 20, 128])
            P.dma(amats[:], amats_d.rearrange("a k q -> k a q"), eng="pool")
            P.dma(wp[:], wpool_d[i].rearrange("g c e -> c g e"), eng="pool")
            mbuf = AR.bf("mbuf", [128, 8, NT])
            mo = AR.live["mbuf"][0]
            vtok = arena_t[:, mo:mo + (NT // 128) * 512].rearrange("p (j q) -> p j q", q=512)
            cqn = AR.bf("cqn", [128, 3, NT])
            ckvn = AR.bf("ckvn", [128, 2, KT])
            krT = AR.bf("krT", [96, KT])
            cs = None
            if ctx:
                cs = AR.bf("cs", [96, 2, 2048])
                P.dma(cs[:, 0, :], mlacs_d[0], eng="pool")
                P.dma(cs[:, 1, :], mlacs_d[1], eng="pool")
                rb = alloc_rb()
                cst = AR.f32("cst", [128, 2, 256 + 96])
                P.memset("pool", cst[:, :, 256:320], 0.0)
                for jj in range(2):
                    P.dma(cst[:, jj, 0:256], cckv_d[i, jj * 128:(jj + 1) * 128, :])
                    P.dma(cst[:, jj, 320:352], ckr_d[i, jj * 128:(jj + 1) * 128, :])
                for jj in range(2):
                    pb = bank()
                    for c in range(2):
                        P.tr(pb[:, c * 128:(c + 1) * 128], cst[:, jj, c * 128:(c + 1) * 128], ident[:, :])
                    P.tr(pb[0:96, 256:384], cst[:, jj, 256:352], ident[:, :])
                    for c in range(2):
                        P.cp("dve", ckvn[:, c, T + jj * 128:T + (jj + 1) * 128], pb[:, c * 128:(c + 1) * 128])
                    P.cp("dve", krT[64:96, T + jj * 128:T + (jj + 1) * 128], pb[64:96, 256:384])
                AR.release("cst")
            stg = None
            if not ctx:
                stg = [AR.f32("stg%d" % j, [128, 288]) for j in range(2)]
            junk = AR.f32("junk", [128, 256])
            ssq1 = AR.f32("ssq1", [128, 2])
            hs1 = alloc_hs()

            def kidx(t):
                return (t // T) * S + (t % T)

            pipe1 = len(hs1) > 1
            if pipe1:
                modulate(hs1[0][0], hs1[0][1], l, m, 0, 512)
            for t0 in range(0, NT, 512):
                n = 512
                hT, sq = hs1[(t0 // 512) % len(hs1)]
                if not pipe1:
                    modulate(hT, sq, l, m, t0, n)
                nparts = []
                if pipe1 and t0 + 512 < NT:
                    nh, nsq = hs1[((t0 // 512) + 1) % 2]
                    nparts = modulate_parts(nh, nsq, l, m, t0 + 512, 512)
                for oc in range(3):
                    pb = bank()
                    for k in range(8):
                        P.mm(pb[:, :n], w1a[:, k, oc * 128:(oc + 1) * 128], hT[:, k, :n], start=(k == 0), stop=(k == 7))
                    P.act(sq[:, oc, :n], pb[:, :n], AF.Square)
                    P.ts("dve", cqn[:, oc, t0:t0 + n], pb[:, :n], vecT[:, oc, 8 + i:9 + i], ALU.mult)
                    run_part(nparts)
                pb = bank()
                for oc in range(3):
                    P.mm(pb[:, :n], ones_bf[:, :], sq[:, oc, :n], start=(oc == 0), stop=(oc == 2))
                rstd_from_ssq(pb[:, :n], n, 384, rstd2[:, :n])
                for oc in range(3):
                    P.tt("pool" if oc == 1 else "dve", cqn[:, oc, t0:t0 + n], cqn[:, oc, t0:t0 + n], rstd2[:, :n], ALU.mult)
                pieces = [(t0, n)] if T >= 512 else [(t0 + a, T) for a in range(0, n, T)]
                for oc in range(2):
                    pb = bank()
                    for k in range(8):
                        P.mm(pb[:, :n], w1a[:, k, 384 + oc * 128:384 + (oc + 1) * 128], hT[:, k, :n],
                             start=(k == 0), stop=(k == 7))
                    P.act(sq[:, oc, :n], pb[:, :n], AF.Square)
                    for (ta, tn) in pieces:
                        P.ts("dve", ckvn[:, oc, kidx(ta):kidx(ta) + tn], pb[:, ta - t0:ta - t0 + tn],
                             vecT[:, oc, 10 + i:11 + i], ALU.mult)
                    run_part(nparts)
                pb = bank()
                for oc in range(2):
                    P.mm(pb[:, :n], ones_bf[:, :], sq[:, oc, :n], start=(oc == 0), stop=(oc == 1))
                rstd_from_ssq(pb[:, :n], n, 256, rstd2[:, :n])
                for oc in range(2):
                    for (ta, tn) in pieces:
                        P.tt("pool" if oc == 1 else "dve", ckvn[:, oc, kidx(ta):kidx(ta) + tn],
                             ckvn[:, oc, kidx(ta):kidx(ta) + tn], rstd2[:, ta - t0:ta - t0 + tn], ALU.mult)
                pb = bank()
                for k in range(8):
                    P.mm(pb[0:96, :n], w1a[:, k, 576:672], hT[:, k, :n], start=(k == 0), stop=(k == 7))
                if ctx:
                    p2 = rope_apply(pb, 96, n, 1.0, pm96, cs, t0, rb)
                    P.cp("act", krT[64:96, kidx(t0):kidx(t0) + n], p2[64:96, :n])
                else:
                    for (ta, tn) in pieces:
                        P.cp("dve", krT[64:96, kidx(ta):kidx(ta) + tn], pb[64:96, ta - t0:ta - t0 + tn])
                for j in range(n // 128):
                    pb = bank()
                    for k in range(8):
                        P.mm(pb[:, :], hT[:, k, j * 128:(j + 1) * 128], w1b[:, k, :], start=(k == 0), stop=(k == 7))
                    P.cp(evac_eng(), vtok[:, (t0 // 128) + j, :], pb[:, :])
                    run_part(nparts)
                run_part(nparts, 16)
                if not ctx:
                    for j in range(n // 128):
                        tok = t0 + j * 128
                        b = tok // T
                        pos = tok % T
                        pb = bank()
                        for k in range(8):
                            P.mm(pb[:, 0:288], hT[:, k, j * 128:(j + 1) * 128], w1a[:, k, 384:672],
                                 start=(k == 0), stop=(k == 7))
                        so = stg[j % 2]
                        P.act(junk[:, :], pb[:, 0:256], AF.Square)
                        P.op("dve", lambda e: e.reduce_sum(out=ssq1[:, 0:1], in_=junk[:, :], axis=mybir.AxisListType.X),
                             reads=[junk[:, :]], writes=[ssq1[:, 0:1]])
                        P.act(ssq1[:, 1:2], ssq1[:, 0:1], AF.Ln, bias=epsT[:, 0:1], scale=1.0 / 256)
                        P.act(ssq1[:, 1:2], ssq1[:, 1:2], AF.Exp, scale=-0.5)
                        P.stt("dve", so[:, 0:256], pb[:, 0:256], ssq1[:, 1:2], gkvb[:, i, :], ALU.mult, ALU.mult)
                        P.cp("dve", so[:, 256:288], pb[:, 256:288])
                        P.dma(sckv_d[b, i, pos:pos + 128, :], so[:, 0:256], out_dma=True)
                        P.dma(skr_d[b, i, pos:pos + 128, :], so[:, 256:288], out_dma=True)
            free_hs(hs1)
            pooled = [AR.bf("pooled%d" % j, [128, 512]) for j in range(2)]
            ppend = [None]
            ncs = T // 128
            for t0 in range(0, NT, 512):
                for g in range(4):
                    pb = bank()
                    for j in range(4):
                        ch = t0 // 128 + j
                        cin = ch % ncs
                        contrib = []
                        if cin > 0:
                            contrib.append((ch - 1, 3))
                        contrib.append((ch, 0 if cin == 0 else (2 if cin == ncs - 1 else 1)))
                        if cin < ncs - 1:
                            contrib.append((ch + 1, 4))
                        for ci, (src, kind) in enumerate(contrib):
                            P.mm(pb[:, j * 128:(j + 1) * 128], vtok[:, src, g * 128:(g + 1) * 128],
                                 amats[:, g * 5 + kind, :], start=(ci == 0), stop=(ci == len(contrib) - 1))
                    pl = pooled[g % 2]
                    P.cp(evac_eng(), pl[:, :], pb[:, :])
                    if ppend[0] is not None:
                        ppend[0]()

                    def _pw(pl=pl, g=g, t0=t0):
                        pb2 = bank()
                        P.mm(pb2[:, :], wp[:, g, :], pl[:, :])
                        P.ts("dve", mbuf[:, 4 + g, t0:t0 + 512], pb2[:, :], vecT[:, g, 12 + i:13 + i], ALU.mult)
                    ppend[0] = _pw
            if ppend[0] is not None:
                ppend[0]()
                ppend[0] = None
            for nm in ("pooled0", "pooled1", "junk", "ssq1", "w1a", "w1b", "wp", "amats"):
                AR.release(nm)
            if not ctx:
                AR.release("stg0")
                AR.release("stg1")
            if stages == 1 and l == trunc_l[0]:
                raise _Trunc()
            ada_hooks = None
            if ADA_INTERLEAVE and (not ctx) and l + 1 < 4:
                ada_alloc(l + 1, 2)
                ada_issue([0, 1])

                def _h0():
                    ada_compute([0, 1])
                    ada_issue([2, 3])

                def _h1():
                    ada_compute([2, 3])
                    ada_issue([4, 5])

                def _hp():
                    ada_compute([4, 5])
                    ada_finish()
                ada_hooks = {0: _h0, 1: _h1, "post": _hp}
            wuq = AR.bf("wuq", [128, 3, 768])
            wuk = AR.bf("wuk", [128, 2, 8, 64])
            wuv = AR.bf("wuv", [128, 2, 8, 64])
            P.dma(wuq[:], wuq_d[i].rearrange("(c p) n -> p c n", p=128), eng="pool")
            wukv_v = wukv_d[i].rearrange("(c p) (h t d) -> p c h t d", p=128, h=8, t=2)
            for c in range(2):
                P.dma(wuk[:, c, :, :], wukv_v[:, c, :, 0, :], eng="pool")
                P.dma(wuv[:, c, :, :], wukv_v[:, c, :, 1, :], eng="pool")
            def load_wg():
                wg_ = AR.bf("wg", [128, 8, 1024])
                load_w(wg_[:, :, 0:512], wv_in, 672, 512)
                load_w(wg_[:, :, 512:1024], wv_in, 1696, 512)
                return wg_

            def load_wout():
                wout_ = AR.bf("wout", [128, 8, 1024])
                load_w(wout_, woe_d[i].rearrange("(c p) n -> p c n", p=128), 0, 1024)
                return wout_
            wg = load_wg()
            if not ctx:
                wout = load_wout()
            NB = 2
            TT = nseq * T
            nkt = KT // 128
            qh = [AR.bf("qh%d" % b, [96, TT]) for b in range(NB)]
            kh = [AR.bf("kh%d" % b, [96, KT]) for b in range(NB)]
            vh = [AR.bf("vh%d" % b, [128, nkt, 128]) for b in range(NB)]
            pT = [AR.bf("pT%d" % b, [128, 512]) for b in range(4)]
            rc = AR.f32("rc", [64, 512])
            for b in range(NB):
                P.memset("pool", vh[b][:, :, 64:128], 1.0)
                P.cp("pool", kh[b][64:96, :], krT[64:96, 0:KT])
            pend = [None]

            def flush_pend():
                if pend[0] is not None:
                    pend[0]()
                    pend[0] = None

            def build_steps(h, b):
                st_ = []
                for ka in range(0, KT, 512):
                    def f(ka=ka):
                        kn = min(512, KT - ka)
                        pb = bank("g")
                        for c in range(2):
                            P.mm(pb[0:64, :kn], wuk[:, c, h, :], ckvn[:, c, ka:ka + kn], start=(c == 0), stop=(c == 1))
                        P.cp("dve" if ctx else "act", kh[b][0:64, ka:ka + kn], pb[0:64, :kn])
                    st_.append(f)
                for ja in range(0, nkt, 8):
                    def f(ja=ja):
                        jn = min(8, nkt - ja)
                        pb = bank("g")
                        for jj in range(jn):
                            for c in range(2):
                                P.mm(pb[:, jj * 64:(jj + 1) * 64], ckvn[:, c, (ja + jj) * 128:(ja + jj + 1) * 128],
                                     wuv[:, c, h, :], start=(c == 0), stop=(c == 1))
                        P.cp("dve" if ctx else "act", vh[b][:, ja:ja + jn, 0:64], pb[:, 0:jn * 64].rearrange("p (j d) -> p j d", d=64))
                    st_.append(f)
                for qa in range(0, TT, 512):
                    hold = {}

                    def f(qa=qa, hold=hold):
                        pb = bank("g")
                        for c in range(3):
                            P.mm(pb[0:96, :512], wuq[:, c, h * 96:(h + 1) * 96], cqn[:, c, qa:qa + 512],
                                 start=(c == 0), stop=(c == 2))
                        if ctx:
                            hold["b"] = rope_a(pb, 96, 512, MLA_SCALE, pm96, cs, qa, rb)
                        else:
                            P.act(qh[b][:, qa:qa + 512], pb[0:96, :512], AF.Copy, scale=float(MLA_SCALE))
                    st_.append(f)
                    if ctx:
                        def f2(qa=qa, hold=hold):
                            p2 = hold["b"]()
                            P.cp("dve", qh[b][0:96, qa:qa + 512], p2[0:96, :512])
                        st_.append(f2)
                return st_

            for f in build_steps(0, 0):
                f()
            for h in range(8):
                b = h % NB
                half = h % 2
                inj = build_steps(h + 1, (h + 1) % NB) if h + 1 < 8 else []
                if ctx:
                    total_steps = (T // 512) * nkc
                    every = max(1, total_steps // (len(inj) + 1))
                    stepc = 0
                    for qa in range(0, T, 512):
                        po = bank("o")
                        sc_ps = {}

                        def issue_s(j):
                            ps_ = bank("s")
                            P.mm(ps_[:, :512], kh[b][:, j * 128:(j + 1) * 128], qh[b][:, qa:qa + 512])
                            sc_ps[j] = ps_

                        for j in range(3):
                            issue_s(j)
                        for j in range(nkc):
                            pt = pT[j % 4]
                            P.act(pt[:, :], sc_ps.pop(j)[:, :], AF.Exp)
                            if j + 3 < nkc:
                                issue_s(j + 3)
                            P.mm(po[:, :], vh[b][:, j, :], pt[:, :], start=(j == 0), stop=(j == nkc - 1))
                            stepc += 1
                            if inj and stepc % every == 0:
                                inj.pop(0)()
                        P.recip(rc[0:64, :], po[64:128, :])
                        P.tt("dve", mbuf[half * 64:(half + 1) * 64, h // 2, qa:qa + 512],
                             po[0:64, :], rc[0:64, :], ALU.mult)
                else:
                    for p in range(nseq // 2):
                        pts = []
                        for a in range(2):
                            sq_ = 2 * p + a
                            ps_ = bank("s")
                            for j in range(2):
                                P.mm(ps_[:, j * 256:(j + 1) * 256], kh[b][:, sq_ * 256 + j * 128:sq_ * 256 + (j + 1) * 128],
                                     qh[b][:, sq_ * 256:(sq_ + 1) * 256])
                            pt = pT[(2 * (p + h * (nseq // 2)) + a) % 4]
                            P.act(pt[:, :], ps_[:, :], AF.Exp)
                            pts.append(pt)
                        flush_pend()
                        for _ in range(3):
                            if inj:
                                inj.pop(0)()

                        def fin(p=p, pts=pts, b=b, h=h, half=half):
                            po = bank("o")
                            for a in range(2):
                                sq_ = 2 * p + a
                                for j in range(2):
                                    P.mm(po[:, a * 256:(a + 1) * 256], vh[b][:, 2 * sq_ + j, :], pts[a][:, j * 256:(j + 1) * 256],
                                         start=(j == 0), stop=(j == 1))
                            P.cp("dve", rc[0:64, :], po[64:128, :])
                            P.act(rc[0:64, :], rc[0:64, :], AF.Ln)
                            P.act(rc[0:64, :], rc[0:64, :], AF.Exp, scale=-1.0)
                            P.tt("dve", mbuf[half * 64:(half + 1) * 64, h // 2, p * 512:(p + 1) * 512],
                                 po[0:64, :], rc[0:64, :], ALU.mult)
                        pend[0] = fin
                while inj:
                    inj.pop(0)()
            flush_pend()
            for nm in ["qh%d" % b for b in range(NB)] + ["kh%d" % b for b in range(NB)] + ["vh%d" % b for b in range(NB)] + \
                      ["pT%d" % b for b in range(4)] + ["rc", "wuq", "wuk", "wuv", "cqn", "ckvn", "krT"]:
                AR.release(nm)
            if ctx:
                AR.release("cs")
                free_rb()
                wout = load_wout()
            if stages == 2 and l == trunc_l[0]:
                raise _Trunc()
            stage3(l, m, NT, mbuf, wg, wout, ada_hooks, nxt)
            for nm in ("wg", "wout", "mbuf"):
                AR.release(nm)

        def odd_layer(l, m, NT, nseq, T, ctx, nxt=None):
            i = l // 2
            S = T + (256 if ctx else 0)
            KT = nseq * S
            nkc = S // 128
            wv_in = wino_d[i].rearrange("(c p) n -> p c n", p=128)
            qT = AR.bf("mbuf", [128, 8, NT])
            kd = AR.bf("kd", [128, 4, KT])
            va = AR.bf("va", [128, KT // 128, 4, 128])
            wq0 = pref.pop("wq0", None)
            if wq0 is None:
                wq0 = AR.bf("wq0", [128, 8, 512])
                load_w(wq0, wv_in, 0, 512)
            wq1 = pref.pop("wq1", None)
            if wq1 is None:
                wq1 = AR.bf("wq1", [128, 8, 512])
                load_w(wq1, wv_in, 512, 512)
            wqs = [wq0, wq1]
            wk = pref.pop("wk", None)
            if wk is None:
                wk = AR.bf("wk", [128, 8, 256])
                load_w(wk, wv_in, 1024, 256)
            nkv = 256 if ctx else 512
            wkv = AR.bf("wkv", [128, 8, nkv])
            load_w(wkv, wv_in, 1536 - nkv, nkv)
            P.memset("pool", va[:, :, :, 64:128], 1.0)
            cs = None
            qs = None
            if ctx:
                cs = AR.bf("cs", [128, 2, 2048])
                P.dma(cs[:, 0, :], swacs_d[0], eng="pool")
                P.dma(cs[:, 1, :], swacs_d[1], eng="pool")
                rb = alloc_rb()
                cst = AR.f32("cst", [128, 2, 256])
                cdup = AR.f32("cdup", [128, 4, 128])
                for jj in range(2):
                    P.dma(cst[:, jj, :], cv_d[i, jj * 128:(jj + 1) * 128, :])
                for jj in range(2):
                    P.cp("dve", va[:, T // 128 + jj, :, 0:64], cst[:, jj, :].rearrange("p (h d) -> p h d", h=4))
                for jj in range(2):
                    P.dma(cst[:, jj, :], ck_d[i, jj * 128:(jj + 1) * 128, :])
                for jj in range(2):
                    P.cp("dve", cdup[:, :, 0:64], cst[:, jj, :].rearrange("p (h d) -> p h d", h=4))
                    P.cp("pool", cdup[:, :, 64:128], cst[:, jj, :].rearrange("p (h d) -> p h d", h=4))
                    pb = bank()
                    for kvh in range(4):
                        P.tr(pb[:, kvh * 128:(kvh + 1) * 128], cdup[:, kvh, :], ident[:, :])
                    for kvh in range(4):
                        P.cp("dve", kd[:, kvh, T + jj * 128:T + (jj + 1) * 128], pb[:, kvh * 128:(kvh + 1) * 128])
                AR.release("cst")
                AR.release("cdup")
            stg = None
            if not ctx:
                stg = [AR.f32("stg%d" % j, [128, 512]) for j in range(2)]
            if ctx:
                hs1 = [(AR.bf("hT", [128, 8, 512]), AR.bf("sq", [128, 4, 512])),
                       (AR.bf("hT1", [128, 8, 512]), AR.bf("sq1", [128, 4, 512]))]
            else:
                hs1 = alloc_hs()

            def kidx(t):
                return (t // T) * S + (t % T)

            pipe1 = len(hs1) > 1
            rpend = [None]
            if pipe1:
                modulate(hs1[0][0], hs1[0][1], l, m, 0, 512)
            for t0 in range(0, NT, 512):
                n = 512
                hT, sq = hs1[(t0 // 512) % len(hs1)]
                if not pipe1:
                    modulate(hT, sq, l, m, t0, n)
                pieces = [(t0, n)] if T >= 512 else [(t0 + a, T) for a in range(0, n, T)]
                nparts = []
                if pipe1 and t0 + 512 < NT:
                    nh, nsq = hs1[((t0 // 512) + 1) % 2]
                    nparts = modulate_parts(nh, nsq, l, m, t0 + 512, 512)
                for oc in range(8):
                    pb = bank()
                    for k in range(8):
                        P.mm(pb[:, :n], wqs[oc // 4][:, k, (oc % 4) * 128:(oc % 4 + 1) * 128], hT[:, k, :n],
                             start=(k == 0), stop=(k == 7))
                    if ctx:
                        pb_fn = rope_a(pb, 128, n, SWA_SCALE, pm128, cs, t0, rb)
                        if rpend[0] is not None:
                            rpend[0]()

                        def _fin_q(pb_fn=pb_fn, oc=oc, t0=t0, n=n):
                            p2 = pb_fn()
                            P.cp("act", qT[:, oc, t0:t0 + n], p2[:, :n])
                        rpend[0] = _fin_q
                    else:
                        P.ts("dve", qT[:, oc, t0:t0 + n], pb[:, :n], SWA_SCALE, ALU.mult)
                    run_part(nparts)
                for kc in range(2):
                    pb = bank()
                    for k in range(8):
                        P.mm(pb[:, :n], wk[:, k, kc * 128:(kc + 1) * 128], hT[:, k, :n], start=(k == 0), stop=(k == 7))
                    def _copies(srcs, kc=kc):
                        for (src, so, sn, ko) in srcs:
                            for hh in range(2):
                                for dh in range(2):
                                    P.cp("act" if dh != hh else "dve",
                                         kd[dh * 64:(dh + 1) * 64, 2 * kc + hh, ko:ko + sn],
                                         src[hh * 64:(hh + 1) * 64, so:so + sn])
                    if ctx:
                        pb_fn = rope_a(pb, 128, n, 1.0, pm128, cs, t0, rb)
                        if rpend[0] is not None:
                            rpend[0]()

                        def _fin_k(pb_fn=pb_fn, t0=t0, n=n, _copies=_copies):
                            p2 = pb_fn()
                            _copies([(p2, 0, n, kidx(t0))])
                        rpend[0] = _fin_k
                    else:
                        _copies([(pb, ta - t0, tn, kidx(ta)) for (ta, tn) in pieces])
                run_part(nparts, 16)
                for j in range(n // 128):
                    tok = t0 + j * 128
                    pb = bank()
                    for k in range(8):
                        P.mm(pb[:, 0:nkv], hT[:, k, j * 128:(j + 1) * 128], wkv[:, k, :], start=(k == 0), stop=(k == 7))
                    P.cp("dve", va[:, kidx(tok) // 128, :, 0:64], pb[:, nkv - 256:nkv].rearrange("p (h d) -> p h d", h=4))
                    if not ctx:
                        b = tok // T
                        pos = tok % T
                        so = stg[j % 2]
                        P.cp("act", so[:, :], pb[:, :])
                        P.dma(sk_d[b, i, pos:pos + 128, :], so[:, 0:256], out_dma=True)
                        P.dma(sv_d[b, i, pos:pos + 128, :], so[:, 256:512], out_dma=True)
                    if j == 0 and rpend[0] is not None:
                        rpend[0]()
                        rpend[0] = None
            free_hs(hs1)
            for nm in ("wq0", "wq1", "wk", "wkv"):
                AR.release(nm)
            if ctx:
                AR.release("cs")
                free_rb()
            else:
                AR.release("stg0")
                AR.release("stg1")
            if stages == 1 and l == trunc_l[0]:
                raise _Trunc()
            ada_hooks = None
            if ADA_INTERLEAVE and (not ctx) and l + 1 < 4:
                ada_alloc(l + 1, 2)
                ada_issue([0, 1])

                def _h0():
                    ada_compute([0, 1])
                    ada_issue([2, 3])

                def _h1():
                    ada_compute([2, 3])
                    ada_issue([4, 5])

                def _hp():
                    ada_compute([4, 5])
                    ada_finish()
                ada_hooks = {0: _h0, 1: _h1, "post": _hp}
            vaB = AR.bf("vaB", [128, KT // 128, 4, 128])
            wg = AR.bf("wg", [128, 8, 1024])
            wout = AR.bf("wout", [128, 8, 1024])
            load_w(wg, wv_in, 1536, 1024)
            load_w(wout, woo_d[i].rearrange("(c p) n -> p c n", p=128), 0, 1024)
            pT = [AR.bf("pT%d" % b, [128, 512]) for b in range(4)]
            rc = AR.f32("rc", [128, 512])
            P.memset("pool", vaB[:, :, :, 0:64], 1.0)
            nch_all = KT // 128
            for ja in range(0, nch_all, 6):
                jb = min(nch_all, ja + 6)
                P.cp("pool", vaB[:, ja:jb, :, 64:128], va[:, ja:jb, :, 0:64])
            mbuf = qT
            pools["o4"] = [4, 5, 6, 7]
            rr["o4"] = 0
            pend = []

            def flush_pend(keep=0):
                while len(pend) > keep:
                    pend.pop(0)()

            def finalize_pair(c, pos, t_lo):
                hA, hB = 2 * c, 2 * c + 1
                P.ts("dve", rc[0:64, :], pos[0][64:128, :], esink[64:128, i * 16 + hA:i * 16 + hA + 1], ALU.add)
                P.ts("dve", rc[64:128, :], pos[1][0:64, :], esink[0:64, i * 16 + hB:i * 16 + hB + 1], ALU.add)
                if ctx:
                    P.recip(rc[:, :], rc[:, :])
                else:
                    P.act(rc[:, :], rc[:, :], AF.Ln)
                    P.act(rc[:, :], rc[:, :], AF.Exp, scale=-1.0)
                P.tt("dve", mbuf[0:64, c, t_lo:t_lo + 512], pos[0][0:64, :], rc[0:64, :], ALU.mult)
                P.tt("dve", mbuf[64:128, c, t_lo:t_lo + 512], pos[1][64:128, :], rc[64:128, :], ALU.mult)

            ucount = 0
            for c in range(8):
                kvh = c // 2
                ntile = (nseq // 2) if not ctx else (T // 512)
                for tix in range(ntile):
                    pos = []
                    if ctx:
                        qt = tix
                        q0 = qt * 512
                        pos = [bank("o4"), bank("o4")]
                        jobs = []
                        for jj in range(2):
                            jobs.append((T // 128 + jj, 0, 512, []))
                        for j in range(4 * qt - 1, 4 * qt + 5):
                            if j < 0 or j >= T // 128:
                                continue
                            nlo = max(4 * qt, j - 1)
                            nhi = min(4 * qt + 3, j + 1)
                            mk = []
                            for nb in range(nlo, nhi + 1):
                                if nb == j - 1:
                                    mk.append((nb, 0))
                                elif nb == j + 1:
                                    mk.append((nb, 1))
                            jobs.append((j, (nlo - 4 * qt) * 128, (nhi - 4 * qt + 1) * 128, mk))
                        sc_ps = {}

                        def issue_s(ji):
                            kc, lo, hi, mk_ = jobs[ji]
                            for half in (0, 1):
                                r0, r1 = half * 64, half * 64 + 64
                                ps_ = bank("s")
                                P.mm(ps_[:, lo:hi], kd[r0:r1, kvh, kc * 128:(kc + 1) * 128], qT[r0:r1, c, q0 + lo:q0 + hi],
                                     start=True, stop=(len(mk_) == 0))
                                sc_ps[(ji, half)] = ps_
                            for half in (0, 1):
                                for mi, (nb, which) in enumerate(mk_):
                                    cl = (nb - 4 * qt) * 128
                                    P.mm(sc_ps[(ji, half)][:, cl:cl + 128], identb[:, :], masks[:, which, :],
                                         start=False, stop=(mi == len(mk_) - 1))

                        for ji in range(min(2, len(jobs))):
                            issue_s(ji)
                        for ji, (kc, lo, hi, mk) in enumerate(jobs):
                            pts = []
                            for half in (0, 1):
                                pt = pT[(2 * ji + half) % 4]
                                P.act(pt[:, lo:hi], sc_ps.pop((ji, half))[:, lo:hi], AF.Exp)
                                pts.append(pt)
                            if ji + 2 < len(jobs):
                                issue_s(ji + 2)
                            for half in (0, 1):
                                vsrc = va if half == 0 else vaB
                                P.mm(pos[half][:, lo:hi], vsrc[:, kc, kvh, :], pts[half][:, lo:hi],
                                     start=(ji == 0), stop=(ji == len(jobs) - 1))
                            if ji == 2:
                                flush_pend()
                        pend.append(lambda c=c, pos=pos, q0=q0: finalize_pair(c, pos, q0))
                        continue
                    for half in (0, 1):
                        r0, r1 = half * 64, half * 64 + 64
                        vsrc = va if half == 0 else vaB
                        p = tix
                        pts = []
                        for a in range(2):
                            sq_ = 2 * p + a
                            ps_ = bank("s")
                            for j in range(2):
                                P.mm(ps_[:, j * 256:(j + 1) * 256], kd[r0:r1, kvh, sq_ * 256 + j * 128:sq_ * 256 + (j + 1) * 128],
                                     qT[r0:r1, c, sq_ * 256:(sq_ + 1) * 256])
                            pt = pT[(2 * ucount + a) % 4]
                            P.act(pt[:, :], ps_[:, :], AF.Exp)
                            pts.append(pt)
                        ucount += 1
                        flush_pend()
                        po = bank("o4")
                        pos.append(po)

                        def pv(p=p, pts=pts, po=po, vsrc=vsrc, kvh=kvh):
                            for a in range(2):
                                sq_ = 2 * p + a
                                for j in range(2):
                                    P.mm(po[:, a * 256:(a + 1) * 256], vsrc[:, 2 * sq_ + j, kvh, :], pts[a][:, j * 256:(j + 1) * 256],
                                         start=(j == 0), stop=(j == 1))
                        pend.append(pv)
                        if half == 1:
                            pend.append(lambda c=c, pos=pos, p=p: finalize_pair(c, pos, p * 512))
            flush_pend()
            AR.release("vaB")
            for nm in ["pT%d" % b for b in range(4)] + ["rc", "kd", "va"]:
                AR.release(nm)
            if stages == 2 and l == trunc_l[0]:
                raise _Trunc()
            stage3(l, m, NT, mbuf, wg, wout, ada_hooks, nxt)
            for nm in ("wg", "wout", "mbuf"):
                AR.release(nm)

        def load_x(src, NT):
            stage = AR.f32("xstage", [128, 4, 1024])
            for t0 in range(0, NT, 512):
                for j in range(4):
                    P.dma(stage[:, j, :], src[t0 + j * 128:t0 + (j + 1) * 128, :])
                for c in range(8):
                    pb = bank()
                    for j in range(4):
                        P.tr(pb[:, j * 128:(j + 1) * 128], stage[:, j, c * 128:(c + 1) * 128], ident[:, :])
                    P.cp(evac_eng(), xT[:, c, t0:t0 + 512], pb[:, :])
            AR.release("xstage")

        def store_x(dst, NT):
            stage = AR.f32("xstage", [128, 4, 1024])
            for t0 in range(0, NT, 512):
                for j in range(4):
                    for hb in range(2):
                        pb = bank()
                        for cc in range(4):
                            c = hb * 4 + cc
                            P.tr(pb[:, cc * 128:(cc + 1) * 128], xT[:, c, t0 + j * 128:t0 + (j + 1) * 128], ident[:, :])
                        P.cp(evac_eng(), stage[:, j, hb * 512:(hb + 1) * 512], pb[:, :])
                    P.dma(dst[t0 + j * 128:t0 + (j + 1) * 128, :], stage[:, j, :], out_dma=True)
            AR.release("xstage")

        trunc_l = [-1]
        try:
          for (src, dst, m, NT, nseq, T, ctx) in ((xp_d, yp_d, 0, 1024, 4, 256, False),
                                                   (xs_d, ys_d, 1, 2048, 1, 2048, True)):
              nl = nl_p if not ctx else nl_s
              if nl < 0:
                  continue
              load_x(src, NT)
              if not ada0_done[0]:
                  ada_all(0)
                  ada0_done[0] = True
              trunc_l[0] = nl - 1 if ((ctx and nl_s >= 0) or (not ctx and nl_s < 0)) else -1
              for l in range(nl):
                  if l + 1 < nl:
                      nxt = (l + 1, ctx, not ctx)
                  elif (not ctx) and nl_s > 0:
                      nxt = (0, True, True)
                  else:
                      nxt = None
                  if l % 2 == 0:
                      even_layer(l, m, NT, nseq, T, ctx, nxt)
                  else:
                      odd_layer(l, m, NT, nseq, T, ctx, nxt)
              store_x(dst, NT)

        except _Trunc:
            P.dma(yp_d[0:128, 0:512], rstd[:, :], out_dma=True)

        P.finish()
        build_program.stats = {e: len(P.ops[e]) for e in ENGS}
        build_program.peak = AR.peak
    return nc


_CONSTS = None


def _consts():
    global _CONSTS
    if _CONSTS is None:
        mla, swa = _rope_tables()
        pm96, pm128 = _perm_mats()
        _CONSTS = dict(ident=np.eye(128, dtype=np.float32), pm96=pm96, pm128=pm128, masks=_masks(),
                       amats=_pool_mats().reshape(20, 128, 128), mla_cs=mla, swa_cs=swa)
    return _CONSTS


def kernel(x_prompt, x_sample, cache_ckv, cache_krope, cache_k, cache_v, c, c_ctx,
           ada_w, ada_b, norm_pre, norm_post,
           mla_w_in, mla_g_qn, mla_g_kvn, mla_w_uq, mla_w_ukv, pool_w, pool_scale, mixa_w_out,
           swa_w_in, swa_sink, swa_w_out):
    f = lambda a: np.ascontiguousarray(np.asarray(a, dtype=np.float32))
    x_prompt, x_sample = f(x_prompt), f(x_sample)
    cache_ckv, cache_krope, cache_k, cache_v = f(cache_ckv), f(cache_krope), f(cache_k), f(cache_v)
    c, c_ctx = f(c), f(c_ctx)
    vecs = np.zeros((14, 1024), np.float32)
    vecs[0:4] = f(norm_pre)
    vecs[4:8] = f(norm_post)
    vecs[8:10, :384] = f(mla_g_qn)
    vecs[10:12, :256] = f(mla_g_kvn)
    vecs[12:14, :512] = f(pool_scale)
    shared = dict(ada_w=f(ada_w), ada_b=f(ada_b), vecs=vecs, gkvn=f(mla_g_kvn), sink=f(swa_sink).reshape(32),
                  w_in_e=f(mla_w_in), w_uq=f(mla_w_uq), w_ukv=f(mla_w_ukv), w_pool=f(pool_w), w_out_e=f(mixa_w_out),
                  w_in_o=f(swa_w_in), w_out_o=f(swa_w_out))
    shared.update(_consts())
    in_maps = []
    for i in range(8):
        d = dict(shared)
        d["xp"] = x_prompt[4 * i:4 * i + 4].reshape(1024, 1024)
        d["xs"] = x_sample[i]
        d["cckv"] = cache_ckv[i]
        d["ckr"] = cache_krope[i]
        d["ck"] = cache_k[i].reshape(2, 256, 256)
        d["cv"] = cache_v[i].reshape(2, 256, 256)
        d["cc"] = np.stack([c_ctx, c[i]], axis=0)
        in_maps.append(d)
    nc = build_program()
    res = run_bass_kernel_spmd(nc, in_maps, core_ids=list(range(8)))
    R = res.results
    y_prompt = np.concatenate([r["yp"].reshape(4, 256, 1024) for r in R], axis=0)
    y_sample = np.stack([r["ys"] for r in R], axis=0)
    st_ckv = np.concatenate([r["st_ckv"] for r in R], axis=0)
    st_kr = np.concatenate([r["st_kr"] for r in R], axis=0)
    st_k = np.concatenate([r["st_k"].reshape(4, 2, 256, 4, 64) for r in R], axis=0)
    st_v = np.concatenate([r["st_v"].reshape(4, 2, 256, 4, 64) for r in R], axis=0)
    return (y_prompt.astype(np.float32), y_sample.astype(np.float32), st_ckv.astype(np.float32),
            st_kr.astype(np.float32), st_k.astype(np.float32), st_v.astype(np.float32))
```

```python
import numpy as np
from contextlib import ExitStack
import concourse.bass as bass
import concourse.mybir as mybir
from concourse.bass_utils import run_bass_kernel_spmd

F32 = mybir.dt.float32
BF16 = mybir.dt.bfloat16
AF = mybir.ActivationFunctionType
ALU = mybir.AluOpType

ENGS = ("pe", "act", "dve", "pool", "sp")
SAME_ENGINE_RAW = True
EMBED_LAST_WAIT = True
EMBED_ENGINES = ("pe", "act", "dve")
N_DMA_SEMS = 24
BUCKET = 4096

D = 1024
EPS = 1e-6
MLA_SCALE = 96 ** -0.5
SWA_SCALE = 64 ** -0.5


class Prog:
    def __init__(self, nc, stack):
        self.nc = nc
        self.stack = stack
        self.ops = {e: [] for e in ENGS}
        self.recs = {}
        self.known = {e: {} for e in ENGS}
        self.eng_sem = {e: stack.enter_context(nc.semaphore("s_" + e)) for e in ENGS}
        self.dma_sems = [stack.enter_context(nc.semaphore("s_dma%d" % i)) for i in range(N_DMA_SEMS)]
        self.dma_cnt = [0] * N_DMA_SEMS
        self.dma_rr = 0
        self.dma_rr_pool = 0
        self.out_dma_tokens = []

    def sb(self, name, shape, dtype):
        return self.stack.enter_context(self.nc.sbuf_tensor("sb_" + name, list(shape), dtype))

    def ps(self, name, shape, dtype=F32):
        return self.stack.enter_context(self.nc.psum_tensor("pp_" + name, list(shape), dtype))

    @staticmethod
    def _box(ap):
        shp = list(ap.tensor.shape)
        row = 1
        for s in shp[1:]:
            row *= s
        isz = mybir.dt.size(ap.dtype)
        off = int(ap.offset)
        dims = ap.ap
        p0 = off // row
        f0 = off % row
        pstep, pcnt = dims[0]
        if pstep == row or (pcnt == 1 and len(dims) > 1):
            p1 = p0 + pcnt
            rest = dims[1:]
        elif pstep == 0:
            p1 = p0 + 1
            rest = dims[1:]
        else:
            p1 = p0 + 1
            rest = dims
        ext = 0
        for st, cn in rest:
            ext += abs(st) * (cn - 1)
        return (p0, p1, f0 * isz, (f0 + ext + 1) * isz)

    @staticmethod
    def _tracked(ap):
        return str(ap.space).upper() in ("SB", "PSUM")

    def op(self, eng, fn, reads=(), writes=(), dma=False, out_dma=False):
        idx = len(self.ops[eng])
        waits = []
        kn = self.known[eng]

        def need(tok, raw=False):
            if tok[0] == 'e':
                if tok[1] == eng and not (raw and SAME_ENGINE_RAW and eng != "pe"):
                    return
                key = tok[1]
            else:
                key = ('d', tok[1])
            if kn.get(key, -1) >= tok[2]:
                return
            kn[key] = tok[2]
            waits.append(tok)

        if dma:
            if eng == "pool":
                k = 8 + self.dma_rr_pool
                self.dma_rr_pool = (self.dma_rr_pool + 1) % (N_DMA_SEMS - 8)
            else:
                k = self.dma_rr
                self.dma_rr = (self.dma_rr + 1) % 8
            if self.dma_cnt[k] > 0:
                need(('d', k, self.dma_cnt[k] * 16))
            self.dma_cnt[k] += 1
            mytok = ('d', k, self.dma_cnt[k] * 16)
        else:
            mytok = ('e', eng, idx)

        rb = [(ap.name, self._box(ap)) for ap in reads if self._tracked(ap) and str(ap.space).upper() != "PSUM"]
        wb = [(ap.name, self._box(ap)) for ap in writes if self._tracked(ap) and str(ap.space).upper() != "PSUM"]
        for ap in list(reads) + list(writes):
            if str(ap.space).upper() == "PSUM":
                ent = (ap.name, (0, 128, 0, 1 << 20))
                if ent not in wb:
                    wb.append(ent)
        for name, box in rb:
            tr = self.recs.get(name)
            if tr is None:
                continue
            for b in range(box[2] // BUCKET, (box[3] - 1) // BUCKET + 1):
                for r in tr.get(b, ()):
                    if r[3] and r[2]:
                        rbx = r[0]
                        if rbx[0] < box[1] and box[0] < rbx[1] and rbx[2] < box[3] and box[2] < rbx[3]:
                            need(r[1], True)
        for name, box in wb:
            tr = self.recs.get(name)
            if tr is None:
                continue
            for b in range(box[2] // BUCKET, (box[3] - 1) // BUCKET + 1):
                lst = tr.get(b)
                if not lst:
                    continue
                for r in lst:
                    if r[3]:
                        rbx = r[0]
                        if rbx[0] < box[1] and box[0] < rbx[1] and rbx[2] < box[3] and box[2] < rbx[3]:
                            need(r[1])
                            if box[0] <= rbx[0] and box[1] >= rbx[1] and box[2] <= rbx[2] and box[3] >= rbx[3]:
                                r[3] = False
                tr[b] = [r for r in lst if r[3]]
        for name, box in rb:
            tr = self.recs.setdefault(name, {})
            rec = [box, mytok, False, True]
            for b in range(box[2] // BUCKET, (box[3] - 1) // BUCKET + 1):
                lst = tr.setdefault(b, [])
                if not dma:
                    for r in lst:
                        if r[3] and (not r[2]) and r[1][0] == 'e' and r[1][1] == eng and r[0] == box:
                            r[3] = False
                lst.append(rec)
        for name, box in wb:
            tr = self.recs.setdefault(name, {})
            rec = [box, mytok, True, True]
            for b in range(box[2] // BUCKET, (box[3] - 1) // BUCKET + 1):
                tr.setdefault(b, []).append(rec)
        o = dict(fn=fn, waits=waits, tok=mytok, marked=False)
        self.ops[eng].append(o)
        if out_dma:
            self.out_dma_tokens.append(mytok)
        return o

    def dma(self, out, in_, eng="sp", out_dma=False):
        return self.op(eng, lambda e: e.dma_start(out=out, in_=in_), reads=[in_], writes=[out],
                       dma=True, out_dma=out_dma)

    def mm(self, out, lhsT, rhs, start=True, stop=True):
        return self.op("pe", lambda e: e.matmul(out, lhsT, rhs, start=start, stop=stop),
                       reads=[lhsT, rhs] + ([] if start else [out]), writes=[out])

    def tr(self, out, in_, ident):
        return self.op("pe", lambda e: e.transpose(out, in_, ident), reads=[in_, ident], writes=[out])

    def act(self, out, in_, func, bias=None, scale=None, accum=None):
        kw = {}
        rd = [in_]
        wr = [out]
        if bias is not None:
            kw["bias"] = bias
            if not isinstance(bias, (int, float)):
                rd.append(bias)
        if scale is not None:
            kw["scale"] = scale
            if not isinstance(scale, (int, float)):
                rd.append(scale)
        if accum is not None:
            kw["accum_out"] = accum
            wr.append(accum)
        return self.op("act", lambda e: e.activation(out=out, in_=in_, func=func, **kw), reads=rd, writes=wr)

    def cp(self, eng, out, in_):
        if eng == "act":
            return self.op("act", lambda e: e.copy(out=out, in_=in_), reads=[in_], writes=[out])
        return self.op(eng, lambda e: e.tensor_copy(out=out, in_=in_), reads=[in_], writes=[out])

    def tt(self, eng, out, in0, in1, op):
        return self.op(eng, lambda e: e.tensor_tensor(out=out, in0=in0, in1=in1, op=op), reads=[in0, in1], writes=[out])

    def ts(self, eng, out, in0, s1, op0, s2=None, op1=None):
        rd = [in0]
        if not isinstance(s1, (int, float)):
            rd.append(s1)
        if s2 is not None and not isinstance(s2, (int, float)):
            rd.append(s2)
        if op1 is None:
            return self.op(eng, lambda e: e.tensor_scalar(out=out, in0=in0, scalar1=s1, scalar2=None, op0=op0),
                           reads=rd, writes=[out])
        return self.op(eng, lambda e: e.tensor_scalar(out=out, in0=in0, scalar1=s1, scalar2=s2, op0=op0, op1=op1),
                       reads=rd, writes=[out])

    def stt(self, eng, out, in0, scalar, in1, op0, op1):
        rd = [in0, in1]
        if not isinstance(scalar, (int, float)):
            rd.append(scalar)
        return self.op(eng, lambda e: e.scalar_tensor_tensor(out=out, in0=in0, scalar=scalar, in1=in1, op0=op0, op1=op1),
                       reads=rd, writes=[out])

    def memset(self, eng, out, val):
        return self.op(eng, lambda e: e.memset(out, val), writes=[out])

    def recip(self, out, in_):
        return self.op("dve", lambda e: e.reciprocal(out=out, in_=in_), reads=[in_], writes=[out])

    def finish(self):
        nc = self.nc
        for e in ENGS:
            for o in self.ops[e]:
                for tok in o["waits"]:
                    if tok[0] == 'e':
                        self.ops[tok[1]][tok[2]]["marked"] = True
        cnt_at = {}
        for e in ENGS:
            c = 0
            arr = []
            for o in self.ops[e]:
                if o["marked"]:
                    c += 1
                arr.append(c)
            cnt_at[e] = arr
        fw = {}
        for tok in self.out_dma_tokens:
            fw[tok[1]] = max(fw.get(tok[1], 0), tok[2])

        with nc.Block() as block:
            def emit(ename, engobj):
                for o in self.ops[ename]:
                    ws = o["waits"]
                    emb = None
                    if EMBED_LAST_WAIT and ws and ename in EMBED_ENGINES and o["tok"][0] == 'e':
                        emb = ws[-1]
                        ws = ws[:-1]
                    for tok in ws:
                        if tok[0] == 'e':
                            engobj.wait_ge(self.eng_sem[tok[1]], cnt_at[tok[1]][tok[2]])
                        else:
                            engobj.wait_ge(self.dma_sems[tok[1]], tok[2])
                    ins = o["fn"](engobj)
                    if emb is not None:
                        if emb[0] == 'e':
                            ins._wait_ge(self.eng_sem[emb[1]], cnt_at[emb[1]][emb[2]])
                        else:
                            ins._wait_ge(self.dma_sems[emb[1]], emb[2])
                    if o["tok"][0] == 'd':
                        ins.then_inc(self.dma_sems[o["tok"][1]], 16)
                    elif o["marked"]:
                        ins.then_inc(self.eng_sem[ename], 1)
                if ename == "sp":
                    for k, v in fw.items():
                        engobj.wait_ge(self.dma_sems[k], v)

            @block.sync
            def _(sync):
                emit("sp", sync)

            @block.tensor
            def _(tensor):
                emit("pe", tensor)

            @block.scalar
            def _(scalar):
                emit("act", scalar)

            @block.vector
            def _(vector):
                emit("dve", vector)

            @block.gpsimd
            def _(gpsimd):
                emit("pool", gpsimd)


class _Trunc(Exception):
    pass


class Arena:
    def __init__(self, tensor, nelem):
        self.t = tensor
        self.n = nelem
        self.free = [(0, nelem)]
        self.live = {}
        self.peak = 0

    def alloc(self, name, nelem_bf16):
        nelem_bf16 = (nelem_bf16 + 15) // 16 * 16
        for i, (o, s) in enumerate(self.free):
            if s >= nelem_bf16:
                if s == nelem_bf16:
                    self.free.pop(i)
                else:
                    self.free[i] = (o + nelem_bf16, s - nelem_bf16)
                self.live[name] = (o, nelem_bf16)
                used = self.n - sum(s for _, s in self.free)
                self.peak = max(self.peak, used)
                return o
        raise RuntimeError("arena OOM allocating %s (%d); live=%s free=%s" % (name, nelem_bf16, self.live, self.free))

    def release(self, name):
        o, s = self.live.pop(name)
        self.free.append((o, s))
        self.free.sort()
        merged = []
        for o, s in self.free:
            if merged and merged[-1][0] + merged[-1][1] == o:
                merged[-1] = (merged[-1][0], merged[-1][1] + s)
            else:
                merged.append((o, s))
        self.free = merged

    def bf(self, name, shape):
        n = 1
        for s in shape[1:]:
            n *= s
        o = self.alloc(name, n)
        v = self.t[0:shape[0], o:o + n]
        if len(shape) == 3:
            v = v.rearrange("p (a b) -> p a b", a=shape[1])
        elif len(shape) == 4:
            v = v.rearrange("p (a b c) -> p a b c", a=shape[1], b=shape[2])
        return v

    def f32(self, name, shape):
        n = 1
        for s in shape[1:]:
            n *= s
        o = self.alloc(name, 2 * n)
        v = self.t[0:shape[0], o:o + 2 * n].bitcast(F32)
        if len(shape) == 3:
            v = v.rearrange("p (a b) -> p a b", a=shape[1])
        elif len(shape) == 4:
            v = v.rearrange("p (a b c) -> p a b c", a=shape[1], b=shape[2])
        return v


def _rope_tables():
    t = np.arange(2048)
    row = (t // 64).astype(np.float64)
    col = (t % 64).astype(np.float64)
    mla = np.zeros((2, 96, 2048), np.float32)
    mla[0, 0:64] = 1.0
    for r in range(32):
        pos = row if r < 16 else col
        i = r % 16
        f = i % 8
        inv = 10000.0 ** (-(2.0 * f) / 16.0)
        ang = pos * inv
        mla[0, 64 + r] = np.cos(ang)
        mla[1, 64 + r] = np.sin(ang) if i < 8 else -np.sin(ang)
    swa = np.zeros((2, 128, 2048), np.float32)
    for p in range(128):
        d = p % 64
        pos = row if d < 32 else col
        i = d % 32
        f = i % 16
        inv = 10000.0 ** (-(2.0 * f) / 32.0)
        ang = pos * inv
        swa[0, p] = np.cos(ang)
        swa[1, p] = np.sin(ang) if i < 16 else -np.sin(ang)
    return mla, swa


def _perm_mats():
    pm96 = np.zeros((96, 96), np.float32)
    for r in range(32):
        i = r % 16
        partner = r + 8 if i < 8 else r - 8
        pm96[64 + partner, 64 + r] = 1.0
    pm128 = np.zeros((128, 128), np.float32)
    for m in range(128):
        i = m % 32
        partner = m + 16 if i < 16 else m - 16
        pm128[partner, m] = 1.0
    return pm96, pm128


def _pool_mats():
    T = 384
    out = np.zeros((4, 5, 128, 128), np.float32)
    for g, w in enumerate((2, 4, 8, 16)):
        A = np.zeros((T, T), np.float64)
        for t in range(T):
            lo = min(max(t - w // 2, 0), T)
            hi = min(max(t + w // 2, 0), T)
            A[lo:hi, t] = 1.0 / (hi - lo)
            A[t, t] -= 1.0
        out[g, 0] = A[0:128, 0:128]
        out[g, 1] = A[128:256, 128:256]
        out[g, 2] = A[256:384, 256:384]
        out[g, 3] = A[0:128, 128:256]
        out[g, 4] = A[128:256, 0:128]
    return out


def _masks():
    k = np.arange(128)[:, None]
    q = np.arange(128)[None, :]
    m = np.zeros((2, 128, 128), np.float32)
    m[0] = np.where(k <= q, 0.0, -30000.0)
    m[1] = np.where(q <= k, 0.0, -30000.0)
    return m


def build_program(nl_p=4, nl_s=4, stages=3, dbg=None):
    nc = bass.Bass("TRN2", target_bir_lowering=False)

    def din(name, shape):
        return nc.dram_tensor(name, list(shape), F32, kind="ExternalInput").ap()

    def dout(name, shape):
        return nc.dram_tensor(name, list(shape), F32, kind="ExternalOutput").ap()

    xp_d = din("xp", [1024, 1024])
    xs_d = din("xs", [2048, 1024])
    cckv_d = din("cckv", [2, 256, 256])
    ckr_d = din("ckr", [2, 256, 32])
    ck_d = din("ck", [2, 256, 256])
    cv_d = din("cv", [2, 256, 256])
    cc_d = din("cc", [2, 1024])
    adaw_d = din("ada_w", [4, 1024, 3072])
    adab_d = din("ada_b", [4, 3072])
    vecs_d = din("vecs", [14, 1024])
    gkvn_d = din("gkvn", [2, 256])
    sink_d = din("sink", [32])
    wine_d = din("w_in_e", [2, 1024, 2208])
    wuq_d = din("w_uq", [2, 384, 768])
    wukv_d = din("w_ukv", [2, 256, 1024])
    wpool_d = din("w_pool", [2, 4, 128, 128])
    woe_d = din("w_out_e", [2, 1024, 1024])
    wino_d = din("w_in_o", [2, 1024, 2560])
    woo_d = din("w_out_o", [2, 1024, 1024])
    ident_d = din("ident", [128, 128])
    pm96_d = din("pm96", [96, 96])
    pm128_d = din("pm128", [128, 128])
    masks_d = din("masks", [2, 128, 128])
    amats_d = din("amats", [20, 128, 128])
    mlacs_d = din("mla_cs", [2, 96, 2048])
    swacs_d = din("swa_cs", [2, 128, 2048])

    yp_d = dout("yp", [1024, 1024])
    ys_d = dout("ys", [2048, 1024])
    sckv_d = dout("st_ckv", [4, 2, 256, 256])
    skr_d = dout("st_kr", [4, 2, 256, 32])
    sk_d = dout("st_k", [4, 2, 256, 256])
    sv_d = dout("st_v", [4, 2, 256, 256])

    with ExitStack() as st:
        P = Prog(nc, st)
        xT = P.sb("xT", [128, 8, 2048], F32)
        ident = P.sb("ident", [128, 128], F32)
        ones_bf = P.sb("ones_bf", [128, 128], BF16)
        identb = P.sb("identb", [128, 128], BF16)
        scb = P.sb("scb", [128, 8, 2], BF16)
        pm96 = P.sb("pm96", [96, 96], BF16)
        pm128 = P.sb("pm128", [128, 128], BF16)
        masks = P.sb("masks", [128, 2, 128], BF16)
        epsT = P.sb("epsT", [128, 1], F32)
        mod = P.sb("mod", [128, 4, 48], F32)
        vecT = P.sb("vecT", [128, 8, 32], F32)
        coefA = P.sb("coefA", [128, 4, 2, 8], F32)
        coefB = P.sb("coefB", [128, 4, 2, 8], F32)
        coefG = P.sb("coefG", [128, 4, 2, 8], F32)
        gkvb = P.sb("gkvb", [128, 2, 256], F32)
        esink = P.sb("esink", [128, 32], F32)
        rstd = P.sb("rstd", [128, 512], F32)
        rstd2 = P.sb("rstd2", [128, 512], F32)
        tmpf = [P.sb("tmpf%d" % i, [128, 512], F32) for i in range(2)]
        ARENA_N = 64 * 1024
        arena_t = P.sb("arena", [128, ARENA_N], BF16)
        AR = Arena(arena_t, ARENA_N)
        banks = [P.ps("ps%d" % i, [128, 512], F32) for i in range(8)]
        rr = {"all": 0, "s": 0, "o": 0, "g": 0}
        pools = {"all": list(range(8)), "s": [0, 1, 2, 3], "o": [4, 5], "g": [6, 7]}

        def bank(pool="all"):
            lst = pools[pool]
            b = banks[lst[rr[pool] % len(lst)]]
            rr[pool] += 1
            return b

        evac_rr = [0]

        def evac_eng():
            evac_rr[0] += 1
            return "dve" if evac_rr[0] % 2 else "act"

        P.dma(ident[:], ident_d)
        P.dma(identb[:], ident_d, eng="pool")
        P.dma(pm96[:], pm96_d, eng="pool")
        P.dma(pm128[:], pm128_d, eng="pool")
        P.dma(masks[:], masks_d.rearrange("a k q -> k a q"), eng="pool")
        P.dma(gkvb[:].rearrange("p a n -> p (a n)"), gkvn_d.rearrange("a n -> (a n)").partition_broadcast(128))
        P.dma(esink[:], sink_d.partition_broadcast(128))
        P.memset("dve", ones_bf[:], 1.0)
        P.memset("dve", epsT[:], EPS)
        P.act(esink[:], esink[:], AF.Exp)

        if dbg == "consts":
            P.dma(yp_d[0:128, 0:32], esink[:], out_dma=True)
            P.finish()
            return nc
        vst = AR.f32("vst", [32, 1024])
        P.memset("dve", vst[:], 0.0)
        P.dma(vst[0:14, :], vecs_d)
        pv = bank()
        for c in range(8):
            P.tr(pv[:, c * 32:(c + 1) * 32], vst[0:32, c * 128:(c + 1) * 128], ident[0:32, 0:32])
        for c in range(8):
            P.cp("dve", vecT[:, c, :], pv[:, c * 32:(c + 1) * 32])
        AR.release("vst")

        if dbg == "vec":
            P.dma(yp_d[0:128, 0:256], vecT[:].rearrange("p a b -> p (a b)"), out_dma=True)
            P.finish()
            return nc
        ccT = AR.f32("ccT", [128, 2, 8])
        for m in range(2):
            P.dma(ccT[:, m, :], cc_d[m].rearrange("(p c) -> p c", c=8))
        for m in range(2):
            P.act(scb[:, :, m], ccT[:, m, :], AF.Silu)
        AR.release("ccT")
        modv = mod[:].rearrange("p l (j c m) -> p l j c m", j=3, c=8)
        ada_state = {}

        def ada_alloc(l, nbuf):
            ada_state["l"] = l
            ada_state["adab"] = AR.bf("adab", [1, 3072])
            ada_state["bufs"] = [AR.bf("adw%d" % i, [128, 8, 512]) for i in range(nbuf)]
            ada_state["nbuf"] = nbuf
            P.dma(ada_state["adab"][:], adab_d[l:l + 1, :], eng="pool")

        def ada_issue(blks):
            l = ada_state["l"]
            wv = adaw_d[l].rearrange("(p c) n -> p c n", c=8)
            for blk in blks:
                wt = ada_state["bufs"][blk % ada_state["nbuf"]]
                P.dma(wt[:], wv[:, :, blk * 512:(blk + 1) * 512], eng="pool")

        def ada_compute(blks):
            l = ada_state["l"]
            adab = ada_state["adab"]
            for blk in blks:
                wt = ada_state["bufs"][blk % ada_state["nbuf"]]
                pm = bank()
                for nci in range(4):
                    nch = blk * 4 + nci
                    for c in range(8):
                        P.mm(pm[:, 2 * nci:2 * nci + 2], wt[:, c, nci * 128:(nci + 1) * 128], scb[:, c, :],
                             start=(c == 0), stop=False)
                    P.mm(pm[:, 2 * nci:2 * nci + 2], adab[0:1, nch * 128:(nch + 1) * 128], ones_bf[0:1, 0:2],
                         start=False, stop=True)
                P.cp("dve", mod[:, l, blk * 8:(blk + 1) * 8], pm[:, 0:8])

        def ada_finish():
            l = ada_state["l"]
            for m in range(2):
                P.stt("dve", coefA[:, l, m, :], modv[:, l, 1, :, m], 1.0, vecT[:, :, l], ALU.add, ALU.mult)
                P.cp("dve", coefB[:, l, m, :], modv[:, l, 0, :, m])
                P.tt("dve", coefG[:, l, m, :], modv[:, l, 2, :, m], vecT[:, :, 4 + l], ALU.mult)
            for i in range(ada_state["nbuf"]):
                AR.release("adw%d" % i)
            AR.release("adab")
            ada_state.clear()

        def ada_all(l):
            ada_alloc(l, 3)
            for blk in range(6):
                ada_issue([blk])
                ada_compute([blk])
            ada_finish()

        ADA_INTERLEAVE = nl_p >= 4
        ada0_done = [False]
        if not ADA_INTERLEAVE:
            for l in range(1, 4):
                ada_all(l)
        if dbg == "ada":
            P.dma(yp_d[0:128, 0:192], mod[:].rearrange("p a b -> p (a b)"), out_dma=True)
            P.dma(yp_d[128:256, 0:64], coefA[:].rearrange("p a b c -> p (a b c)"), out_dma=True)
            P.dma(yp_d[256:384, 0:64], coefG[:].rearrange("p a b c -> p (a b c)"), out_dma=True)
            P.finish()
            return nc
        def rstd_from_ssq(ps_ssq, n, dim, out):
            P.act(out, ps_ssq, AF.Ln, bias=epsT[:, 0:1], scale=1.0 / dim)
            P.act(out, out, AF.Exp, scale=-0.5)

        def modulate_parts(hT, sq, l, m, t0, n):
            parts = []

            ns_ = sq.shape[1]

            def stats_a():
                for c in range(8):
                    P.act(sq[:, c, :n], xT[:, c, t0:t0 + n], AF.Square)

            def stats_b():
                pb = bank()
                for c in range(8):
                    P.mm(pb[:, :n], ones_bf[:, :], sq[:, c, :n], start=(c == 0), stop=(c == 7))
                rstd_from_ssq(pb[:, :n], n, 1024, rstd[:, :n])

            def stats():
                pb = bank()
                for c0 in range(0, 8, ns_):
                    for c in range(c0, c0 + ns_):
                        P.act(sq[:, c % ns_, :n], xT[:, c, t0:t0 + n], AF.Square)
                    for c in range(c0, c0 + ns_):
                        P.mm(pb[:, :n], ones_bf[:, :], sq[:, c % ns_, :n], start=(c == 0), stop=(c == 7))
                rstd_from_ssq(pb[:, :n], n, 1024, rstd[:, :n])
            if ns_ >= 8:
                parts.extend([stats_a, (lambda: None), stats_b])
            else:
                parts.append(stats)
            for c in range(8):
                def ap(c=c):
                    tf = tmpf[c % 2]
                    P.stt("dve", tf[:, :n], xT[:, c, t0:t0 + n], coefA[:, l, m, c:c + 1], rstd[:, :n], ALU.mult, ALU.mult)
                    P.act(hT[:, c, :n], tf[:, :n], AF.Identity, bias=coefB[:, l, m, c:c + 1], scale=1.0)
                parts.append(ap)
            return parts

        def modulate(hT, sq, l, m, t0, n):
            for f in modulate_parts(hT, sq, l, m, t0, n):
                f()

        def run_part(parts, k=1):
            for _ in range(k):
                if parts:
                    parts.pop(0)()

        def load_w(dst, src_rows_view, col0, ncols):
            for c in range(dst.shape[1]):
                P.dma(dst[:, c, :], src_rows_view[:, c, col0:col0 + ncols], eng="pool")

        def alloc_hs():
            hs = [(AR.bf("hT", [128, 8, 512]), AR.bf("sq", [128, 8, 512]))]
            try:
                a = AR.bf("hT1", [128, 8, 512])
                try:
                    b = AR.bf("sq1", [128, 8, 512])
                    hs.append((a, b))
                except RuntimeError:
                    AR.release("hT1")
            except RuntimeError:
                pass
            return hs

        def free_hs(hs):
            AR.release("hT")
            AR.release("sq")
            if len(hs) > 1:
                AR.release("hT1")
                AR.release("sq1")

        def stage3(l, m, NT, mbuf, wg, wout, hooks=None, nxt=None):
            oT = AR.f32("oT", [128, 8, 512])
            hs3 = alloc_hs()
            sg = [AR.bf("sg%d" % i, [128, 512]) for i in range(2)]
            if nxt is not None:
                prefetch_w1(*nxt)
            tiles = list(range(0, NT, 512))
            n = 512
            pipe = len(hs3) > 1
            if pipe:
                modulate(hs3[0][0], hs3[0][1], l, m, tiles[0], n)
            for ti, t0 in enumerate(tiles):
                if hooks and ti in hooks:
                    hooks[ti]()
                hT, sq = hs3[ti % len(hs3)]
                if not pipe:
                    modulate(hT, sq, l, m, t0, n)
                nparts = []
                if pipe and ti + 1 < len(tiles):
                    nh, nsq = hs3[(ti + 1) % 2]
                    nparts = modulate_parts(nh, nsq, l, m, tiles[ti + 1], n)
                for mc in range(8):
                    pb = bank()
                    for k in range(8):
                        P.mm(pb[:, :n], wg[:, k, mc * 128:(mc + 1) * 128], hT[:, k, :n], start=(k == 0), stop=(k == 7))
                    s_ = sg[mc % 2]
                    P.act(s_[:, :n], pb[:, :n], AF.Silu)
                    P.tt("dve" if mc % 2 else "pool", mbuf[:, mc, t0:t0 + n], mbuf[:, mc, t0:t0 + n], s_[:, :n], ALU.mult)
                    if mc == 1:
                        run_part(nparts)
                for dc in range(8):
                    pb = bank()
                    for k in range(8):
                        P.mm(pb[:, :n], wout[:, k, dc * 128:(dc + 1) * 128], mbuf[:, k, t0:t0 + n],
                             start=(k == 0), stop=(k == 7))
                    P.cp("dve", oT[:, dc, :n], pb[:, :n])
                    P.act(sq[:, dc, :n], pb[:, :n], AF.Square)
                    run_part(nparts)
                run_part(nparts, 16)
                pb = bank()
                for dc in range(8):
                    P.mm(pb[:, :n], ones_bf[:, :], sq[:, dc, :n], start=(dc == 0), stop=(dc == 7))
                rstd_from_ssq(pb[:, :n], n, 1024, rstd2[:, :n])
                for dc in range(8):
                    tf = tmpf[dc % 2]
                    P.stt("dve", tf[:, :n], oT[:, dc, :n], coefG[:, l, m, dc:dc + 1], rstd2[:, :n], ALU.mult, ALU.mult)
                    P.tt("pool", xT[:, dc, t0:t0 + n], xT[:, dc, t0:t0 + n], tf[:, :n], ALU.add)
            if hooks and "post" in hooks:
                hooks["post"]()
            free_hs(hs3)
            for nm in ("oT", "sg0", "sg1"):
                AR.release(nm)

        rope_rr = [0]

        def rope_a(src_ps, pr, n, scale, pm, cs, t0, rb):
            qc, qsn = rb[rope_rr[0] % len(rb)]
            rope_rr[0] += 1
            P.stt("dve", qc[0:pr, :n], src_ps[0:pr, :n], float(scale), cs[0:pr, 0, t0:t0 + n], ALU.mult, ALU.mult)
            P.stt("dve", qsn[0:pr, :n], src_ps[0:pr, :n], float(scale), cs[0:pr, 1, t0:t0 + n], ALU.mult, ALU.mult)

            def phase_b():
                p2 = bank("g")
                P.mm(p2[0:pr, :n], identb[0:pr, 0:pr], qc[0:pr, :n], start=True, stop=False)
                P.mm(p2[0:pr, :n], pm[:, :], qsn[0:pr, :n], start=False, stop=True)
                return p2
            return phase_b

        def rope_apply(src_ps, pr, n, scale, pm, cs, t0, rb):
            return rope_a(src_ps, pr, n, scale, pm, cs, t0, rb)()

        def alloc_rb():
            return [(AR.bf("rqc%d" % i, [128, 512]), AR.bf("rqs%d" % i, [128, 512])) for i in range(2)]

        def free_rb():
            for i in range(2):
                AR.release("rqc%d" % i)
                AR.release("rqs%d" % i)

        pref = {}

        def prefetch_w1(lnext, ctx_next, full):
            i2 = lnext // 2
            if (not full) and lnext % 2 == 1:
                return
            try:
                if lnext % 2 == 0:
                    wv = wine_d[i2].rearrange("(c p) n -> p c n", p=128)
                    a = AR.bf("w1a", [128, 8, 672])
                    load_w(a, wv, 0, 672)
                    pref["w1a"] = a
                    if full:
                        b_ = AR.bf("w1b", [128, 8, 512])
                        load_w(b_, wv, 1184, 512)
                        pref["w1b"] = b_
                else:
                    wv = wino_d[i2].rearrange("(c p) n -> p c n", p=128)
                    a = AR.bf("wq0", [128, 8, 512])
                    load_w(a, wv, 0, 512)
                    pref["wq0"] = a
                    if full:
                        b_ = AR.bf("wq1", [128, 8, 512])
                        load_w(b_, wv, 512, 512)
                        pref["wq1"] = b_
                        k_ = AR.bf("wk", [128, 8, 256])
                        load_w(k_, wv, 1024, 256)
                        pref["wk"] = k_
            except RuntimeError:
                pass

        def even_layer(l, m, NT, nseq, T, ctx, nxt=None):
            i = l // 2
            S = T + (256 if ctx else 0)
            KT = nseq * S
            nkc = S // 128
            wv_in = wine_d[i].rearrange("(c p) n -> p c n", p=128)
            w1a = pref.pop("w1a", None)
            if w1a is None:
                w1a = AR.bf("w1a", [128, 8, 672])
                load_w(w1a, wv_in, 0, 672)
            w1b = pref.pop("w1b", None)
            if w1b is None:
                w1b = AR.bf("w1b", [128, 8, 512])
                load_w(w1b, wv_in, 1184, 512)
            wp = AR.bf("wp", [128, 4, 128])
            amats = AR.bf("amats", [128, 20, 128])
            P.dma(amats[:], amats_d.rearrange("a k q -> k a q"), eng="pool")
            P.dma(wp[:], wpool_d[i].rearrange("g c e -> c g e"), eng="pool")
            mbuf = AR.bf("mbuf", [128, 8, NT])
            mo = AR.live["mbuf"][0]
            vtok = arena_t[:, mo:mo + (NT // 128) * 512].rearrange("p (j q) -> p j q", q=512)
            cqn = AR.bf("cqn", [128, 3, NT])
            ckvn = AR.bf("ckvn", [128, 2, KT])
            krT = AR.bf("krT", [96, KT])
            cs = None
            if ctx:
                cs = AR.bf("cs", [96, 2, 2048])
                P.dma(cs[:, 0, :], mlacs_d[0], eng="pool")
                P.dma(cs[:, 1, :], mlacs_d[1], eng="pool")
                rb = alloc_rb()
                cst = AR.f32("cst", [128, 2, 256 + 96])
                P.memset("pool", cst[:, :, 256:320], 0.0)
                for jj in range(2):
                    P.dma(cst[:, jj, 0:256], cckv_d[i, jj * 128:(jj + 1) * 128, :])
                    P.dma(cst[:, jj, 320:352], ckr_d[i, jj * 128:(jj + 1) * 128, :])
                for jj in range(2):
                    pb = bank()
                    for c in range(2):
                        P.tr(pb[:, c * 128:(c + 1) * 128], cst[:, jj, c * 128:(c + 1) * 128], ident[:, :])
                    P.tr(pb[0:96, 256:384], cst[:, jj, 256:352], ident[:, :])
                    for c in range(2):
                        P.cp("dve", ckvn[:, c, T + jj * 128:T + (jj + 1) * 128], pb[:, c * 128:(c + 1) * 128])
                    P.cp("dve", krT[64:96, T + jj * 128:T + (jj + 1) * 128], pb[64:96, 256:384])
                AR.release("cst")
            stg = None
            if not ctx:
                stg = [AR.f32("stg%d" % j, [128, 288]) for j in range(2)]
            junk = AR.f32("junk", [128, 256])
            ssq1 = AR.f32("ssq1", [128, 2])
            hs1 = alloc_hs()

            def kidx(t):
                return (t // T) * S + (t % T)

            pipe1 = len(hs1) > 1
            if pipe1:
                modulate(hs1[0][0], hs1[0][1], l, m, 0, 512)
            for t0 in range(0, NT, 512):
                n = 512
                hT, sq = hs1[(t0 // 512) % len(hs1)]
                if not pipe1:
                    modulate(hT, sq, l, m, t0, n)
                nparts = []
                if pipe1 and t0 + 512 < NT:
                    nh, nsq = hs1[((t0 // 512) + 1) % 2]
                    nparts = modulate_parts(nh, nsq, l, m, t0 + 512, 512)
                for oc in range(3):
                    pb = bank()
                    for k in range(8):
                        P.mm(pb[:, :n], w1a[:, k, oc * 128:(oc + 1) * 128], hT[:, k, :n], start=(k == 0), stop=(k == 7))
                    P.act(sq[:, oc, :n], pb[:, :n], AF.Square)
                    P.ts("dve", cqn[:, oc, t0:t0 + n], pb[:, :n], vecT[:, oc, 8 + i:9 + i], ALU.mult)
                    run_part(nparts)
                pieces = [(t0, n)] if T >= 512 else [(t0 + a, T) for a in range(0, n, T)]
                for oc in range(2):
                    pb = bank()
                    for k in range(8):
                        P.mm(pb[:, :n], w1a[:, k, 384 + oc * 128:384 + (oc + 1) * 128], hT[:, k, :n],
                             start=(k == 0), stop=(k == 7))
                    P.act(sq[:, 4 + oc, :n], pb[:, :n], AF.Square)
                    for (ta, tn) in pieces:
                        P.ts("dve", ckvn[:, oc, kidx(ta):kidx(ta) + tn], pb[:, ta - t0:ta - t0 + tn],
                             vecT[:, oc, 10 + i:11 + i], ALU.mult)
                    run_part(nparts)
                pb = bank()
                for oc in range(3):
                    P.mm(pb[:, :n], ones_bf[:, :], sq[:, oc, :n], start=(oc == 0), stop=(oc == 2))
                rstd_from_ssq(pb[:, :n], n, 384, rstd2[:, :n])
                for oc in range(3):
                    P.tt("pool" if oc == 1 else "dve", cqn[:, oc, t0:t0 + n], cqn[:, oc, t0:t0 + n], rstd2[:, :n], ALU.mult)
                pb = bank()
                for k in range(8):
                    P.mm(pb[0:96, :n], w1a[:, k, 576:672], hT[:, k, :n], start=(k == 0), stop=(k == 7))
                if ctx:
                    p2 = rope_apply(pb, 96, n, 1.0, pm96, cs, t0, rb)
                    P.cp("act", krT[64:96, kidx(t0):kidx(t0) + n], p2[64:96, :n])
                else:
                    for (ta, tn) in pieces:
                        P.cp("dve", krT[64:96, kidx(ta):kidx(ta) + tn], pb[64:96, ta - t0:ta - t0 + tn])
                pb = bank()
                for oc in range(2):
                    P.mm(pb[:, :n], ones_bf[:, :], sq[:, 4 + oc, :n], start=(oc == 0), stop=(oc == 1))
                rstd_from_ssq(pb[:, :n], n, 256, rstd2[:, :n])
                for oc in range(2):
                    for (ta, tn) in pieces:
                        P.tt("pool" if oc == 1 else "dve", ckvn[:, oc, kidx(ta):kidx(ta) + tn],
                             ckvn[:, oc, kidx(ta):kidx(ta) + tn], rstd2[:, ta - t0:ta - t0 + tn], ALU.mult)
                for j in range(n // 128):
                    pb = bank()
                    for k in range(8):
                        P.mm(pb[:, :], hT[:, k, j * 128:(j + 1) * 128], w1b[:, k, :], start=(k == 0), stop=(k == 7))
                    P.cp(evac_eng(), vtok[:, (t0 // 128) + j, :], pb[:, :])
                    run_part(nparts)
                run_part(nparts, 16)
                if not ctx:
                    for j in range(n // 128):
                        tok = t0 + j * 128
                        b = tok // T
                        pos = tok % T
                        pb = bank()
                        for k in range(8):
                            P.mm(pb[:, 0:288], hT[:, k, j * 128:(j + 1) * 128], w1a[:, k, 384:672],
                                 start=(k == 0), stop=(k == 7))
                        so = stg[j % 2]
                        P.act(junk[:, :], pb[:, 0:256], AF.Square)
                        P.op("dve", lambda e: e.reduce_sum(out=ssq1[:, 0:1], in_=junk[:, :], axis=mybir.AxisListType.X),
                             reads=[junk[:, :]], writes=[ssq1[:, 0:1]])
                        P.act(ssq1[:, 1:2], ssq1[:, 0:1], AF.Ln, bias=epsT[:, 0:1], scale=1.0 / 256)
                        P.act(ssq1[:, 1:2], ssq1[:, 1:2], AF.Exp, scale=-0.5)
                        P.stt("dve", so[:, 0:256], pb[:, 0:256], ssq1[:, 1:2], gkvb[:, i, :], ALU.mult, ALU.mult)
                        P.cp("dve", so[:, 256:288], pb[:, 256:288])
                        P.dma(sckv_d[b, i, pos:pos + 128, :], so[:, 0:256], out_dma=True)
                        P.dma(skr_d[b, i, pos:pos + 128, :], so[:, 256:288], out_dma=True)
            free_hs(hs1)
            pooled = [AR.bf("pooled%d" % j, [128, 512]) for j in range(2)]
            ppend = [None]
            ncs = T // 128
            for t0 in range(0, NT, 512):
                for g in range(4):
                    pb = bank()
                    for j in range(4):
                        ch = t0 // 128 + j
                        cin = ch % ncs
                        contrib = []
                        if cin > 0:
                            contrib.append((ch - 1, 3))
                        contrib.append((ch, 0 if cin == 0 else (2 if cin == ncs - 1 else 1)))
                        if cin < ncs - 1:
                            contrib.append((ch + 1, 4))
                        for ci, (src, kind) in enumerate(contrib):
                            P.mm(pb[:, j * 128:(j + 1) * 128], vtok[:, src, g * 128:(g + 1) * 128],
                                 amats[:, g * 5 + kind, :], start=(ci == 0), stop=(ci == len(contrib) - 1))
                    pl = pooled[g % 2]
                    P.cp(evac_eng(), pl[:, :], pb[:, :])
                    if ppend[0] is not None:
                        ppend[0]()

                    def _pw(pl=pl, g=g, t0=t0):
                        pb2 = bank()
                        P.mm(pb2[:, :], wp[:, g, :], pl[:, :])
                        P.ts("dve", mbuf[:, 4 + g, t0:t0 + 512], pb2[:, :], vecT[:, g, 12 + i:13 + i], ALU.mult)
                    ppend[0] = _pw
            if ppend[0] is not None:
                ppend[0]()
                ppend[0] = None
            for nm in ("pooled0", "pooled1", "junk", "ssq1", "w1a", "w1b", "wp", "amats"):
                AR.release(nm)
            if not ctx:
                AR.release("stg0")
                AR.release("stg1")
            if stages == 1 and l == trunc_l[0]:
                raise _Trunc()
            ada_hooks = None
            if ADA_INTERLEAVE and (not ctx) and l + 1 < 4:
                ada_alloc(l + 1, 2)
                ada_issue([0, 1])

                def _h0():
                    ada_compute([0, 1])
                    ada_issue([2, 3])

                def _h1():
                    ada_compute([2, 3])
                    ada_issue([4, 5])

                def _hp():
                    ada_compute([4, 5])
                    ada_finish()
                ada_hooks = {0: _h0, 1: _h1, "post": _hp}
            wuq = AR.bf("wuq", [128, 3, 768])
            wuk = AR.bf("wuk", [128, 2, 8, 64])
            wuv = AR.bf("wuv", [128, 2, 8, 64])
            P.dma(wuq[:], wuq_d[i].rearrange("(c p) n -> p c n", p=128), eng="pool")
            wukv_v = wukv_d[i].rearrange("(c p) (h t d) -> p c h t d", p=128, h=8, t=2)
            for c in range(2):
                P.dma(wuk[:, c, :, :], wukv_v[:, c, :, 0, :], eng="pool")
                P.dma(wuv[:, c, :, :], wukv_v[:, c, :, 1, :], eng="pool")
            def load_wg():
                wg_ = AR.bf("wg", [128, 8, 1024])
                load_w(wg_[:, :, 0:512], wv_in, 672, 512)
                load_w(wg_[:, :, 512:1024], wv_in, 1696, 512)
                return wg_

            def load_wout():
                wout_ = AR.bf("wout", [128, 8, 1024])
                load_w(wout_, woe_d[i].rearrange("(c p) n -> p c n", p=128), 0, 1024)
                return wout_
            wg = load_wg()
            if not ctx:
                wout = load_wout()
            NB = 2
            TT = nseq * T
            nkt = KT // 128
            qh = [AR.bf("qh%d" % b, [96, TT]) for b in range(NB)]
            kh = [AR.bf("kh%d" % b, [96, KT]) for b in range(NB)]
            vh = [AR.bf("vh%d" % b, [128, nkt, 128]) for b in range(NB)]
            pT = [AR.bf("pT%d" % b, [128, 512]) for b in range(4)]
            rc = AR.f32("rc", [64, 512])
            for b in range(NB):
                P.memset("pool", vh[b][:, :, 64:128], 1.0)
                P.cp("pool", kh[b][64:96, :], krT[64:96, 0:KT])
            pend = [None]

            def flush_pend():
                if pend[0] is not None:
                    pend[0]()
                    pend[0] = None

            def build_steps(h, b):
                st_ = []
                for ka in range(0, KT, 512):
                    def f(ka=ka):
                        kn = min(512, KT - ka)
                        pb = bank("g")
                        for c in range(2):
                            P.mm(pb[0:64, :kn], wuk[:, c, h, :], ckvn[:, c, ka:ka + kn], start=(c == 0), stop=(c == 1))
                        P.cp("dve" if ctx else "act", kh[b][0:64, ka:ka + kn], pb[0:64, :kn])
                    st_.append(f)
                for ja in range(0, nkt, 8):
                    def f(ja=ja):
                        jn = min(8, nkt - ja)
                        pb = bank("g")
                        for jj in range(jn):
                            for c in range(2):
                                P.mm(pb[:, jj * 64:(jj + 1) * 64], ckvn[:, c, (ja + jj) * 128:(ja + jj + 1) * 128],
                                     wuv[:, c, h, :], start=(c == 0), stop=(c == 1))
                        P.cp("dve" if ctx else "act", vh[b][:, ja:ja + jn, 0:64], pb[:, 0:jn * 64].rearrange("p (j d) -> p j d", d=64))
                    st_.append(f)
                for qa in range(0, TT, 512):
                    hold = {}

                    def f(qa=qa, hold=hold):
                        pb = bank("g")
                        for c in range(3):
                            P.mm(pb[0:96, :512], wuq[:, c, h * 96:(h + 1) * 96], cqn[:, c, qa:qa + 512],
                                 start=(c == 0), stop=(c == 2))
                        if ctx:
                            hold["b"] = rope_a(pb, 96, 512, MLA_SCALE, pm96, cs, qa, rb)
                        else:
                            P.act(qh[b][:, qa:qa + 512], pb[0:96, :512], AF.Copy, scale=float(MLA_SCALE))
                    st_.append(f)
                    if ctx:
                        def f2(qa=qa, hold=hold):
                            p2 = hold["b"]()
                            P.cp("dve", qh[b][0:96, qa:qa + 512], p2[0:96, :512])
                        st_.append(f2)
                return st_

            for f in build_steps(0, 0):
                f()
            for h in range(8):
                b = h % NB
                half = h % 2
                inj = build_steps(h + 1, (h + 1) % NB) if h + 1 < 8 else []
                if ctx:
                    total_steps = (T // 512) * nkc
                    every = max(1, total_steps // (len(inj) + 1))
                    stepc = 0
                    for qa in range(0, T, 512):
                        po = bank("o")
                        sc_ps = {}

                        def issue_s(j):
                            ps_ = bank("s")
                            P.mm(ps_[:, :512], kh[b][:, j * 128:(j + 1) * 128], qh[b][:, qa:qa + 512])
                            sc_ps[j] = ps_

                        for j in range(3):
                            issue_s(j)
                        for j in range(nkc):
                            pt = pT[j % 4]
                            P.act(pt[:, :], sc_ps.pop(j)[:, :], AF.Exp)
                            if j + 3 < nkc:
                                issue_s(j + 3)
                            P.mm(po[:, :], vh[b][:, j, :], pt[:, :], start=(j == 0), stop=(j == nkc - 1))
                            stepc += 1
                            if inj and stepc % every == 0:
                                inj.pop(0)()
                        P.recip(rc[0:64, :], po[64:128, :])
                        P.tt("dve", mbuf[half * 64:(half + 1) * 64, h // 2, qa:qa + 512],
                             po[0:64, :], rc[0:64, :], ALU.mult)
                else:
                    for p in range(nseq // 2):
                        pts = []
                        for a in range(2):
                            sq_ = 2 * p + a
                            ps_ = bank("s")
                            for j in range(2):
                                P.mm(ps_[:, j * 256:(j + 1) * 256], kh[b][:, sq_ * 256 + j * 128:sq_ * 256 + (j + 1) * 128],
                                     qh[b][:, sq_ * 256:(sq_ + 1) * 256])
                            pt = pT[(2 * (p + h * (nseq // 2)) + a) % 4]
                            P.act(pt[:, :], ps_[:, :], AF.Exp)
                            pts.append(pt)
                        flush_pend()
                        for _ in range(3):
                            if inj:
                                inj.pop(0)()

                        def fin(p=p, pts=pts, b=b, h=h, half=half):
                            po = bank("o")
                            for a in range(2):
                                sq_ = 2 * p + a
                                for j in range(2):
                                    P.mm(po[:, a * 256:(a + 1) * 256], vh[b][:, 2 * sq_ + j, :], pts[a][:, j * 256:(j + 1) * 256],
                                         start=(j == 0), stop=(j == 1))
                            P.cp("dve", rc[0:64, :], po[64:128, :])
                            P.act(rc[0:64, :], rc[0:64, :], AF.Ln)
                            P.act(rc[0:64, :], rc[0:64, :], AF.Exp, scale=-1.0)
                            P.tt("dve", mbuf[half * 64:(half + 1) * 64, h // 2, p * 512:(p + 1) * 512],
                                 po[0:64, :], rc[0:64, :], ALU.mult)
                        pend[0] = fin
                while inj:
                    inj.pop(0)()
            flush_pend()
            for nm in ["qh%d" % b for b in range(NB)] + ["kh%d" % b for b in range(NB)] + ["vh%d" % b for b in range(NB)] + \
                      ["pT%d" % b for b in range(4)] + ["rc", "wuq", "wuk", "wuv", "cqn", "ckvn", "krT"]:
                AR.release(nm)
            if ctx:
                AR.release("cs")
                free_rb()
                wout = load_wout()
            if stages == 2 and l == trunc_l[0]:
                raise _Trunc()
            stage3(l, m, NT, mbuf, wg, wout, ada_hooks, nxt)
            for nm in ("wg", "wout", "mbuf"):
                AR.release(nm)

        def odd_layer(l, m, NT, nseq, T, ctx, nxt=None):
            i = l // 2
            S = T + (256 if ctx else 0)
            KT = nseq * S
            nkc = S // 128
            wv_in = wino_d[i].rearrange("(c p) n -> p c n", p=128)
            qT = AR.bf("mbuf", [128, 8, NT])
            kd = AR.bf("kd", [128, 4, KT])
            va = AR.bf("va", [128, KT // 128, 4, 128])
            wq0 = pref.pop("wq0", None)
            if wq0 is None:
                wq0 = AR.bf("wq0", [128, 8, 512])
                load_w(wq0, wv_in, 0, 512)
            wq1 = pref.pop("wq1", None)
            if wq1 is None:
                wq1 = AR.bf("wq1", [128, 8, 512])
                load_w(wq1, wv_in, 512, 512)
            wqs = [wq0, wq1]
            wk = pref.pop("wk", None)
            if wk is None:
                wk = AR.bf("wk", [128, 8, 256])
                load_w(wk, wv_in, 1024, 256)
            nkv = 256 if ctx else 512
            wkv = AR.bf("wkv", [128, 8, nkv])
            load_w(wkv, wv_in, 1536 - nkv, nkv)
            P.memset("pool", va[:, :, :, 64:128], 1.0)
            cs = None
            qs = None
            if ctx:
                cs = AR.bf("cs", [128, 2, 2048])
                P.dma(cs[:, 0, :], swacs_d[0], eng="pool")
                P.dma(cs[:, 1, :], swacs_d[1], eng="pool")
                rb = alloc_rb()
                cst = AR.f32("cst", [128, 2, 256])
                cdup = AR.f32("cdup", [128, 4, 128])
                for jj in range(2):
                    P.dma(cst[:, jj, :], cv_d[i, jj * 128:(jj + 1) * 128, :])
                for jj in range(2):
                    P.cp("dve", va[:, T // 128 + jj, :, 0:64], cst[:, jj, :].rearrange("p (h d) -> p h d", h=4))
                for jj in range(2):
                    P.dma(cst[:, jj, :], ck_d[i, jj * 128:(jj + 1) * 128, :])
                for jj in range(2):
                    P.cp("dve", cdup[:, :, 0:64], cst[:, jj, :].rearrange("p (h d) -> p h d", h=4))
                    P.cp("pool", cdup[:, :, 64:128], cst[:, jj, :].rearrange("p (h d) -> p h d", h=4))
                    pb = bank()
                    for kvh in range(4):
                        P.tr(pb[:, kvh * 128:(kvh + 1) * 128], cdup[:, kvh, :], ident[:, :])
                    for kvh in range(4):
                        P.cp("dve", kd[:, kvh, T + jj * 128:T + (jj + 1) * 128], pb[:, kvh * 128:(kvh + 1) * 128])
                AR.release("cst")
                AR.release("cdup")
            stg = None
            if not ctx:
                stg = [AR.f32("stg%d" % j, [128, 512]) for j in range(2)]
            if ctx:
                hs1 = [(AR.bf("hT", [128, 8, 512]), AR.bf("sq", [128, 4, 512])),
                       (AR.bf("hT1", [128, 8, 512]), AR.bf("sq1", [128, 4, 512]))]
            else:
                hs1 = alloc_hs()

            def kidx(t):
                return (t // T) * S + (t % T)

            pipe1 = len(hs1) > 1
            rpend = [None]
            if pipe1:
                modulate(hs1[0][0], hs1[0][1], l, m, 0, 512)
            for t0 in range(0, NT, 512):
                n = 512
                hT, sq = hs1[(t0 // 512) % len(hs1)]
                if not pipe1:
                    modulate(hT, sq, l, m, t0, n)
                pieces = [(t0, n)] if T >= 512 else [(t0 + a, T) for a in range(0, n, T)]
                nparts = []
                if pipe1 and t0 + 512 < NT:
                    nh, nsq = hs1[((t0 // 512) + 1) % 2]
                    nparts = modulate_parts(nh, nsq, l, m, t0 + 512, 512)
                for oc in range(8):
                    pb = bank()
                    for k in range(8):
                        P.mm(pb[:, :n], wqs[oc // 4][:, k, (oc % 4) * 128:(oc % 4 + 1) * 128], hT[:, k, :n],
                             start=(k == 0), stop=(k == 7))
                    if ctx:
                        pb_fn = rope_a(pb, 128, n, SWA_SCALE, pm128, cs, t0, rb)
                        if rpend[0] is not None:
                            rpend[0]()

                        def _fin_q(pb_fn=pb_fn, oc=oc, t0=t0, n=n):
                            p2 = pb_fn()
                            P.cp("act", qT[:, oc, t0:t0 + n], p2[:, :n])
                        rpend[0] = _fin_q
                    else:
                        P.ts("dve", qT[:, oc, t0:t0 + n], pb[:, :n], SWA_SCALE, ALU.mult)
                    run_part(nparts)
                for kc in range(2):
                    pb = bank()
                    for k in range(8):
                        P.mm(pb[:, :n], wk[:, k, kc * 128:(kc + 1) * 128], hT[:, k, :n], start=(k == 0), stop=(k == 7))
                    def _copies(srcs, kc=kc):
                        for (src, so, sn, ko) in srcs:
                            for hh in range(2):
                                for dh in range(2):
                                    P.cp("act" if dh != hh else "dve",
                                         kd[dh * 64:(dh + 1) * 64, 2 * kc + hh, ko:ko + sn],
                                         src[hh * 64:(hh + 1) * 64, so:so + sn])
                    if ctx:
                        pb_fn = rope_a(pb, 128, n, 1.0, pm128, cs, t0, rb)
                        if rpend[0] is not None:
                            rpend[0]()

                        def _fin_k(pb_fn=pb_fn, t0=t0, n=n, _copies=_copies):
                            p2 = pb_fn()
                            _copies([(p2, 0, n, kidx(t0))])
                        rpend[0] = _fin_k
                    else:
                        _copies([(pb, ta - t0, tn, kidx(ta)) for (ta, tn) in pieces])
                run_part(nparts, 16)
                for j in range(n // 128):
                    tok = t0 + j * 128
                    pb = bank()
                    for k in range(8):
                        P.mm(pb[:, 0:nkv], hT[:, k, j * 128:(j + 1) * 128], wkv[:, k, :], start=(k == 0), stop=(k == 7))
                    P.cp("dve", va[:, kidx(tok) // 128, :, 0:64], pb[:, nkv - 256:nkv].rearrange("p (h d) -> p h d", h=4))
                    if not ctx:
                        b = tok // T
                        pos = tok % T
                        so = stg[j % 2]
                        P.cp("act", so[:, :], pb[:, :])
                        P.dma(sk_d[b, i, pos:pos + 128, :], so[:, 0:256], out_dma=True)
                        P.dma(sv_d[b, i, pos:pos + 128, :], so[:, 256:512], out_dma=True)
                    if j == 0 and rpend[0] is not None:
                        rpend[0]()
                        rpend[0] = None
            free_hs(hs1)
            for nm in ("wq0", "wq1", "wk", "wkv"):
                AR.release(nm)
            if ctx:
                AR.release("cs")
                free_rb()
            else:
                AR.release("stg0")
                AR.release("stg1")
            if stages == 1 and l == trunc_l[0]:
                raise _Trunc()
            ada_hooks = None
            if ADA_INTERLEAVE and (not ctx) and l + 1 < 4:
                ada_alloc(l + 1, 2)
                ada_issue([0, 1])

                def _h0():
                    ada_compute([0, 1])
                    ada_issue([2, 3])

                def _h1():
                    ada_compute([2, 3])
                    ada_issue([4, 5])

                def _hp():
                    ada_compute([4, 5])
                    ada_finish()
                ada_hooks = {0: _h0, 1: _h1, "post": _hp}
            vaB = AR.bf("vaB", [128, KT // 128, 4, 128])
            wg = AR.bf("wg", [128, 8, 1024])
            wout = AR.bf("wout", [128, 8, 1024])
            load_w(wg, wv_in, 1536, 1024)
            load_w(wout, woo_d[i].rearrange("(c p) n -> p c n", p=128), 0, 1024)
            pT = [AR.bf("pT%d" % b, [128, 512]) for b in range(4)]
            rc = AR.f32("rc", [128, 512])
            P.memset("pool", vaB[:, :, :, 0:64], 1.0)
            nch_all = KT // 128
            for ja in range(0, nch_all, 6):
                jb = min(nch_all, ja + 6)
                P.cp("pool", vaB[:, ja:jb, :, 64:128], va[:, ja:jb, :, 0:64])
            mbuf = qT
            pools["o4"] = [4, 5, 6, 7]
            rr["o4"] = 0
            pend = []

            def flush_pend(keep=0):
                while len(pend) > keep:
                    pend.pop(0)()

            def finalize_pair(c, pos, t_lo):
                hA, hB = 2 * c, 2 * c + 1
                P.ts("dve", rc[0:64, :], pos[0][64:128, :], esink[64:128, i * 16 + hA:i * 16 + hA + 1], ALU.add)
                P.ts("dve", rc[64:128, :], pos[1][0:64, :], esink[0:64, i * 16 + hB:i * 16 + hB + 1], ALU.add)
                if ctx:
                    P.recip(rc[:, :], rc[:, :])
                else:
                    P.act(rc[:, :], rc[:, :], AF.Ln)
                    P.act(rc[:, :], rc[:, :], AF.Exp, scale=-1.0)
                P.tt("dve", mbuf[0:64, c, t_lo:t_lo + 512], pos[0][0:64, :], rc[0:64, :], ALU.mult)
                P.tt("dve", mbuf[64:128, c, t_lo:t_lo + 512], pos[1][64:128, :], rc[64:128, :], ALU.mult)

            ucount = 0
            for c in range(8):
                kvh = c // 2
                ntile = (nseq // 2) if not ctx else (T // 512)
                for tix in range(ntile):
                    pos = []
                    if ctx:
                        qt = tix
                        q0 = qt * 512
                        pos = [bank("o4"), bank("o4")]
                        jobs = []
                        for jj in range(2):
                            jobs.append((T // 128 + jj, 0, 512, []))
                        for j in range(4 * qt - 1, 4 * qt + 5):
                            if j < 0 or j >= T // 128:
                                continue
                            nlo = max(4 * qt, j - 1)
                            nhi = min(4 * qt + 3, j + 1)
                            mk = []
                            for nb in range(nlo, nhi + 1):
                                if nb == j - 1:
                                    mk.append((nb, 0))
                                elif nb == j + 1:
                                    mk.append((nb, 1))
                            jobs.append((j, (nlo - 4 * qt) * 128, (nhi - 4 * qt + 1) * 128, mk))
                        sc_ps = {}

                        def issue_s(ji):
                            kc, lo, hi, mk_ = jobs[ji]
                            for half in (0, 1):
                                r0, r1 = half * 64, half * 64 + 64
                                ps_ = bank("s")
                                P.mm(ps_[:, lo:hi], kd[r0:r1, kvh, kc * 128:(kc + 1) * 128], qT[r0:r1, c, q0 + lo:q0 + hi],
                                     start=True, stop=(len(mk_) == 0))
                                sc_ps[(ji, half)] = ps_
                            for half in (0, 1):
                                for mi, (nb, which) in enumerate(mk_):
                                    cl = (nb - 4 * qt) * 128
                                    P.mm(sc_ps[(ji, half)][:, cl:cl + 128], identb[:, :], masks[:, which, :],
                                         start=False, stop=(mi == len(mk_) - 1))

                        for ji in range(min(2, len(jobs))):
                            issue_s(ji)
                        for ji, (kc, lo, hi, mk) in enumerate(jobs):
                            pts = []
                            for half in (0, 1):
                                pt = pT[(2 * ji + half) % 4]
                                P.act(pt[:, lo:hi], sc_ps.pop((ji, half))[:, lo:hi], AF.Exp)
                                pts.append(pt)
                            if ji + 2 < len(jobs):
                                issue_s(ji + 2)
                            for half in (0, 1):
                                vsrc = va if half == 0 else vaB
                                P.mm(pos[half][:, lo:hi], vsrc[:, kc, kvh, :], pts[half][:, lo:hi],
                                     start=(ji == 0), stop=(ji == len(jobs) - 1))
                            if ji == 2:
                                flush_pend()
                        pend.append(lambda c=c, pos=pos, q0=q0: finalize_pair(c, pos, q0))
                        continue
                    for half in (0, 1):
                        r0, r1 = half * 64, half * 64 + 64
                        vsrc = va if half == 0 else vaB
                        p = tix
                        pts = []
                        for a in range(2):
                            sq_ = 2 * p + a
                            ps_ = bank("s")
                            for j in range(2):
                                P.mm(ps_[:, j * 256:(j + 1) * 256], kd[r0:r1, kvh, sq_ * 256 + j * 128:sq_ * 256 + (j + 1) * 128],
                                     qT[r0:r1, c, sq_ * 256:(sq_ + 1) * 256])
                            pt = pT[(2 * ucount + a) % 4]
                            P.act(pt[:, :], ps_[:, :], AF.Exp)
                            pts.append(pt)
                        ucount += 1
                        flush_pend()
                        po = bank("o4")
                        pos.append(po)

                        def pv(p=p, pts=pts, po=po, vsrc=vsrc, kvh=kvh):
                            for a in range(2):
                                sq_ = 2 * p + a
                                for j in range(2):
                                    P.mm(po[:, a * 256:(a + 1) * 256], vsrc[:, 2 * sq_ + j, kvh, :], pts[a][:, j * 256:(j + 1) * 256],
                                         start=(j == 0), stop=(j == 1))
                        pend.append(pv)
                        if half == 1:
                            pend.append(lambda c=c, pos=pos, p=p: finalize_pair(c, pos, p * 512))
            flush_pend()
            AR.release("vaB")
            for nm in ["pT%d" % b for b in range(4)] + ["rc", "kd", "va"]:
                AR.release(nm)
            if stages == 2 and l == trunc_l[0]:
                raise _Trunc()
            stage3(l, m, NT, mbuf, wg, wout, ada_hooks, nxt)
            for nm in ("wg", "wout", "mbuf"):
                AR.release(nm)

        def load_x(src, NT):
            stage = AR.f32("xstage", [128, 4, 1024])
            for t0 in range(0, NT, 512):
                for j in range(4):
                    P.dma(stage[:, j, :], src[t0 + j * 128:t0 + (j + 1) * 128, :])
                for c in range(8):
                    pb = bank()
                    for j in range(4):
                        P.tr(pb[:, j * 128:(j + 1) * 128], stage[:, j, c * 128:(c + 1) * 128], ident[:, :])
                    P.cp(evac_eng(), xT[:, c, t0:t0 + 512], pb[:, :])
            AR.release("xstage")

        def store_x(dst, NT):
            stage = AR.f32("xstage", [128, 4, 1024])
            for t0 in range(0, NT, 512):
                for j in range(4):
                    for hb in range(2):
                        pb = bank()
                        for cc in range(4):
                            c = hb * 4 + cc
                            P.tr(pb[:, cc * 128:(cc + 1) * 128], xT[:, c, t0 + j * 128:t0 + (j + 1) * 128], ident[:, :])
                        P.cp(evac_eng(), stage[:, j, hb * 512:(hb + 1) * 512], pb[:, :])
                    P.dma(dst[t0 + j * 128:t0 + (j + 1) * 128, :], stage[:, j, :], out_dma=True)
            AR.release("xstage")

        trunc_l = [-1]
        try:
          for (src, dst, m, NT, nseq, T, ctx) in ((xp_d, yp_d, 0, 1024, 4, 256, False),
                                                   (xs_d, ys_d, 1, 2048, 1, 2048, True)):
              nl = nl_p if not ctx else nl_s
              if nl < 0:
                  continue
              load_x(src, NT)
              if not ada0_done[0]:
                  ada_all(0)
                  ada0_done[0] = True
              trunc_l[0] = nl - 1 if ((ctx and nl_s >= 0) or (not ctx and nl_s < 0)) else -1
              for l in range(nl):
                  if l + 1 < nl:
                      nxt = (l + 1, ctx, not ctx)
                  elif (not ctx) and nl_s > 0:
                      nxt = (0, True, True)
                  else:
                      nxt = None
                  if l % 2 == 0:
                      even_layer(l, m, NT, nseq, T, ctx, nxt)
                  else:
                      odd_layer(l, m, NT, nseq, T, ctx, nxt)
              store_x(dst, NT)

        except _Trunc:
            P.dma(yp_d[0:128, 0:512], rstd[:, :], out_dma=True)

        P.finish()
        build_program.stats = {e: len(P.ops[e]) for e in ENGS}
        build_program.peak = AR.peak
    return nc


_CONSTS = None


def _consts():
    global _CONSTS
    if _CONSTS is None:
        mla, swa = _rope_tables()
        pm96, pm128 = _perm_mats()
        _CONSTS = dict(ident=np.eye(128, dtype=np.float32), pm96=pm96, pm128=pm128, masks=_masks(),
                       amats=_pool_mats().reshape(20, 128, 128), mla_cs=mla, swa_cs=swa)
    return _CONSTS


def kernel(x_prompt, x_sample, cache_ckv, cache_krope, cache_k, cache_v, c, c_ctx,
           ada_w, ada_b, norm_pre, norm_post,
           mla_w_in, mla_g_qn, mla_g_kvn, mla_w_uq, mla_w_ukv, pool_w, pool_scale, mixa_w_out,
           swa_w_in, swa_sink, swa_w_out):
    f = lambda a: np.ascontiguousarray(np.asarray(a, dtype=np.float32))
    x_prompt, x_sample = f(x_prompt), f(x_sample)
    cache_ckv, cache_krope, cache_k, cache_v = f(cache_ckv), f(cache_krope), f(cache_k), f(cache_v)
    c, c_ctx = f(c), f(c_ctx)
    vecs = np.zeros((14, 1024), np.float32)
    vecs[0:4] = f(norm_pre)
    vecs[4:8] = f(norm_post)
    vecs[8:10, :384] = f(mla_g_qn)
    vecs[10:12, :256] = f(mla_g_kvn)
    vecs[12:14, :512] = f(pool_scale)
    shared = dict(ada_w=f(ada_w), ada_b=f(ada_b), vecs=vecs, gkvn=f(mla_g_kvn), sink=f(swa_sink).reshape(32),
                  w_in_e=f(mla_w_in), w_uq=f(mla_w_uq), w_ukv=f(mla_w_ukv), w_pool=f(pool_w), w_out_e=f(mixa_w_out),
                  w_in_o=f(swa_w_in), w_out_o=f(swa_w_out))
    shared.update(_consts())
    in_maps = []
    for i in range(8):
        d = dict(shared)
        d["xp"] = x_prompt[4 * i:4 * i + 4].reshape(1024, 1024)
        d["xs"] = x_sample[i]
        d["cckv"] = cache_ckv[i]
        d["ckr"] = cache_krope[i]
        d["ck"] = cache_k[i].reshape(2, 256, 256)
        d["cv"] = cache_v[i].reshape(2, 256, 256)
        d["cc"] = np.stack([c_ctx, c[i]], axis=0)
        in_maps.append(d)
    nc = build_program()
    res = run_bass_kernel_spmd(nc, in_maps, core_ids=list(range(8)))
    R = res.results
    y_prompt = np.concatenate([r["yp"].reshape(4, 256, 1024) for r in R], axis=0)
    y_sample = np.stack([r["ys"] for r in R], axis=0)
    st_ckv = np.concatenate([r["st_ckv"] for r in R], axis=0)
    st_kr = np.concatenate([r["st_kr"] for r in R], axis=0)
    st_k = np.concatenate([r["st_k"].reshape(4, 2, 256, 4, 64) for r in R], axis=0)
    st_v = np.concatenate([r["st_v"].reshape(4, 2, 256, 4, 64) for r in R], axis=0)
    return (y_prompt.astype(np.float32), y_sample.astype(np.float32), st_ckv.astype(np.float32),
            st_kr.astype(np.float32), st_k.astype(np.float32), st_v.astype(np.float32))
```

```python
import numpy as np
from contextlib import ExitStack
import concourse.bass as bass
import concourse.mybir as mybir
from concourse.bass_utils import run_bass_kernel_spmd

F32 = mybir.dt.float32
BF16 = mybir.dt.bfloat16
AF = mybir.ActivationFunctionType
ALU = mybir.AluOpType

ENGS = ("pe", "act", "dve", "pool", "sp")
SAME_ENGINE_RAW = True
EMBED_LAST_WAIT = True
EMBED_ENGINES = ("pe", "act", "dve")
N_DMA_SEMS = 24
BUCKET = 4096

D = 1024
EPS = 1e-6
MLA_SCALE = 96 ** -0.5
SWA_SCALE = 64 ** -0.5


class Prog:
    def __init__(self, nc, stack):
        self.nc = nc
        self.stack = stack
        self.ops = {e: [] for e in ENGS}
        self.recs = {}
        self.known = {e: {} for e in ENGS}
        self.eng_sem = {e: stack.enter_context(nc.semaphore("s_" + e)) for e in ENGS}
        self.dma_sems = [stack.enter_context(nc.semaphore("s_dma%d" % i)) for i in range(N_DMA_SEMS)]
        self.dma_cnt = [0] * N_DMA_SEMS
        self.dma_rr = 0
        self.dma_rr_pool = 0
        self.out_dma_tokens = []

    def sb(self, name, shape, dtype):
        return self.stack.enter_context(self.nc.sbuf_tensor("sb_" + name, list(shape), dtype))

    def ps(self, name, shape, dtype=F32):
        return self.stack.enter_context(self.nc.psum_tensor("pp_" + name, list(shape), dtype))

    @staticmethod
    def _box(ap):
        shp = list(ap.tensor.shape)
        row = 1
        for s in shp[1:]:
            row *= s
        isz = mybir.dt.size(ap.dtype)
        off = int(ap.offset)
        dims = ap.ap
        p0 = off // row
        f0 = off % row
        pstep, pcnt = dims[0]
        if pstep == row or (pcnt == 1 and len(dims) > 1):
            p1 = p0 + pcnt
            rest = dims[1:]
        elif pstep == 0:
            p1 = p0 + 1
            rest = dims[1:]
        else:
            p1 = p0 + 1
            rest = dims
        ext = 0
        for st, cn in rest:
            ext += abs(st) * (cn - 1)
        return (p0, p1, f0 * isz, (f0 + ext + 1) * isz)

    @staticmethod
    def _tracked(ap):
        return str(ap.space).upper() in ("SB", "PSUM")

    def op(self, eng, fn, reads=(), writes=(), dma=False, out_dma=False):
        idx = len(self.ops[eng])
        waits = []
        kn = self.known[eng]

        def need(tok, raw=False):
            if tok[0] == 'e':
                if tok[1] == eng and not (raw and SAME_ENGINE_RAW and eng != "pe"):
                    return
                key = tok[1]
            else:
                key = ('d', tok[1])
            if kn.get(key, -1) >= tok[2]:
                return
            kn[key] = tok[2]
            waits.append(tok)

        if dma:
            if eng == "pool":
                k = 8 + self.dma_rr_pool
                self.dma_rr_pool = (self.dma_rr_pool + 1) % (N_DMA_SEMS - 8)
            else:
                k = self.dma_rr
                self.dma_rr = (self.dma_rr + 1) % 8
            if self.dma_cnt[k] > 0:
                need(('d', k, self.dma_cnt[k] * 16))
            self.dma_cnt[k] += 1
            mytok = ('d', k, self.dma_cnt[k] * 16)
        else:
            mytok = ('e', eng, idx)

        rb = [(ap.name, self._box(ap)) for ap in reads if self._tracked(ap) and str(ap.space).upper() != "PSUM"]
        wb = [(ap.name, self._box(ap)) for ap in writes if self._tracked(ap) and str(ap.space).upper() != "PSUM"]
        for ap in list(reads) + list(writes):
            if str(ap.space).upper() == "PSUM":
                ent = (ap.name, (0, 128, 0, 1 << 20))
                if ent not in wb:
                    wb.append(ent)
        for name, box in rb:
            tr = self.recs.get(name)
            if tr is None:
                continue
            for b in range(box[2] // BUCKET, (box[3] - 1) // BUCKET + 1):
                for r in tr.get(b, ()):
                    if r[3] and r[2]:
                        rbx = r[0]
                        if rbx[0] < box[1] and box[0] < rbx[1] and rbx[2] < box[3] and box[2] < rbx[3]:
                            need(r[1], True)
        for name, box in wb:
            tr = self.recs.get(name)
            if tr is None:
                continue
            for b in range(box[2] // BUCKET, (box[3] - 1) // BUCKET + 1):
                lst = tr.get(b)
                if not lst:
                    continue
                for r in lst:
                    if r[3]:
                        rbx = r[0]
                        if rbx[0] < box[1] and box[0] < rbx[1] and rbx[2] < box[3] and box[2] < rbx[3]:
                            need(r[1])
                            if box[0] <= rbx[0] and box[1] >= rbx[1] and box[2] <= rbx[2] and box[3] >= rbx[3]:
                                r[3] = False
                tr[b] = [r for r in lst if r[3]]
        for name, box in rb:
            tr = self.recs.setdefault(name, {})
            rec = [box, mytok, False, True]
            for b in range(box[2] // BUCKET, (box[3] - 1) // BUCKET + 1):
                lst = tr.setdefault(b, [])
                if not dma:
                    for r in lst:
                        if r[3] and (not r[2]) and r[1][0] == 'e' and r[1][1] == eng and r[0] == box:
                            r[3] = False
                lst.append(rec)
        for name, box in wb:
            tr = self.recs.setdefault(name, {})
            rec = [box, mytok, True, True]
            for b in range(box[2] // BUCKET, (box[3] - 1) // BUCKET + 1):
                tr.setdefault(b, []).append(rec)
        o = dict(fn=fn, waits=waits, tok=mytok, marked=False)
        self.ops[eng].append(o)
        if out_dma:
            self.out_dma_tokens.append(mytok)
        return o

    def dma(self, out, in_, eng="sp", out_dma=False):
        return self.op(eng, lambda e: e.dma_start(out=out, in_=in_), reads=[in_], writes=[out],
                       dma=True, out_dma=out_dma)

    def mm(self, out, lhsT, rhs, start=True, stop=True):
        return self.op("pe", lambda e: e.matmul(out, lhsT, rhs, start=start, stop=stop),
                       reads=[lhsT, rhs] + ([] if start else [out]), writes=[out])

    def tr(self, out, in_, ident):
        return self.op("pe", lambda e: e.transpose(out, in_, ident), reads=[in_, ident], writes=[out])

    def act(self, out, in_, func, bias=None, scale=None, accum=None):
        kw = {}
        rd = [in_]
        wr = [out]
        if bias is not None:
            kw["bias"] = bias
            if not isinstance(bias, (int, float)):
                rd.append(bias)
        if scale is not None:
            kw["scale"] = scale
            if not isinstance(scale, (int, float)):
                rd.append(scale)
        if accum is not None:
            kw["accum_out"] = accum
            wr.append(accum)
        return self.op("act", lambda e: e.activation(out=out, in_=in_, func=func, **kw), reads=rd, writes=wr)

    def cp(self, eng, out, in_):
        if eng == "act":
            return self.op("act", lambda e: e.copy(out=out, in_=in_), reads=[in_], writes=[out])
        return self.op(eng, lambda e: e.tensor_copy(out=out, in_=in_), reads=[in_], writes=[out])

    def tt(self, eng, out, in0, in1, op):
        return self.op(eng, lambda e: e.tensor_tensor(out=out, in0=in0, in1=in1, op=op), reads=[in0, in1], writes=[out])

    def ts(self, eng, out, in0, s1, op0, s2=None, op1=None):
        rd = [in0]
        if not isinstance(s1, (int, float)):
            rd.append(s1)
        if s2 is not None and not isinstance(s2, (int, float)):
            rd.append(s2)
        if op1 is None:
            return self.op(eng, lambda e: e.tensor_scalar(out=out, in0=in0, scalar1=s1, scalar2=None, op0=op0),
                           reads=rd, writes=[out])
        return self.op(eng, lambda e: e.tensor_scalar(out=out, in0=in0, scalar1=s1, scalar2=s2, op0=op0, op1=op1),
                       reads=rd, writes=[out])

    def stt(self, eng, out, in0, scalar, in1, op0, op1):
        rd = [in0, in1]
        if not isinstance(scalar, (int, float)):
            rd.append(scalar)
        return self.op(eng, lambda e: e.scalar_tensor_tensor(out=out, in0=in0, scalar=scalar, in1=in1, op0=op0, op1=op1),
                       reads=rd, writes=[out])

    def memset(self, eng, out, val):
        return self.op(eng, lambda e: e.memset(out, val), writes=[out])

    def recip(self, out, in_):
        return self.op("dve", lambda e: e.reciprocal(out=out, in_=in_), reads=[in_], writes=[out])

    def finish(self):
        nc = self.nc
        for e in ENGS:
            for o in self.ops[e]:
                for tok in o["waits"]:
                    if tok[0] == 'e':
                        self.ops[tok[1]][tok[2]]["marked"] = True
        cnt_at = {}
        for e in ENGS:
            c = 0
            arr = []
            for o in self.ops[e]:
                if o["marked"]:
                    c += 1
                arr.append(c)
            cnt_at[e] = arr
        fw = {}
        for tok in self.out_dma_tokens:
            fw[tok[1]] = max(fw.get(tok[1], 0), tok[2])

        with nc.Block() as block:
            def emit(ename, engobj):
                for o in self.ops[ename]:
                    ws = o["waits"]
                    emb = None
                    if EMBED_LAST_WAIT and ws and ename in EMBED_ENGINES and o["tok"][0] == 'e':
                        emb = ws[-1]
                        ws = ws[:-1]
                    for tok in ws:
                        if tok[0] == 'e':
                            engobj.wait_ge(self.eng_sem[tok[1]], cnt_at[tok[1]][tok[2]])
                        else:
                            engobj.wait_ge(self.dma_sems[tok[1]], tok[2])
                    ins = o["fn"](engobj)
                    if emb is not None:
                        if emb[0] == 'e':
                            ins._wait_ge(self.eng_sem[emb[1]], cnt_at[emb[1]][emb[2]])
                        else:
                            ins._wait_ge(self.dma_sems[emb[1]], emb[2])
                    if o["tok"][0] == 'd':
                        ins.then_inc(self.dma_sems[o["tok"][1]], 16)
                    elif o["marked"]:
                        ins.then_inc(self.eng_sem[ename], 1)
                if ename == "sp":
                    for k, v in fw.items():
                        engobj.wait_ge(self.dma_sems[k], v)

            @block.sync
            def _(sync):
                emit("sp", sync)

            @block.tensor
            def _(tensor):
                emit("pe", tensor)

            @block.scalar
            def _(scalar):
                emit("act", scalar)

            @block.vector
            def _(vector):
                emit("dve", vector)

            @block.gpsimd
            def _(gpsimd):
                emit("pool", gpsimd)


class _Trunc(Exception):
    pass


class Arena:
    def __init__(self, tensor, nelem):
        self.t = tensor
        self.n = nelem
        self.free = [(0, nelem)]
        self.live = {}
        self.peak = 0

    def alloc(self, name, nelem_bf16):
        nelem_bf16 = (nelem_bf16 + 15) // 16 * 16
        for i, (o, s) in enumerate(self.free):
            if s >= nelem_bf16:
                if s == nelem_bf16:
                    self.free.pop(i)
                else:
                    self.free[i] = (o + nelem_bf16, s - nelem_bf16)
                self.live[name] = (o, nelem_bf16)
                used = self.n - sum(s for _, s in self.free)
                self.peak = max(self.peak, used)
                return o
        raise RuntimeError("arena OOM allocating %s (%d); live=%s free=%s" % (name, nelem_bf16, self.live, self.free))

    def release(self, name):
        o, s = self.live.pop(name)
        self.free.append((o, s))
        self.free.sort()
        merged = []
        for o, s in self.free:
            if merged and merged[-1][0] + merged[-1][1] == o:
                merged[-1] = (merged[-1][0], merged[-1][1] + s)
            else:
                merged.append((o, s))
        self.free = merged

    def bf(self, name, shape):
        n = 1
        for s in shape[1:]:
            n *= s
        o = self.alloc(name, n)
        v = self.t[0:shape[0], o:o + n]
        if len(shape) == 3:
            v = v.rearrange("p (a b) -> p a b", a=shape[1])
        elif len(shape) == 4:
            v = v.rearrange("p (a b c) -> p a b c", a=shape[1], b=shape[2])
        return v

    def f32(self, name, shape):
        n = 1
        for s in shape[1:]:
            n *= s
        o = self.alloc(name, 2 * n)
        v = self.t[0:shape[0], o:o + 2 * n].bitcast(F32)
        if len(shape) == 3:
            v = v.rearrange("p (a b) -> p a b", a=shape[1])
        elif len(shape) == 4:
            v = v.rearrange("p (a b c) -> p a b c", a=shape[1], b=shape[2])
        return v


def _rope_tables():
    t = np.arange(2048)
    row = (t // 64).astype(np.float64)
    col = (t % 64).astype(np.float64)
    mla = np.zeros((2, 96, 2048), np.float32)
    mla[0, 0:64] = 1.0
    for r in range(32):
        pos = row if r < 16 else col
        i = r % 16
        f = i % 8
        inv = 10000.0 ** (-(2.0 * f) / 16.0)
        ang = pos * inv
        mla[0, 64 + r] = np.cos(ang)
        mla[1, 64 + r] = np.sin(ang) if i < 8 else -np.sin(ang)
    swa = np.zeros((2, 128, 2048), np.float32)
    for p in range(128):
        d = p % 64
        pos = row if d < 32 else col
        i = d % 32
        f = i % 16
        inv = 10000.0 ** (-(2.0 * f) / 32.0)
        ang = pos * inv
        swa[0, p] = np.cos(ang)
        swa[1, p] = np.sin(ang) if i < 16 else -np.sin(ang)
    return mla, swa


def _perm_mats():
    pm96 = np.zeros((96, 96), np.float32)
    for r in range(32):
        i = r % 16
        partner = r + 8 if i < 8 else r - 8
        pm96[64 + partner, 64 + r] = 1.0
    pm128 = np.zeros((128, 128), np.float32)
    for m in range(128):
        i = m % 32
        partner = m + 16 if i < 16 else m - 16
        pm128[partner, m] = 1.0
    return pm96, pm128


def _pool_mats():
    T = 384
    out = np.zeros((4, 5, 128, 128), np.float32)
    for g, w in enumerate((2, 4, 8, 16)):
        A = np.zeros((T, T), np.float64)
        for t in range(T):
            lo = min(max(t - w // 2, 0), T)
            hi = min(max(t + w // 2, 0), T)
            A[lo:hi, t] = 1.0 / (hi - lo)
            A[t, t] -= 1.0
        out[g, 0] = A[0:128, 0:128]
        out[g, 1] = A[128:256, 128:256]
        out[g, 2] = A[256:384, 256:384]
        out[g, 3] = A[0:128, 128:256]
        out[g, 4] = A[128:256, 0:128]
    return out


def _masks():
    k = np.arange(128)[:, None]
    q = np.arange(128)[None, :]
    m = np.zeros((2, 128, 128), np.float32)
    m[0] = np.where(k <= q, 0.0, -30000.0)
    m[1] = np.where(q <= k, 0.0, -30000.0)
    return m


def build_program(nl_p=4, nl_s=4, stages=3, dbg=None):
    nc = bass.Bass("TRN2", target_bir_lowering=False)

    def din(name, shape):
        return nc.dram_tensor(name, list(shape), F32, kind="ExternalInput").ap()

    def dout(name, shape):
        return nc.dram_tensor(name, list(shape), F32, kind="ExternalOutput").ap()

    xp_d = din("xp", [1024, 1024])
    xs_d = din("xs", [2048, 1024])
    cckv_d = din("cckv", [2, 256, 256])
    ckr_d = din("ckr", [2, 256, 32])
    ck_d = din("ck", [2, 256, 256])
    cv_d = din("cv", [2, 256, 256])
    cc_d = din("cc", [2, 1024])
    adaw_d = din("ada_w", [4, 1024, 3072])
    adab_d = din("ada_b", [4, 3072])
    vecs_d = din("vecs", [14, 1024])
    gkvn_d = din("gkvn", [2, 256])
    sink_d = din("sink", [32])
    wine_d = din("w_in_e", [2, 1024, 2208])
    wuq_d = din("w_uq", [2, 384, 768])
    wukv_d = din("w_ukv", [2, 256, 1024])
    wpool_d = din("w_pool", [2, 4, 128, 128])
    woe_d = din("w_out_e", [2, 1024, 1024])
    wino_d = din("w_in_o", [2, 1024, 2560])
    woo_d = din("w_out_o", [2, 1024, 1024])
    ident_d = din("ident", [128, 128])
    pm96_d = din("pm96", [96, 96])
    pm128_d = din("pm128", [128, 128])
    masks_d = din("masks", [2, 128, 128])
    amats_d = din("amats", [20, 128, 128])
    mlacs_d = din("mla_cs", [2, 96, 2048])
    swacs_d = din("swa_cs", [2, 128, 2048])

    yp_d = dout("yp", [1024, 1024])
    ys_d = dout("ys", [2048, 1024])
    sckv_d = dout("st_ckv", [4, 2, 256, 256])
    skr_d = dout("st_kr", [4, 2, 256, 32])
    sk_d = dout("st_k", [4, 2, 256, 256])
    sv_d = dout("st_v", [4, 2, 256, 256])

    with ExitStack() as st:
        P = Prog(nc, st)
        xT = P.sb("xT", [128, 8, 2048], F32)
        ident = P.sb("ident", [128, 128], F32)
        ones_bf = P.sb("ones_bf", [128, 128], BF16)
        identb = P.sb("identb", [128, 128], BF16)
        scb = P.sb("scb", [128, 8, 2], BF16)
        pm96 = P.sb("pm96", [96, 96], BF16)
        pm128 = P.sb("pm128", [128, 128], BF16)
        masks = P.sb("masks", [128, 2, 128], BF16)
        epsT = P.sb("epsT", [128, 1], F32)
        mod = P.sb("mod", [128, 4, 48], F32)
        vecT = P.sb("vecT", [128, 8, 32], F32)
        coefA = P.sb("coefA", [128, 4, 2, 8], F32)
        coefB = P.sb("coefB", [128, 4, 2, 8], F32)
        coefG = P.sb("coefG", [128, 4, 2, 8], F32)
        gkvb = P.sb("gkvb", [128, 2, 256], F32)
        esink = P.sb("esink", [128, 32], F32)
        rstd = P.sb("rstd", [128, 512], F32)
        rstd2 = P.sb("rstd2", [128, 512], F32)
        tmpf = [P.sb("tmpf%d" % i, [128, 512], F32) for i in range(2)]
        ARENA_N = 64 * 1024
        arena_t = P.sb("arena", [128, ARENA_N], BF16)
        AR = Arena(arena_t, ARENA_N)
        banks = [P.ps("ps%d" % i, [128, 512], F32) for i in range(8)]
        rr = {"all": 0, "s": 0, "o": 0, "g": 0}
        pools = {"all": list(range(8)), "s": [0, 1, 2, 3], "o": [4, 5], "g": [6, 7]}

        def bank(pool="all"):
            lst = pools[pool]
            b = banks[lst[rr[pool] % len(lst)]]
            rr[pool] += 1
            return b

        evac_rr = [0]

        def evac_eng():
            evac_rr[0] += 1
            return "dve" if evac_rr[0] % 2 else "act"

        P.dma(ident[:], ident_d)
        P.dma(identb[:], ident_d, eng="pool")
        P.dma(pm96[:], pm96_d, eng="pool")
        P.dma(pm128[:], pm128_d, eng="pool")
        P.dma(masks[:], masks_d.rearrange("a k q -> k a q"), eng="pool")
        P.dma(gkvb[:].rearrange("p a n -> p (a n)"), gkvn_d.rearrange("a n -> (a n)").partition_broadcast(128))
        P.dma(esink[:], sink_d.partition_broadcast(128))
        P.memset("dve", ones_bf[:], 1.0)
        P.memset("dve", epsT[:], EPS)
        P.act(esink[:], esink[:], AF.Exp)

        if dbg == "consts":
            P.dma(yp_d[0:128, 0:32], esink[:], out_dma=True)
            P.finish()
            return nc
        vst = AR.f32("vst", [32, 1024])
        P.memset("dve", vst[:], 0.0)
        P.dma(vst[0:14, :], vecs_d)
        pv = bank()
        for c in range(8):
            P.tr(pv[:, c * 32:(c + 1) * 32], vst[0:32, c * 128:(c + 1) * 128], ident[0:32, 0:32])
        for c in range(8):
            P.cp("dve", vecT[:, c, :], pv[:, c * 32:(c + 1) * 32])
        AR.release("vst")

        if dbg == "vec":
            P.dma(yp_d[0:128, 0:256], vecT[:].rearrange("p a b -> p (a b)"), out_dma=True)
            P.finish()
            return nc
        ccT = AR.f32("ccT", [128, 2, 8])
        for m in range(2):
            P.dma(ccT[:, m, :], cc_d[m].rearrange("(p c) -> p c", c=8))
        for m in range(2):
            P.act(scb[:, :, m], ccT[:, m, :], AF.Silu)
        AR.release("ccT")
        modv = mod[:].rearrange("p l (j c m) -> p l j c m", j=3, c=8)
        ada_state = {}

        def ada_alloc(l, nbuf):
            ada_state["l"] = l
            ada_state["adab"] = AR.bf("adab", [1, 3072])
            ada_state["bufs"] = [AR.bf("adw%d" % i, [128, 8, 512]) for i in range(nbuf)]
            ada_state["nbuf"] = nbuf
            P.dma(ada_state["adab"][:], adab_d[l:l + 1, :], eng="pool")

        def ada_issue(blks):
            l = ada_state["l"]
            wv = adaw_d[l].rearrange("(p c) n -> p c n", c=8)
            for blk in blks:
                wt = ada_state["bufs"][blk % ada_state["nbuf"]]
                P.dma(wt[:], wv[:, :, blk * 512:(blk + 1) * 512], eng="pool")

        def ada_compute(blks):
            l = ada_state["l"]
            adab = ada_state["adab"]
            for blk in blks:
                wt = ada_state["bufs"][blk % ada_state["nbuf"]]
                pm = bank()
                for nci in range(4):
                    nch = blk * 4 + nci
                    for c in range(8):
                        P.mm(pm[:, 2 * nci:2 * nci + 2], wt[:, c, nci * 128:(nci + 1) * 128], scb[:, c, :],
                             start=(c == 0), stop=False)
                    P.mm(pm[:, 2 * nci:2 * nci + 2], adab[0:1, nch * 128:(nch + 1) * 128], ones_bf[0:1, 0:2],
                         start=False, stop=True)
                P.cp("dve", mod[:, l, blk * 8:(blk + 1) * 8], pm[:, 0:8])

        def ada_finish():
            l = ada_state["l"]
            for m in range(2):
                P.stt("dve", coefA[:, l, m, :], modv[:, l, 1, :, m], 1.0, vecT[:, :, l], ALU.add, ALU.mult)
                P.cp("dve", coefB[:, l, m, :], modv[:, l, 0, :, m])
                P.tt("dve", coefG[:, l, m, :], modv[:, l, 2, :, m], vecT[:, :, 4 + l], ALU.mult)
            for i in range(ada_state["nbuf"]):
                AR.release("adw%d" % i)
            AR.release("adab")
            ada_state.clear()

        def ada_all(l):
            ada_alloc(l, 3)
            for blk in range(6):
                ada_issue([blk])
                ada_compute([blk])
            ada_finish()

        ADA_INTERLEAVE = nl_p >= 4
        ada0_done = [False]
        if not ADA_INTERLEAVE:
            for l in range(1, 4):
                ada_all(l)
        if dbg == "ada":
            P.dma(yp_d[0:128, 0:192], mod[:].rearrange("p a b -> p (a b)"), out_dma=True)
            P.dma(yp_d[128:256, 0:64], coefA[:].rearrange("p a b c -> p (a b c)"), out_dma=True)
            P.dma(yp_d[256:384, 0:64], coefG[:].rearrange("p a b c -> p (a b c)"), out_dma=True)
            P.finish()
            return nc
        def rstd_from_ssq(ps_ssq, n, dim, out):
            P.act(out, ps_ssq, AF.Ln, bias=epsT[:, 0:1], scale=1.0 / dim)
            P.act(out, out, AF.Exp, scale=-0.5)

        def modulate_parts(hT, sq, l, m, t0, n):
            parts = []

            ns_ = sq.shape[1]

            def stats_a():
                for c in range(8):
                    P.act(sq[:, c, :n], xT[:, c, t0:t0 + n], AF.Square)

            def stats_b():
                pb = bank()
                for c in range(8):
                    P.mm(pb[:, :n], ones_bf[:, :], sq[:, c, :n], start=(c == 0), stop=(c == 7))
                rstd_from_ssq(pb[:, :n], n, 1024, rstd[:, :n])

            def stats():
                pb = bank()
                for c0 in range(0, 8, ns_):
                    for c in range(c0, c0 + ns_):
                        P.act(sq[:, c % ns_, :n], xT[:, c, t0:t0 + n], AF.Square)
                    for c in range(c0, c0 + ns_):
                        P.mm(pb[:, :n], ones_bf[:, :], sq[:, c % ns_, :n], start=(c == 0), stop=(c == 7))
                rstd_from_ssq(pb[:, :n], n, 1024, rstd[:, :n])
            if ns_ >= 8:
                parts.extend([stats_a, (lambda: None), stats_b])
            else:
                parts.append(stats)
            for c in range(8):
                def ap(c=c):
                    tf = tmpf[c % 2]
                    P.stt("dve", tf[:, :n], xT[:, c, t0:t0 + n], coefA[:, l, m, c:c + 1], rstd[:, :n], ALU.mult, ALU.mult)
                    P.act(hT[:, c, :n], tf[:, :n], AF.Identity, bias=coefB[:, l, m, c:c + 1], scale=1.0)
                parts.append(ap)
            return parts

        def modulate(hT, sq, l, m, t0, n):
            for f in modulate_parts(hT, sq, l, m, t0, n):
                f()

        def run_part(parts, k=1):
            for _ in range(k):
                if parts:
                    parts.pop(0)()

        def load_w(dst, src_rows_view, col0, ncols):
            for c in range(dst.shape[1]):
                P.dma(dst[:, c, :], src_rows_view[:, c, col0:col0 + ncols], eng="pool")

        def alloc_hs():
            hs = [(AR.bf("hT", [128, 8, 512]), AR.bf("sq", [128, 8, 512]))]
            try:
                a = AR.bf("hT1", [128, 8, 512])
                try:
                    b = AR.bf("sq1", [128, 8, 512])
                    hs.append((a, b))
                except RuntimeError:
                    AR.release("hT1")
            except RuntimeError:
                pass
            return hs

        def free_hs(hs):
            AR.release("hT")
            AR.release("sq")
            if len(hs) > 1:
                AR.release("hT1")
                AR.release("sq1")

        store_state = {"dst": None, "done": False}

        def store_tile(dst, stage2, t0):
            for j in range(4):
                for hb in range(2):
                    pb = bank()
                    for cc in range(4):
                        c = hb * 4 + cc
                        P.tr(pb[:, cc * 128:(cc + 1) * 128], xT[:, c, t0 + j * 128:t0 + (j + 1) * 128], ident[:, :])
                    P.cp(evac_eng(), stage2[:, j % 2, hb * 512:(hb + 1) * 512], pb[:, :])
                P.dma(dst[t0 + j * 128:t0 + (j + 1) * 128, :], stage2[:, j % 2, :], out_dma=True)

        def stage3(l, m, NT, mbuf, wg, wout, hooks=None, nxt=None):
            oT = AR.f32("oT", [128, 8, 512])
            hs3 = alloc_hs()
            sg = [AR.bf("sg%d" % i, [128, 512]) for i in range(2)]
            if nxt is not None:
                prefetch_w1(*nxt)
            stage2 = None
            spend = [None]
            if store_state["dst"] is not None:
                try:
                    stage2 = AR.f32("xst2", [128, 2, 1024])
                except RuntimeError:
                    stage2 = None
            tiles = list(range(0, NT, 512))
            n = 512
            pipe = len(hs3) > 1
            if pipe:
                modulate(hs3[0][0], hs3[0][1], l, m, tiles[0], n)
            for ti, t0 in enumerate(tiles):
                if hooks and ti in hooks:
                    hooks[ti]()
                hT, sq = hs3[ti % len(hs3)]
                if not pipe:
                    modulate(hT, sq, l, m, t0, n)
                nparts = []
                if pipe and ti + 1 < len(tiles):
                    nh, nsq = hs3[(ti + 1) % 2]
                    nparts = modulate_parts(nh, nsq, l, m, tiles[ti + 1], n)
                for mc in range(8):
                    pb = bank()
                    for k in range(8):
                        P.mm(pb[:, :n], wg[:, k, mc * 128:(mc + 1) * 128], hT[:, k, :n], start=(k == 0), stop=(k == 7))
                    s_ = sg[mc % 2]
                    P.act(s_[:, :n], pb[:, :n], AF.Silu)
                    P.tt("dve" if mc % 2 else "pool", mbuf[:, mc, t0:t0 + n], mbuf[:, mc, t0:t0 + n], s_[:, :n], ALU.mult)
                    if mc == 1:
                        run_part(nparts)
                if spend[0] is not None:
                    spend[0]()
                    spend[0] = None
                for dc in range(8):
                    pb = bank()
                    for k in range(8):
                        P.mm(pb[:, :n], wout[:, k, dc * 128:(dc + 1) * 128], mbuf[:, k, t0:t0 + n],
                             start=(k == 0), stop=(k == 7))
                    P.cp("dve", oT[:, dc, :n], pb[:, :n])
                    P.act(sq[:, dc, :n], pb[:, :n], AF.Square)
                    run_part(nparts)
                run_part(nparts, 16)
                pb = bank()
                for dc in range(8):
                    P.mm(pb[:, :n], ones_bf[:, :], sq[:, dc, :n], start=(dc == 0), stop=(dc == 7))
                rstd_from_ssq(pb[:, :n], n, 1024, rstd2[:, :n])
                for dc in range(8):
                    tf = tmpf[dc % 2]
                    P.stt("dve", tf[:, :n], oT[:, dc, :n], coefG[:, l, m, dc:dc + 1], rstd2[:, :n], ALU.mult, ALU.mult)
                    P.tt("pool", xT[:, dc, t0:t0 + n], xT[:, dc, t0:t0 + n], tf[:, :n], ALU.add)
                if stage2 is not None:
                    spend[0] = (lambda t0=t0: store_tile(store_state["dst"], stage2, t0))
            if hooks and "post" in hooks:
                hooks["post"]()
            if stage2 is not None:
                if spend[0] is not None:
                    spend[0]()
                AR.release("xst2")
                store_state["done"] = True
            free_hs(hs3)
            for nm in ("oT", "sg0", "sg1"):
                AR.release(nm)

        rope_rr = [0]

        def rope_a(src_ps, pr, n, scale, pm, cs, t0, rb):
            qc, qsn = rb[rope_rr[0] % len(rb)]
            rope_rr[0] += 1
            P.stt("dve", qc[0:pr, :n], src_ps[0:pr, :n], float(scale), cs[0:pr, 0, t0:t0 + n], ALU.mult, ALU.mult)
            P.stt("dve", qsn[0:pr, :n], src_ps[0:pr, :n], float(scale), cs[0:pr, 1, t0:t0 + n], ALU.mult, ALU.mult)

            def phase_b():
                p2 = bank("g")
                P.mm(p2[0:pr, :n], identb[0:pr, 0:pr], qc[0:pr, :n], start=True, stop=False)
                P.mm(p2[0:pr, :n], pm[:, :], qsn[0:pr, :n], start=False, stop=True)
                return p2
            return phase_b

        def rope_apply(src_ps, pr, n, scale, pm, cs, t0, rb):
            return rope_a(src_ps, pr, n, scale, pm, cs, t0, rb)()

        def alloc_rb():
            return [(AR.bf("rqc%d" % i, [128, 512]), AR.bf("rqs%d" % i, [128, 512])) for i in range(2)]

        def free_rb():
            for i in range(2):
                AR.release("rqc%d" % i)
                AR.release("rqs%d" % i)

        pref = {}

        def prefetch_w1(lnext, ctx_next, full):
            i2 = lnext // 2
            if (not full) and lnext % 2 == 1:
                return
            try:
                if lnext % 2 == 0:
                    wv = wine_d[i2].rearrange("(c p) n -> p c n", p=128)
                    a = AR.bf("w1a", [128, 8, 672])
                    load_w(a, wv, 0, 672)
                    pref["w1a"] = a
                    if full:
                        b_ = AR.bf("w1b", [128, 8, 512])
                        load_w(b_, wv, 1184, 512)
                        pref["w1b"] = b_
                else:
                    wv = wino_d[i2].rearrange("(c p) n -> p c n", p=128)
                    a = AR.bf("wq0", [128, 8, 512])
                    load_w(a, wv, 0, 512)
                    pref["wq0"] = a
                    if full:
                        b_ = AR.bf("wq1", [128, 8, 512])
                        load_w(b_, wv, 512, 512)
                        pref["wq1"] = b_
                        k_ = AR.bf("wk", [128, 8, 256])
                        load_w(k_, wv, 1024, 256)
                        pref["wk"] = k_
            except RuntimeError:
                pass

        def even_layer(l, m, NT, nseq, T, ctx, nxt=None):
            i = l // 2
            S = T + (256 if ctx else 0)
            KT = nseq * S
            nkc = S // 128
            wv_in = wine_d[i].rearrange("(c p) n -> p c n", p=128)
            w1a = pref.pop("w1a", None)
            if w1a is None:
                w1a = AR.bf("w1a", [128, 8, 672])
                load_w(w1a, wv_in, 0, 672)
            w1b = pref.pop("w1b", None)
            if w1b is None:
                w1b = AR.bf("w1b", [128, 8, 512])
                load_w(w1b, wv_in, 1184, 512)
            wp = AR.bf("wp", [128, 4, 128])
            amats = AR.bf("amats", [128, 20, 128])
            P.dma(amats[:], amats_d.rearrange("a k q -> k a q"), eng="pool")
            P.dma(wp[:], wpool_d[i].rearrange("g c e -> c g e"), eng="pool")
            mbuf = AR.bf("mbuf", [128, 8, NT])
            mo = AR.live["mbuf"][0]
            vtok = arena_t[:, mo:mo + (NT // 128) * 512].rearrange("p (j q) -> p j q", q=512)
            cqn = AR.bf("cqn", [128, 3, NT])
            ckvn = AR.bf("ckvn", [128, 2, KT])
            krT = AR.bf("krT", [96, KT])
            cs = None
            if ctx:
                cs = AR.bf("cs", [96, 2, 2048])
                P.dma(cs[:, 0, :], mlacs_d[0], eng="pool")
                P.dma(cs[:, 1, :], mlacs_d[1], eng="pool")
                rb = alloc_rb()
                cst = AR.f32("cst", [128, 2, 256 + 96])
                P.memset("pool", cst[:, :, 256:320], 0.0)
                for jj in range(2):
                    P.dma(cst[:, jj, 0:256], cckv_d[i, jj * 128:(jj + 1) * 128, :])
                    P.dma(cst[:, jj, 320:352], ckr_d[i, jj * 128:(jj + 1) * 128, :])
                for jj in range(2):
                    pb = bank()
                    for c in range(2):
                        P.tr(pb[:, c * 128:(c + 1) * 128], cst[:, jj, c * 128:(c + 1) * 128], ident[:, :])
                    P.tr(pb[0:96, 256:384], cst[:, jj, 256:352], ident[:, :])
                    for c in range(2):
                        P.cp("dve", ckvn[:, c, T + jj * 128:T + (jj + 1) * 128], pb[:, c * 128:(c + 1) * 128])
                    P.cp("dve", krT[64:96, T + jj * 128:T + (jj + 1) * 128], pb[64:96, 256:384])
                AR.release("cst")
            stg = None
            if not ctx:
                stg = [AR.f32("stg%d" % j, [128, 288]) for j in range(2)]
            junk = AR.f32("junk", [128, 256])
            ssq1 = AR.f32("ssq1", [128, 2])
            hs1 = alloc_hs()

            def kidx(t):
                return (t // T) * S + (t % T)

            pipe1 = len(hs1) > 1
            if pipe1:
                modulate(hs1[0][0], hs1[0][1], l, m, 0, 512)
            for t0 in range(0, NT, 512):
                n = 512
                hT, sq = hs1[(t0 // 512) % len(hs1)]
                if not pipe1:
                    modulate(hT, sq, l, m, t0, n)
                nparts = []
                if pipe1 and t0 + 512 < NT:
                    nh, nsq = hs1[((t0 // 512) + 1) % 2]
                    nparts = modulate_parts(nh, nsq, l, m, t0 + 512, 512)
                for oc in range(3):
                    pb = bank()
                    for k in range(8):
                        P.mm(pb[:, :n], w1a[:, k, oc * 128:(oc + 1) * 128], hT[:, k, :n], start=(k == 0), stop=(k == 7))
                    P.act(sq[:, oc, :n], pb[:, :n], AF.Square)
                    P.ts("dve", cqn[:, oc, t0:t0 + n], pb[:, :n], vecT[:, oc, 8 + i:9 + i], ALU.mult)
                    run_part(nparts)
                pieces = [(t0, n)] if T >= 512 else [(t0 + a, T) for a in range(0, n, T)]
                for oc in range(2):
                    pb = bank()
                    for k in range(8):
                        P.mm(pb[:, :n], w1a[:, k, 384 + oc * 128:384 + (oc + 1) * 128], hT[:, k, :n],
                             start=(k == 0), stop=(k == 7))
                    P.act(sq[:, 4 + oc, :n], pb[:, :n], AF.Square)
                    for (ta, tn) in pieces:
                        P.ts("dve", ckvn[:, oc, kidx(ta):kidx(ta) + tn], pb[:, ta - t0:ta - t0 + tn],
                             vecT[:, oc, 10 + i:11 + i], ALU.mult)
                    run_part(nparts)
                pb = bank()
                for oc in range(3):
                    P.mm(pb[:, :n], ones_bf[:, :], sq[:, oc, :n], start=(oc == 0), stop=(oc == 2))
                rstd_from_ssq(pb[:, :n], n, 384, rstd2[:, :n])
                for oc in range(3):
                    P.tt("pool" if oc == 1 else "dve", cqn[:, oc, t0:t0 + n], cqn[:, oc, t0:t0 + n], rstd2[:, :n], ALU.mult)
                pb = bank()
                for k in range(8):
                    P.mm(pb[0:96, :n], w1a[:, k, 576:672], hT[:, k, :n], start=(k == 0), stop=(k == 7))
                if ctx:
                    p2 = rope_apply(pb, 96, n, 1.0, pm96, cs, t0, rb)
                    P.cp("act", krT[64:96, kidx(t0):kidx(t0) + n], p2[64:96, :n])
                else:
                    for (ta, tn) in pieces:
                        P.cp("dve", krT[64:96, kidx(ta):kidx(ta) + tn], pb[64:96, ta - t0:ta - t0 + tn])
                pb = bank()
                for oc in range(2):
                    P.mm(pb[:, :n], ones_bf[:, :], sq[:, 4 + oc, :n], start=(oc == 0), stop=(oc == 1))
                rstd_from_ssq(pb[:, :n], n, 256, rstd2[:, :n])
                for oc in range(2):
                    for (ta, tn) in pieces:
                        P.tt("pool" if oc == 1 else "dve", ckvn[:, oc, kidx(ta):kidx(ta) + tn],
                             ckvn[:, oc, kidx(ta):kidx(ta) + tn], rstd2[:, ta - t0:ta - t0 + tn], ALU.mult)
                for j in range(n // 128):
                    pb = bank()
                    for k in range(8):
                        P.mm(pb[:, :], hT[:, k, j * 128:(j + 1) * 128], w1b[:, k, :], start=(k == 0), stop=(k == 7))
                    P.cp(evac_eng(), vtok[:, (t0 // 128) + j, :], pb[:, :])
                    run_part(nparts)
                run_part(nparts, 16)
                if not ctx:
                    for j in range(n // 128):
                        tok = t0 + j * 128
                        b = tok // T
                        pos = tok % T
                        pb = bank()
                        for k in range(8):
                            P.mm(pb[:, 0:288], hT[:, k, j * 128:(j + 1) * 128], w1a[:, k, 384:672],
                                 start=(k == 0), stop=(k == 7))
                        so = stg[j % 2]
                        P.act(junk[:, :], pb[:, 0:256], AF.Square)
                        P.op("dve", lambda e: e.reduce_sum(out=ssq1[:, 0:1], in_=junk[:, :], axis=mybir.AxisListType.X),
                             reads=[junk[:, :]], writes=[ssq1[:, 0:1]])
                        P.act(ssq1[:, 1:2], ssq1[:, 0:1], AF.Ln, bias=epsT[:, 0:1], scale=1.0 / 256)
                        P.act(ssq1[:, 1:2], ssq1[:, 1:2], AF.Exp, scale=-0.5)
                        P.stt("dve", so[:, 0:256], pb[:, 0:256], ssq1[:, 1:2], gkvb[:, i, :], ALU.mult, ALU.mult)
                        P.cp("dve", so[:, 256:288], pb[:, 256:288])
                        P.dma(sckv_d[b, i, pos:pos + 128, :], so[:, 0:256], out_dma=True)
                        P.dma(skr_d[b, i, pos:pos + 128, :], so[:, 256:288], out_dma=True)
            free_hs(hs1)
            pooled = [AR.bf("pooled%d" % j, [128, 512]) for j in range(2)]
            ppend = [None]
            ncs = T // 128
            for t0 in range(0, NT, 512):
                for g in range(4):
                    pb = bank()
                    for j in range(4):
                        ch = t0 // 128 + j
                        cin = ch % ncs
                        contrib = []
                        if cin > 0:
                            contrib.append((ch - 1, 3))
                        contrib.append((ch, 0 if cin == 0 else (2 if cin == ncs - 1 else 1)))
                        if cin < ncs - 1:
                            contrib.append((ch + 1, 4))
                        for ci, (src, kind) in enumerate(contrib):
                            P.mm(pb[:, j * 128:(j + 1) * 128], vtok[:, src, g * 128:(g + 1) * 128],
                                 amats[:, g * 5 + kind, :], start=(ci == 0), stop=(ci == len(contrib) - 1))
                    pl = pooled[g % 2]
                    P.cp(evac_eng(), pl[:, :], pb[:, :])
                    if ppend[0] is not None:
                        ppend[0]()

                    def _pw(pl=pl, g=g, t0=t0):
                        pb2 = bank()
                        P.mm(pb2[:, :], wp[:, g, :], pl[:, :])
                        P.ts("dve", mbuf[:, 4 + g, t0:t0 + 512], pb2[:, :], vecT[:, g, 12 + i:13 + i], ALU.mult)
                    ppend[0] = _pw
            if ppend[0] is not None:
                ppend[0]()
                ppend[0] = None
            for nm in ("pooled0", "pooled1", "junk", "ssq1", "w1a", "w1b", "wp", "amats"):
                AR.release(nm)
            if not ctx:
                AR.release("stg0")
                AR.release("stg1")
            if stages == 1 and l == trunc_l[0]:
                raise _Trunc()
            ada_hooks = None
            if ADA_INTERLEAVE and (not ctx) and l + 1 < 4:
                ada_alloc(l + 1, 2)
                ada_issue([0, 1])

                def _h0():
                    ada_compute([0, 1])
                    ada_issue([2, 3])

                def _h1():
                    ada_compute([2, 3])
                    ada_issue([4, 5])

                def _hp():
                    ada_compute([4, 5])
                    ada_finish()
                ada_hooks = {0: _h0, 1: _h1, "post": _hp}
            wuq = AR.bf("wuq", [128, 3, 768])
            wuk = AR.bf("wuk", [128, 2, 8, 64])
            wuv = AR.bf("wuv", [128, 2, 8, 64])
            P.dma(wuq[:], wuq_d[i].rearrange("(c p) n -> p c n", p=128), eng="pool")
            wukv_v = wukv_d[i].rearrange("(c p) (h t d) -> p c h t d", p=128, h=8, t=2)
            for c in range(2):
                P.dma(wuk[:, c, :, :], wukv_v[:, c, :, 0, :], eng="pool")
                P.dma(wuv[:, c, :, :], wukv_v[:, c, :, 1, :], eng="pool")
            def load_wg():
                wg_ = AR.bf("wg", [128, 8, 1024])
                load_w(wg_[:, :, 0:512], wv_in, 672, 512)
                load_w(wg_[:, :, 512:1024], wv_in, 1696, 512)
                return wg_

            def load_wout():
                wout_ = AR.bf("wout", [128, 8, 1024])
                load_w(wout_, woe_d[i].rearrange("(c p) n -> p c n", p=128), 0, 1024)
                return wout_
            wg = load_wg()
            if not ctx:
                wout = load_wout()
            NB = 2
            TT = nseq * T
            nkt = KT // 128
            qh = [AR.bf("qh%d" % b, [96, TT]) for b in range(NB)]
            kh = [AR.bf("kh%d" % b, [96, KT]) for b in range(NB)]
            vh = [AR.bf("vh%d" % b, [128, nkt, 128]) for b in range(NB)]
            pT = [AR.bf("pT%d" % b, [128, 512]) for b in range(4)]
            rc = AR.f32("rc", [64, 512])
            for b in range(NB):
                P.memset("pool", vh[b][:, :, 64:128], 1.0)
                P.cp("pool", kh[b][64:96, :], krT[64:96, 0:KT])
            pend = [None]

            def flush_pend():
                if pend[0] is not None:
                    pend[0]()
                    pend[0] = None

            def build_steps(h, b):
                st_ = []
                for ka in range(0, KT, 512):
                    def f(ka=ka):
                        kn = min(512, KT - ka)
                        pb = bank("g")
                        for c in range(2):
                            P.mm(pb[0:64, :kn], wuk[:, c, h, :], ckvn[:, c, ka:ka + kn], start=(c == 0), stop=(c == 1))
                        P.cp("dve" if ctx else "act", kh[b][0:64, ka:ka + kn], pb[0:64, :kn])
                    st_.append(f)
                for ja in range(0, nkt, 8):
                    def f(ja=ja):
                        jn = min(8, nkt - ja)
                        pb = bank("g")
                        for jj in range(jn):
                            for c in range(2):
                                P.mm(pb[:, jj * 64:(jj + 1) * 64], ckvn[:, c, (ja + jj) * 128:(ja + jj + 1) * 128],
                                     wuv[:, c, h, :], start=(c == 0), stop=(c == 1))
                        P.cp("dve" if ctx else "act", vh[b][:, ja:ja + jn, 0:64], pb[:, 0:jn * 64].rearrange("p (j d) -> p j d", d=64))
                    st_.append(f)
                for qa in range(0, TT, 512):
                    hold = {}

                    def f(qa=qa, hold=hold):
                        pb = bank("g")
                        for c in range(3):
                            P.mm(pb[0:96, :512], wuq[:, c, h * 96:(h + 1) * 96], cqn[:, c, qa:qa + 512],
                                 start=(c == 0), stop=(c == 2))
                        if ctx:
                            hold["b"] = rope_a(pb, 96, 512, MLA_SCALE, pm96, cs, qa, rb)
                        else:
                            P.act(qh[b][:, qa:qa + 512], pb[0:96, :512], AF.Copy, scale=float(MLA_SCALE))
                    st_.append(f)
                    if ctx:
                        def f2(qa=qa, hold=hold):
                            p2 = hold["b"]()
                            P.cp("dve", qh[b][0:96, qa:qa + 512], p2[0:96, :512])
                        st_.append(f2)
                return st_

            for f in build_steps(0, 0):
                f()
            for h in range(8):
                b = h % NB
                half = h % 2
                inj = build_steps(h + 1, (h + 1) % NB) if h + 1 < 8 else []
                if ctx:
                    total_steps = (T // 512) * nkc
                    every = max(1, total_steps // (len(inj) + 1))
                    stepc = 0
                    for qa in range(0, T, 512):
                        po = bank("o")
                        sc_ps = {}

                        def issue_s(j):
                            ps_ = bank("s")
                            P.mm(ps_[:, :512], kh[b][:, j * 128:(j + 1) * 128], qh[b][:, qa:qa + 512])
                            sc_ps[j] = ps_

                        for j in range(3):
                            issue_s(j)
                        for j in range(nkc):
                            pt = pT[j % 4]
                            P.act(pt[:, :], sc_ps.pop(j)[:, :], AF.Exp)
                            if j + 3 < nkc:
                                issue_s(j + 3)
                            P.mm(po[:, :], vh[b][:, j, :], pt[:, :], start=(j == 0), stop=(j == nkc - 1))
                            stepc += 1
                            if inj and stepc % every == 0:
                                inj.pop(0)()
                        P.recip(rc[0:64, :], po[64:128, :])
                        P.tt("dve", mbuf[half * 64:(half + 1) * 64, h // 2, qa:qa + 512],
                             po[0:64, :], rc[0:64, :], ALU.mult)
                else:
                    for p in range(nseq // 2):
                        pts = []
                        for a in range(2):
                            sq_ = 2 * p + a
                            ps_ = bank("s")
                            for j in range(2):
                                P.mm(ps_[:, j * 256:(j + 1) * 256], kh[b][:, sq_ * 256 + j * 128:sq_ * 256 + (j + 1) * 128],
                                     qh[b][:, sq_ * 256:(sq_ + 1) * 256])
                            pt = pT[(2 * (p + h * (nseq // 2)) + a) % 4]
                            P.act(pt[:, :], ps_[:, :], AF.Exp)
                            pts.append(pt)
                        flush_pend()
                        for _ in range(3):
                            if inj:
                                inj.pop(0)()

                        def fin(p=p, pts=pts, b=b, h=h, half=half):
                            po = bank("o")
                            for a in range(2):
                                sq_ = 2 * p + a
                                for j in range(2):
                                    P.mm(po[:, a * 256:(a + 1) * 256], vh[b][:, 2 * sq_ + j, :], pts[a][:, j * 256:(j + 1) * 256],
                                         start=(j == 0), stop=(j == 1))
                            P.cp("dve", rc[0:64, :], po[64:128, :])
                            P.act(rc[0:64, :], rc[0:64, :], AF.Ln)
                            P.act(rc[0:64, :], rc[0:64, :], AF.Exp, scale=-1.0)
                            P.tt("dve", mbuf[half * 64:(half + 1) * 64, h // 2, p * 512:(p + 1) * 512],
                                 po[0:64, :], rc[0:64, :], ALU.mult)
                        pend[0] = fin
                while inj:
                    inj.pop(0)()
            flush_pend()
            for nm in ["qh%d" % b for b in range(NB)] + ["kh%d" % b for b in range(NB)] + ["vh%d" % b for b in range(NB)] + \
                      ["pT%d" % b for b in range(4)] + ["rc", "wuq", "wuk", "wuv", "cqn", "ckvn", "krT"]:
                AR.release(nm)
            if ctx:
                AR.release("cs")
                free_rb()
                wout = load_wout()
            if stages == 2 and l == trunc_l[0]:
                raise _Trunc()
            stage3(l, m, NT, mbuf, wg, wout, ada_hooks, nxt)
            for nm in ("wg", "wout", "mbuf"):
                AR.release(nm)

        def odd_layer(l, m, NT, nseq, T, ctx, nxt=None):
            i = l // 2
            S = T + (256 if ctx else 0)
            KT = nseq * S
            nkc = S // 128
            wv_in = wino_d[i].rearrange("(c p) n -> p c n", p=128)
            qT = AR.bf("mbuf", [128, 8, NT])
            kd = AR.bf("kd", [128, 4, KT])
            va = AR.bf("va", [128, KT // 128, 4, 128])
            wq0 = pref.pop("wq0", None)
            if wq0 is None:
                wq0 = AR.bf("wq0", [128, 8, 512])
                load_w(wq0, wv_in, 0, 512)
            wq1 = pref.pop("wq1", None)
            if wq1 is None:
                wq1 = AR.bf("wq1", [128, 8, 512])
                load_w(wq1, wv_in, 512, 512)
            wqs = [wq0, wq1]
            wk = pref.pop("wk", None)
            if wk is None:
                wk = AR.bf("wk", [128, 8, 256])
                load_w(wk, wv_in, 1024, 256)
            nkv = 256 if ctx else 512
            wkv = AR.bf("wkv", [128, 8, nkv])
            load_w(wkv, wv_in, 1536 - nkv, nkv)
            P.memset("pool", va[:, :, :, 64:128], 1.0)
            cs = None
            qs = None
            if ctx:
                cs = AR.bf("cs", [128, 2, 2048])
                P.dma(cs[:, 0, :], swacs_d[0], eng="pool")
                P.dma(cs[:, 1, :], swacs_d[1], eng="pool")
                rb = alloc_rb()
                cst = AR.f32("cst", [128, 2, 256])
                cdup = AR.f32("cdup", [128, 4, 128])
                for jj in range(2):
                    P.dma(cst[:, jj, :], cv_d[i, jj * 128:(jj + 1) * 128, :])
                for jj in range(2):
                    P.cp("dve", va[:, T // 128 + jj, :, 0:64], cst[:, jj, :].rearrange("p (h d) -> p h d", h=4))
                for jj in range(2):
                    P.dma(cst[:, jj, :], ck_d[i, jj * 128:(jj + 1) * 128, :])
                for jj in range(2):
                    P.cp("dve", cdup[:, :, 0:64], cst[:, jj, :].rearrange("p (h d) -> p h d", h=4))
                    P.cp("pool", cdup[:, :, 64:128], cst[:, jj, :].rearrange("p (h d) -> p h d", h=4))
                    pb = bank()
                    for kvh in range(4):
                        P.tr(pb[:, kvh * 128:(kvh + 1) * 128], cdup[:, kvh, :], ident[:, :])
                    for kvh in range(4):
                        P.cp("dve", kd[:, kvh, T + jj * 128:T + (jj + 1) * 128], pb[:, kvh * 128:(kvh + 1) * 128])
                AR.release("cst")
                AR.release("cdup")
            stg = None
            if not ctx:
                stg = [AR.f32("stg%d" % j, [128, 512]) for j in range(2)]
            if ctx:
                hs1 = [(AR.bf("hT", [128, 8, 512]), AR.bf("sq", [128, 4, 512])),
                       (AR.bf("hT1", [128, 8, 512]), AR.bf("sq1", [128, 4, 512]))]
            else:
                hs1 = alloc_hs()

            def kidx(t):
                return (t // T) * S + (t % T)

            pipe1 = len(hs1) > 1
            rpend = [None]
            if pipe1:
                modulate(hs1[0][0], hs1[0][1], l, m, 0, 512)
            for t0 in range(0, NT, 512):
                n = 512
                hT, sq = hs1[(t0 // 512) % len(hs1)]
                if not pipe1:
                    modulate(hT, sq, l, m, t0, n)
                pieces = [(t0, n)] if T >= 512 else [(t0 + a, T) for a in range(0, n, T)]
                nparts = []
                if pipe1 and t0 + 512 < NT:
                    nh, nsq = hs1[((t0 // 512) + 1) % 2]
                    nparts = modulate_parts(nh, nsq, l, m, t0 + 512, 512)
                for oc in range(8):
                    pb = bank()
                    for k in range(8):
                        P.mm(pb[:, :n], wqs[oc // 4][:, k, (oc % 4) * 128:(oc % 4 + 1) * 128], hT[:, k, :n],
                             start=(k == 0), stop=(k == 7))
                    if ctx:
                        pb_fn = rope_a(pb, 128, n, SWA_SCALE, pm128, cs, t0, rb)
                        if rpend[0] is not None:
                            rpend[0]()

                        def _fin_q(pb_fn=pb_fn, oc=oc, t0=t0, n=n):
                            p2 = pb_fn()
                            P.cp("act", qT[:, oc, t0:t0 + n], p2[:, :n])
                        rpend[0] = _fin_q
                    else:
                        P.ts("dve", qT[:, oc, t0:t0 + n], pb[:, :n], SWA_SCALE, ALU.mult)
                    run_part(nparts)
                for kc in range(2):
                    pb = bank()
                    for k in range(8):
                        P.mm(pb[:, :n], wk[:, k, kc * 128:(kc + 1) * 128], hT[:, k, :n], start=(k == 0), stop=(k == 7))
                    def _copies(srcs, kc=kc):
                        for (src, so, sn, ko) in srcs:
                            for hh in range(2):
                                for dh in range(2):
                                    P.cp("act" if dh != hh else "dve",
                                         kd[dh * 64:(dh + 1) * 64, 2 * kc + hh, ko:ko + sn],
                                         src[hh * 64:(hh + 1) * 64, so:so + sn])
                    if ctx:
                        pb_fn = rope_a(pb, 128, n, 1.0, pm128, cs, t0, rb)
                        if rpend[0] is not None:
                            rpend[0]()

                        def _fin_k(pb_fn=pb_fn, t0=t0, n=n, _copies=_copies):
                            p2 = pb_fn()
                            _copies([(p2, 0, n, kidx(t0))])
                        rpend[0] = _fin_k
                    else:
                        _copies([(pb, ta - t0, tn, kidx(ta)) for (ta, tn) in pieces])
                run_part(nparts, 16)
                for j in range(n // 128):
                    tok = t0 + j * 128
                    pb = bank()
                    for k in range(8):
                        P.mm(pb[:, 0:nkv], hT[:, k, j * 128:(j + 1) * 128], wkv[:, k, :], start=(k == 0), stop=(k == 7))
                    P.cp("dve", va[:, kidx(tok) // 128, :, 0:64], pb[:, nkv - 256:nkv].rearrange("p (h d) -> p h d", h=4))
                    if not ctx:
                        b = tok // T
                        pos = tok % T
                        so = stg[j % 2]
                        P.cp("act", so[:, :], pb[:, :])
                        P.dma(sk_d[b, i, pos:pos + 128, :], so[:, 0:256], out_dma=True)
                        P.dma(sv_d[b, i, pos:pos + 128, :], so[:, 256:512], out_dma=True)
                    if j == 0 and rpend[0] is not None:
                        rpend[0]()
                        rpend[0] = None
            free_hs(hs1)
            for nm in ("wq0", "wq1", "wk", "wkv"):
                AR.release(nm)
            if ctx:
                AR.release("cs")
                free_rb()
            else:
                AR.release("stg0")
                AR.release("stg1")
            if stages == 1 and l == trunc_l[0]:
                raise _Trunc()
            ada_hooks = None
            if ADA_INTERLEAVE and (not ctx) and l + 1 < 4:
                ada_alloc(l + 1, 2)
                ada_issue([0, 1])

                def _h0():
                    ada_compute([0, 1])
                    ada_issue([2, 3])

                def _h1():
                    ada_compute([2, 3])
                    ada_issue([4, 5])

                def _hp():
                    ada_compute([4, 5])
                    ada_finish()
                ada_hooks = {0: _h0, 1: _h1, "post": _hp}
            vaB = AR.bf("vaB", [128, KT // 128, 4, 128])
            wg = AR.bf("wg", [128, 8, 1024])
            wout = AR.bf("wout", [128, 8, 1024])
            load_w(wg, wv_in, 1536, 1024)
            load_w(wout, woo_d[i].rearrange("(c p) n -> p c n", p=128), 0, 1024)
            pT = [AR.bf("pT%d" % b, [128, 512]) for b in range(4)]
            rc = AR.f32("rc", [128, 512])
            P.memset("pool", vaB[:, :, :, 0:64], 1.0)
            nch_all = KT // 128
            for ja in range(0, nch_all, 6):
                jb = min(nch_all, ja + 6)
                P.cp("pool", vaB[:, ja:jb, :, 64:128], va[:, ja:jb, :, 0:64])
            mbuf = qT
            pools["o4"] = [4, 5, 6, 7]
            rr["o4"] = 0
            pend = []

            def flush_pend(keep=0):
                while len(pend) > keep:
                    pend.pop(0)()

            def finalize_pair(c, pos, t_lo):
                hA, hB = 2 * c, 2 * c + 1
                P.ts("dve", rc[0:64, :], pos[0][64:128, :], esink[64:128, i * 16 + hA:i * 16 + hA + 1], ALU.add)
                P.ts("dve", rc[64:128, :], pos[1][0:64, :], esink[0:64, i * 16 + hB:i * 16 + hB + 1], ALU.add)
                if ctx:
                    P.recip(rc[:, :], rc[:, :])
                else:
                    P.act(rc[:, :], rc[:, :], AF.Ln)
                    P.act(rc[:, :], rc[:, :], AF.Exp, scale=-1.0)
                P.tt("dve", mbuf[0:64, c, t_lo:t_lo + 512], pos[0][0:64, :], rc[0:64, :], ALU.mult)
                P.tt("dve", mbuf[64:128, c, t_lo:t_lo + 512], pos[1][64:128, :], rc[64:128, :], ALU.mult)

            ucount = 0
            for c in range(8):
                kvh = c // 2
                ntile = (nseq // 2) if not ctx else (T // 512)
                for tix in range(ntile):
                    pos = []
                    if ctx:
                        qt = tix
                        q0 = qt * 512
                        pos = [bank("o4"), bank("o4")]
                        jobs = []
                        for jj in range(2):
                            jobs.append((T // 128 + jj, 0, 512, []))
                        for j in range(4 * qt - 1, 4 * qt + 5):
                            if j < 0 or j >= T // 128:
                                continue
                            nlo = max(4 * qt, j - 1)
                            nhi = min(4 * qt + 3, j + 1)
                            mk = []
                            for nb in range(nlo, nhi + 1):
                                if nb == j - 1:
                                    mk.append((nb, 0))
                                elif nb == j + 1:
                                    mk.append((nb, 1))
                            jobs.append((j, (nlo - 4 * qt) * 128, (nhi - 4 * qt + 1) * 128, mk))
                        sc_ps = {}

                        def issue_s(ji):
                            kc, lo, hi, mk_ = jobs[ji]
                            for half in (0, 1):
                                r0, r1 = half * 64, half * 64 + 64
                                ps_ = bank("s")
                                P.mm(ps_[:, lo:hi], kd[r0:r1, kvh, kc * 128:(kc + 1) * 128], qT[r0:r1, c, q0 + lo:q0 + hi],
                                     start=True, stop=(len(mk_) == 0))
                                sc_ps[(ji, half)] = ps_
                            for half in (0, 1):
                                for mi, (nb, which) in enumerate(mk_):
                                    cl = (nb - 4 * qt) * 128
                                    P.mm(sc_ps[(ji, half)][:, cl:cl + 128], identb[:, :], masks[:, which, :],
                                         start=False, stop=(mi == len(mk_) - 1))

                        for ji in range(min(2, len(jobs))):
                            issue_s(ji)
                        for ji, (kc, lo, hi, mk) in enumerate(jobs):
                            pts = []
                            for half in (0, 1):
                                pt = pT[(2 * ji + half) % 4]
                                P.act(pt[:, lo:hi], sc_ps.pop((ji, half))[:, lo:hi], AF.Exp)
                                pts.append(pt)
                            if ji + 2 < len(jobs):
                                issue_s(ji + 2)
                            for half in (0, 1):
                                vsrc = va if half == 0 else vaB
                                P.mm(pos[half][:, lo:hi], vsrc[:, kc, kvh, :], pts[half][:, lo:hi],
                                     start=(ji == 0), stop=(ji == len(jobs) - 1))
                            if ji == 2:
                                flush_pend()
                        pend.append(lambda c=c, pos=pos, q0=q0: finalize_pair(c, pos, q0))
                        continue
                    for half in (0, 1):
                        r0, r1 = half * 64, half * 64 + 64
                        vsrc = va if half == 0 else vaB
                        p = tix
                        pts = []
                        for a in range(2):
                            sq_ = 2 * p + a
                            ps_ = bank("s")
                            for j in range(2):
                                P.mm(ps_[:, j * 256:(j + 1) * 256], kd[r0:r1, kvh, sq_ * 256 + j * 128:sq_ * 256 + (j + 1) * 128],
                                     qT[r0:r1, c, sq_ * 256:(sq_ + 1) * 256])
                            pt = pT[(2 * ucount + a) % 4]
                            P.act(pt[:, :], ps_[:, :], AF.Exp)
                            pts.append(pt)
                        ucount += 1
                        flush_pend()
                        po = bank("o4")
                        pos.append(po)

                        def pv(p=p, pts=pts, po=po, vsrc=vsrc, kvh=kvh):
                            for a in range(2):
                                sq_ = 2 * p + a
                                for j in range(2):
                                    P.mm(po[:, a * 256:(a + 1) * 256], vsrc[:, 2 * sq_ + j, kvh, :], pts[a][:, j * 256:(j + 1) * 256],
                                         start=(j == 0), stop=(j == 1))
                        pend.append(pv)
                        if half == 1:
                            pend.append(lambda c=c, pos=pos, p=p: finalize_pair(c, pos, p * 512))
            flush_pend()
            AR.release("vaB")
            for nm in ["pT%d" % b for b in range(4)] + ["rc", "kd", "va"]:
                AR.release(nm)
            if stages == 2 and l == trunc_l[0]:
                raise _Trunc()
            stage3(l, m, NT, mbuf, wg, wout, ada_hooks, nxt)
            for nm in ("wg", "wout", "mbuf"):
                AR.release(nm)

        def load_x(src, NT):
            stage = AR.f32("xstage", [128, 4, 1024])
            for t0 in range(0, NT, 512):
                for j in range(4):
                    P.dma(stage[:, j, :], src[t0 + j * 128:t0 + (j + 1) * 128, :])
                for c in range(8):
                    pb = bank()
                    for j in range(4):
                        P.tr(pb[:, j * 128:(j + 1) * 128], stage[:, j, c * 128:(c + 1) * 128], ident[:, :])
                    P.cp(evac_eng(), xT[:, c, t0:t0 + 512], pb[:, :])
            AR.release("xstage")

        def store_x(dst, NT):
            stage = AR.f32("xstage", [128, 4, 1024])
            for t0 in range(0, NT, 512):
                for j in range(4):
                    for hb in range(2):
                        pb = bank()
                        for cc in range(4):
                            c = hb * 4 + cc
                            P.tr(pb[:, cc * 128:(cc + 1) * 128], xT[:, c, t0 + j * 128:t0 + (j + 1) * 128], ident[:, :])
                        P.cp(evac_eng(), stage[:, j, hb * 512:(hb + 1) * 512], pb[:, :])
                    P.dma(dst[t0 + j * 128:t0 + (j + 1) * 128, :], stage[:, j, :], out_dma=True)
            AR.release("xstage")

        trunc_l = [-1]
        try:
          for (src, dst, m, NT, nseq, T, ctx) in ((xp_d, yp_d, 0, 1024, 4, 256, False),
                                                   (xs_d, ys_d, 1, 2048, 1, 2048, True)):
              nl = nl_p if not ctx else nl_s
              if nl < 0:
                  continue
              load_x(src, NT)
              if not ada0_done[0]:
                  ada_all(0)
                  ada0_done[0] = True
              trunc_l[0] = nl - 1 if ((ctx and nl_s >= 0) or (not ctx and nl_s < 0)) else -1
              for l in range(nl):
                  if l + 1 < nl:
                      nxt = (l + 1, ctx, not ctx)
                  elif (not ctx) and nl_s > 0:
                      nxt = (0, True, True)
                  else:
                      nxt = None
                  store_state["dst"] = dst if (l == nl - 1 and stages == 3) else None
                  store_state["done"] = False
                  if l % 2 == 0:
                      even_layer(l, m, NT, nseq, T, ctx, nxt)
                  else:
                      odd_layer(l, m, NT, nseq, T, ctx, nxt)
              if not store_state["done"]:
                  store_x(dst, NT)
              store_state["dst"] = None

        except _Trunc:
            P.dma(yp_d[0:128, 0:512], rstd[:, :], out_dma=True)

        P.finish()
        build_program.stats = {e: len(P.ops[e]) for e in ENGS}
        build_program.peak = AR.peak
    return nc


_CONSTS = None


def _consts():
    global _CONSTS
    if _CONSTS is None:
        mla, swa = _rope_tables()
        pm96, pm128 = _perm_mats()
        _CONSTS = dict(ident=np.eye(128, dtype=np.float32), pm96=pm96, pm128=pm128, masks=_masks(),
                       amats=_pool_mats().reshape(20, 128, 128), mla_cs=mla, swa_cs=swa)
    return _CONSTS


def kernel(x_prompt, x_sample, cache_ckv, cache_krope, cache_k, cache_v, c, c_ctx,
           ada_w, ada_b, norm_pre, norm_post,
           mla_w_in, mla_g_qn, mla_g_kvn, mla_w_uq, mla_w_ukv, pool_w, pool_scale, mixa_w_out,
           swa_w_in, swa_sink, swa_w_out):
    f = lambda a: np.ascontiguousarray(np.asarray(a, dtype=np.float32))
    x_prompt, x_sample = f(x_prompt), f(x_sample)
    cache_ckv, cache_krope, cache_k, cache_v = f(cache_ckv), f(cache_krope), f(cache_k), f(cache_v)
    c, c_ctx = f(c), f(c_ctx)
    vecs = np.zeros((14, 1024), np.float32)
    vecs[0:4] = f(norm_pre)
    vecs[4:8] = f(norm_post)
    vecs[8:10, :384] = f(mla_g_qn)
    vecs[10:12, :256] = f(mla_g_kvn)
    vecs[12:14, :512] = f(pool_scale)
    shared = dict(ada_w=f(ada_w), ada_b=f(ada_b), vecs=vecs, gkvn=f(mla_g_kvn), sink=f(swa_sink).reshape(32),
                  w_in_e=f(mla_w_in), w_uq=f(mla_w_uq), w_ukv=f(mla_w_ukv), w_pool=f(pool_w), w_out_e=f(mixa_w_out),
                  w_in_o=f(swa_w_in), w_out_o=f(swa_w_out))
    shared.update(_consts())
    in_maps = []
    for i in range(8):
        d = dict(shared)
        d["xp"] = x_prompt[4 * i:4 * i + 4].reshape(1024, 1024)
        d["xs"] = x_sample[i]
        d["cckv"] = cache_ckv[i]
        d["ckr"] = cache_krope[i]
        d["ck"] = cache_k[i].reshape(2, 256, 256)
        d["cv"] = cache_v[i].reshape(2, 256, 256)
        d["cc"] = np.stack([c_ctx, c[i]], axis=0)
        in_maps.append(d)
    nc = build_program()
    res = run_bass_kernel_spmd(nc, in_maps, core_ids=list(range(8)))
    R = res.results
    y_prompt = np.concatenate([r["yp"].reshape(4, 256, 1024) for r in R], axis=0)
    y_sample = np.stack([r["ys"] for r in R], axis=0)
    st_ckv = np.concatenate([r["st_ckv"] for r in R], axis=0)
    st_kr = np.concatenate([r["st_kr"] for r in R], axis=0)
    st_k = np.concatenate([r["st_k"].reshape(4, 2, 256, 4, 64) for r in R], axis=0)
    st_v = np.concatenate([r["st_v"].reshape(4, 2, 256, 4, 64) for r in R], axis=0)
    return (y_prompt.astype(np.float32), y_sample.astype(np.float32), st_ckv.astype(np.float32),
            st_kr.astype(np.float32), st_k.astype(np.float32), st_v.astype(np.float32))
```

```python
import numpy as np
from contextlib import ExitStack
import concourse.bass as bass
import concourse.mybir as mybir
from concourse.bass_utils import run_bass_kernel_spmd

F32 = mybir.dt.float32
BF16 = mybir.dt.bfloat16
AF = mybir.ActivationFunctionType
ALU = mybir.AluOpType

ENGS = ("pe", "act", "dve", "pool", "sp")
SAME_ENGINE_RAW = True
EMBED_LAST_WAIT = True
EMBED_ENGINES = ("pe", "act", "dve")
N_DMA_SEMS = 24
BUCKET = 4096

D = 1024
EPS = 1e-6
MLA_SCALE = 96 ** -0.5
SWA_SCALE = 64 ** -0.5


class Prog:
    def __init__(self, nc, stack):
        self.nc = nc
        self.stack = stack
        self.ops = {e: [] for e in ENGS}
        self.recs = {}
        self.known = {e: {} for e in ENGS}
        self.eng_sem = {e: stack.enter_context(nc.semaphore("s_" + e)) for e in ENGS}
        self.dma_sems = [stack.enter_context(nc.semaphore("s_dma%d" % i)) for i in range(N_DMA_SEMS)]
        self.dma_cnt = [0] * N_DMA_SEMS
        self.dma_rr = 0
        self.dma_rr_pool = 0
        self.out_dma_tokens = []

    def sb(self, name, shape, dtype):
        return self.stack.enter_context(self.nc.sbuf_tensor("sb_" + name, list(shape), dtype))

    def ps(self, name, shape, dtype=F32):
        return self.stack.enter_context(self.nc.psum_tensor("pp_" + name, list(shape), dtype))

    @staticmethod
    def _box(ap):
        shp = list(ap.tensor.shape)
        row = 1
        for s in shp[1:]:
            row *= s
        isz = mybir.dt.size(ap.dtype)
        off = int(ap.offset)
        dims = ap.ap
        p0 = off // row
        f0 = off % row
        pstep, pcnt = dims[0]
        if pstep == row or (pcnt == 1 and len(dims) > 1):
            p1 = p0 + pcnt
            rest = dims[1:]
        elif pstep == 0:
            p1 = p0 + 1
            rest = dims[1:]
        else:
            p1 = p0 + 1
            rest = dims
        ext = 0
        for st, cn in rest:
            ext += abs(st) * (cn - 1)
        return (p0, p1, f0 * isz, (f0 + ext + 1) * isz)

    @staticmethod
    def _tracked(ap):
        return str(ap.space).upper() in ("SB", "PSUM")

    def op(self, eng, fn, reads=(), writes=(), dma=False, out_dma=False):
        idx = len(self.ops[eng])
        waits = []
        kn = self.known[eng]

        def need(tok, raw=False):
            if tok[0] == 'e':
                if tok[1] == eng and not (raw and SAME_ENGINE_RAW and eng != "pe"):
                    return
                key = tok[1]
            else:
                key = ('d', tok[1])
            if kn.get(key, -1) >= tok[2]:
                return
            kn[key] = tok[2]
            waits.append(tok)

        if dma:
            if eng == "pool":
                k = 8 + self.dma_rr_pool
                self.dma_rr_pool = (self.dma_rr_pool + 1) % (N_DMA_SEMS - 8)
            else:
                k = self.dma_rr
                self.dma_rr = (self.dma_rr + 1) % 8
            if self.dma_cnt[k] > 0:
                need(('d', k, self.dma_cnt[k] * 16))
            self.dma_cnt[k] += 1
            mytok = ('d', k, self.dma_cnt[k] * 16)
        else:
            mytok = ('e', eng, idx)

        rb = [(ap.name, self._box(ap)) for ap in reads if self._tracked(ap) and str(ap.space).upper() != "PSUM"]
        wb = [(ap.name, self._box(ap)) for ap in writes if self._tracked(ap) and str(ap.space).upper() != "PSUM"]
        for ap in list(reads) + list(writes):
            if str(ap.space).upper() == "PSUM":
                ent = (ap.name, (0, 128, 0, 1 << 20))
                if ent not in wb:
                    wb.append(ent)
        for name, box in rb:
            tr = self.recs.get(name)
            if tr is None:
                continue
            for b in range(box[2] // BUCKET, (box[3] - 1) // BUCKET + 1):
                for r in tr.get(b, ()):
                    if r[3] and r[2]:
                        rbx = r[0]
                        if rbx[0] < box[1] and box[0] < rbx[1] and rbx[2] < box[3] and box[2] < rbx[3]:
                            need(r[1], True)
        for name, box in wb:
            tr = self.recs.get(name)
            if tr is None:
                continue
            for b in range(box[2] // BUCKET, (box[3] - 1) // BUCKET + 1):
                lst = tr.get(b)
                if not lst:
                    continue
                for r in lst:
                    if r[3]:
                        rbx = r[0]
                        if rbx[0] < box[1] and box[0] < rbx[1] and rbx[2] < box[3] and box[2] < rbx[3]:
                            need(r[1])
                            if box[0] <= rbx[0] and box[1] >= rbx[1] and box[2] <= rbx[2] and box[3] >= rbx[3]:
                                r[3] = False
                tr[b] = [r for r in lst if r[3]]
        for name, box in rb:
            tr = self.recs.setdefault(name, {})
            rec = [box, mytok, False, True]
            for b in range(box[2] // BUCKET, (box[3] - 1) // BUCKET + 1):
                lst = tr.setdefault(b, [])
                if not dma:
                    for r in lst:
                        if r[3] and (not r[2]) and r[1][0] == 'e' and r[1][1] == eng and r[0] == box:
                            r[3] = False
                lst.append(rec)
        for name, box in wb:
            tr = self.recs.setdefault(name, {})
            rec = [box, mytok, True, True]
            for b in range(box[2] // BUCKET, (box[3] - 1) // BUCKET + 1):
                tr.setdefault(b, []).append(rec)
        o = dict(fn=fn, waits=waits, tok=mytok, marked=False)
        self.ops[eng].append(o)
        if out_dma:
            self.out_dma_tokens.append(mytok)
        return o

    def dma(self, out, in_, eng="sp", out_dma=False):
        return self.op(eng, lambda e: e.dma_start(out=out, in_=in_), reads=[in_], writes=[out],
                       dma=True, out_dma=out_dma)

    def mm(self, out, lhsT, rhs, start=True, stop=True):
        return self.op("pe", lambda e: e.matmul(out, lhsT, rhs, start=start, stop=stop),
                       reads=[lhsT, rhs] + ([] if start else [out]), writes=[out])

    def tr(self, out, in_, ident):
        return self.op("pe", lambda e: e.transpose(out, in_, ident), reads=[in_, ident], writes=[out])

    def act(self, out, in_, func, bias=None, scale=None, accum=None):
        kw = {}
        rd = [in_]
        wr = [out]
        if bias is not None:
            kw["bias"] = bias
            if not isinstance(bias, (int, float)):
                rd.append(bias)
        if scale is not None:
            kw["scale"] = scale
            if not isinstance(scale, (int, float)):
                rd.append(scale)
        if accum is not None:
            kw["accum_out"] = accum
            wr.append(accum)
        return self.op("act", lambda e: e.activation(out=out, in_=in_, func=func, **kw), reads=rd, writes=wr)

    def cp(self, eng, out, in_):
        if eng == "act":
            return self.op("act", lambda e: e.copy(out=out, in_=in_), reads=[in_], writes=[out])
        return self.op(eng, lambda e: e.tensor_copy(out=out, in_=in_), reads=[in_], writes=[out])

    def tt(self, eng, out, in0, in1, op):
        return self.op(eng, lambda e: e.tensor_tensor(out=out, in0=in0, in1=in1, op=op), reads=[in0, in1], writes=[out])

    def ts(self, eng, out, in0, s1, op0, s2=None, op1=None):
        rd = [in0]
        if not isinstance(s1, (int, float)):
            rd.append(s1)
        if s2 is not None and not isinstance(s2, (int, float)):
            rd.append(s2)
        if op1 is None:
            return self.op(eng, lambda e: e.tensor_scalar(out=out, in0=in0, scalar1=s1, scalar2=None, op0=op0),
                           reads=rd, writes=[out])
        return self.op(eng, lambda e: e.tensor_scalar(out=out, in0=in0, scalar1=s1, scalar2=s2, op0=op0, op1=op1),
                       reads=rd, writes=[out])

    def stt(self, eng, out, in0, scalar, in1, op0, op1):
        rd = [in0, in1]
        if not isinstance(scalar, (int, float)):
            rd.append(scalar)
        return self.op(eng, lambda e: e.scalar_tensor_tensor(out=out, in0=in0, scalar=scalar, in1=in1, op0=op0, op1=op1),
                       reads=rd, writes=[out])

    def memset(self, eng, out, val):
        return self.op(eng, lambda e: e.memset(out, val), writes=[out])

    def recip(self, out, in_):
        return self.op("dve", lambda e: e.reciprocal(out=out, in_=in_), reads=[in_], writes=[out])

    def finish(self):
        nc = self.nc
        for e in ENGS:
            for o in self.ops[e]:
                for tok in o["waits"]:
                    if tok[0] == 'e':
                        self.ops[tok[1]][tok[2]]["marked"] = True
        cnt_at = {}
        for e in ENGS:
            c = 0
            arr = []
            for o in self.ops[e]:
                if o["marked"]:
                    c += 1
                arr.append(c)
            cnt_at[e] = arr
        fw = {}
        for tok in self.out_dma_tokens:
            fw[tok[1]] = max(fw.get(tok[1], 0), tok[2])

        with nc.Block() as block:
            def emit(ename, engobj):
                for o in self.ops[ename]:
                    ws = o["waits"]
                    emb = None
                    if EMBED_LAST_WAIT and ws and ename in EMBED_ENGINES and o["tok"][0] == 'e':
                        emb = ws[-1]
                        ws = ws[:-1]
                    for tok in ws:
                        if tok[0] == 'e':
                            engobj.wait_ge(self.eng_sem[tok[1]], cnt_at[tok[1]][tok[2]])
                        else:
                            engobj.wait_ge(self.dma_sems[tok[1]], tok[2])
                    ins = o["fn"](engobj)
                    if emb is not None:
                        if emb[0] == 'e':
                            ins._wait_ge(self.eng_sem[emb[1]], cnt_at[emb[1]][emb[2]])
                        else:
                            ins._wait_ge(self.dma_sems[emb[1]], emb[2])
                    if o["tok"][0] == 'd':
                        ins.then_inc(self.dma_sems[o["tok"][1]], 16)
                    elif o["marked"]:
                        ins.then_inc(self.eng_sem[ename], 1)
                if ename == "sp":
                    for k, v in fw.items():
                        engobj.wait_ge(self.dma_sems[k], v)

            @block.sync
            def _(sync):
                emit("sp", sync)

            @block.tensor
            def _(tensor):
                emit("pe", tensor)

            @block.scalar
            def _(scalar):
                emit("act", scalar)

            @block.vector
            def _(vector):
                emit("dve", vector)

            @block.gpsimd
            def _(gpsimd):
                emit("pool", gpsimd)


class _Trunc(Exception):
    pass


class Arena:
    def __init__(self, tensor, nelem):
        self.t = tensor
        self.n = nelem
        self.free = [(0, nelem)]
        self.live = {}
        self.peak = 0

    def alloc(self, name, nelem_bf16):
        nelem_bf16 = (nelem_bf16 + 15) // 16 * 16
        for i, (o, s) in enumerate(self.free):
            if s >= nelem_bf16:
                if s == nelem_bf16:
                    self.free.pop(i)
                else:
                    self.free[i] = (o + nelem_bf16, s - nelem_bf16)
                self.live[name] = (o, nelem_bf16)
                used = self.n - sum(s for _, s in self.free)
                self.peak = max(self.peak, used)
                return o
        raise RuntimeError("arena OOM allocating %s (%d); live=%s free=%s" % (name, nelem_bf16, self.live, self.free))

    def release(self, name):
        o, s = self.live.pop(name)
        self.free.append((o, s))
        self.free.sort()
        merged = []
        for o, s in self.free:
            if merged and merged[-1][0] + merged[-1][1] == o:
                merged[-1] = (merged[-1][0], merged[-1][1] + s)
            else:
                merged.append((o, s))
        self.free = merged

    def bf(self, name, shape):
        n = 1
        for s in shape[1:]:
            n *= s
        o = self.alloc(name, n)
        v = self.t[0:shape[0], o:o + n]
        if len(shape) == 3:
            v = v.rearrange("p (a b) -> p a b", a=shape[1])
        elif len(shape) == 4:
            v = v.rearrange("p (a b c) -> p a b c", a=shape[1], b=shape[2])
        return v

    def f32(self, name, shape):
        n = 1
        for s in shape[1:]:
            n *= s
        o = self.alloc(name, 2 * n)
        v = self.t[0:shape[0], o:o + 2 * n].bitcast(F32)
        if len(shape) == 3:
            v = v.rearrange("p (a b) -> p a b", a=shape[1])
        elif len(shape) == 4:
            v = v.rearrange("p (a b c) -> p a b c", a=shape[1], b=shape[2])
        return v


def _rope_tables():
    t = np.arange(2048)
    row = (t // 64).astype(np.float64)
    col = (t % 64).astype(np.float64)
    mla = np.zeros((2, 96, 2048), np.float32)
    mla[0, 0:64] = 1.0
    for r in range(32):
        pos = row if r < 16 else col
        i = r % 16
        f = i % 8
        inv = 10000.0 ** (-(2.0 * f) / 16.0)
        ang = pos * inv
        mla[0, 64 + r] = np.cos(ang)
        mla[1, 64 + r] = np.sin(ang) if i < 8 else -np.sin(ang)
    swa = np.zeros((2, 128, 2048), np.float32)
    for p in range(128):
        d = p % 64
        pos = row if d < 32 else col
        i = d % 32
        f = i % 16
        inv = 10000.0 ** (-(2.0 * f) / 32.0)
        ang = pos * inv
        swa[0, p] = np.cos(ang)
        swa[1, p] = np.sin(ang) if i < 16 else -np.sin(ang)
    return mla, swa


def _perm_mats():
    pm96 = np.zeros((96, 96), np.float32)
    for r in range(32):
        i = r % 16
        partner = r + 8 if i < 8 else r - 8
        pm96[64 + partner, 64 + r] = 1.0
    pm128 = np.zeros((128, 128), np.float32)
    for m in range(128):
        i = m % 32
        partner = m + 16 if i < 16 else m - 16
        pm128[partner, m] = 1.0
    return pm96, pm128


def _pool_mats():
    T = 384
    out = np.zeros((4, 5, 128, 128), np.float32)
    for g, w in enumerate((2, 4, 8, 16)):
        A = np.zeros((T, T), np.float64)
        for t in range(T):
            lo = min(max(t - w // 2, 0), T)
            hi = min(max(t + w // 2, 0), T)
            A[lo:hi, t] = 1.0 / (hi - lo)
            A[t, t] -= 1.0
        out[g, 0] = A[0:128, 0:128]
        out[g, 1] = A[128:256, 128:256]
        out[g, 2] = A[256:384, 256:384]
        out[g, 3] = A[0:128, 128:256]
        out[g, 4] = A[128:256, 0:128]
    return out


def _masks():
    k = np.arange(128)[:, None]
    q = np.arange(128)[None, :]
    m = np.zeros((2, 128, 128), np.float32)
    m[0] = np.where(k <= q, 0.0, -30000.0)
    m[1] = np.where(q <= k, 0.0, -30000.0)
    return m


def build_program(nl_p=4, nl_s=4, stages=3, dbg=None):
    nc = bass.Bass("TRN2", target_bir_lowering=False)

    def din(name, shape):
        return nc.dram_tensor(name, list(shape), F32, kind="ExternalInput").ap()

    def dout(name, shape):
        return nc.dram_tensor(name, list(shape), F32, kind="ExternalOutput").ap()

    xp_d = din("xp", [1024, 1024])
    xs_d = din("xs", [2048, 1024])
    cckv_d = din("cckv", [2, 256, 256])
    ckr_d = din("ckr", [2, 256, 32])
    ck_d = din("ck", [2, 256, 256])
    cv_d = din("cv", [2, 256, 256])
    cc_d = din("cc", [2, 1024])
    adaw_d = din("ada_w", [4, 1024, 3072])
    adab_d = din("ada_b", [4, 3072])
    vecs_d = din("vecs", [14, 1024])
    gkvn_d = din("gkvn", [2, 256])
    sink_d = din("sink", [32])
    wine_d = din("w_in_e", [2, 1024, 2208])
    wuq_d = din("w_uq", [2, 384, 768])
    wukv_d = din("w_ukv", [2, 256, 1024])
    wpool_d = din("w_pool", [2, 4, 128, 128])
    woe_d = din("w_out_e", [2, 1024, 1024])
    wino_d = din("w_in_o", [2, 1024, 2560])
    woo_d = din("w_out_o", [2, 1024, 1024])
    ident_d = din("ident", [128, 128])
    pm96_d = din("pm96", [96, 96])
    pm128_d = din("pm128", [128, 128])
    masks_d = din("masks", [2, 128, 128])
    amats_d = din("amats", [20, 128, 128])
    mlacs_d = din("mla_cs", [2, 96, 2048])
    swacs_d = din("swa_cs", [2, 128, 2048])

    yp_d = dout("yp", [1024, 1024])
    ys_d = dout("ys", [2048, 1024])
    sckv_d = dout("st_ckv", [4, 2, 256, 256])
    skr_d = dout("st_kr", [4, 2, 256, 32])
    sk_d = dout("st_k", [4, 2, 256, 256])
    sv_d = dout("st_v", [4, 2, 256, 256])

    with ExitStack() as st:
        P = Prog(nc, st)
        xT = P.sb("xT", [128, 8, 2048], F32)
        ident = P.sb("ident", [128, 128], F32)
        ones_bf = P.sb("ones_bf", [128, 128], BF16)
        identb = P.sb("identb", [128, 128], BF16)
        scb = P.sb("scb", [128, 8, 2], BF16)
        pm96 = P.sb("pm96", [96, 96], BF16)
        pm128 = P.sb("pm128", [128, 128], BF16)
        masks = P.sb("masks", [128, 2, 128], BF16)
        epsT = P.sb("epsT", [128, 1], F32)
        mod = P.sb("mod", [128, 4, 48], F32)
        vecT = P.sb("vecT", [128, 8, 32], F32)
        coefA = P.sb("coefA", [128, 4, 2, 8], F32)
        coefB = P.sb("coefB", [128, 4, 2, 8], F32)
        coefG = P.sb("coefG", [128, 4, 2, 8], F32)
        gkvb = P.sb("gkvb", [128, 2, 256], F32)
        esink = P.sb("esink", [128, 32], F32)
        rstd = P.sb("rstd", [128, 512], F32)
        rstd2 = P.sb("rstd2", [128, 512], F32)
        tmpf = [P.sb("tmpf%d" % i, [128, 512], F32) for i in range(2)]
        ARENA_N = 64 * 1024
        arena_t = P.sb("arena", [128, ARENA_N], BF16)
        AR = Arena(arena_t, ARENA_N)
        banks = [P.ps("ps%d" % i, [128, 512], F32) for i in range(8)]
        rr = {"all": 0, "s": 0, "o": 0, "g": 0}
        pools = {"all": list(range(8)), "s": [0, 1, 2, 3], "o": [4, 5], "g": [6, 7]}

        def bank(pool="all"):
            lst = pools[pool]
            b = banks[lst[rr[pool] % len(lst)]]
            rr[pool] += 1
            return b

        evac_rr = [0]

        def evac_eng():
            evac_rr[0] += 1
            return "dve" if evac_rr[0] % 2 else "act"

        P.dma(ident[:], ident_d)
        P.dma(identb[:], ident_d, eng="pool")
        P.dma(pm96[:], pm96_d, eng="pool")
        P.dma(pm128[:], pm128_d, eng="pool")
        P.dma(masks[:], masks_d.rearrange("a k q -> k a q"), eng="pool")
        P.dma(gkvb[:].rearrange("p a n -> p (a n)"), gkvn_d.rearrange("a n -> (a n)").partition_broadcast(128))
        P.dma(esink[:], sink_d.partition_broadcast(128))
        P.memset("dve", ones_bf[:], 1.0)
        P.memset("dve", epsT[:], EPS)
        P.act(esink[:], esink[:], AF.Exp)

        if dbg == "consts":
            P.dma(yp_d[0:128, 0:32], esink[:], out_dma=True)
            P.finish()
            return nc
        vst = AR.f32("vst", [32, 1024])
        P.memset("dve", vst[:], 0.0)
        P.dma(vst[0:14, :], vecs_d)
        pv = bank()
        for c in range(8):
            P.tr(pv[:, c * 32:(c + 1) * 32], vst[0:32, c * 128:(c + 1) * 128], ident[0:32, 0:32])
        for c in range(8):
            P.cp("dve", vecT[:, c, :], pv[:, c * 32:(c + 1) * 32])
        AR.release("vst")

        if dbg == "vec":
            P.dma(yp_d[0:128, 0:256], vecT[:].rearrange("p a b -> p (a b)"), out_dma=True)
            P.finish()
            return nc
        ccT = AR.f32("ccT", [128, 2, 8])
        for m in range(2):
            P.dma(ccT[:, m, :], cc_d[m].rearrange("(p c) -> p c", c=8))
        for m in range(2):
            P.act(scb[:, :, m], ccT[:, m, :], AF.Silu)
        AR.release("ccT")
        modv = mod[:].rearrange("p l (j c m) -> p l j c m", j=3, c=8)
        ada_state = {}

        def ada_alloc(l, nbuf):
            ada_state["l"] = l
            ada_state["adab"] = AR.bf("adab", [1, 3072])
            ada_state["bufs"] = [AR.bf("adw%d" % i, [128, 8, 512]) for i in range(nbuf)]
            ada_state["nbuf"] = nbuf
            P.dma(ada_state["adab"][:], adab_d[l:l + 1, :], eng="pool")

        def ada_issue(blks):
            l = ada_state["l"]
            wv = adaw_d[l].rearrange("(p c) n -> p c n", c=8)
            for blk in blks:
                wt = ada_state["bufs"][blk % ada_state["nbuf"]]
                P.dma(wt[:], wv[:, :, blk * 512:(blk + 1) * 512], eng="pool")

        def ada_compute(blks):
            l = ada_state["l"]
            adab = ada_state["adab"]
            for blk in blks:
                wt = ada_state["bufs"][blk % ada_state["nbuf"]]
                pm = bank()
                for nci in range(4):
                    nch = blk * 4 + nci
                    for c in range(8):
                        P.mm(pm[:, 2 * nci:2 * nci + 2], wt[:, c, nci * 128:(nci + 1) * 128], scb[:, c, :],
                             start=(c == 0), stop=False)
                    P.mm(pm[:, 2 * nci:2 * nci + 2], adab[0:1, nch * 128:(nch + 1) * 128], ones_bf[0:1, 0:2],
                         start=False, stop=True)
                P.cp("dve", mod[:, l, blk * 8:(blk + 1) * 8], pm[:, 0:8])

        def ada_finish():
            l = ada_state["l"]
            for m in range(2):
                P.stt("dve", coefA[:, l, m, :], modv[:, l, 1, :, m], 1.0, vecT[:, :, l], ALU.add, ALU.mult)
                P.cp("dve", coefB[:, l, m, :], modv[:, l, 0, :, m])
                P.tt("dve", coefG[:, l, m, :], modv[:, l, 2, :, m], vecT[:, :, 4 + l], ALU.mult)
            for i in range(ada_state["nbuf"]):
                AR.release("adw%d" % i)
            AR.release("adab")
            ada_state.clear()

        def ada_all(l):
            ada_alloc(l, 3)
            for blk in range(6):
                ada_issue([blk])
                ada_compute([blk])
            ada_finish()

        ADA_INTERLEAVE = nl_p >= 4
        ada0_done = [False]
        if not ADA_INTERLEAVE:
            for l in range(1, 4):
                ada_all(l)
        if dbg == "ada":
            P.dma(yp_d[0:128, 0:192], mod[:].rearrange("p a b -> p (a b)"), out_dma=True)
            P.dma(yp_d[128:256, 0:64], coefA[:].rearrange("p a b c -> p (a b c)"), out_dma=True)
            P.dma(yp_d[256:384, 0:64], coefG[:].rearrange("p a b c -> p (a b c)"), out_dma=True)
            P.finish()
            return nc
        def rstd_from_ssq(ps_ssq, n, dim, out):
            P.act(out, ps_ssq, AF.Ln, bias=epsT[:, 0:1], scale=1.0 / dim)
            P.act(out, out, AF.Exp, scale=-0.5)

        def modulate_parts(hT, sq, l, m, t0, n):
            parts = []

            ns_ = sq.shape[1]

            def stats_a():
                for c in range(8):
                    P.act(sq[:, c, :n], xT[:, c, t0:t0 + n], AF.Square)

            def stats_b():
                pb = bank()
                for c in range(8):
                    P.mm(pb[:, :n], ones_bf[:, :], sq[:, c, :n], start=(c == 0), stop=(c == 7))
                rstd_from_ssq(pb[:, :n], n, 1024, rstd[:, :n])

            def stats():
                pb = bank()
                for c0 in range(0, 8, ns_):
                    for c in range(c0, c0 + ns_):
                        P.act(sq[:, c % ns_, :n], xT[:, c, t0:t0 + n], AF.Square)
                    for c in range(c0, c0 + ns_):
                        P.mm(pb[:, :n], ones_bf[:, :], sq[:, c % ns_, :n], start=(c == 0), stop=(c == 7))
                rstd_from_ssq(pb[:, :n], n, 1024, rstd[:, :n])
            if ns_ >= 8:
                parts.extend([stats_a, (lambda: None), stats_b])
            else:
                parts.append(stats)
            for c in range(8):
                def ap(c=c):
                    tf = tmpf[c % 2]
                    P.stt("dve", tf[:, :n], xT[:, c, t0:t0 + n], coefA[:, l, m, c:c + 1], rstd[:, :n], ALU.mult, ALU.mult)
                    P.act(hT[:, c, :n], tf[:, :n], AF.Identity, bias=coefB[:, l, m, c:c + 1], scale=1.0)
                parts.append(ap)
            return parts

        def modulate(hT, sq, l, m, t0, n):
            for f in modulate_parts(hT, sq, l, m, t0, n):
                f()

        def run_part(parts, k=1):
            for _ in range(k):
                if parts:
                    parts.pop(0)()

        def load_w(dst, src_rows_view, col0, ncols):
            for c in range(dst.shape[1]):
                P.dma(dst[:, c, :], src_rows_view[:, c, col0:col0 + ncols], eng="pool")

        def alloc_hs():
            hs = [(AR.bf("hT", [128, 8, 512]), AR.bf("sq", [128, 8, 512]))]
            try:
                a = AR.bf("hT1", [128, 8, 512])
                try:
                    b = AR.bf("sq1", [128, 8, 512])
                    hs.append((a, b))
                except RuntimeError:
                    AR.release("hT1")
            except RuntimeError:
                pass
            return hs

        def free_hs(hs):
            AR.release("hT")
            AR.release("sq")
            if len(hs) > 1:
                AR.release("hT1")
                AR.release("sq1")

        store_state = {"dst": None, "done": False}

        def store_tile(dst, stage2, t0):
            for j in range(4):
                for hb in range(2):
                    pb = bank()
                    for cc in range(4):
                        c = hb * 4 + cc
                        P.tr(pb[:, cc * 128:(cc + 1) * 128], xT[:, c, t0 + j * 128:t0 + (j + 1) * 128], ident[:, :])
                    P.cp(evac_eng(), stage2[:, j % 2, hb * 512:(hb + 1) * 512], pb[:, :])
                P.dma(dst[t0 + j * 128:t0 + (j + 1) * 128, :], stage2[:, j % 2, :], out_dma=True)

        def stage3(l, m, NT, mbuf, wg, wout, hooks=None, nxt=None):
            oT = AR.f32("oT", [128, 8, 512])
            hs3 = alloc_hs()
            sg = [AR.bf("sg%d" % i, [128, 512]) for i in range(2)]
            if nxt is not None:
                prefetch_w1(*nxt)
            stage2 = None
            spend = [None]
            if store_state["dst"] is not None:
                try:
                    stage2 = AR.f32("xst2", [128, 2, 1024])
                except RuntimeError:
                    stage2 = None
            tiles = list(range(0, NT, 512))
            n = 512
            pipe = len(hs3) > 1
            if pipe:
                modulate(hs3[0][0], hs3[0][1], l, m, tiles[0], n)
            for ti, t0 in enumerate(tiles):
                if hooks and ti in hooks:
                    hooks[ti]()
                hT, sq = hs3[ti % len(hs3)]
                if not pipe:
                    modulate(hT, sq, l, m, t0, n)
                nparts = []
                if pipe and ti + 1 < len(tiles):
                    nh, nsq = hs3[(ti + 1) % 2]
                    nparts = modulate_parts(nh, nsq, l, m, tiles[ti + 1], n)
                for mc in range(8):
                    pb = bank()
                    for k in range(8):
                        P.mm(pb[:, :n], wg[:, k, mc * 128:(mc + 1) * 128], hT[:, k, :n], start=(k == 0), stop=(k == 7))
                    s_ = sg[mc % 2]
                    P.act(s_[:, :n], pb[:, :n], AF.Silu)
                    P.tt("dve" if mc % 2 else "pool", mbuf[:, mc, t0:t0 + n], mbuf[:, mc, t0:t0 + n], s_[:, :n], ALU.mult)
                    if mc == 1:
                        run_part(nparts)
                if spend[0] is not None:
                    spend[0]()
                    spend[0] = None
                for dc in range(8):
                    pb = bank()
                    for k in range(8):
                        P.mm(pb[:, :n], wout[:, k, dc * 128:(dc + 1) * 128], mbuf[:, k, t0:t0 + n],
                             start=(k == 0), stop=(k == 7))
                    P.cp("dve", oT[:, dc, :n], pb[:, :n])
                    P.act(sq[:, dc, :n], pb[:, :n], AF.Square)
                    run_part(nparts)
                run_part(nparts, 16)
                pb = bank()
                for dc in range(8):
                    P.mm(pb[:, :n], ones_bf[:, :], sq[:, dc, :n], start=(dc == 0), stop=(dc == 7))
                rstd_from_ssq(pb[:, :n], n, 1024, rstd2[:, :n])
                for dc in range(8):
                    tf = tmpf[dc % 2]
                    P.stt("dve", tf[:, :n], oT[:, dc, :n], coefG[:, l, m, dc:dc + 1], rstd2[:, :n], ALU.mult, ALU.mult)
                    P.tt("pool" if dc % 2 == 0 else "dve", xT[:, dc, t0:t0 + n], xT[:, dc, t0:t0 + n], tf[:, :n], ALU.add)
                if stage2 is not None:
                    spend[0] = (lambda t0=t0: store_tile(store_state["dst"], stage2, t0))
            if hooks and "post" in hooks:
                hooks["post"]()
            if stage2 is not None:
                if spend[0] is not None:
                    spend[0]()
                AR.release("xst2")
                store_state["done"] = True
            free_hs(hs3)
            for nm in ("oT", "sg0", "sg1"):
                AR.release(nm)

        rope_rr = [0]

        def rope_a(src_ps, pr, n, scale, pm, cs, t0, rb):
            qc, qsn = rb[rope_rr[0] % len(rb)]
            rope_rr[0] += 1
            P.stt("dve", qc[0:pr, :n], src_ps[0:pr, :n], float(scale), cs[0:pr, 0, t0:t0 + n], ALU.mult, ALU.mult)
            P.stt("dve", qsn[0:pr, :n], src_ps[0:pr, :n], float(scale), cs[0:pr, 1, t0:t0 + n], ALU.mult, ALU.mult)

            def phase_b():
                p2 = bank("g")
                P.mm(p2[0:pr, :n], identb[0:pr, 0:pr], qc[0:pr, :n], start=True, stop=False)
                P.mm(p2[0:pr, :n], pm[:, :], qsn[0:pr, :n], start=False, stop=True)
                return p2
            return phase_b

        def rope_apply(src_ps, pr, n, scale, pm, cs, t0, rb):
            return rope_a(src_ps, pr, n, scale, pm, cs, t0, rb)()

        def alloc_rb():
            return [(AR.bf("rqc%d" % i, [128, 512]), AR.bf("rqs%d" % i, [128, 512])) for i in range(2)]

        def free_rb():
            for i in range(2):
                AR.release("rqc%d" % i)
                AR.release("rqs%d" % i)

        pref = {}

        def prefetch_w1(lnext, ctx_next, full):
            i2 = lnext // 2
            if (not full) and lnext % 2 == 1:
                return
            try:
                if lnext % 2 == 0:
                    wv = wine_d[i2].rearrange("(c p) n -> p c n", p=128)
                    a = AR.bf("w1a", [128, 8, 672])
                    load_w(a, wv, 0, 672)
                    pref["w1a"] = a
                    if full:
                        b_ = AR.bf("w1b", [128, 8, 512])
                        load_w(b_, wv, 1184, 512)
                        pref["w1b"] = b_
                else:
                    wv = wino_d[i2].rearrange("(c p) n -> p c n", p=128)
                    a = AR.bf("wq0", [128, 8, 512])
                    load_w(a, wv, 0, 512)
                    pref["wq0"] = a
                    if full:
                        b_ = AR.bf("wq1", [128, 8, 512])
                        load_w(b_, wv, 512, 512)
                        pref["wq1"] = b_
                        k_ = AR.bf("wk", [128, 8, 256])
                        load_w(k_, wv, 1024, 256)
                        pref["wk"] = k_
            except RuntimeError:
                pass

        def even_layer(l, m, NT, nseq, T, ctx, nxt=None):
            i = l // 2
            S = T + (256 if ctx else 0)
            KT = nseq * S
            nkc = S // 128
            wv_in = wine_d[i].rearrange("(c p) n -> p c n", p=128)
            w1a = pref.pop("w1a", None)
            if w1a is None:
                w1a = AR.bf("w1a", [128, 8, 672])
                load_w(w1a, wv_in, 0, 672)
            w1b = pref.pop("w1b", None)
            if w1b is None:
                w1b = AR.bf("w1b", [128, 8, 512])
                load_w(w1b, wv_in, 1184, 512)
            wp = AR.bf("wp", [128, 4, 128])
            amats = AR.bf("amats", [128, 20, 128])
            P.dma(amats[:], amats_d.rearrange("a k q -> k a q"), eng="pool")
            P.dma(wp[:], wpool_d[i].rearrange("g c e -> c g e"), eng="pool")
            mbuf = AR.bf("mbuf", [128, 8, NT])
            mo = AR.live["mbuf"][0]
            vtok = arena_t[:, mo:mo + (NT // 128) * 512].rearrange("p (j q) -> p j q", q=512)
            cqn = AR.bf("cqn", [128, 3, NT])
            ckvn = AR.bf("ckvn", [128, 2, KT])
            krT = AR.bf("krT", [96, KT])
            cs = None
            if ctx:
                cs = AR.bf("cs", [96, 2, 2048])
                P.dma(cs[:, 0, :], mlacs_d[0], eng="pool")
                P.dma(cs[:, 1, :], mlacs_d[1], eng="pool")
                rb = alloc_rb()
                cst = AR.f32("cst", [128, 2, 256 + 96])
                P.memset("pool", cst[:, :, 256:320], 0.0)
                for jj in range(2):
                    P.dma(cst[:, jj, 0:256], cckv_d[i, jj * 128:(jj + 1) * 128, :])
                    P.dma(cst[:, jj, 320:352], ckr_d[i, jj * 128:(jj + 1) * 128, :])
                for jj in range(2):
                    pb = bank()
                    for c in range(2):
                        P.tr(pb[:, c * 128:(c + 1) * 128], cst[:, jj, c * 128:(c + 1) * 128], ident[:, :])
                    P.tr(pb[0:96, 256:384], cst[:, jj, 256:352], ident[:, :])
                    for c in range(2):
                        P.cp("dve", ckvn[:, c, T + jj * 128:T + (jj + 1) * 128], pb[:, c * 128:(c + 1) * 128])
                    P.cp("dve", krT[64:96, T + jj * 128:T + (jj + 1) * 128], pb[64:96, 256:384])
                AR.release("cst")
            stg = None
            if not ctx:
                stg = [AR.f32("stg%d" % j, [128, 288]) for j in range(2)]
            junk = AR.f32("junk", [128, 256])
            ssq1 = AR.f32("ssq1", [128, 2])
            hs1 = alloc_hs()

            def kidx(t):
                return (t // T) * S + (t % T)

            pipe1 = len(hs1) > 1
            if pipe1:
                modulate(hs1[0][0], hs1[0][1], l, m, 0, 512)
            for t0 in range(0, NT, 512):
                n = 512
                hT, sq = hs1[(t0 // 512) % len(hs1)]
                if not pipe1:
                    modulate(hT, sq, l, m, t0, n)
                nparts = []
                if pipe1 and t0 + 512 < NT:
                    nh, nsq = hs1[((t0 // 512) + 1) % 2]
                    nparts = modulate_parts(nh, nsq, l, m, t0 + 512, 512)
                for oc in range(3):
                    pb = bank()
                    for k in range(8):
                        P.mm(pb[:, :n], w1a[:, k, oc * 128:(oc + 1) * 128], hT[:, k, :n], start=(k == 0), stop=(k == 7))
                    P.act(sq[:, oc, :n], pb[:, :n], AF.Square)
                    P.ts("dve", cqn[:, oc, t0:t0 + n], pb[:, :n], vecT[:, oc, 8 + i:9 + i], ALU.mult)
                    run_part(nparts)
                pieces = [(t0, n)] if T >= 512 else [(t0 + a, T) for a in range(0, n, T)]
                for oc in range(2):
                    pb = bank()
                    for k in range(8):
                        P.mm(pb[:, :n], w1a[:, k, 384 + oc * 128:384 + (oc + 1) * 128], hT[:, k, :n],
                             start=(k == 0), stop=(k == 7))
                    P.act(sq[:, 4 + oc, :n], pb[:, :n], AF.Square)
                    for (ta, tn) in pieces:
                        P.ts("dve", ckvn[:, oc, kidx(ta):kidx(ta) + tn], pb[:, ta - t0:ta - t0 + tn],
                             vecT[:, oc, 10 + i:11 + i], ALU.mult)
                    run_part(nparts)
                pb = bank()
                for oc in range(3):
                    P.mm(pb[:, :n], ones_bf[:, :], sq[:, oc, :n], start=(oc == 0), stop=(oc == 2))
                rstd_from_ssq(pb[:, :n], n, 384, rstd2[:, :n])
                for oc in range(3):
                    P.tt("pool" if oc == 1 else "dve", cqn[:, oc, t0:t0 + n], cqn[:, oc, t0:t0 + n], rstd2[:, :n], ALU.mult)
                pb = bank()
                for k in range(8):
                    P.mm(pb[0:96, :n], w1a[:, k, 576:672], hT[:, k, :n], start=(k == 0), stop=(k == 7))
                if ctx:
                    p2 = rope_apply(pb, 96, n, 1.0, pm96, cs, t0, rb)
                    P.cp("act", krT[64:96, kidx(t0):kidx(t0) + n], p2[64:96, :n])
                else:
                    for (ta, tn) in pieces:
                        P.cp("dve", krT[64:96, kidx(ta):kidx(ta) + tn], pb[64:96, ta - t0:ta - t0 + tn])
                pb = bank()
                for oc in range(2):
                    P.mm(pb[:, :n], ones_bf[:, :], sq[:, 4 + oc, :n], start=(oc == 0), stop=(oc == 1))
                rstd_from_ssq(pb[:, :n], n, 256, rstd2[:, :n])
                for oc in range(2):
                    for (ta, tn) in pieces:
                        P.tt("pool" if oc == 1 else "dve", ckvn[:, oc, kidx(ta):kidx(ta) + tn],
                             ckvn[:, oc, kidx(ta):kidx(ta) + tn], rstd2[:, ta - t0:ta - t0 + tn], ALU.mult)
                for j in range(n // 128):
                    pb = bank()
                    for k in range(8):
                        P.mm(pb[:, :], hT[:, k, j * 128:(j + 1) * 128], w1b[:, k, :], start=(k == 0), stop=(k == 7))
                    P.cp(evac_eng(), vtok[:, (t0 // 128) + j, :], pb[:, :])
                    run_part(nparts)
                run_part(nparts, 16)
                if not ctx:
                    for j in range(n // 128):
                        tok = t0 + j * 128
                        b = tok // T
                        pos = tok % T
                        pb = bank()
                        for k in range(8):
                            P.mm(pb[:, 0:288], hT[:, k, j * 128:(j + 1) * 128], w1a[:, k, 384:672],
                                 start=(k == 0), stop=(k == 7))
                        so = stg[j % 2]
                        P.act(junk[:, :], pb[:, 0:256], AF.Square)
                        P.op("dve", lambda e: e.reduce_sum(out=ssq1[:, 0:1], in_=junk[:, :], axis=mybir.AxisListType.X),
                             reads=[junk[:, :]], writes=[ssq1[:, 0:1]])
                        P.act(ssq1[:, 1:2], ssq1[:, 0:1], AF.Ln, bias=epsT[:, 0:1], scale=1.0 / 256)
                        P.act(ssq1[:, 1:2], ssq1[:, 1:2], AF.Exp, scale=-0.5)
                        P.stt("dve", so[:, 0:256], pb[:, 0:256], ssq1[:, 1:2], gkvb[:, i, :], ALU.mult, ALU.mult)
                        P.cp("dve", so[:, 256:288], pb[:, 256:288])
                        P.dma(sckv_d[b, i, pos:pos + 128, :], so[:, 0:256], out_dma=True)
                        P.dma(skr_d[b, i, pos:pos + 128, :], so[:, 256:288], out_dma=True)
            free_hs(hs1)
            pooled = [AR.bf("pooled%d" % j, [128, 512]) for j in range(2)]
            ppend = [None]
            ncs = T // 128
            for t0 in range(0, NT, 512):
                for g in range(4):
                    pb = bank()
                    for j in range(4):
                        ch = t0 // 128 + j
                        cin = ch % ncs
                        contrib = []
                        if cin > 0:
                            contrib.append((ch - 1, 3))
                        contrib.append((ch, 0 if cin == 0 else (2 if cin == ncs - 1 else 1)))
                        if cin < ncs - 1:
                            contrib.append((ch + 1, 4))
                        for ci, (src, kind) in enumerate(contrib):
                            P.mm(pb[:, j * 128:(j + 1) * 128], vtok[:, src, g * 128:(g + 1) * 128],
                                 amats[:, g * 5 + kind, :], start=(ci == 0), stop=(ci == len(contrib) - 1))
                    pl = pooled[g % 2]
                    P.cp(evac_eng(), pl[:, :], pb[:, :])
                    if ppend[0] is not None:
                        ppend[0]()

                    def _pw(pl=pl, g=g, t0=t0):
                        pb2 = bank()
                        P.mm(pb2[:, :], wp[:, g, :], pl[:, :])
                        P.ts("dve", mbuf[:, 4 + g, t0:t0 + 512], pb2[:, :], vecT[:, g, 12 + i:13 + i], ALU.mult)
                    ppend[0] = _pw
            if ppend[0] is not None:
                ppend[0]()
                ppend[0] = None
            for nm in ("pooled0", "pooled1", "junk", "ssq1", "w1a", "w1b", "wp", "amats"):
                AR.release(nm)
            if not ctx:
                AR.release("stg0")
                AR.release("stg1")
            if stages == 1 and l == trunc_l[0]:
                raise _Trunc()
            ada_hooks = None
            if ADA_INTERLEAVE and (not ctx) and l + 1 < 4:
                ada_alloc(l + 1, 2)
                ada_issue([0, 1])

                def _h0():
                    ada_compute([0, 1])
                    ada_issue([2, 3])

                def _h1():
                    ada_compute([2, 3])
                    ada_issue([4, 5])

                def _hp():
                    ada_compute([4, 5])
                    ada_finish()
                ada_hooks = {0: _h0, 1: _h1, "post": _hp}
            wuq = AR.bf("wuq", [128, 3, 768])
            wuk = AR.bf("wuk", [128, 2, 8, 64])
            wuv = AR.bf("wuv", [128, 2, 8, 64])
            P.dma(wuq[:], wuq_d[i].rearrange("(c p) n -> p c n", p=128), eng="pool")
            wukv_v = wukv_d[i].rearrange("(c p) (h t d) -> p c h t d", p=128, h=8, t=2)
            for c in range(2):
                P.dma(wuk[:, c, :, :], wukv_v[:, c, :, 0, :], eng="pool")
                P.dma(wuv[:, c, :, :], wukv_v[:, c, :, 1, :], eng="pool")
            def load_wg():
                wg_ = AR.bf("wg", [128, 8, 1024])
                load_w(wg_[:, :, 0:512], wv_in, 672, 512)
                load_w(wg_[:, :, 512:1024], wv_in, 1696, 512)
                return wg_

            def load_wout():
                wout_ = AR.bf("wout", [128, 8, 1024])
                load_w(wout_, woe_d[i].rearrange("(c p) n -> p c n", p=128), 0, 1024)
                return wout_
            wg = load_wg()
            if not ctx:
                wout = load_wout()
            NB = 2
            TT = nseq * T
            nkt = KT // 128
            qh = [AR.bf("qh%d" % b, [96, TT]) for b in range(NB)]
            kh = [AR.bf("kh%d" % b, [96, KT]) for b in range(NB)]
            vh = [AR.bf("vh%d" % b, [128, nkt, 128]) for b in range(NB)]
            pT = [AR.bf("pT%d" % b, [128, 512]) for b in range(4)]
            rc = AR.f32("rc", [64, 512])
            for b in range(NB):
                P.memset("pool", vh[b][:, :, 64:128], 1.0)
                P.cp("pool", kh[b][64:96, :], krT[64:96, 0:KT])
            pend = [None]

            def flush_pend():
                if pend[0] is not None:
                    pend[0]()
                    pend[0] = None

            def build_steps(h, b):
                st_ = []
                for ka in range(0, KT, 512):
                    def f(ka=ka):
                        kn = min(512, KT - ka)
                        pb = bank("g")
                        for c in range(2):
                            P.mm(pb[0:64, :kn], wuk[:, c, h, :], ckvn[:, c, ka:ka + kn], start=(c == 0), stop=(c == 1))
                        P.cp("dve" if ctx else "act", kh[b][0:64, ka:ka + kn], pb[0:64, :kn])
                    st_.append(f)
                for ja in range(0, nkt, 8):
                    def f(ja=ja):
                        jn = min(8, nkt - ja)
                        pb = bank("g")
                        for jj in range(jn):
                            for c in range(2):
                                P.mm(pb[:, jj * 64:(jj + 1) * 64], ckvn[:, c, (ja + jj) * 128:(ja + jj + 1) * 128],
                                     wuv[:, c, h, :], start=(c == 0), stop=(c == 1))
                        P.cp("dve" if ctx else "act", vh[b][:, ja:ja + jn, 0:64], pb[:, 0:jn * 64].rearrange("p (j d) -> p j d", d=64))
                    st_.append(f)
                for qa in range(0, TT, 512):
                    hold = {}

                    def f(qa=qa, hold=hold):
                        pb = bank("g")
                        for c in range(3):
                            P.mm(pb[0:96, :512], wuq[:, c, h * 96:(h + 1) * 96], cqn[:, c, qa:qa + 512],
                                 start=(c == 0), stop=(c == 2))
                        if ctx:
                            hold["b"] = rope_a(pb, 96, 512, MLA_SCALE, pm96, cs, qa, rb)
                        else:
                            P.act(qh[b][:, qa:qa + 512], pb[0:96, :512], AF.Copy, scale=float(MLA_SCALE))
                    st_.append(f)
                    if ctx:
                        def f2(qa=qa, hold=hold):
                            p2 = hold["b"]()
                            P.cp("dve", qh[b][0:96, qa:qa + 512], p2[0:96, :512])
                        st_.append(f2)
                return st_

            for f in build_steps(0, 0):
                f()
            for h in range(8):
                b = h % NB
                half = h % 2
                inj = build_steps(h + 1, (h + 1) % NB) if h + 1 < 8 else []
                if ctx:
                    total_steps = (T // 512) * nkc
                    every = max(1, total_steps // (len(inj) + 1))
                    stepc = 0
                    for qa in range(0, T, 512):
                        po = bank("o")
                        sc_ps = {}

                        def issue_s(j):
                            ps_ = bank("s")
                            P.mm(ps_[:, :512], kh[b][:, j * 128:(j + 1) * 128], qh[b][:, qa:qa + 512])
                            sc_ps[j] = ps_

                        for j in range(3):
                            issue_s(j)
                        for j in range(nkc):
                            pt = pT[j % 4]
                            P.act(pt[:, :], sc_ps.pop(j)[:, :], AF.Exp)
                            if j + 3 < nkc:
                                issue_s(j + 3)
                            P.mm(po[:, :], vh[b][:, j, :], pt[:, :], start=(j == 0), stop=(j == nkc - 1))
                            stepc += 1
                            if inj and stepc % every == 0:
                                inj.pop(0)()
                        P.recip(rc[0:64, :], po[64:128, :])
                        P.tt("dve", mbuf[half * 64:(half + 1) * 64, h // 2, qa:qa + 512],
                             po[0:64, :], rc[0:64, :], ALU.mult)
                else:
                    for p in range(nseq // 2):
                        pts = []
                        for a in range(2):
                            sq_ = 2 * p + a
                            ps_ = bank("s")
                            for j in range(2):
                                P.mm(ps_[:, j * 256:(j + 1) * 256], kh[b][:, sq_ * 256 + j * 128:sq_ * 256 + (j + 1) * 128],
                                     qh[b][:, sq_ * 256:(sq_ + 1) * 256])
                            pt = pT[(2 * (p + h * (nseq // 2)) + a) % 4]
                            P.act(pt[:, :], ps_[:, :], AF.Exp)
                            pts.append(pt)
                        flush_pend()
                        for _ in range(3):
                            if inj:
                                inj.pop(0)()

                        def fin(p=p, pts=pts, b=b, h=h, half=half):
                            po = bank("o")
                            for a in range(2):
                                sq_ = 2 * p + a
                                for j in range(2):
                                    P.mm(po[:, a * 256:(a + 1) * 256], vh[b][:, 2 * sq_ + j, :], pts[a][:, j * 256:(j + 1) * 256],
                                         start=(j == 0), stop=(j == 1))
                            P.cp("dve", rc[0:64, :], po[64:128, :])
                            P.act(rc[0:64, :], rc[0:64, :], AF.Ln)
                            P.act(rc[0:64, :], rc[0:64, :], AF.Exp, scale=-1.0)
                            P.tt("dve", mbuf[half * 64:(half + 1) * 64, h // 2, p * 512:(p + 1) * 512],
                                 po[0:64, :], rc[0:64, :], ALU.mult)
                        pend[0] = fin
                while inj:
                    inj.pop(0)()
            flush_pend()
            for nm in ["qh%d" % b for b in range(NB)] + ["kh%d" % b for b in range(NB)] + ["vh%d" % b for b in range(NB)] + \
                      ["pT%d" % b for b in range(4)] + ["rc", "wuq", "wuk", "wuv", "cqn", "ckvn", "krT"]:
                AR.release(nm)
            if ctx:
                AR.release("cs")
                free_rb()
                wout = load_wout()
            if stages == 2 and l == trunc_l[0]:
                raise _Trunc()
            stage3(l, m, NT, mbuf, wg, wout, ada_hooks, nxt)
            for nm in ("wg", "wout", "mbuf"):
                AR.release(nm)

        def odd_layer(l, m, NT, nseq, T, ctx, nxt=None):
            i = l // 2
            S = T + (256 if ctx else 0)
            KT = nseq * S
            nkc = S // 128
            wv_in = wino_d[i].rearrange("(c p) n -> p c n", p=128)
            qT = AR.bf("mbuf", [128, 8, NT])
            kd = AR.bf("kd", [128, 4, KT])
            va = AR.bf("va", [128, KT // 128, 4, 128])
            wq0 = pref.pop("wq0", None)
            if wq0 is None:
                wq0 = AR.bf("wq0", [128, 8, 512])
                load_w(wq0, wv_in, 0, 512)
            wq1 = pref.pop("wq1", None)
            if wq1 is None:
                wq1 = AR.bf("wq1", [128, 8, 512])
                load_w(wq1, wv_in, 512, 512)
            wqs = [wq0, wq1]
            wk = pref.pop("wk", None)
            if wk is None:
                wk = AR.bf("wk", [128, 8, 256])
                load_w(wk, wv_in, 1024, 256)
            nkv = 256 if ctx else 512
            wkv = AR.bf("wkv", [128, 8, nkv])
            load_w(wkv, wv_in, 1536 - nkv, nkv)
            P.memset("pool", va[:, :, :, 64:128], 1.0)
            cs = None
            qs = None
            if ctx:
                cs = AR.bf("cs", [128, 2, 2048])
                P.dma(cs[:, 0, :], swacs_d[0], eng="pool")
                P.dma(cs[:, 1, :], swacs_d[1], eng="pool")
                rb = alloc_rb()
                cst = AR.f32("cst", [128, 2, 256])
                cdup = AR.f32("cdup", [128, 4, 128])
                for jj in range(2):
                    P.dma(cst[:, jj, :], cv_d[i, jj * 128:(jj + 1) * 128, :])
                for jj in range(2):
                    P.cp("dve", va[:, T // 128 + jj, :, 0:64], cst[:, jj, :].rearrange("p (h d) -> p h d", h=4))
                for jj in range(2):
                    P.dma(cst[:, jj, :], ck_d[i, jj * 128:(jj + 1) * 128, :])
                for jj in range(2):
                    P.cp("dve", cdup[:, :, 0:64], cst[:, jj, :].rearrange("p (h d) -> p h d", h=4))
                    P.cp("pool", cdup[:, :, 64:128], cst[:, jj, :].rearrange("p (h d) -> p h d", h=4))
                    pb = bank()
                    for kvh in range(4):
                        P.tr(pb[:, kvh * 128:(kvh + 1) * 128], cdup[:, kvh, :], ident[:, :])
                    for kvh in range(4):
                        P.cp("dve", kd[:, kvh, T + jj * 128:T + (jj + 1) * 128], pb[:, kvh * 128:(kvh + 1) * 128])
                AR.release("cst")
                AR.release("cdup")
            stg = None
            if not ctx:
                stg = [AR.f32("stg%d" % j, [128, 512]) for j in range(2)]
            if ctx:
                hs1 = [(AR.bf("hT", [128, 8, 512]), AR.bf("sq", [128, 4, 512])),
                       (AR.bf("hT1", [128, 8, 512]), AR.bf("sq1", [128, 4, 512]))]
            else:
                hs1 = alloc_hs()

            def kidx(t):
                return (t // T) * S + (t % T)

            pipe1 = len(hs1) > 1
            rpend = [None]
            if pipe1:
                modulate(hs1[0][0], hs1[0][1], l, m, 0, 512)
            for t0 in range(0, NT, 512):
                n = 512
                hT, sq = hs1[(t0 // 512) % len(hs1)]
                if not pipe1:
                    modulate(hT, sq, l, m, t0, n)
                pieces = [(t0, n)] if T >= 512 else [(t0 + a, T) for a in range(0, n, T)]
                nparts = []
                if pipe1 and t0 + 512 < NT:
                    nh, nsq = hs1[((t0 // 512) + 1) % 2]
                    nparts = modulate_parts(nh, nsq, l, m, t0 + 512, 512)
                for oc in range(8):
                    pb = bank()
                    for k in range(8):
                        P.mm(pb[:, :n], wqs[oc // 4][:, k, (oc % 4) * 128:(oc % 4 + 1) * 128], hT[:, k, :n],
                             start=(k == 0), stop=(k == 7))
                    if ctx:
                        pb_fn = rope_a(pb, 128, n, SWA_SCALE, pm128, cs, t0, rb)
                        if rpend[0] is not None:
                            rpend[0]()

                        def _fin_q(pb_fn=pb_fn, oc=oc, t0=t0, n=n):
                            p2 = pb_fn()
                            P.cp("act", qT[:, oc, t0:t0 + n], p2[:, :n])
                        rpend[0] = _fin_q
                    else:
                        P.ts("dve", qT[:, oc, t0:t0 + n], pb[:, :n], SWA_SCALE, ALU.mult)
                    run_part(nparts)
                for kc in range(2):
                    pb = bank()
                    for k in range(8):
                        P.mm(pb[:, :n], wk[:, k, kc * 128:(kc + 1) * 128], hT[:, k, :n], start=(k == 0), stop=(k == 7))
                    def _copies(srcs, kc=kc):
                        for (src, so, sn, ko) in srcs:
                            for hh in range(2):
                                for dh in range(2):
                                    P.cp("act" if dh != hh else "dve",
                                         kd[dh * 64:(dh + 1) * 64, 2 * kc + hh, ko:ko + sn],
                                         src[hh * 64:(hh + 1) * 64, so:so + sn])
                    if ctx:
                        pb_fn = rope_a(pb, 128, n, 1.0, pm128, cs, t0, rb)
                        if rpend[0] is not None:
                            rpend[0]()

                        def _fin_k(pb_fn=pb_fn, t0=t0, n=n, _copies=_copies):
                            p2 = pb_fn()
                            _copies([(p2, 0, n, kidx(t0))])
                        rpend[0] = _fin_k
                    else:
                        _copies([(pb, ta - t0, tn, kidx(ta)) for (ta, tn) in pieces])
                run_part(nparts, 16)
                for j in range(n // 128):
                    tok = t0 + j * 128
                    pb = bank()
                    for k in range(8):
                        P.mm(pb[:, 0:nkv], hT[:, k, j * 128:(j + 1) * 128], wkv[:, k, :], start=(k == 0), stop=(k == 7))
                    P.cp("dve", va[:, kidx(tok) // 128, :, 0:64], pb[:, nkv - 256:nkv].rearrange("p (h d) -> p h d", h=4))
                    if not ctx:
                        b = tok // T
                        pos = tok % T
                        so = stg[j % 2]
                        P.cp("act", so[:, :], pb[:, :])
                        P.dma(sk_d[b, i, pos:pos + 128, :], so[:, 0:256], out_dma=True)
                        P.dma(sv_d[b, i, pos:pos + 128, :], so[:, 256:512], out_dma=True)
                    if j == 0 and rpend[0] is not None:
                        rpend[0]()
                        rpend[0] = None
            free_hs(hs1)
            for nm in ("wq0", "wq1", "wk", "wkv"):
                AR.release(nm)
            if ctx:
                AR.release("cs")
                free_rb()
            else:
                AR.release("stg0")
                AR.release("stg1")
            if stages == 1 and l == trunc_l[0]:
                raise _Trunc()
            ada_hooks = None
            if ADA_INTERLEAVE and (not ctx) and l + 1 < 4:
                ada_alloc(l + 1, 2)
                ada_issue([0, 1])

                def _h0():
                    ada_compute([0, 1])
                    ada_issue([2, 3])

                def _h1():
                    ada_compute([2, 3])
                    ada_issue([4, 5])

                def _hp():
                    ada_compute([4, 5])
                    ada_finish()
                ada_hooks = {0: _h0, 1: _h1, "post": _hp}
            vaB = AR.bf("vaB", [128, KT // 128, 4, 128])
            wg = AR.bf("wg", [128, 8, 1024])
            wout = AR.bf("wout", [128, 8, 1024])
            load_w(wg, wv_in, 1536, 1024)
            load_w(wout, woo_d[i].rearrange("(c p) n -> p c n", p=128), 0, 1024)
            pT = [AR.bf("pT%d" % b, [128, 512]) for b in range(4)]
            rc = AR.f32("rc", [128, 512])
            P.memset("pool", vaB[:, :, :, 0:64], 1.0)
            nch_all = KT // 128
            for ja in range(0, nch_all, 6):
                jb = min(nch_all, ja + 6)
                P.cp("pool", vaB[:, ja:jb, :, 64:128], va[:, ja:jb, :, 0:64])
            mbuf = qT
            pools["o4"] = [4, 5, 6, 7]
            rr["o4"] = 0
            pend = []

            def flush_pend(keep=0):
                while len(pend) > keep:
                    pend.pop(0)()

            def finalize_pair(c, pos, t_lo):
                hA, hB = 2 * c, 2 * c + 1
                P.ts("dve", rc[0:64, :], pos[0][64:128, :], esink[64:128, i * 16 + hA:i * 16 + hA + 1], ALU.add)
                P.ts("dve", rc[64:128, :], pos[1][0:64, :], esink[0:64, i * 16 + hB:i * 16 + hB + 1], ALU.add)
                if ctx:
                    P.recip(rc[:, :], rc[:, :])
                else:
                    P.act(rc[:, :], rc[:, :], AF.Ln)
                    P.act(rc[:, :], rc[:, :], AF.Exp, scale=-1.0)
                P.tt("dve", mbuf[0:64, c, t_lo:t_lo + 512], pos[0][0:64, :], rc[0:64, :], ALU.mult)
                P.tt("dve", mbuf[64:128, c, t_lo:t_lo + 512], pos[1][64:128, :], rc[64:128, :], ALU.mult)

            ucount = 0
            for c in range(8):
                kvh = c // 2
                ntile = (nseq // 2) if not ctx else (T // 512)
                for tix in range(ntile):
                    pos = []
                    if ctx:
                        qt = tix
                        q0 = qt * 512
                        pos = [bank("o4"), bank("o4")]
                        jobs = []
                        for jj in range(2):
                            jobs.append((T // 128 + jj, 0, 512, []))
                        for j in range(4 * qt - 1, 4 * qt + 5):
                            if j < 0 or j >= T // 128:
                                continue
                            nlo = max(4 * qt, j - 1)
                            nhi = min(4 * qt + 3, j + 1)
                            mk = []
                            for nb in range(nlo, nhi + 1):
                                if nb == j - 1:
                                    mk.append((nb, 0))
                                elif nb == j + 1:
                                    mk.append((nb, 1))
                            jobs.append((j, (nlo - 4 * qt) * 128, (nhi - 4 * qt + 1) * 128, mk))
                        sc_ps = {}

                        def issue_s(ji):
                            kc, lo, hi, mk_ = jobs[ji]
                            for half in (0, 1):
                                r0, r1 = half * 64, half * 64 + 64
                                ps_ = bank("s")
                                P.mm(ps_[:, lo:hi], kd[r0:r1, kvh, kc * 128:(kc + 1) * 128], qT[r0:r1, c, q0 + lo:q0 + hi],
                                     start=True, stop=(len(mk_) == 0))
                                sc_ps[(ji, half)] = ps_
                            for half in (0, 1):
                                for mi, (nb, which) in enumerate(mk_):
                                    cl = (nb - 4 * qt) * 128
                                    P.mm(sc_ps[(ji, half)][:, cl:cl + 128], identb[:, :], masks[:, which, :],
                                         start=False, stop=(mi == len(mk_) - 1))

                        for ji in range(min(2, len(jobs))):
                            issue_s(ji)
                        for ji, (kc, lo, hi, mk) in enumerate(jobs):
                            pts = []
                            for half in (0, 1):
                                pt = pT[(2 * ji + half) % 4]
                                P.act(pt[:, lo:hi], sc_ps.pop((ji, half))[:, lo:hi], AF.Exp)
                                pts.append(pt)
                            if ji + 2 < len(jobs):
                                issue_s(ji + 2)
                            for half in (0, 1):
                                vsrc = va if half == 0 else vaB
                                P.mm(pos[half][:, lo:hi], vsrc[:, kc, kvh, :], pts[half][:, lo:hi],
                                     start=(ji == 0), stop=(ji == len(jobs) - 1))
                            if ji == 2:
                                flush_pend()
                        pend.append(lambda c=c, pos=pos, q0=q0: finalize_pair(c, pos, q0))
                        continue
                    for half in (0, 1):
                        r0, r1 = half * 64, half * 64 + 64
                        vsrc = va if half == 0 else vaB
                        p = tix
                        pts = []
                        for a in range(2):
                            sq_ = 2 * p + a
                            ps_ = bank("s")
                            for j in range(2):
                                P.mm(ps_[:, j * 256:(j + 1) * 256], kd[r0:r1, kvh, sq_ * 256 + j * 128:sq_ * 256 + (j + 1) * 128],
                                     qT[r0:r1, c, sq_ * 256:(sq_ + 1) * 256])
                            pt = pT[(2 * ucount + a) % 4]
                            P.act(pt[:, :], ps_[:, :], AF.Exp)
                            pts.append(pt)
                        ucount += 1
                        flush_pend()
                        po = bank("o4")
                        pos.append(po)

                        def pv(p=p, pts=pts, po=po, vsrc=vsrc, kvh=kvh):
                            for a in range(2):
                                sq_ = 2 * p + a
                                for j in range(2):
                                    P.mm(po[:, a * 256:(a + 1) * 256], vsrc[:, 2 * sq_ + j, kvh, :], pts[a][:, j * 256:(j + 1) * 256],
                                         start=(j == 0), stop=(j == 1))
                        pend.append(pv)
                        if half == 1:
                            pend.append(lambda c=c, pos=pos, p=p: finalize_pair(c, pos, p * 512))
            flush_pend()
            AR.release("vaB")
            for nm in ["pT%d" % b for b in range(4)] + ["rc", "kd", "va"]:
                AR.release(nm)
            if stages == 2 and l == trunc_l[0]:
                raise _Trunc()
            stage3(l, m, NT, mbuf, wg, wout, ada_hooks, nxt)
            for nm in ("wg", "wout", "mbuf"):
                AR.release(nm)

        def load_x(src, NT):
            stage = AR.f32("xstage", [128, 4, 1024])
            for t0 in range(0, NT, 512):
                for j in range(4):
                    P.dma(stage[:, j, :], src[t0 + j * 128:t0 + (j + 1) * 128, :])
                for c in range(8):
                    pb = bank()
                    for j in range(4):
                        P.tr(pb[:, j * 128:(j + 1) * 128], stage[:, j, c * 128:(c + 1) * 128], ident[:, :])
                    P.cp(evac_eng(), xT[:, c, t0:t0 + 512], pb[:, :])
            AR.release("xstage")

        def store_x(dst, NT):
            stage = AR.f32("xstage", [128, 4, 1024])
            for t0 in range(0, NT, 512):
                for j in range(4):
                    for hb in range(2):
                        pb = bank()
                        for cc in range(4):
                            c = hb * 4 + cc
                            P.tr(pb[:, cc * 128:(cc + 1) * 128], xT[:, c, t0 + j * 128:t0 + (j + 1) * 128], ident[:, :])
                        P.cp(evac_eng(), stage[:, j, hb * 512:(hb + 1) * 512], pb[:, :])
                    P.dma(dst[t0 + j * 128:t0 + (j + 1) * 128, :], stage[:, j, :], out_dma=True)
            AR.release("xstage")

        trunc_l = [-1]
        try:
          for (src, dst, m, NT, nseq, T, ctx) in ((xp_d, yp_d, 0, 1024, 4, 256, False),
                                                   (xs_d, ys_d, 1, 2048, 1, 2048, True)):
              nl = nl_p if not ctx else nl_s
              if nl < 0:
                  continue
              load_x(src, NT)
              if not ada0_done[0]:
                  ada_all(0)
                  ada0_done[0] = True
              trunc_l[0] = nl - 1 if ((ctx and nl_s >= 0) or (not ctx and nl_s < 0)) else -1
              for l in range(nl):
                  if l + 1 < nl:
                      nxt = (l + 1, ctx, not ctx)
                  elif (not ctx) and nl_s > 0:
                      nxt = (0, True, True)
                  else:
                      nxt = None
                  store_state["dst"] = dst if (l == nl - 1 and stages == 3) else None
                  store_state["done"] = False
                  if l % 2 == 0:
                      even_layer(l, m, NT, nseq, T, ctx, nxt)
                  else:
                      odd_layer(l, m, NT, nseq, T, ctx, nxt)
              if not store_state["done"]:
                  store_x(dst, NT)
              store_state["dst"] = None

        except _Trunc:
            P.dma(yp_d[0:128, 0:512], rstd[:, :], out_dma=True)

        P.finish()
        build_program.stats = {e: len(P.ops[e]) for e in ENGS}
        build_program.peak = AR.peak
    return nc


_CONSTS = None


def _consts():
    global _CONSTS
    if _CONSTS is None:
        mla, swa = _rope_tables()
        pm96, pm128 = _perm_mats()
        _CONSTS = dict(ident=np.eye(128, dtype=np.float32), pm96=pm96, pm128=pm128, masks=_masks(),
                       amats=_pool_mats().reshape(20, 128, 128), mla_cs=mla, swa_cs=swa)
    return _CONSTS


def kernel(x_prompt, x_sample, cache_ckv, cache_krope, cache_k, cache_v, c, c_ctx,
           ada_w, ada_b, norm_pre, norm_post,
           mla_w_in, mla_g_qn, mla_g_kvn, mla_w_uq, mla_w_ukv, pool_w, pool_scale, mixa_w_out,
           swa_w_in, swa_sink, swa_w_out):
    f = lambda a: np.ascontiguousarray(np.asarray(a, dtype=np.float32))
    x_prompt, x_sample = f(x_prompt), f(x_sample)
    cache_ckv, cache_krope, cache_k, cache_v = f(cache_ckv), f(cache_krope), f(cache_k), f(cache_v)
    c, c_ctx = f(c), f(c_ctx)
    vecs = np.zeros((14, 1024), np.float32)
    vecs[0:4] = f(norm_pre)
    vecs[4:8] = f(norm_post)
    vecs[8:10, :384] = f(mla_g_qn)
    vecs[10:12, :256] = f(mla_g_kvn)
    vecs[12:14, :512] = f(pool_scale)
    shared = dict(ada_w=f(ada_w), ada_b=f(ada_b), vecs=vecs, gkvn=f(mla_g_kvn), sink=f(swa_sink).reshape(32),
                  w_in_e=f(mla_w_in), w_uq=f(mla_w_uq), w_ukv=f(mla_w_ukv), w_pool=f(pool_w), w_out_e=f(mixa_w_out),
                  w_in_o=f(swa_w_in), w_out_o=f(swa_w_out))
    shared.update(_consts())
    in_maps = []
    for i in range(8):
        d = dict(shared)
        d["xp"] = x_prompt[4 * i:4 * i + 4].reshape(1024, 1024)
        d["xs"] = x_sample[i]
        d["cckv"] = cache_ckv[i]
        d["ckr"] = cache_krope[i]
        d["ck"] = cache_k[i].reshape(2, 256, 256)
        d["cv"] = cache_v[i].reshape(2, 256, 256)
        d["cc"] = np.stack([c_ctx, c[i]], axis=0)
        in_maps.append(d)
    nc = build_program()
    res = run_bass_kernel_spmd(nc, in_maps, core_ids=list(range(8)))
    R = res.results
    y_prompt = np.concatenate([r["yp"].reshape(4, 256, 1024) for r in R], axis=0)
    y_sample = np.stack([r["ys"] for r in R], axis=0)
    st_ckv = np.concatenate([r["st_ckv"] for r in R], axis=0)
    st_kr = np.concatenate([r["st_kr"] for r in R], axis=0)
    st_k = np.concatenate([r["st_k"].reshape(4, 2, 256, 4, 64) for r in R], axis=0)
    st_v = np.concatenate([r["st_v"].reshape(4, 2, 256, 4, 64) for r in R], axis=0)
    return (y_prompt.astype(np.float32), y_sample.astype(np.float32), st_ckv.astype(np.float32),
            st_kr.astype(np.float32), st_k.astype(np.float32), st_v.astype(np.float32))
```

```python
import numpy as np
from contextlib import ExitStack
import concourse.bass as bass
import concourse.mybir as mybir
from concourse.bass_utils import run_bass_kernel_spmd

F32 = mybir.dt.float32
BF16 = mybir.dt.bfloat16
AF = mybir.ActivationFunctionType
ALU = mybir.AluOpType

ENGS = ("pe", "act", "dve", "pool", "sp")
SAME_ENGINE_RAW = True
EMBED_LAST_WAIT = True
EMBED_ENGINES = ("pe", "act", "dve")
N_DMA_SEMS = 24
BUCKET = 4096

D = 1024
EPS = 1e-6
MLA_SCALE = 96 ** -0.5
SWA_SCALE = 64 ** -0.5


class Prog:
    def __init__(self, nc, stack):
        self.nc = nc
        self.stack = stack
        self.ops = {e: [] for e in ENGS}
        self.recs = {}
        self.known = {e: {} for e in ENGS}
        self.eng_sem = {e: stack.enter_context(nc.semaphore("s_" + e)) for e in ENGS}
        self.dma_sems = [stack.enter_context(nc.semaphore("s_dma%d" % i)) for i in range(N_DMA_SEMS)]
        self.dma_cnt = [0] * N_DMA_SEMS
        self.dma_rr = 0
        self.dma_rr_pool = 0
        self.out_dma_tokens = []

    def sb(self, name, shape, dtype):
        return self.stack.enter_context(self.nc.sbuf_tensor("sb_" + name, list(shape), dtype))

    def ps(self, name, shape, dtype=F32):
        return self.stack.enter_context(self.nc.psum_tensor("pp_" + name, list(shape), dtype))

    @staticmethod
    def _box(ap):
        shp = list(ap.tensor.shape)
        row = 1
        for s in shp[1:]:
            row *= s
        isz = mybir.dt.size(ap.dtype)
        off = int(ap.offset)
        dims = ap.ap
        p0 = off // row
        f0 = off % row
        pstep, pcnt = dims[0]
        if pstep == row or (pcnt == 1 and len(dims) > 1):
            p1 = p0 + pcnt
            rest = dims[1:]
        elif pstep == 0:
            p1 = p0 + 1
            rest = dims[1:]
        else:
            p1 = p0 + 1
            rest = dims
        ext = 0
        for st, cn in rest:
            ext += abs(st) * (cn - 1)
        return (p0, p1, f0 * isz, (f0 + ext + 1) * isz)

    @staticmethod
    def _tracked(ap):
        return str(ap.space).upper() in ("SB", "PSUM")

    def op(self, eng, fn, reads=(), writes=(), dma=False, out_dma=False):
        idx = len(self.ops[eng])
        waits = []
        kn = self.known[eng]

        def need(tok, raw=False):
            if tok[0] == 'e':
                if tok[1] == eng and not (raw and SAME_ENGINE_RAW and eng != "pe"):
                    return
                key = tok[1]
            else:
                key = ('d', tok[1])
            if kn.get(key, -1) >= tok[2]:
                return
            kn[key] = tok[2]
            waits.append(tok)

        if dma:
            if eng == "pool":
                k = 8 + self.dma_rr_pool
                self.dma_rr_pool = (self.dma_rr_pool + 1) % (N_DMA_SEMS - 8)
            else:
                k = self.dma_rr
                self.dma_rr = (self.dma_rr + 1) % 8
            if self.dma_cnt[k] > 0:
                need(('d', k, self.dma_cnt[k] * 16))
            self.dma_cnt[k] += 1
            mytok = ('d', k, self.dma_cnt[k] * 16)
        else:
            mytok = ('e', eng, idx)

        rb = [(ap.name, self._box(ap)) for ap in reads if self._tracked(ap) and str(ap.space).upper() != "PSUM"]
        wb = [(ap.name, self._box(ap)) for ap in writes if self._tracked(ap) and str(ap.space).upper() != "PSUM"]
        for ap in list(reads) + list(writes):
            if str(ap.space).upper() == "PSUM":
                ent = (ap.name, (0, 128, 0, 1 << 20))
                if ent not in wb:
                    wb.append(ent)
        for name, box in rb:
            tr = self.recs.get(name)
            if tr is None:
                continue
            for b in range(box[2] // BUCKET, (box[3] - 1) // BUCKET + 1):
                for r in tr.get(b, ()):
                    if r[3] and r[2]:
                        rbx = r[0]
                        if rbx[0] < box[1] and box[0] < rbx[1] and rbx[2] < box[3] and box[2] < rbx[3]:
                            need(r[1], True)
        for name, box in wb:
            tr = self.recs.get(name)
            if tr is None:
                continue
            for b in range(box[2] // BUCKET, (box[3] - 1) // BUCKET + 1):
                lst = tr.get(b)
                if not lst:
                    continue
                for r in lst:
                    if r[3]:
                        rbx = r[0]
                        if rbx[0] < box[1] and box[0] < rbx[1] and rbx[2] < box[3] and box[2] < rbx[3]:
                            need(r[1])
                            if box[0] <= rbx[0] and box[1] >= rbx[1] and box[2] <= rbx[2] and box[3] >= rbx[3]:
                                r[3] = False
                tr[b] = [r for r in lst if r[3]]
        for name, box in rb:
            tr = self.recs.setdefault(name, {})
            rec = [box, mytok, False, True]
            for b in range(box[2] // BUCKET, (box[3] - 1) // BUCKET + 1):
                lst = tr.setdefault(b, [])
                if not dma:
                    for r in lst:
                        if r[3] and (not r[2]) and r[1][0] == 'e' and r[1][1] == eng and r[0] == box:
                            r[3] = False
                lst.append(rec)
        for name, box in wb:
            tr = self.recs.setdefault(name, {})
            rec = [box, mytok, True, True]
            for b in range(box[2] // BUCKET, (box[3] - 1) // BUCKET + 1):
                tr.setdefault(b, []).append(rec)
        o = dict(fn=fn, waits=waits, tok=mytok, marked=False)
        self.ops[eng].append(o)
        if out_dma:
            self.out_dma_tokens.append(mytok)
        return o

    def dma(self, out, in_, eng="sp", out_dma=False):
        return self.op(eng, lambda e: e.dma_start(out=out, in_=in_), reads=[in_], writes=[out],
                       dma=True, out_dma=out_dma)

    def mm(self, out, lhsT, rhs, start=True, stop=True):
        return self.op("pe", lambda e: e.matmul(out, lhsT, rhs, start=start, stop=stop),
                       reads=[lhsT, rhs] + ([] if start else [out]), writes=[out])

    def tr(self, out, in_, ident):
        return self.op("pe", lambda e: e.transpose(out, in_, ident), reads=[in_, ident], writes=[out])

    def act(self, out, in_, func, bias=None, scale=None, accum=None):
        kw = {}
        rd = [in_]
        wr = [out]
        if bias is not None:
            kw["bias"] = bias
            if not isinstance(bias, (int, float)):
                rd.append(bias)
        if scale is not None:
            kw["scale"] = scale
            if not isinstance(scale, (int, float)):
                rd.append(scale)
        if accum is not None:
            kw["accum_out"] = accum
            wr.append(accum)
        return self.op("act", lambda e: e.activation(out=out, in_=in_, func=func, **kw), reads=rd, writes=wr)

    def cp(self, eng, out, in_):
        if eng == "act":
            return self.op("act", lambda e: e.copy(out=out, in_=in_), reads=[in_], writes=[out])
        return self.op(eng, lambda e: e.tensor_copy(out=out, in_=in_), reads=[in_], writes=[out])

    def tt(self, eng, out, in0, in1, op):
        return self.op(eng, lambda e: e.tensor_tensor(out=out, in0=in0, in1=in1, op=op), reads=[in0, in1], writes=[out])

    def ts(self, eng, out, in0, s1, op0, s2=None, op1=None):
        rd = [in0]
        if not isinstance(s1, (int, float)):
            rd.append(s1)
        if s2 is not None and not isinstance(s2, (int, float)):
            rd.append(s2)
        if op1 is None:
            return self.op(eng, lambda e: e.tensor_scalar(out=out, in0=in0, scalar1=s1, scalar2=None, op0=op0),
                           reads=rd, writes=[out])
        return self.op(eng, lambda e: e.tensor_scalar(out=out, in0=in0, scalar1=s1, scalar2=s2, op0=op0, op1=op1),
                       reads=rd, writes=[out])

    def stt(self, eng, out, in0, scalar, in1, op0, op1):
        rd = [in0, in1]
        if not isinstance(scalar, (int, float)):
            rd.append(scalar)
        return self.op(eng, lambda e: e.scalar_tensor_tensor(out=out, in0=in0, scalar=scalar, in1=in1, op0=op0, op1=op1),
                       reads=rd, writes=[out])

    def memset(self, eng, out, val):
        return self.op(eng, lambda e: e.memset(out, val), writes=[out])

    def recip(self, out, in_):
        return self.op("dve", lambda e: e.reciprocal(out=out, in_=in_), reads=[in_], writes=[out])

    def finish(self):
        nc = self.nc
        for e in ENGS:
            for o in self.ops[e]:
                for tok in o["waits"]:
                    if tok[0] == 'e':
                        self.ops[tok[1]][tok[2]]["marked"] = True
        cnt_at = {}
        for e in ENGS:
            c = 0
            arr = []
            for o in self.ops[e]:
                if o["marked"]:
                    c += 1
                arr.append(c)
            cnt_at[e] = arr
        fw = {}
        for tok in self.out_dma_tokens:
            fw[tok[1]] = max(fw.get(tok[1], 0), tok[2])

        with nc.Block() as block:
            def emit(ename, engobj):
                for o in self.ops[ename]:
                    ws = o["waits"]
                    emb = None
                    if EMBED_LAST_WAIT and ws and ename in EMBED_ENGINES and o["tok"][0] == 'e':
                        emb = ws[-1]
                        ws = ws[:-1]
                    for tok in ws:
                        if tok[0] == 'e':
                            engobj.wait_ge(self.eng_sem[tok[1]], cnt_at[tok[1]][tok[2]])
                        else:
                            engobj.wait_ge(self.dma_sems[tok[1]], tok[2])
                    ins = o["fn"](engobj)
                    if emb is not None:
                        if emb[0] == 'e':
                            ins._wait_ge(self.eng_sem[emb[1]], cnt_at[emb[1]][emb[2]])
                        else:
                            ins._wait_ge(self.dma_sems[emb[1]], emb[2])
                    if o["tok"][0] == 'd':
                        ins.then_inc(self.dma_sems[o["tok"][1]], 16)
                    elif o["marked"]:
                        ins.then_inc(self.eng_sem[ename], 1)
                if ename == "sp":
                    for k, v in fw.items():
                        engobj.wait_ge(self.dma_sems[k], v)

            @block.sync
            def _(sync):
                emit("sp", sync)

            @block.tensor
            def _(tensor):
                emit("pe", tensor)

            @block.scalar
            def _(scalar):
                emit("act", scalar)

            @block.vector
            def _(vector):
                emit("dve", vector)

            @block.gpsimd
            def _(gpsimd):
                emit("pool", gpsimd)


class _Trunc(Exception):
    pass


class Arena:
    def __init__(self, tensor, nelem):
        self.t = tensor
        self.n = nelem
        self.free = [(0, nelem)]
        self.live = {}
        self.peak = 0

    def alloc(self, name, nelem_bf16):
        nelem_bf16 = (nelem_bf16 + 15) // 16 * 16
        for i, (o, s) in enumerate(self.free):
            if s >= nelem_bf16:
                if s == nelem_bf16:
                    self.free.pop(i)
                else:
                    self.free[i] = (o + nelem_bf16, s - nelem_bf16)
                self.live[name] = (o, nelem_bf16)
                used = self.n - sum(s for _, s in self.free)
                self.peak = max(self.peak, used)
                return o
        raise RuntimeError("arena OOM allocating %s (%d); live=%s free=%s" % (name, nelem_bf16, self.live, self.free))

    def release(self, name):
        o, s = self.live.pop(name)
        self.free.append((o, s))
        self.free.sort()
        merged = []
        for o, s in self.free:
            if merged and merged[-1][0] + merged[-1][1] == o:
                merged[-1] = (merged[-1][0], merged[-1][1] + s)
            else:
                merged.append((o, s))
        self.free = merged

    def bf(self, name, shape):
        n = 1
        for s in shape[1:]:
            n *= s
        o = self.alloc(name, n)
        v = self.t[0:shape[0], o:o + n]
        if len(shape) == 3:
            v = v.rearrange("p (a b) -> p a b", a=shape[1])
        elif len(shape) == 4:
            v = v.rearrange("p (a b c) -> p a b c", a=shape[1], b=shape[2])
        return v

    def f32(self, name, shape):
        n = 1
        for s in shape[1:]:
            n *= s
        o = self.alloc(name, 2 * n)
        v = self.t[0:shape[0], o:o + 2 * n].bitcast(F32)
        if len(shape) == 3:
            v = v.rearrange("p (a b) -> p a b", a=shape[1])
        elif len(shape) == 4:
            v = v.rearrange("p (a b c) -> p a b c", a=shape[1], b=shape[2])
        return v


def _rope_tables():
    t = np.arange(2048)
    row = (t // 64).astype(np.float64)
    col = (t % 64).astype(np.float64)
    mla = np.zeros((2, 96, 2048), np.float32)
    mla[0, 0:64] = 1.0
    for r in range(32):
        pos = row if r < 16 else col
        i = r % 16
        f = i % 8
        inv = 10000.0 ** (-(2.0 * f) / 16.0)
        ang = pos * inv
        mla[0, 64 + r] = np.cos(ang)
        mla[1, 64 + r] = np.sin(ang) if i < 8 else -np.sin(ang)
    swa = np.zeros((2, 128, 2048), np.float32)
    for p in range(128):
        d = p % 64
        pos = row if d < 32 else col
        i = d % 32
        f = i % 16
        inv = 10000.0 ** (-(2.0 * f) / 32.0)
        ang = pos * inv
        swa[0, p] = np.cos(ang)
        swa[1, p] = np.sin(ang) if i < 16 else -np.sin(ang)
    return mla, swa


def _perm_mats():
    pm96 = np.zeros((96, 96), np.float32)
    for r in range(32):
        i = r % 16
        partner = r + 8 if i < 8 else r - 8
        pm96[64 + partner, 64 + r] = 1.0
    pm128 = np.zeros((128, 128), np.float32)
    for m in range(128):
        i = m % 32
        partner = m + 16 if i < 16 else m - 16
        pm128[partner, m] = 1.0
    return pm96, pm128


def _pool_mats():
    T = 384
    out = np.zeros((4, 5, 128, 128), np.float32)
    for g, w in enumerate((2, 4, 8, 16)):
        A = np.zeros((T, T), np.float64)
        for t in range(T):
            lo = min(max(t - w // 2, 0), T)
            hi = min(max(t + w // 2, 0), T)
            A[lo:hi, t] = 1.0 / (hi - lo)
            A[t, t] -= 1.0
        out[g, 0] = A[0:128, 0:128]
        out[g, 1] = A[128:256, 128:256]
        out[g, 2] = A[256:384, 256:384]
        out[g, 3] = A[0:128, 128:256]
        out[g, 4] = A[128:256, 0:128]
    return out


def _masks():
    k = np.arange(128)[:, None]
    q = np.arange(128)[None, :]
    m = np.zeros((2, 128, 128), np.float32)
    m[0] = np.where(k <= q, 0.0, -30000.0)
    m[1] = np.where(q <= k, 0.0, -30000.0)
    return m


def build_program(nl_p=4, nl_s=4, stages=3, dbg=None):
    nc = bass.Bass("TRN2", target_bir_lowering=False)

    def din(name, shape):
        return nc.dram_tensor(name, list(shape), F32, kind="ExternalInput").ap()

    def dout(name, shape):
        return nc.dram_tensor(name, list(shape), F32, kind="ExternalOutput").ap()

    xp_d = din("xp", [1024, 1024])
    xs_d = din("xs", [2048, 1024])
    cckv_d = din("cckv", [2, 256, 256])
    ckr_d = din("ckr", [2, 256, 32])
    ck_d = din("ck", [2, 256, 256])
    cv_d = din("cv", [2, 256, 256])
    cc_d = din("cc", [2, 1024])
    adaw_d = din("ada_w", [4, 1024, 3072])
    adab_d = din("ada_b", [4, 3072])
    vecs_d = din("vecs", [14, 1024])
    gkvn_d = din("gkvn", [2, 256])
    sink_d = din("sink", [32])
    wine_d = din("w_in_e", [2, 1024, 2208])
    wuq_d = din("w_uq", [2, 384, 768])
    wukv_d = din("w_ukv", [2, 256, 1024])
    wpool_d = din("w_pool", [2, 4, 128, 128])
    woe_d = din("w_out_e", [2, 1024, 1024])
    wino_d = din("w_in_o", [2, 1024, 2560])
    woo_d = din("w_out_o", [2, 1024, 1024])
    ident_d = din("ident", [128, 128])
    pm96_d = din("pm96", [96, 96])
    pm128_d = din("pm128", [128, 128])
    masks_d = din("masks", [2, 128, 128])
    amats_d = din("amats", [20, 128, 128])
    mlacs_d = din("mla_cs", [2, 96, 2048])
    swacs_d = din("swa_cs", [2, 128, 2048])

    yp_d = dout("yp", [1024, 1024])
    ys_d = dout("ys", [2048, 1024])
    sckv_d = dout("st_ckv", [4, 2, 256, 256])
    skr_d = dout("st_kr", [4, 2, 256, 32])
    sk_d = dout("st_k", [4, 2, 256, 256])
    sv_d = dout("st_v", [4, 2, 256, 256])

    with ExitStack() as st:
        P = Prog(nc, st)
        xT = P.sb("xT", [128, 8, 2048], F32)
        ident = P.sb("ident", [128, 128], F32)
        ones_bf = P.sb("ones_bf", [128, 128], BF16)
        identb = P.sb("identb", [128, 128], BF16)
        scb = P.sb("scb", [128, 8, 2], BF16)
        pm96 = P.sb("pm96", [96, 96], BF16)
        pm128 = P.sb("pm128", [128, 128], BF16)
        masks = P.sb("masks", [128, 2, 128], BF16)
        epsT = P.sb("epsT", [128, 1], F32)
        mod = P.sb("mod", [128, 4, 48], F32)
        vecT = P.sb("vecT", [128, 8, 32], F32)
        coefA = P.sb("coefA", [128, 4, 2, 8], F32)
        coefB = P.sb("coefB", [128, 4, 2, 8], F32)
        coefG = P.sb("coefG", [128, 4, 2, 8], F32)
        gkvb = P.sb("gkvb", [128, 2, 256], F32)
        esink = P.sb("esink", [128, 32], F32)
        rstd = P.sb("rstd", [128, 512], F32)
        rstd2 = P.sb("rstd2", [128, 512], F32)
        tmpf = [P.sb("tmpf%d" % i, [128, 512], F32) for i in range(2)]
        ARENA_N = 64 * 1024
        arena_t = P.sb("arena", [128, ARENA_N], BF16)
        AR = Arena(arena_t, ARENA_N)
        banks = [P.ps("ps%d" % i, [128, 512], F32) for i in range(8)]
        rr = {"all": 0, "s": 0, "o": 0, "g": 0}
        pools = {"all": list(range(8)), "s": [0, 1, 2, 3], "o": [4, 5], "g": [6, 7]}

        def bank(pool="all"):
            lst = pools[pool]
            b = banks[lst[rr[pool] % len(lst)]]
            rr[pool] += 1
            return b

        evac_rr = [0]

        def evac_eng():
            evac_rr[0] += 1
            return "dve" if evac_rr[0] % 2 else "act"

        P.dma(ident[:], ident_d)
        P.dma(identb[:], ident_d, eng="pool")
        P.dma(pm96[:], pm96_d, eng="pool")
        P.dma(pm128[:], pm128_d, eng="pool")
        P.dma(masks[:], masks_d.rearrange("a k q -> k a q"), eng="pool")
        P.dma(gkvb[:].rearrange("p a n -> p (a n)"), gkvn_d.rearrange("a n -> (a n)").partition_broadcast(128))
        P.dma(esink[:], sink_d.partition_broadcast(128))
        P.memset("dve", ones_bf[:], 1.0)
        P.memset("dve", epsT[:], EPS)
        P.act(esink[:], esink[:], AF.Exp)

        if dbg == "consts":
            P.dma(yp_d[0:128, 0:32], esink[:], out_dma=True)
            P.finish()
            return nc
        vst = AR.f32("vst", [32, 1024])
        P.memset("dve", vst[:], 0.0)
        P.dma(vst[0:14, :], vecs_d)
        pv = bank()
        for c in range(8):
            P.tr(pv[:, c * 32:(c + 1) * 32], vst[0:32, c * 128:(c + 1) * 128], ident[0:32, 0:32])
        for c in range(8):
            P.cp("dve", vecT[:, c, :], pv[:, c * 32:(c + 1) * 32])
        AR.release("vst")

        if dbg == "vec":
            P.dma(yp_d[0:128, 0:256], vecT[:].rearrange("p a b -> p (a b)"), out_dma=True)
            P.finish()
            return nc
        ccT = AR.f32("ccT", [128, 2, 8])
        for m in range(2):
            P.dma(ccT[:, m, :], cc_d[m].rearrange("(p c) -> p c", c=8))
        for m in range(2):
            P.act(scb[:, :, m], ccT[:, m, :], AF.Silu)
        AR.release("ccT")
        modv = mod[:].rearrange("p l (j c m) -> p l j c m", j=3, c=8)
        ada_state = {}

        def ada_alloc(l, nbuf):
            ada_state["l"] = l
            ada_state["adab"] = AR.bf("adab", [1, 3072])
            ada_state["bufs"] = [AR.bf("adw%d" % i, [128, 8, 512]) for i in range(nbuf)]
            ada_state["nbuf"] = nbuf
            P.dma(ada_state["adab"][:], adab_d[l:l + 1, :], eng="pool")

        def ada_issue(blks):
            l = ada_state["l"]
            wv = adaw_d[l].rearrange("(p c) n -> p c n", c=8)
            for blk in blks:
                wt = ada_state["bufs"][blk % ada_state["nbuf"]]
                P.dma(wt[:], wv[:, :, blk * 512:(blk + 1) * 512], eng="pool")

        def ada_compute(blks):
            l = ada_state["l"]
            adab = ada_state["adab"]
            for blk in blks:
                wt = ada_state["bufs"][blk % ada_state["nbuf"]]
                pm = bank()
                for nci in range(4):
                    nch = blk * 4 + nci
                    for c in range(8):
                        P.mm(pm[:, 2 * nci:2 * nci + 2], wt[:, c, nci * 128:(nci + 1) * 128], scb[:, c, :],
                             start=(c == 0), stop=False)
                    P.mm(pm[:, 2 * nci:2 * nci + 2], adab[0:1, nch * 128:(nch + 1) * 128], ones_bf[0:1, 0:2],
                         start=False, stop=True)
                P.cp("dve", mod[:, l, blk * 8:(blk + 1) * 8], pm[:, 0:8])

        def ada_finish():
            l = ada_state["l"]
            for m in range(2):
                P.stt("dve", coefA[:, l, m, :], modv[:, l, 1, :, m], 1.0, vecT[:, :, l], ALU.add, ALU.mult)
                P.cp("dve", coefB[:, l, m, :], modv[:, l, 0, :, m])
                P.tt("dve", coefG[:, l, m, :], modv[:, l, 2, :, m], vecT[:, :, 4 + l], ALU.mult)
            for i in range(ada_state["nbuf"]):
                AR.release("adw%d" % i)
            AR.release("adab")
            ada_state.clear()

        def ada_all(l):
            ada_alloc(l, 3)
            for blk in range(6):
                ada_issue([blk])
                ada_compute([blk])
            ada_finish()

        ADA_INTERLEAVE = nl_p >= 4
        ada0_done = [False]
        if not ADA_INTERLEAVE:
            for l in range(1, 4):
                ada_all(l)
        if dbg == "ada":
            P.dma(yp_d[0:128, 0:192], mod[:].rearrange("p a b -> p (a b)"), out_dma=True)
            P.dma(yp_d[128:256, 0:64], coefA[:].rearrange("p a b c -> p (a b c)"), out_dma=True)
            P.dma(yp_d[256:384, 0:64], coefG[:].rearrange("p a b c -> p (a b c)"), out_dma=True)
            P.finish()
            return nc
        def rstd_from_ssq(ps_ssq, n, dim, out):
            P.act(out, ps_ssq, AF.Ln, bias=epsT[:, 0:1], scale=1.0 / dim)
            P.act(out, out, AF.Exp, scale=-0.5)

        def modulate_parts(hT, sq, l, m, t0, n):
            parts = []

            ns_ = sq.shape[1]

            def stats_a():
                for c in range(8):
                    P.act(sq[:, c, :n], xT[:, c, t0:t0 + n], AF.Square)

            def stats_b():
                pb = bank()
                for c in range(8):
                    P.mm(pb[:, :n], ones_bf[:, :], sq[:, c, :n], start=(c == 0), stop=(c == 7))
                rstd_from_ssq(pb[:, :n], n, 1024, rstd[:, :n])

            def stats():
                pb = bank()
                for c0 in range(0, 8, ns_):
                    for c in range(c0, c0 + ns_):
                        P.act(sq[:, c % ns_, :n], xT[:, c, t0:t0 + n], AF.Square)
                    for c in range(c0, c0 + ns_):
                        P.mm(pb[:, :n], ones_bf[:, :], sq[:, c % ns_, :n], start=(c == 0), stop=(c == 7))
                rstd_from_ssq(pb[:, :n], n, 1024, rstd[:, :n])
            if ns_ >= 8:
                parts.extend([stats_a, (lambda: None), stats_b])
            else:
                parts.append(stats)
            for c in range(8):
                def ap(c=c):
                    tf = tmpf[c % 2]
                    P.stt("dve", tf[:, :n], xT[:, c, t0:t0 + n], coefA[:, l, m, c:c + 1], rstd[:, :n], ALU.mult, ALU.mult)
                    P.act(hT[:, c, :n], tf[:, :n], AF.Identity, bias=coefB[:, l, m, c:c + 1], scale=1.0)
                parts.append(ap)
            return parts

        def modulate(hT, sq, l, m, t0, n):
            for f in modulate_parts(hT, sq, l, m, t0, n):
                f()

        def run_part(parts, k=1):
            for _ in range(k):
                if parts:
                    parts.pop(0)()

        def load_w(dst, src_rows_view, col0, ncols):
            for c in range(dst.shape[1]):
                P.dma(dst[:, c, :], src_rows_view[:, c, col0:col0 + ncols], eng="pool")

        def alloc_hs():
            hs = [(AR.bf("hT", [128, 8, 512]), AR.bf("sq", [128, 8, 512]))]
            try:
                a = AR.bf("hT1", [128, 8, 512])
                try:
                    b = AR.bf("sq1", [128, 8, 512])
                    hs.append((a, b))
                except RuntimeError:
                    AR.release("hT1")
            except RuntimeError:
                pass
            return hs

        def free_hs(hs):
            AR.release("hT")
            AR.release("sq")
            if len(hs) > 1:
                AR.release("hT1")
                AR.release("sq1")

        store_state = {"dst": None, "done": False}

        def store_tile(dst, stage2, t0):
            for j in range(4):
                for hb in range(2):
                    pb = bank()
                    for cc in range(4):
                        c = hb * 4 + cc
                        P.tr(pb[:, cc * 128:(cc + 1) * 128], xT[:, c, t0 + j * 128:t0 + (j + 1) * 128], ident[:, :])
                    P.cp(evac_eng(), stage2[:, j % 2, hb * 512:(hb + 1) * 512], pb[:, :])
                P.dma(dst[t0 + j * 128:t0 + (j + 1) * 128, :], stage2[:, j % 2, :], out_dma=True)

        def stage3(l, m, NT, mbuf, wg, wout, hooks=None, nxt=None):
            oT = AR.f32("oT", [128, 8, 512])
            hs3 = alloc_hs()
            sg = [AR.bf("sg%d" % i, [128, 512]) for i in range(2)]
            if nxt is not None:
                prefetch_w1(*nxt)
            stage2 = None
            spend = [None]
            if store_state["dst"] is not None:
                try:
                    stage2 = AR.f32("xst2", [128, 2, 1024])
                except RuntimeError:
                    stage2 = None
            tiles = list(range(0, NT, 512))
            n = 512
            pipe = len(hs3) > 1
            if pipe:
                modulate(hs3[0][0], hs3[0][1], l, m, tiles[0], n)
            for ti, t0 in enumerate(tiles):
                if hooks and ti in hooks:
                    hooks[ti]()
                hT, sq = hs3[ti % len(hs3)]
                if not pipe:
                    modulate(hT, sq, l, m, t0, n)
                nparts = []
                if pipe and ti + 1 < len(tiles):
                    nh, nsq = hs3[(ti + 1) % 2]
                    nparts = modulate_parts(nh, nsq, l, m, tiles[ti + 1], n)
                for mc in range(8):
                    pb = bank()
                    for k in range(8):
                        P.mm(pb[:, :n], wg[:, k, mc * 128:(mc + 1) * 128], hT[:, k, :n], start=(k == 0), stop=(k == 7))
                    s_ = sg[mc % 2]
                    P.act(s_[:, :n], pb[:, :n], AF.Silu)
                    P.tt("dve" if mc % 2 else "pool", mbuf[:, mc, t0:t0 + n], mbuf[:, mc, t0:t0 + n], s_[:, :n], ALU.mult)
                    if mc == 1:
                        run_part(nparts)
                if spend[0] is not None:
                    spend[0]()
                    spend[0] = None
                for dc in range(8):
                    pb = bank()
                    for k in range(8):
                        P.mm(pb[:, :n], wout[:, k, dc * 128:(dc + 1) * 128], mbuf[:, k, t0:t0 + n],
                             start=(k == 0), stop=(k == 7))
                    P.cp("dve", oT[:, dc, :n], pb[:, :n])
                    P.act(sq[:, dc, :n], pb[:, :n], AF.Square)
                    run_part(nparts)
                run_part(nparts, 16)
                pb = bank()
                for dc in range(8):
                    P.mm(pb[:, :n], ones_bf[:, :], sq[:, dc, :n], start=(dc == 0), stop=(dc == 7))
                rstd_from_ssq(pb[:, :n], n, 1024, rstd2[:, :n])
                for dc in range(8):
                    tf = tmpf[dc % 2]
                    P.stt("dve", tf[:, :n], oT[:, dc, :n], coefG[:, l, m, dc:dc + 1], rstd2[:, :n], ALU.mult, ALU.mult)
                    P.tt("pool" if dc % 4 == 0 else "dve", xT[:, dc, t0:t0 + n], xT[:, dc, t0:t0 + n], tf[:, :n], ALU.add)
                if stage2 is not None:
                    spend[0] = (lambda t0=t0: store_tile(store_state["dst"], stage2, t0))
            if hooks and "post" in hooks:
                hooks["post"]()
            if stage2 is not None:
                if spend[0] is not None:
                    spend[0]()
                AR.release("xst2")
                store_state["done"] = True
            free_hs(hs3)
            for nm in ("oT", "sg0", "sg1"):
                AR.release(nm)

        rope_rr = [0]

        def rope_a(src_ps, pr, n, scale, pm, cs, t0, rb):
            qc, qsn = rb[rope_rr[0] % len(rb)]
            rope_rr[0] += 1
            P.stt("dve", qc[0:pr, :n], src_ps[0:pr, :n], float(scale), cs[0:pr, 0, t0:t0 + n], ALU.mult, ALU.mult)
            P.stt("dve", qsn[0:pr, :n], src_ps[0:pr, :n], float(scale), cs[0:pr, 1, t0:t0 + n], ALU.mult, ALU.mult)

            def phase_b():
                p2 = bank("g")
                P.mm(p2[0:pr, :n], identb[0:pr, 0:pr], qc[0:pr, :n], start=True, stop=False)
                P.mm(p2[0:pr, :n], pm[:, :], qsn[0:pr, :n], start=False, stop=True)
                return p2
            return phase_b

        def rope_apply(src_ps, pr, n, scale, pm, cs, t0, rb):
            return rope_a(src_ps, pr, n, scale, pm, cs, t0, rb)()

        def alloc_rb():
            return [(AR.bf("rqc%d" % i, [128, 512]), AR.bf("rqs%d" % i, [128, 512])) for i in range(2)]

        def free_rb():
            for i in range(2):
                AR.release("rqc%d" % i)
                AR.release("rqs%d" % i)

        pref = {}

        def prefetch_w1(lnext, ctx_next, full):
            i2 = lnext // 2
            if (not full) and lnext % 2 == 1:
                return
            try:
                if lnext % 2 == 0:
                    wv = wine_d[i2].rearrange("(c p) n -> p c n", p=128)
                    a = AR.bf("w1a", [128, 8, 672])
                    load_w(a, wv, 0, 672)
                    pref["w1a"] = a
                    if full:
                        b_ = AR.bf("w1b", [128, 8, 512])
                        load_w(b_, wv, 1184, 512)
                        pref["w1b"] = b_
                else:
                    wv = wino_d[i2].rearrange("(c p) n -> p c n", p=128)
                    a = AR.bf("wq0", [128, 8, 512])
                    load_w(a, wv, 0, 512)
                    pref["wq0"] = a
                    if full:
                        b_ = AR.bf("wq1", [128, 8, 512])
                        load_w(b_, wv, 512, 512)
                        pref["wq1"] = b_
                        k_ = AR.bf("wk", [128, 8, 256])
                        load_w(k_, wv, 1024, 256)
                        pref["wk"] = k_
            except RuntimeError:
                pass

        def even_layer(l, m, NT, nseq, T, ctx, nxt=None):
            i = l // 2
            S = T + (256 if ctx else 0)
            KT = nseq * S
            nkc = S // 128
            wv_in = wine_d[i].rearrange("(c p) n -> p c n", p=128)
            w1a = pref.pop("w1a", None)
            if w1a is None:
                w1a = AR.bf("w1a", [128, 8, 672])
                load_w(w1a, wv_in, 0, 672)
            w1b = pref.pop("w1b", None)
            if w1b is None:
                w1b = AR.bf("w1b", [128, 8, 512])
                load_w(w1b, wv_in, 1184, 512)
            wp = AR.bf("wp", [128, 4, 128])
            amats = AR.bf("amats", [128, 20, 128])
            P.dma(amats[:], amats_d.rearrange("a k q -> k a q"), eng="pool")
            P.dma(wp[:], wpool_d[i].rearrange("g c e -> c g e"), eng="pool")
            mbuf = AR.bf("mbuf", [128, 8, NT])
            mo = AR.live["mbuf"][0]
            vtok = arena_t[:, mo:mo + (NT // 128) * 512].rearrange("p (j q) -> p j q", q=512)
            cqn = AR.bf("cqn", [128, 3, NT])
            ckvn = AR.bf("ckvn", [128, 2, KT])
            krT = AR.bf("krT", [96, KT])
            cs = None
            if ctx:
                cs = AR.bf("cs", [96, 2, 2048])
                P.dma(cs[:, 0, :], mlacs_d[0], eng="pool")
                P.dma(cs[:, 1, :], mlacs_d[1], eng="pool")
                rb = alloc_rb()
                cst = AR.f32("cst", [128, 2, 256 + 96])
                P.memset("pool", cst[:, :, 256:320], 0.0)
                for jj in range(2):
                    P.dma(cst[:, jj, 0:256], cckv_d[i, jj * 128:(jj + 1) * 128, :])
                    P.dma(cst[:, jj, 320:352], ckr_d[i, jj * 128:(jj + 1) * 128, :])
                for jj in range(2):
                    pb = bank()
                    for c in range(2):
                        P.tr(pb[:, c * 128:(c + 1) * 128], cst[:, jj, c * 128:(c + 1) * 128], ident[:, :])
                    P.tr(pb[0:96, 256:384], cst[:, jj, 256:352], ident[:, :])
                    for c in range(2):
                        P.cp("dve", ckvn[:, c, T + jj * 128:T + (jj + 1) * 128], pb[:, c * 128:(c + 1) * 128])
                    P.cp("dve", krT[64:96, T + jj * 128:T + (jj + 1) * 128], pb[64:96, 256:384])
                AR.release("cst")
            stg = None
            if not ctx:
                stg = [AR.f32("stg%d" % j, [128, 288]) for j in range(2)]
            junk = AR.f32("junk", [128, 256])
            ssq1 = AR.f32("ssq1", [128, 2])
            hs1 = alloc_hs()

            def kidx(t):
                return (t // T) * S + (t % T)

            pipe1 = len(hs1) > 1
            if pipe1:
                modulate(hs1[0][0], hs1[0][1], l, m, 0, 512)
            for t0 in range(0, NT, 512):
                n = 512
                hT, sq = hs1[(t0 // 512) % len(hs1)]
                if not pipe1:
                    modulate(hT, sq, l, m, t0, n)
                nparts = []
                if pipe1 and t0 + 512 < NT:
                    nh, nsq = hs1[((t0 // 512) + 1) % 2]
                    nparts = modulate_parts(nh, nsq, l, m, t0 + 512, 512)
                for oc in range(3):
                    pb = bank()
                    for k in range(8):
                        P.mm(pb[:, :n], w1a[:, k, oc * 128:(oc + 1) * 128], hT[:, k, :n], start=(k == 0), stop=(k == 7))
                    P.act(sq[:, oc, :n], pb[:, :n], AF.Square)
                    P.ts("dve", cqn[:, oc, t0:t0 + n], pb[:, :n], vecT[:, oc, 8 + i:9 + i], ALU.mult)
                    run_part(nparts)
                pieces = [(t0, n)] if T >= 512 else [(t0 + a, T) for a in range(0, n, T)]
                for oc in range(2):
                    pb = bank()
                    for k in range(8):
                        P.mm(pb[:, :n], w1a[:, k, 384 + oc * 128:384 + (oc + 1) * 128], hT[:, k, :n],
                             start=(k == 0), stop=(k == 7))
                    P.act(sq[:, 4 + oc, :n], pb[:, :n], AF.Square)
                    for (ta, tn) in pieces:
                        P.ts("dve", ckvn[:, oc, kidx(ta):kidx(ta) + tn], pb[:, ta - t0:ta - t0 + tn],
                             vecT[:, oc, 10 + i:11 + i], ALU.mult)
                    run_part(nparts)
                pb = bank()
                for oc in range(3):
                    P.mm(pb[:, :n], ones_bf[:, :], sq[:, oc, :n], start=(oc == 0), stop=(oc == 2))
                rstd_from_ssq(pb[:, :n], n, 384, rstd2[:, :n])
                for oc in range(3):
                    P.tt("pool" if oc == 1 else "dve", cqn[:, oc, t0:t0 + n], cqn[:, oc, t0:t0 + n], rstd2[:, :n], ALU.mult)
                pb = bank()
                for k in range(8):
                    P.mm(pb[0:96, :n], w1a[:, k, 576:672], hT[:, k, :n], start=(k == 0), stop=(k == 7))
                if ctx:
                    p2 = rope_apply(pb, 96, n, 1.0, pm96, cs, t0, rb)
                    P.cp("act", krT[64:96, kidx(t0):kidx(t0) + n], p2[64:96, :n])
                else:
                    for (ta, tn) in pieces:
                        P.cp("dve", krT[64:96, kidx(ta):kidx(ta) + tn], pb[64:96, ta - t0:ta - t0 + tn])
                pb = bank()
                for oc in range(2):
                    P.mm(pb[:, :n], ones_bf[:, :], sq[:, 4 + oc, :n], start=(oc == 0), stop=(oc == 1))
                rstd_from_ssq(pb[:, :n], n, 256, rstd2[:, :n])
                for oc in range(2):
                    for (ta, tn) in pieces:
                        P.tt("pool" if oc == 1 else "dve", ckvn[:, oc, kidx(ta):kidx(ta) + tn],
                             ckvn[:, oc, kidx(ta):kidx(ta) + tn], rstd2[:, ta - t0:ta - t0 + tn], ALU.mult)
                for j in range(n // 128):
                    pb = bank()
                    for k in range(8):
                        P.mm(pb[:, :], hT[:, k, j * 128:(j + 1) * 128], w1b[:, k, :], start=(k == 0), stop=(k == 7))
                    P.cp(evac_eng(), vtok[:, (t0 // 128) + j, :], pb[:, :])
                    run_part(nparts)
                run_part(nparts, 16)
                if not ctx:
                    for j in range(n // 128):
                        tok = t0 + j * 128
                        b = tok // T
                        pos = tok % T
                        pb = bank()
                        for k in range(8):
                            P.mm(pb[:, 0:288], hT[:, k, j * 128:(j + 1) * 128], w1a[:, k, 384:672],
                                 start=(k == 0), stop=(k == 7))
                        so = stg[j % 2]
                        P.act(junk[:, :], pb[:, 0:256], AF.Square)
                        P.op("dve", lambda e: e.reduce_sum(out=ssq1[:, 0:1], in_=junk[:, :], axis=mybir.AxisListType.X),
                             reads=[junk[:, :]], writes=[ssq1[:, 0:1]])
                        P.act(ssq1[:, 1:2], ssq1[:, 0:1], AF.Ln, bias=epsT[:, 0:1], scale=1.0 / 256)
                        P.act(ssq1[:, 1:2], ssq1[:, 1:2], AF.Exp, scale=-0.5)
                        P.stt("dve", so[:, 0:256], pb[:, 0:256], ssq1[:, 1:2], gkvb[:, i, :], ALU.mult, ALU.mult)
                        P.cp("dve", so[:, 256:288], pb[:, 256:288])
                        P.dma(sckv_d[b, i, pos:pos + 128, :], so[:, 0:256], out_dma=True)
                        P.dma(skr_d[b, i, pos:pos + 128, :], so[:, 256:288], out_dma=True)
            free_hs(hs1)
            pooled = [AR.bf("pooled%d" % j, [128, 512]) for j in range(2)]
            ppend = [None]
            ncs = T // 128
            for t0 in range(0, NT, 512):
                for g in range(4):
                    pb = bank()
                    for j in range(4):
                        ch = t0 // 128 + j
                        cin = ch % ncs
                        contrib = []
                        if cin > 0:
                            contrib.append((ch - 1, 3))
                        contrib.append((ch, 0 if cin == 0 else (2 if cin == ncs - 1 else 1)))
                        if cin < ncs - 1:
                            contrib.append((ch + 1, 4))
                        for ci, (src, kind) in enumerate(contrib):
                            P.mm(pb[:, j * 128:(j + 1) * 128], vtok[:, src, g * 128:(g + 1) * 128],
                                 amats[:, g * 5 + kind, :], start=(ci == 0), stop=(ci == len(contrib) - 1))
                    pl = pooled[g % 2]
                    P.cp(evac_eng(), pl[:, :], pb[:, :])
                    if ppend[0] is not None:
                        ppend[0]()

                    def _pw(pl=pl, g=g, t0=t0):
                        pb2 = bank()
                        P.mm(pb2[:, :], wp[:, g, :], pl[:, :])
                        P.ts("dve", mbuf[:, 4 + g, t0:t0 + 512], pb2[:, :], vecT[:, g, 12 + i:13 + i], ALU.mult)
                    ppend[0] = _pw
            if ppend[0] is not None:
                ppend[0]()
                ppend[0] = None
            for nm in ("pooled0", "pooled1", "junk", "ssq1", "w1a", "w1b", "wp", "amats"):
                AR.release(nm)
            if not ctx:
                AR.release("stg0")
                AR.release("stg1")
            if stages == 1 and l == trunc_l[0]:
                raise _Trunc()
            ada_hooks = None
            if ADA_INTERLEAVE and (not ctx) and l + 1 < 4:
                ada_alloc(l + 1, 2)
                ada_issue([0, 1])

                def _h0():
                    ada_compute([0, 1])
                    ada_issue([2, 3])

                def _h1():
                    ada_compute([2, 3])
                    ada_issue([4, 5])

                def _hp():
                    ada_compute([4, 5])
                    ada_finish()
                ada_hooks = {0: _h0, 1: _h1, "post": _hp}
            wuq = AR.bf("wuq", [128, 3, 768])
            wuk = AR.bf("wuk", [128, 2, 8, 64])
            wuv = AR.bf("wuv", [128, 2, 8, 64])
            P.dma(wuq[:], wuq_d[i].rearrange("(c p) n -> p c n", p=128), eng="pool")
            wukv_v = wukv_d[i].rearrange("(c p) (h t d) -> p c h t d", p=128, h=8, t=2)
            for c in range(2):
                P.dma(wuk[:, c, :, :], wukv_v[:, c, :, 0, :], eng="pool")
                P.dma(wuv[:, c, :, :], wukv_v[:, c, :, 1, :], eng="pool")
            def load_wg():
                wg_ = AR.bf("wg", [128, 8, 1024])
                load_w(wg_[:, :, 0:512], wv_in, 672, 512)
                load_w(wg_[:, :, 512:1024], wv_in, 1696, 512)
                return wg_

            def load_wout():
                wout_ = AR.bf("wout", [128, 8, 1024])
                load_w(wout_, woe_d[i].rearrange("(c p) n -> p c n", p=128), 0, 1024)
                return wout_
            wg = load_wg()
            if not ctx:
                wout = load_wout()
            NB = 2
            TT = nseq * T
            nkt = KT // 128
            qh = [AR.bf("qh%d" % b, [96, TT]) for b in range(NB)]
            kh = [AR.bf("kh%d" % b, [96, KT]) for b in range(NB)]
            vh = [AR.bf("vh%d" % b, [128, nkt, 128]) for b in range(NB)]
            pT = [AR.bf("pT%d" % b, [128, 512]) for b in range(4)]
            rc = AR.f32("rc", [64, 512])
            for b in range(NB):
                P.memset("pool", vh[b][:, :, 64:128], 1.0)
                P.cp("pool", kh[b][64:96, :], krT[64:96, 0:KT])
            pend = [None]

            def flush_pend():
                if pend[0] is not None:
                    pend[0]()
                    pend[0] = None

            def build_steps(h, b):
                st_ = []
                for ka in range(0, KT, 512):
                    def f(ka=ka):
                        kn = min(512, KT - ka)
                        pb = bank("g")
                        for c in range(2):
                            P.mm(pb[0:64, :kn], wuk[:, c, h, :], ckvn[:, c, ka:ka + kn], start=(c == 0), stop=(c == 1))
                        P.cp("dve" if ctx else "act", kh[b][0:64, ka:ka + kn], pb[0:64, :kn])
                    st_.append(f)
                for ja in range(0, nkt, 8):
                    def f(ja=ja):
                        jn = min(8, nkt - ja)
                        pb = bank("g")
                        for jj in range(jn):
                            for c in range(2):
                                P.mm(pb[:, jj * 64:(jj + 1) * 64], ckvn[:, c, (ja + jj) * 128:(ja + jj + 1) * 128],
                                     wuv[:, c, h, :], start=(c == 0), stop=(c == 1))
                        P.cp("dve" if ctx else "act", vh[b][:, ja:ja + jn, 0:64], pb[:, 0:jn * 64].rearrange("p (j d) -> p j d", d=64))
                    st_.append(f)
                for qa in range(0, TT, 512):
                    hold = {}

                    def f(qa=qa, hold=hold):
                        pb = bank("g")
                        for c in range(3):
                            P.mm(pb[0:96, :512], wuq[:, c, h * 96:(h + 1) * 96], cqn[:, c, qa:qa + 512],
                                 start=(c == 0), stop=(c == 2))
                        if ctx:
                            hold["b"] = rope_a(pb, 96, 512, MLA_SCALE, pm96, cs, qa, rb)
                        else:
                            P.act(qh[b][:, qa:qa + 512], pb[0:96, :512], AF.Copy, scale=float(MLA_SCALE))
                    st_.append(f)
                    if ctx:
                        def f2(qa=qa, hold=hold):
                            p2 = hold["b"]()
                            P.cp("dve", qh[b][0:96, qa:qa + 512], p2[0:96, :512])
                        st_.append(f2)
                return st_

            for f in build_steps(0, 0):
                f()
            for h in range(8):
                b = h % NB
                half = h % 2
                inj = build_steps(h + 1, (h + 1) % NB) if h + 1 < 8 else []
                if ctx:
                    total_steps = (T // 512) * nkc
                    every = max(1, total_steps // (len(inj) + 1))
                    stepc = 0
                    for qa in range(0, T, 512):
                        po = bank("o")
                        sc_ps = {}

                        def issue_s(j):
                            ps_ = bank("s")
                            P.mm(ps_[:, :512], kh[b][:, j * 128:(j + 1) * 128], qh[b][:, qa:qa + 512])
                            sc_ps[j] = ps_

                        for j in range(3):
                            issue_s(j)
                        for j in range(nkc):
                            pt = pT[j % 4]
                            P.act(pt[:, :], sc_ps.pop(j)[:, :], AF.Exp)
                            if j + 3 < nkc:
                                issue_s(j + 3)
                            P.mm(po[:, :], vh[b][:, j, :], pt[:, :], start=(j == 0), stop=(j == nkc - 1))
                            stepc += 1
                            if inj and stepc % every == 0:
                                inj.pop(0)()
                        P.recip(rc[0:64, :], po[64:128, :])
                        P.tt("dve", mbuf[half * 64:(half + 1) * 64, h // 2, qa:qa + 512],
                             po[0:64, :], rc[0:64, :], ALU.mult)
                else:
                    for p in range(nseq // 2):
                        pts = []
                        for a in range(2):
                            sq_ = 2 * p + a
                            ps_ = bank("s")
                            for j in range(2):
                                P.mm(ps_[:, j * 256:(j + 1) * 256], kh[b][:, sq_ * 256 + j * 128:sq_ * 256 + (j + 1) * 128],
                                     qh[b][:, sq_ * 256:(sq_ + 1) * 256])
                            pt = pT[(2 * (p + h * (nseq // 2)) + a) % 4]
                            P.act(pt[:, :], ps_[:, :], AF.Exp)
                            pts.append(pt)
                        flush_pend()
                        for _ in range(3):
                            if inj:
                                inj.pop(0)()

                        def fin(p=p, pts=pts, b=b, h=h, half=half):
                            po = bank("o")
                            for a in range(2):
                                sq_ = 2 * p + a
                                for j in range(2):
                                    P.mm(po[:, a * 256:(a + 1) * 256], vh[b][:, 2 * sq_ + j, :], pts[a][:, j * 256:(j + 1) * 256],
                                         start=(j == 0), stop=(j == 1))
                            P.cp("dve", rc[0:64, :], po[64:128, :])
                            P.act(rc[0:64, :], rc[0:64, :], AF.Ln)
                            P.act(rc[0:64, :], rc[0:64, :], AF.Exp, scale=-1.0)
                            P.tt("dve", mbuf[half * 64:(half + 1) * 64, h // 2, p * 512:(p + 1) * 512],
                                 po[0:64, :], rc[0:64, :], ALU.mult)
                        pend[0] = fin
                while inj:
                    inj.pop(0)()
            flush_pend()
            for nm in ["qh%d" % b for b in range(NB)] + ["kh%d" % b for b in range(NB)] + ["vh%d" % b for b in range(NB)] + \
                      ["pT%d" % b for b in range(4)] + ["rc", "wuq", "wuk", "wuv", "cqn", "ckvn", "krT"]:
                AR.release(nm)
            if ctx:
                AR.release("cs")
                free_rb()
                wout = load_wout()
            if stages == 2 and l == trunc_l[0]:
                raise _Trunc()
            stage3(l, m, NT, mbuf, wg, wout, ada_hooks, nxt)
            for nm in ("wg", "wout", "mbuf"):
                AR.release(nm)

        def odd_layer(l, m, NT, nseq, T, ctx, nxt=None):
            i = l // 2
            S = T + (256 if ctx else 0)
            KT = nseq * S
            nkc = S // 128
            wv_in = wino_d[i].rearrange("(c p) n -> p c n", p=128)
            qT = AR.bf("mbuf", [128, 8, NT])
            kd = AR.bf("kd", [128, 4, KT])
            va = AR.bf("va", [128, KT // 128, 4, 128])
            wq0 = pref.pop("wq0", None)
            if wq0 is None:
                wq0 = AR.bf("wq0", [128, 8, 512])
                load_w(wq0, wv_in, 0, 512)
            wq1 = pref.pop("wq1", None)
            if wq1 is None:
                wq1 = AR.bf("wq1", [128, 8, 512])
                load_w(wq1, wv_in, 512, 512)
            wqs = [wq0, wq1]
            wk = pref.pop("wk", None)
            if wk is None:
                wk = AR.bf("wk", [128, 8, 256])
                load_w(wk, wv_in, 1024, 256)
            nkv = 256 if ctx else 512
            wkv = AR.bf("wkv", [128, 8, nkv])
            load_w(wkv, wv_in, 1536 - nkv, nkv)
            P.memset("pool", va[:, :, :, 64:128], 1.0)
            cs = None
            qs = None
            if ctx:
                cs = AR.bf("cs", [128, 2, 2048])
                P.dma(cs[:, 0, :], swacs_d[0], eng="pool")
                P.dma(cs[:, 1, :], swacs_d[1], eng="pool")
                rb = alloc_rb()
                cst = AR.f32("cst", [128, 2, 256])
                cdup = AR.f32("cdup", [128, 4, 128])
                for jj in range(2):
                    P.dma(cst[:, jj, :], cv_d[i, jj * 128:(jj + 1) * 128, :])
                for jj in range(2):
                    P.cp("dve", va[:, T // 128 + jj, :, 0:64], cst[:, jj, :].rearrange("p (h d) -> p h d", h=4))
                for jj in range(2):
                    P.dma(cst[:, jj, :], ck_d[i, jj * 128:(jj + 1) * 128, :])
                for jj in range(2):
                    P.cp("dve", cdup[:, :, 0:64], cst[:, jj, :].rearrange("p (h d) -> p h d", h=4))
                    P.cp("pool", cdup[:, :, 64:128], cst[:, jj, :].rearrange("p (h d) -> p h d", h=4))
                    pb = bank()
                    for kvh in range(4):
                        P.tr(pb[:, kvh * 128:(kvh + 1) * 128], cdup[:, kvh, :], ident[:, :])
                    for kvh in range(4):
                        P.cp("dve", kd[:, kvh, T + jj * 128:T + (jj + 1) * 128], pb[:, kvh * 128:(kvh + 1) * 128])
                AR.release("cst")
                AR.release("cdup")
            stg = None
            if not ctx:
                stg = [AR.f32("stg%d" % j, [128, 512]) for j in range(2)]
            if ctx:
                hs1 = [(AR.bf("hT", [128, 8, 512]), AR.bf("sq", [128, 4, 512])),
                       (AR.bf("hT1", [128, 8, 512]), AR.bf("sq1", [128, 4, 512]))]
            else:
                hs1 = alloc_hs()

            def kidx(t):
                return (t // T) * S + (t % T)

            pipe1 = len(hs1) > 1
            rpend = [None]
            if pipe1:
                modulate(hs1[0][0], hs1[0][1], l, m, 0, 512)
            for t0 in range(0, NT, 512):
                n = 512
                hT, sq = hs1[(t0 // 512) % len(hs1)]
                if not pipe1:
                    modulate(hT, sq, l, m, t0, n)
                pieces = [(t0, n)] if T >= 512 else [(t0 + a, T) for a in range(0, n, T)]
                nparts = []
                if pipe1 and t0 + 512 < NT:
                    nh, nsq = hs1[((t0 // 512) + 1) % 2]
                    nparts = modulate_parts(nh, nsq, l, m, t0 + 512, 512)
                for oc in range(8):
                    pb = bank()
                    for k in range(8):
                        P.mm(pb[:, :n], wqs[oc // 4][:, k, (oc % 4) * 128:(oc % 4 + 1) * 128], hT[:, k, :n],
                             start=(k == 0), stop=(k == 7))
                    if ctx:
                        pb_fn = rope_a(pb, 128, n, SWA_SCALE, pm128, cs, t0, rb)
                        if rpend[0] is not None:
                            rpend[0]()

                        def _fin_q(pb_fn=pb_fn, oc=oc, t0=t0, n=n):
                            p2 = pb_fn()
                            P.cp("act", qT[:, oc, t0:t0 + n], p2[:, :n])
                        rpend[0] = _fin_q
                    else:
                        P.ts("dve", qT[:, oc, t0:t0 + n], pb[:, :n], SWA_SCALE, ALU.mult)
                    run_part(nparts)
                for kc in range(2):
                    pb = bank()
                    for k in range(8):
                        P.mm(pb[:, :n], wk[:, k, kc * 128:(kc + 1) * 128], hT[:, k, :n], start=(k == 0), stop=(k == 7))
                    def _copies(srcs, kc=kc):
                        for (src, so, sn, ko) in srcs:
                            for hh in range(2):
                                for dh in range(2):
                                    P.cp("act" if dh != hh else "dve",
                                         kd[dh * 64:(dh + 1) * 64, 2 * kc + hh, ko:ko + sn],
                                         src[hh * 64:(hh + 1) * 64, so:so + sn])
                    if ctx:
                        pb_fn = rope_a(pb, 128, n, 1.0, pm128, cs, t0, rb)
                        if rpend[0] is not None:
                            rpend[0]()

                        def _fin_k(pb_fn=pb_fn, t0=t0, n=n, _copies=_copies):
                            p2 = pb_fn()
                            _copies([(p2, 0, n, kidx(t0))])
                        rpend[0] = _fin_k
                    else:
                        _copies([(pb, ta - t0, tn, kidx(ta)) for (ta, tn) in pieces])
                run_part(nparts, 16)
                for j in range(n // 128):
                    tok = t0 + j * 128
                    pb = bank()
                    for k in range(8):
                        P.mm(pb[:, 0:nkv], hT[:, k, j * 128:(j + 1) * 128], wkv[:, k, :], start=(k == 0), stop=(k == 7))
                    P.cp("dve", va[:, kidx(tok) // 128, :, 0:64], pb[:, nkv - 256:nkv].rearrange("p (h d) -> p h d", h=4))
                    if not ctx:
                        b = tok // T
                        pos = tok % T
                        so = stg[j % 2]
                        P.cp("act", so[:, :], pb[:, :])
                        P.dma(sk_d[b, i, pos:pos + 128, :], so[:, 0:256], out_dma=True)
                        P.dma(sv_d[b, i, pos:pos + 128, :], so[:, 256:512], out_dma=True)
                    if j == 0 and rpend[0] is not None:
                        rpend[0]()
                        rpend[0] = None
            free_hs(hs1)
            for nm in ("wq0", "wq1", "wk", "wkv"):
                AR.release(nm)
            if ctx:
                AR.release("cs")
                free_rb()
            else:
                AR.release("stg0")
                AR.release("stg1")
            if stages == 1 and l == trunc_l[0]:
                raise _Trunc()
            ada_hooks = None
            if ADA_INTERLEAVE and (not ctx) and l + 1 < 4:
                ada_alloc(l + 1, 2)
                ada_issue([0, 1])

                def _h0():
                    ada_compute([0, 1])
                    ada_issue([2, 3])

                def _h1():
                    ada_compute([2, 3])
                    ada_issue([4, 5])

                def _hp():
                    ada_compute([4, 5])
                    ada_finish()
                ada_hooks = {0: _h0, 1: _h1, "post": _hp}
            vaB = AR.bf("vaB", [128, KT // 128, 4, 128])
            wg = AR.bf("wg", [128, 8, 1024])
            wout = AR.bf("wout", [128, 8, 1024])
            load_w(wg, wv_in, 1536, 1024)
            load_w(wout, woo_d[i].rearrange("(c p) n -> p c n", p=128), 0, 1024)
            pT = [AR.bf("pT%d" % b, [128, 512]) for b in range(4)]
            rc = AR.f32("rc", [128, 512])
            P.memset("pool", vaB[:, :, :, 0:64], 1.0)
            nch_all = KT // 128
            for ja in range(0, nch_all, 6):
                jb = min(nch_all, ja + 6)
                P.cp("pool", vaB[:, ja:jb, :, 64:128], va[:, ja:jb, :, 0:64])
            mbuf = qT
            pools["o4"] = [4, 5, 6, 7]
            rr["o4"] = 0
            pend = []

            def flush_pend(keep=0):
                while len(pend) > keep:
                    pend.pop(0)()

            def finalize_pair(c, pos, t_lo):
                hA, hB = 2 * c, 2 * c + 1
                P.ts("dve", rc[0:64, :], pos[0][64:128, :], esink[64:128, i * 16 + hA:i * 16 + hA + 1], ALU.add)
                P.ts("dve", rc[64:128, :], pos[1][0:64, :], esink[0:64, i * 16 + hB:i * 16 + hB + 1], ALU.add)
                if ctx:
                    P.recip(rc[:, :], rc[:, :])
                else:
                    P.act(rc[:, :], rc[:, :], AF.Ln)
                    P.act(rc[:, :], rc[:, :], AF.Exp, scale=-1.0)
                P.tt("dve", mbuf[0:64, c, t_lo:t_lo + 512], pos[0][0:64, :], rc[0:64, :], ALU.mult)
                P.tt("dve", mbuf[64:128, c, t_lo:t_lo + 512], pos[1][64:128, :], rc[64:128, :], ALU.mult)

            ucount = 0
            for c in range(8):
                kvh = c // 2
                ntile = (nseq // 2) if not ctx else (T // 512)
                for tix in range(ntile):
                    pos = []
                    if ctx:
                        qt = tix
                        q0 = qt * 512
                        pos = [bank("o4"), bank("o4")]
                        jobs = []
                        for jj in range(2):
                            jobs.append((T // 128 + jj, 0, 512, []))
                        for j in range(4 * qt - 1, 4 * qt + 5):
                            if j < 0 or j >= T // 128:
                                continue
                            nlo = max(4 * qt, j - 1)
                            nhi = min(4 * qt + 3, j + 1)
                            mk = []
                            for nb in range(nlo, nhi + 1):
                                if nb == j - 1:
                                    mk.append((nb, 0))
                                elif nb == j + 1:
                                    mk.append((nb, 1))
                            jobs.append((j, (nlo - 4 * qt) * 128, (nhi - 4 * qt + 1) * 128, mk))
                        sc_ps = {}

                        def issue_s(ji):
                            kc, lo, hi, mk_ = jobs[ji]
                            for half in (0, 1):
                                r0, r1 = half * 64, half * 64 + 64
                                ps_ = bank("s")
                                P.mm(ps_[:, lo:hi], kd[r0:r1, kvh, kc * 128:(kc + 1) * 128], qT[r0:r1, c, q0 + lo:q0 + hi],
                                     start=True, stop=(len(mk_) == 0))
                                sc_ps[(ji, half)] = ps_
                            for half in (0, 1):
                                for mi, (nb, which) in enumerate(mk_):
                                    cl = (nb - 4 * qt) * 128
                                    P.mm(sc_ps[(ji, half)][:, cl:cl + 128], identb[:, :], masks[:, which, :],
                                         start=False, stop=(mi == len(mk_) - 1))

                        for ji in range(min(2, len(jobs))):
                            issue_s(ji)
                        for ji, (kc, lo, hi, mk) in enumerate(jobs):
                            pts = []
                            for half in (0, 1):
                                pt = pT[(2 * ji + half) % 4]
                                P.act(pt[:, lo:hi], sc_ps.pop((ji, half))[:, lo:hi], AF.Exp)
                                pts.append(pt)
                            if ji + 2 < len(jobs):
                                issue_s(ji + 2)
                            for half in (0, 1):
                                vsrc = va if half == 0 else vaB
                                P.mm(pos[half][:, lo:hi], vsrc[:, kc, kvh, :], pts[half][:, lo:hi],
                                     start=(ji == 0), stop=(ji == len(jobs) - 1))
                            if ji == 2:
                                flush_pend()
                        pend.append(lambda c=c, pos=pos, q0=q0: finalize_pair(c, pos, q0))
                        continue
                    for half in (0, 1):
                        r0, r1 = half * 64, half * 64 + 64
                        vsrc = va if half == 0 else vaB
                        p = tix
                        pts = []
                        for a in range(2):
                            sq_ = 2 * p + a
                            ps_ = bank("s")
                            for j in range(2):
                                P.mm(ps_[:, j * 256:(j + 1) * 256], kd[r0:r1, kvh, sq_ * 256 + j * 128:sq_ * 256 + (j + 1) * 128],
                                     qT[r0:r1, c, sq_ * 256:(sq_ + 1) * 256])
                            pt = pT[(2 * ucount + a) % 4]
                            P.act(pt[:, :], ps_[:, :], AF.Exp)
                            pts.append(pt)
                        ucount += 1
                        flush_pend()
                        po = bank("o4")
                        pos.append(po)

                        def pv(p=p, pts=pts, po=po, vsrc=vsrc, kvh=kvh):
                            for a in range(2):
                                sq_ = 2 * p + a
                                for j in range(2):
                                    P.mm(po[:, a * 256:(a + 1) * 256], vsrc[:, 2 * sq_ + j, kvh, :], pts[a][:, j * 256:(j + 1) * 256],
                                         start=(j == 0), stop=(j == 1))
                        pend.append(pv)
                        if half == 1:
                            pend.append(lambda c=c, pos=pos, p=p: finalize_pair(c, pos, p * 512))
            flush_pend()
            AR.release("vaB")
            for nm in ["pT%d" % b for b in range(4)] + ["rc", "kd", "va"]:
                AR.release(nm)
            if stages == 2 and l == trunc_l[0]:
                raise _Trunc()
            stage3(l, m, NT, mbuf, wg, wout, ada_hooks, nxt)
            for nm in ("wg", "wout", "mbuf"):
                AR.release(nm)

        def load_x(src, NT):
            stage = AR.f32("xstage", [128, 4, 1024])
            for t0 in range(0, NT, 512):
                for j in range(4):
                    P.dma(stage[:, j, :], src[t0 + j * 128:t0 + (j + 1) * 128, :])
                for c in range(8):
                    pb = bank()
                    for j in range(4):
                        P.tr(pb[:, j * 128:(j + 1) * 128], stage[:, j, c * 128:(c + 1) * 128], ident[:, :])
                    P.cp(evac_eng(), xT[:, c, t0:t0 + 512], pb[:, :])
            AR.release("xstage")

        def store_x(dst, NT):
            stage = AR.f32("xstage", [128, 4, 1024])
            for t0 in range(0, NT, 512):
                for j in range(4):
                    for hb in range(2):
                        pb = bank()
                        for cc in range(4):
                            c = hb * 4 + cc
                            P.tr(pb[:, cc * 128:(cc + 1) * 128], xT[:, c, t0 + j * 128:t0 + (j + 1) * 128], ident[:, :])
                        P.cp(evac_eng(), stage[:, j, hb * 512:(hb + 1) * 512], pb[:, :])
                    P.dma(dst[t0 + j * 128:t0 + (j + 1) * 128, :], stage[:, j, :], out_dma=True)
            AR.release("xstage")

        trunc_l = [-1]
        try:
          for (src, dst, m, NT, nseq, T, ctx) in ((xp_d, yp_d, 0, 1024, 4, 256, False),
                                                   (xs_d, ys_d, 1, 2048, 1, 2048, True)):
              nl = nl_p if not ctx else nl_s
              if nl < 0:
                  continue
              load_x(src, NT)
              if not ada0_done[0]:
                  ada_all(0)
                  ada0_done[0] = True
              trunc_l[0] = nl - 1 if ((ctx and nl_s >= 0) or (not ctx and nl_s < 0)) else -1
              for l in range(nl):
                  if l + 1 < nl:
                      nxt = (l + 1, ctx, not ctx)
                  elif (not ctx) and nl_s > 0:
                      nxt = (0, True, True)
                  else:
                      nxt = None
                  store_state["dst"] = dst if (l == nl - 1 and stages == 3) else None
                  store_state["done"] = False
                  if l % 2 == 0:
                      even_layer(l, m, NT, nseq, T, ctx, nxt)
                  else:
                      odd_layer(l, m, NT, nseq, T, ctx, nxt)
              if not store_state["done"]:
                  store_x(dst, NT)
              store_state["dst"] = None

        except _Trunc:
            P.dma(yp_d[0:128, 0:512], rstd[:, :], out_dma=True)

        P.finish()
        build_program.stats = {e: len(P.ops[e]) for e in ENGS}
        build_program.peak = AR.peak
    return nc


_CONSTS = None


def _consts():
    global _CONSTS
    if _CONSTS is None:
        mla, swa = _rope_tables()
        pm96, pm128 = _perm_mats()
        _CONSTS = dict(ident=np.eye(128, dtype=np.float32), pm96=pm96, pm128=pm128, masks=_masks(),
                       amats=_pool_mats().reshape(20, 128, 128), mla_cs=mla, swa_cs=swa)
    return _CONSTS


def kernel(x_prompt, x_sample, cache_ckv, cache_krope, cache_k, cache_v, c, c_ctx,
           ada_w, ada_b, norm_pre, norm_post,
           mla_w_in, mla_g_qn, mla_g_kvn, mla_w_uq, mla_w_ukv, pool_w, pool_scale, mixa_w_out,
           swa_w_in, swa_sink, swa_w_out):
    f = lambda a: np.ascontiguousarray(np.asarray(a, dtype=np.float32))
    x_prompt, x_sample = f(x_prompt), f(x_sample)
    cache_ckv, cache_krope, cache_k, cache_v = f(cache_ckv), f(cache_krope), f(cache_k), f(cache_v)
    c, c_ctx = f(c), f(c_ctx)
    vecs = np.zeros((14, 1024), np.float32)
    vecs[0:4] = f(norm_pre)
    vecs[4:8] = f(norm_post)
    vecs[8:10, :384] = f(mla_g_qn)
    vecs[10:12, :256] = f(mla_g_kvn)
    vecs[12:14, :512] = f(pool_scale)
    shared = dict(ada_w=f(ada_w), ada_b=f(ada_b), vecs=vecs, gkvn=f(mla_g_kvn), sink=f(swa_sink).reshape(32),
                  w_in_e=f(mla_w_in), w_uq=f(mla_w_uq), w_ukv=f(mla_w_ukv), w_pool=f(pool_w), w_out_e=f(mixa_w_out),
                  w_in_o=f(swa_w_in), w_out_o=f(swa_w_out))
    shared.update(_consts())
    in_maps = []
    for i in range(8):
        d = dict(shared)
        d["xp"] = x_prompt[4 * i:4 * i + 4].reshape(1024, 1024)
        d["xs"] = x_sample[i]
        d["cckv"] = cache_ckv[i]
        d["ckr"] = cache_krope[i]
        d["ck"] = cache_k[i].reshape(2, 256, 256)
        d["cv"] = cache_v[i].reshape(2, 256, 256)
        d["cc"] = np.stack([c_ctx, c[i]], axis=0)
        in_maps.append(d)
    nc = build_program()
    res = run_bass_kernel_spmd(nc, in_maps, core_ids=list(range(8)))
    R = res.results
    y_prompt = np.concatenate([r["yp"].reshape(4, 256, 1024) for r in R], axis=0)
    y_sample = np.stack([r["ys"] for r in R], axis=0)
    st_ckv = np.concatenate([r["st_ckv"] for r in R], axis=0)
    st_kr = np.concatenate([r["st_kr"] for r in R], axis=0)
    st_k = np.concatenate([r["st_k"].reshape(4, 2, 256, 4, 64) for r in R], axis=0)
    st_v = np.concatenate([r["st_v"].reshape(4, 2, 256, 4, 64) for r in R], axis=0)
    return (y_prompt.astype(np.float32), y_sample.astype(np.float32), st_ckv.astype(np.float32),
            st_kr.astype(np.float32), st_k.astype(np.float32), st_v.astype(np.float32))
```

```python
import numpy as np
from contextlib import ExitStack
import concourse.bass as bass
import concourse.mybir as mybir
from concourse.bass_utils import run_bass_kernel_spmd

F32 = mybir.dt.float32
BF16 = mybir.dt.bfloat16
AF = mybir.ActivationFunctionType
ALU = mybir.AluOpType

ENGS = ("pe", "act", "dve", "pool", "sp")
SAME_ENGINE_RAW = True
EMBED_LAST_WAIT = True
EMBED_ENGINES = ("pe", "act", "dve")
N_DMA_SEMS = 24
BUCKET = 4096

D = 1024
EPS = 1e-6
MLA_SCALE = 96 ** -0.5
SWA_SCALE = 64 ** -0.5


class Prog:
    def __init__(self, nc, stack):
        self.nc = nc
        self.stack = stack
        self.ops = {e: [] for e in ENGS}
        self.recs = {}
        self.known = {e: {} for e in ENGS}
        self.eng_sem = {e: stack.enter_context(nc.semaphore("s_" + e)) for e in ENGS}
        self.dma_sems = [stack.enter_context(nc.semaphore("s_dma%d" % i)) for i in range(N_DMA_SEMS)]
        self.dma_cnt = [0] * N_DMA_SEMS
        self.dma_rr = 0
        self.dma_rr_pool = 0
        self.out_dma_tokens = []

    def sb(self, name, shape, dtype):
        return self.stack.enter_context(self.nc.sbuf_tensor("sb_" + name, list(shape), dtype))

    def ps(self, name, shape, dtype=F32):
        return self.stack.enter_context(self.nc.psum_tensor("pp_" + name, list(shape), dtype))

    @staticmethod
    def _box(ap):
        shp = list(ap.tensor.shape)
        row = 1
        for s in shp[1:]:
            row *= s
        isz = mybir.dt.size(ap.dtype)
        off = int(ap.offset)
        dims = ap.ap
        p0 = off // row
        f0 = off % row
        pstep, pcnt = dims[0]
        if pstep == row or (pcnt == 1 and len(dims) > 1):
            p1 = p0 + pcnt
            rest = dims[1:]
        elif pstep == 0:
            p1 = p0 + 1
            rest = dims[1:]
        else:
            p1 = p0 + 1
            rest = dims
        ext = 0
        for st, cn in rest:
            ext += abs(st) * (cn - 1)
        return (p0, p1, f0 * isz, (f0 + ext + 1) * isz)

    @staticmethod
    def _tracked(ap):
        return str(ap.space).upper() in ("SB", "PSUM")

    def op(self, eng, fn, reads=(), writes=(), dma=False, out_dma=False):
        idx = len(self.ops[eng])
        waits = []
        kn = self.known[eng]

        def need(tok, raw=False):
            if tok[0] == 'e':
                if tok[1] == eng and not (raw and SAME_ENGINE_RAW and eng != "pe"):
                    return
                key = tok[1]
            else:
                key = ('d', tok[1])
            if kn.get(key, -1) >= tok[2]:
                return
            kn[key] = tok[2]
            waits.append(tok)

        if dma:
            if eng == "pool":
                k = 8 + self.dma_rr_pool
                self.dma_rr_pool = (self.dma_rr_pool + 1) % (N_DMA_SEMS - 8)
            else:
                k = self.dma_rr
                self.dma_rr = (self.dma_rr + 1) % 8
            if self.dma_cnt[k] > 0:
                need(('d', k, self.dma_cnt[k] * 16))
            self.dma_cnt[k] += 1
            mytok = ('d', k, self.dma_cnt[k] * 16)
        else:
            mytok = ('e', eng, idx)

        rb = [(ap.name, self._box(ap)) for ap in reads if self._tracked(ap) and str(ap.space).upper() != "PSUM"]
        wb = [(ap.name, self._box(ap)) for ap in writes if self._tracked(ap) and str(ap.space).upper() != "PSUM"]
        for ap in list(reads) + list(writes):
            if str(ap.space).upper() == "PSUM":
                ent = (ap.name, (0, 128, 0, 1 << 20))
                if ent not in wb:
                    wb.append(ent)
        for name, box in rb:
            tr = self.recs.get(name)
            if tr is None:
                continue
            for b in range(box[2] // BUCKET, (box[3] - 1) // BUCKET + 1):
                for r in tr.get(b, ()):
                    if r[3] and r[2]:
                        rbx = r[0]
                        if rbx[0] < box[1] and box[0] < rbx[1] and rbx[2] < box[3] and box[2] < rbx[3]:
                            need(r[1], True)
        for name, box in wb:
            tr = self.recs.get(name)
            if tr is None:
                continue
            for b in range(box[2] // BUCKET, (box[3] - 1) // BUCKET + 1):
                lst = tr.get(b)
                if not lst:
                    continue
                for r in lst:
                    if r[3]:
                        rbx = r[0]
                        if rbx[0] < box[1] and box[0] < rbx[1] and rbx[2] < box[3] and box[2] < rbx[3]:
                            need(r[1])
                            if box[0] <= rbx[0] and box[1] >= rbx[1] and box[2] <= rbx[2] and box[3] >= rbx[3]:
                                r[3] = False
                tr[b] = [r for r in lst if r[3]]
        for name, box in rb:
            tr = self.recs.setdefault(name, {})
            rec = [box, mytok, False, True]
            for b in range(box[2] // BUCKET, (box[3] - 1) // BUCKET + 1):
                lst = tr.setdefault(b, [])
                if not dma:
                    for r in lst:
                        if r[3] and (not r[2]) and r[1][0] == 'e' and r[1][1] == eng and r[0] == box:
                            r[3] = False
                lst.append(rec)
        for name, box in wb:
            tr = self.recs.setdefault(name, {})
            rec = [box, mytok, True, True]
            for b in range(box[2] // BUCKET, (box[3] - 1) // BUCKET + 1):
                tr.setdefault(b, []).append(rec)
        o = dict(fn=fn, waits=waits, tok=mytok, marked=False)
        self.ops[eng].append(o)
        if out_dma:
            self.out_dma_tokens.append(mytok)
        return o

    def dma(self, out, in_, eng="sp", out_dma=False):
        return self.op(eng, lambda e: e.dma_start(out=out, in_=in_), reads=[in_], writes=[out],
                       dma=True, out_dma=out_dma)

    def mm(self, out, lhsT, rhs, start=True, stop=True):
        return self.op("pe", lambda e: e.matmul(out, lhsT, rhs, start=start, stop=stop),
                       reads=[lhsT, rhs] + ([] if start else [out]), writes=[out])

    def tr(self, out, in_, ident):
        return self.op("pe", lambda e: e.transpose(out, in_, ident), reads=[in_, ident], writes=[out])

    def act(self, out, in_, func, bias=None, scale=None, accum=None):
        kw = {}
        rd = [in_]
        wr = [out]
        if bias is not None:
            kw["bias"] = bias
            if not isinstance(bias, (int, float)):
                rd.append(bias)
        if scale is not None:
            kw["scale"] = scale
            if not isinstance(scale, (int, float)):
                rd.append(scale)
        if accum is not None:
            kw["accum_out"] = accum
            wr.append(accum)
        return self.op("act", lambda e: e.activation(out=out, in_=in_, func=func, **kw), reads=rd, writes=wr)

    def cp(self, eng, out, in_):
        if eng == "act":
            return self.op("act", lambda e: e.copy(out=out, in_=in_), reads=[in_], writes=[out])
        return self.op(eng, lambda e: e.tensor_copy(out=out, in_=in_), reads=[in_], writes=[out])

    def tt(self, eng, out, in0, in1, op):
        return self.op(eng, lambda e: e.tensor_tensor(out=out, in0=in0, in1=in1, op=op), reads=[in0, in1], writes=[out])

    def ts(self, eng, out, in0, s1, op0, s2=None, op1=None):
        rd = [in0]
        if not isinstance(s1, (int, float)):
            rd.append(s1)
        if s2 is not None and not isinstance(s2, (int, float)):
            rd.append(s2)
        if op1 is None:
            return self.op(eng, lambda e: e.tensor_scalar(out=out, in0=in0, scalar1=s1, scalar2=None, op0=op0),
                           reads=rd, writes=[out])
        return self.op(eng, lambda e: e.tensor_scalar(out=out, in0=in0, scalar1=s1, scalar2=s2, op0=op0, op1=op1),
                       reads=rd, writes=[out])

    def stt(self, eng, out, in0, scalar, in1, op0, op1):
        rd = [in0, in1]
        if not isinstance(scalar, (int, float)):
            rd.append(scalar)
        return self.op(eng, lambda e: e.scalar_tensor_tensor(out=out, in0=in0, scalar=scalar, in1=in1, op0=op0, op1=op1),
                       reads=rd, writes=[out])

    def memset(self, eng, out, val):
        return self.op(eng, lambda e: e.memset(out, val), writes=[out])

    def recip(self, out, in_):
        return self.op("dve", lambda e: e.reciprocal(out=out, in_=in_), reads=[in_], writes=[out])

    def finish(self):
        nc = self.nc
        for e in ENGS:
            for o in self.ops[e]:
                for tok in o["waits"]:
                    if tok[0] == 'e':
                        self.ops[tok[1]][tok[2]]["marked"] = True
        cnt_at = {}
        for e in ENGS:
            c = 0
            arr = []
            for o in self.ops[e]:
                if o["marked"]:
                    c += 1
                arr.append(c)
            cnt_at[e] = arr
        fw = {}
        for tok in self.out_dma_tokens:
            fw[tok[1]] = max(fw.get(tok[1], 0), tok[2])

        with nc.Block() as block:
            def emit(ename, engobj):
                for o in self.ops[ename]:
                    ws = o["waits"]
                    emb = None
                    if EMBED_LAST_WAIT and ws and ename in EMBED_ENGINES and o["tok"][0] == 'e':
                        emb = ws[-1]
                        ws = ws[:-1]
                    for tok in ws:
                        if tok[0] == 'e':
                            engobj.wait_ge(self.eng_sem[tok[1]], cnt_at[tok[1]][tok[2]])
                        else:
                            engobj.wait_ge(self.dma_sems[tok[1]], tok[2])
                    ins = o["fn"](engobj)
                    if emb is not None:
                        if emb[0] == 'e':
                            ins._wait_ge(self.eng_sem[emb[1]], cnt_at[emb[1]][emb[2]])
                        else:
                            ins._wait_ge(self.dma_sems[emb[1]], emb[2])
                    if o["tok"][0] == 'd':
                        ins.then_inc(self.dma_sems[o["tok"][1]], 16)
                    elif o["marked"]:
                        ins.then_inc(self.eng_sem[ename], 1)
                if ename == "sp":
                    for k, v in fw.items():
                        engobj.wait_ge(self.dma_sems[k], v)

            @block.sync
            def _(sync):
                emit("sp", sync)

            @block.tensor
            def _(tensor):
                emit("pe", tensor)

            @block.scalar
            def _(scalar):
                emit("act", scalar)

            @block.vector
            def _(vector):
                emit("dve", vector)

            @block.gpsimd
            def _(gpsimd):
                emit("pool", gpsimd)


class _Trunc(Exception):
    pass


class Arena:
    def __init__(self, tensor, nelem):
        self.t = tensor
        self.n = nelem
        self.free = [(0, nelem)]
        self.live = {}
        self.peak = 0

    def alloc(self, name, nelem_bf16):
        nelem_bf16 = (nelem_bf16 + 15) // 16 * 16
        for i, (o, s) in enumerate(self.free):
            if s >= nelem_bf16:
                if s == nelem_bf16:
                    self.free.pop(i)
                else:
                    self.free[i] = (o + nelem_bf16, s - nelem_bf16)
                self.live[name] = (o, nelem_bf16)
                used = self.n - sum(s for _, s in self.free)
                self.peak = max(self.peak, used)
                return o
        raise RuntimeError("arena OOM allocating %s (%d); live=%s free=%s" % (name, nelem_bf16, self.live, self.free))

    def release(self, name):
        o, s = self.live.pop(name)
        self.free.append((o, s))
        self.free.sort()
        merged = []
        for o, s in self.free:
            if merged and merged[-1][0] + merged[-1][1] == o:
                merged[-1] = (merged[-1][0], merged[-1][1] + s)
            else:
                merged.append((o, s))
        self.free = merged

    def bf(self, name, shape):
        n = 1
        for s in shape[1:]:
            n *= s
        o = self.alloc(name, n)
        v = self.t[0:shape[0], o:o + n]
        if len(shape) == 3:
            v = v.rearrange("p (a b) -> p a b", a=shape[1])
        elif len(shape) == 4:
            v = v.rearrange("p (a b c) -> p a b c", a=shape[1], b=shape[2])
        return v

    def f32(self, name, shape):
        n = 1
        for s in shape[1:]:
            n *= s
        o = self.alloc(name, 2 * n)
        v = self.t[0:shape[0], o:o + 2 * n].bitcast(F32)
        if len(shape) == 3:
            v = v.rearrange("p (a b) -> p a b", a=shape[1])
        elif len(shape) == 4:
            v = v.rearrange("p (a b c) -> p a b c", a=shape[1], b=shape[2])
        return v


def _rope_tables():
    t = np.arange(2048)
    row = (t // 64).astype(np.float64)
    col = (t % 64).astype(np.float64)
    mla = np.zeros((2, 96, 2048), np.float32)
    mla[0, 0:64] = 1.0
    for r in range(32):
        pos = row if r < 16 else col
        i = r % 16
        f = i % 8
        inv = 10000.0 ** (-(2.0 * f) / 16.0)
        ang = pos * inv
        mla[0, 64 + r] = np.cos(ang)
        mla[1, 64 + r] = np.sin(ang) if i < 8 else -np.sin(ang)
    swa = np.zeros((2, 128, 2048), np.float32)
    for p in range(128):
        d = p % 64
        pos = row if d < 32 else col
        i = d % 32
        f = i % 16
        inv = 10000.0 ** (-(2.0 * f) / 32.0)
        ang = pos * inv
        swa[0, p] = np.cos(ang)
        swa[1, p] = np.sin(ang) if i < 16 else -np.sin(ang)
    return mla, swa


def _perm_mats():
    pm96 = np.zeros((96, 96), np.float32)
    for r in range(32):
        i = r % 16
        partner = r + 8 if i < 8 else r - 8
        pm96[64 + partner, 64 + r] = 1.0
    pm128 = np.zeros((128, 128), np.float32)
    for m in range(128):
        i = m % 32
        partner = m + 16 if i < 16 else m - 16
        pm128[partner, m] = 1.0
    return pm96, pm128


def _pool_mats():
    T = 384
    out = np.zeros((4, 5, 128, 128), np.float32)
    for g, w in enumerate((2, 4, 8, 16)):
        A = np.zeros((T, T), np.float64)
        for t in range(T):
            lo = min(max(t - w // 2, 0), T)
            hi = min(max(t + w // 2, 0), T)
            A[lo:hi, t] = 1.0 / (hi - lo)
            A[t, t] -= 1.0
        out[g, 0] = A[0:128, 0:128]
        out[g, 1] = A[128:256, 128:256]
        out[g, 2] = A[256:384, 256:384]
        out[g, 3] = A[0:128, 128:256]
        out[g, 4] = A[128:256, 0:128]
    return out


def _masks():
    k = np.arange(128)[:, None]
    q = np.arange(128)[None, :]
    m = np.zeros((2, 128, 128), np.float32)
    m[0] = np.where(k <= q, 0.0, -30000.0)
    m[1] = np.where(q <= k, 0.0, -30000.0)
    return m


def build_program(nl_p=4, nl_s=4, stages=3, dbg=None):
    nc = bass.Bass("TRN2", target_bir_lowering=False)

    def din(name, shape):
        return nc.dram_tensor(name, list(shape), F32, kind="ExternalInput").ap()

    def dout(name, shape):
        return nc.dram_tensor(name, list(shape), F32, kind="ExternalOutput").ap()

    xp_d = din("xp", [1024, 1024])
    xs_d = din("xs", [2048, 1024])
    cckv_d = din("cckv", [2, 256, 256])
    ckr_d = din("ckr", [2, 256, 32])
    ck_d = din("ck", [2, 256, 256])
    cv_d = din("cv", [2, 256, 256])
    cc_d = din("cc", [2, 1024])
    adaw_d = din("ada_w", [4, 1024, 3072])
    adab_d = din("ada_b", [4, 3072])
    vecs_d = din("vecs", [14, 1024])
    gkvn_d = din("gkvn", [2, 256])
    sink_d = din("sink", [32])
    wine_d = din("w_in_e", [2, 1024, 2208])
    wuq_d = din("w_uq", [2, 384, 768])
    wukv_d = din("w_ukv", [2, 256, 1024])
    wpool_d = din("w_pool", [2, 4, 128, 128])
    woe_d = din("w_out_e", [2, 1024, 1024])
    wino_d = din("w_in_o", [2, 1024, 2560])
    woo_d = din("w_out_o", [2, 1024, 1024])
    ident_d = din("ident", [128, 128])
    pm96_d = din("pm96", [96, 96])
    pm128_d = din("pm128", [128, 128])
    masks_d = din("masks", [2, 128, 128])
    amats_d = din("amats", [20, 128, 128])
    mlacs_d = din("mla_cs", [2, 96, 2048])
    swacs_d = din("swa_cs", [2, 128, 2048])

    yp_d = dout("yp", [1024, 1024])
    ys_d = dout("ys", [2048, 1024])
    sckv_d = dout("st_ckv", [4, 2, 256, 256])
    skr_d = dout("st_kr", [4, 2, 256, 32])
    sk_d = dout("st_k", [4, 2, 256, 256])
    sv_d = dout("st_v", [4, 2, 256, 256])

    with ExitStack() as st:
        P = Prog(nc, st)
        xT = P.sb("xT", [128, 8, 2048], F32)
        ident = P.sb("ident", [128, 128], F32)
        ones_bf = P.sb("ones_bf", [128, 128], BF16)
        identb = P.sb("identb", [128, 128], BF16)
        scb = P.sb("scb", [128, 8, 2], BF16)
        pm96 = P.sb("pm96", [96, 96], BF16)
        pm128 = P.sb("pm128", [128, 128], BF16)
        masks = P.sb("masks", [128, 2, 128], BF16)
        epsT = P.sb("epsT", [128, 1], F32)
        mod = P.sb("mod", [128, 4, 48], F32)
        vecT = P.sb("vecT", [128, 8, 32], F32)
        coefA = P.sb("coefA", [128, 4, 2, 8], F32)
        coefB = P.sb("coefB", [128, 4, 2, 8], F32)
        coefG = P.sb("coefG", [128, 4, 2, 8], F32)
        gkvb = P.sb("gkvb", [128, 2, 256], F32)
        esink = P.sb("esink", [128, 32], F32)
        rstd = P.sb("rstd", [128, 512], F32)
        rstd2 = P.sb("rstd2", [128, 512], F32)
        tmpf = [P.sb("tmpf%d" % i, [128, 512], F32) for i in range(2)]
        ARENA_N = 64 * 1024
        arena_t = P.sb("arena", [128, ARENA_N], BF16)
        AR = Arena(arena_t, ARENA_N)
        banks = [P.ps("ps%d" % i, [128, 512], F32) for i in range(8)]
        rr = {"all": 0, "s": 0, "o": 0, "g": 0}
        pools = {"all": list(range(8)), "s": [0, 1, 2, 3], "o": [4, 5], "g": [6, 7]}

        def bank(pool="all"):
            lst = pools[pool]
            b = banks[lst[rr[pool] % len(lst)]]
            rr[pool] += 1
            return b

        evac_rr = [0]

        def evac_eng():
            evac_rr[0] += 1
            return "dve" if evac_rr[0] % 2 else "act"

        P.dma(ident[:], ident_d)
        P.dma(identb[:], ident_d, eng="pool")
        P.dma(pm96[:], pm96_d, eng="pool")
        P.dma(pm128[:], pm128_d, eng="pool")
        P.dma(masks[:], masks_d.rearrange("a k q -> k a q"), eng="pool")
        P.dma(gkvb[:].rearrange("p a n -> p (a n)"), gkvn_d.rearrange("a n -> (a n)").partition_broadcast(128))
        P.dma(esink[:], sink_d.partition_broadcast(128))
        P.memset("dve", ones_bf[:], 1.0)
        P.memset("dve", epsT[:], EPS)
        P.act(esink[:], esink[:], AF.Exp)

        if dbg == "consts":
            P.dma(yp_d[0:128, 0:32], esink[:], out_dma=True)
            P.finish()
            return nc
        vst = AR.f32("vst", [32, 1024])
        P.memset("dve", vst[:], 0.0)
        P.dma(vst[0:14, :], vecs_d)
        pv = bank()
        for c in range(8):
            P.tr(pv[:, c * 32:(c + 1) * 32], vst[0:32, c * 128:(c + 1) * 128], ident[0:32, 0:32])
        for c in range(8):
            P.cp("dve", vecT[:, c, :], pv[:, c * 32:(c + 1) * 32])
        AR.release("vst")

        if dbg == "vec":
            P.dma(yp_d[0:128, 0:256], vecT[:].rearrange("p a b -> p (a b)"), out_dma=True)
            P.finish()
            return nc
        ccT = AR.f32("ccT", [128, 2, 8])
        for m in range(2):
            P.dma(ccT[:, m, :], cc_d[m].rearrange("(p c) -> p c", c=8))
        for m in range(2):
            P.act(scb[:, :, m], ccT[:, m, :], AF.Silu)
        AR.release("ccT")
        modv = mod[:].rearrange("p l (j c m) -> p l j c m", j=3, c=8)
        ada_state = {}

        def ada_alloc(l, nbuf):
            ada_state["l"] = l
            ada_state["adab"] = AR.bf("adab", [1, 3072])
            ada_state["bufs"] = [AR.bf("adw%d" % i, [128, 8, 512]) for i in range(nbuf)]
            ada_state["nbuf"] = nbuf
            P.dma(ada_state["adab"][:], adab_d[l:l + 1, :], eng="pool")

        def ada_issue(blks):
            l = ada_state["l"]
            wv = adaw_d[l].rearrange("(p c) n -> p c n", c=8)
            for blk in blks:
                wt = ada_state["bufs"][blk % ada_state["nbuf"]]
                P.dma(wt[:], wv[:, :, blk * 512:(blk + 1) * 512], eng="pool")

        def ada_compute(blks):
            l = ada_state["l"]
            adab = ada_state["adab"]
            for blk in blks:
                wt = ada_state["bufs"][blk % ada_state["nbuf"]]
                pm = bank()
                for nci in range(4):
                    nch = blk * 4 + nci
                    for c in range(8):
                        P.mm(pm[:, 2 * nci:2 * nci + 2], wt[:, c, nci * 128:(nci + 1) * 128], scb[:, c, :],
                             start=(c == 0), stop=False)
                    P.mm(pm[:, 2 * nci:2 * nci + 2], adab[0:1, nch * 128:(nch + 1) * 128], ones_bf[0:1, 0:2],
                         start=False, stop=True)
                P.cp("dve", mod[:, l, blk * 8:(blk + 1) * 8], pm[:, 0:8])

        def ada_finish():
            l = ada_state["l"]
            for m in range(2):
                P.stt("dve", coefA[:, l, m, :], modv[:, l, 1, :, m], 1.0, vecT[:, :, l], ALU.add, ALU.mult)
                P.cp("dve", coefB[:, l, m, :], modv[:, l, 0, :, m])
                P.tt("dve", coefG[:, l, m, :], modv[:, l, 2, :, m], vecT[:, :, 4 + l], ALU.mult)
            for i in range(ada_state["nbuf"]):
                AR.release("adw%d" % i)
            AR.release("adab")
            ada_state.clear()

        def ada_all(l):
            ada_alloc(l, 3)
            for blk in range(6):
                ada_issue([blk])
                ada_compute([blk])
            ada_finish()

        ADA_INTERLEAVE = nl_p >= 4
        ada0_done = [False]
        if not ADA_INTERLEAVE:
            for l in range(1, 4):
                ada_all(l)
        if dbg == "ada":
            P.dma(yp_d[0:128, 0:192], mod[:].rearrange("p a b -> p (a b)"), out_dma=True)
            P.dma(yp_d[128:256, 0:64], coefA[:].rearrange("p a b c -> p (a b c)"), out_dma=True)
            P.dma(yp_d[256:384, 0:64], coefG[:].rearrange("p a b c -> p (a b c)"), out_dma=True)
            P.finish()
            return nc
        def rstd_from_ssq(ps_ssq, n, dim, out):
            P.act(out, ps_ssq, AF.Ln, bias=epsT[:, 0:1], scale=1.0 / dim)
            P.act(out, out, AF.Exp, scale=-0.5)

        def modulate_parts(hT, sq, l, m, t0, n):
            parts = []

            ns_ = sq.shape[1]

            def stats_a():
                for c in range(8):
                    P.act(sq[:, c, :n], xT[:, c, t0:t0 + n], AF.Square)

            def stats_b():
                pb = bank()
                for c in range(8):
                    P.mm(pb[:, :n], ones_bf[:, :], sq[:, c, :n], start=(c == 0), stop=(c == 7))
                rstd_from_ssq(pb[:, :n], n, 1024, rstd[:, :n])

            def stats():
                pb = bank()
                for c0 in range(0, 8, ns_):
                    for c in range(c0, c0 + ns_):
                        P.act(sq[:, c % ns_, :n], xT[:, c, t0:t0 + n], AF.Square)
                    for c in range(c0, c0 + ns_):
                        P.mm(pb[:, :n], ones_bf[:, :], sq[:, c % ns_, :n], start=(c == 0), stop=(c == 7))
                rstd_from_ssq(pb[:, :n], n, 1024, rstd[:, :n])
            if ns_ >= 8:
                parts.extend([stats_a, (lambda: None), stats_b])
            else:
                parts.append(stats)
            for c in range(8):
                def ap(c=c):
                    tf = tmpf[c % 2]
                    P.stt("dve", tf[:, :n], xT[:, c, t0:t0 + n], coefA[:, l, m, c:c + 1], rstd[:, :n], ALU.mult, ALU.mult)
                    P.act(hT[:, c, :n], tf[:, :n], AF.Identity, bias=coefB[:, l, m, c:c + 1], scale=1.0)
                parts.append(ap)
            return parts

        def modulate(hT, sq, l, m, t0, n):
            for f in modulate_parts(hT, sq, l, m, t0, n):
                f()

        def run_part(parts, k=1):
            for _ in range(k):
                if parts:
                    parts.pop(0)()

        def load_w(dst, src_rows_view, col0, ncols):
            for c in range(dst.shape[1]):
                P.dma(dst[:, c, :], src_rows_view[:, c, col0:col0 + ncols], eng="pool")

        def alloc_hs():
            hs = [(AR.bf("hT", [128, 8, 512]), AR.bf("sq", [128, 8, 512]))]
            try:
                a = AR.bf("hT1", [128, 8, 512])
                try:
                    b = AR.bf("sq1", [128, 8, 512])
                    hs.append((a, b))
                except RuntimeError:
                    AR.release("hT1")
            except RuntimeError:
                pass
            return hs

        def free_hs(hs):
            AR.release("hT")
            AR.release("sq")
            if len(hs) > 1:
                AR.release("hT1")
                AR.release("sq1")

        store_state = {"dst": None, "done": False}

        def store_tile(dst, stage2, t0):
            for j in range(4):
                for hb in range(2):
                    pb = bank()
                    for cc in range(4):
                        c = hb * 4 + cc
                        P.tr(pb[:, cc * 128:(cc + 1) * 128], xT[:, c, t0 + j * 128:t0 + (j + 1) * 128], ident[:, :])
                    P.cp(evac_eng(), stage2[:, j % 2, hb * 512:(hb + 1) * 512], pb[:, :])
                P.dma(dst[t0 + j * 128:t0 + (j + 1) * 128, :], stage2[:, j % 2, :], out_dma=True)

        def stage3(l, m, NT, mbuf, wg, wout, hooks=None, nxt=None):
            oT = AR.f32("oT", [128, 8, 512])
            hs3 = alloc_hs()
            sg = [AR.bf("sg%d" % i, [128, 512]) for i in range(2)]
            if nxt is not None:
                prefetch_w1(*nxt)
            stage2 = None
            spend = [None]
            if store_state["dst"] is not None:
                try:
                    stage2 = AR.f32("xst2", [128, 2, 1024])
                except RuntimeError:
                    stage2 = None
            tiles = list(range(0, NT, 512))
            n = 512
            pipe = len(hs3) > 1
            if pipe:
                modulate(hs3[0][0], hs3[0][1], l, m, tiles[0], n)
            for ti, t0 in enumerate(tiles):
                if hooks and ti in hooks:
                    hooks[ti]()
                hT, sq = hs3[ti % len(hs3)]
                if not pipe:
                    modulate(hT, sq, l, m, t0, n)
                nparts = []
                if pipe and ti + 1 < len(tiles):
                    nh, nsq = hs3[(ti + 1) % 2]
                    nparts = modulate_parts(nh, nsq, l, m, tiles[ti + 1], n)
                for mc in range(8):
                    pb = bank()
                    for k in range(8):
                        P.mm(pb[:, :n], wg[:, k, mc * 128:(mc + 1) * 128], hT[:, k, :n], start=(k == 0), stop=(k == 7))
                    s_ = sg[mc % 2]
                    P.act(s_[:, :n], pb[:, :n], AF.Silu)
                    P.tt("dve" if mc % 4 else "pool", mbuf[:, mc, t0:t0 + n], mbuf[:, mc, t0:t0 + n], s_[:, :n], ALU.mult)
                    if mc == 1:
                        run_part(nparts)
                if spend[0] is not None:
                    spend[0]()
                    spend[0] = None
                for dc in range(8):
                    pb = bank()
                    for k in range(8):
                        P.mm(pb[:, :n], wout[:, k, dc * 128:(dc + 1) * 128], mbuf[:, k, t0:t0 + n],
                             start=(k == 0), stop=(k == 7))
                    P.cp("dve", oT[:, dc, :n], pb[:, :n])
                    P.act(sq[:, dc, :n], pb[:, :n], AF.Square)
                    run_part(nparts)
                run_part(nparts, 16)
                pb = bank()
                for dc in range(8):
                    P.mm(pb[:, :n], ones_bf[:, :], sq[:, dc, :n], start=(dc == 0), stop=(dc == 7))
                rstd_from_ssq(pb[:, :n], n, 1024, rstd2[:, :n])
                for dc in range(8):
                    tf = tmpf[dc % 2]
                    P.stt("dve", tf[:, :n], oT[:, dc, :n], coefG[:, l, m, dc:dc + 1], rstd2[:, :n], ALU.mult, ALU.mult)
                    P.tt("dve", xT[:, dc, t0:t0 + n], xT[:, dc, t0:t0 + n], tf[:, :n], ALU.add)
                if stage2 is not None:
                    spend[0] = (lambda t0=t0: store_tile(store_state["dst"], stage2, t0))
            if hooks and "post" in hooks:
                hooks["post"]()
            if stage2 is not None:
                if spend[0] is not None:
                    spend[0]()
                AR.release("xst2")
                store_state["done"] = True
            free_hs(hs3)
            for nm in ("oT", "sg0", "sg1"):
                AR.release(nm)

        rope_rr = [0]

        def rope_a(src_ps, pr, n, scale, pm, cs, t0, rb):
            qc, qsn = rb[rope_rr[0] % len(rb)]
            rope_rr[0] += 1
            P.stt("dve", qc[0:pr, :n], src_ps[0:pr, :n], float(scale), cs[0:pr, 0, t0:t0 + n], ALU.mult, ALU.mult)
            P.stt("dve", qsn[0:pr, :n], src_ps[0:pr, :n], float(scale), cs[0:pr, 1, t0:t0 + n], ALU.mult, ALU.mult)

            def phase_b():
                p2 = bank("g")
                P.mm(p2[0:pr, :n], identb[0:pr, 0:pr], qc[0:pr, :n], start=True, stop=False)
                P.mm(p2[0:pr, :n], pm[:, :], qsn[0:pr, :n], start=False, stop=True)
                return p2
            return phase_b

        def rope_apply(src_ps, pr, n, scale, pm, cs, t0, rb):
            return rope_a(src_ps, pr, n, scale, pm, cs, t0, rb)()

        def alloc_rb():
            return [(AR.bf("rqc%d" % i, [128, 512]), AR.bf("rqs%d" % i, [128, 512])) for i in range(2)]

        def free_rb():
            for i in range(2):
                AR.release("rqc%d" % i)
                AR.release("rqs%d" % i)

        pref = {}

        def prefetch_w1(lnext, ctx_next, full):
            i2 = lnext // 2
            if (not full) and lnext % 2 == 1:
                return
            try:
                if lnext % 2 == 0:
                    wv = wine_d[i2].rearrange("(c p) n -> p c n", p=128)
                    a = AR.bf("w1a", [128, 8, 672])
                    load_w(a, wv, 0, 672)
                    pref["w1a"] = a
                    if full:
                        b_ = AR.bf("w1b", [128, 8, 512])
                        load_w(b_, wv, 1184, 512)
                        pref["w1b"] = b_
                else:
                    wv = wino_d[i2].rearrange("(c p) n -> p c n", p=128)
                    a = AR.bf("wq0", [128, 8, 512])
                    load_w(a, wv, 0, 512)
                    pref["wq0"] = a
                    if full:
                        b_ = AR.bf("wq1", [128, 8, 512])
                        load_w(b_, wv, 512, 512)
                        pref["wq1"] = b_
                        k_ = AR.bf("wk", [128, 8, 256])
                        load_w(k_, wv, 1024, 256)
                        pref["wk"] = k_
            except RuntimeError:
                pass

        def even_layer(l, m, NT, nseq, T, ctx, nxt=None):
            i = l // 2
            S = T + (256 if ctx else 0)
            KT = nseq * S
            nkc = S // 128
            wv_in = wine_d[i].rearrange("(c p) n -> p c n", p=128)
            w1a = pref.pop("w1a", None)
            if w1a is None:
                w1a = AR.bf("w1a", [128, 8, 672])
                load_w(w1a, wv_in, 0, 672)
            w1b = pref.pop("w1b", None)
            if w1b is None:
                w1b = AR.bf("w1b", [128, 8, 512])
                load_w(w1b, wv_in, 1184, 512)
            wp = AR.bf("wp", [128, 4, 128])
            amats = AR.bf("amats", [128, 20, 128])
            P.dma(amats[:], amats_d.rearrange("a k q -> k a q"), eng="pool")
            P.dma(wp[:], wpool_d[i].rearrange("g c e -> c g e"), eng="pool")
            mbuf = AR.bf("mbuf", [128, 8, NT])
            mo = AR.live["mbuf"][0]
            vtok = arena_t[:, mo:mo + (NT // 128) * 512].rearrange("p (j q) -> p j q", q=512)
            cqn = AR.bf("cqn", [128, 3, NT])
            ckvn = AR.bf("ckvn", [128, 2, KT])
            krT = AR.bf("krT", [96, KT])
            cs = None
            if ctx:
                cs = AR.bf("cs", [96, 2, 2048])
                P.dma(cs[:, 0, :], mlacs_d[0], eng="pool")
                P.dma(cs[:, 1, :], mlacs_d[1], eng="pool")
                rb = alloc_rb()
                cst = AR.f32("cst", [128, 2, 256 + 96])
                P.memset("pool", cst[:, :, 256:320], 0.0)
                for jj in range(2):
                    P.dma(cst[:, jj, 0:256], cckv_d[i, jj * 128:(jj + 1) * 128, :])
                    P.dma(cst[:, jj, 320:352], ckr_d[i, jj * 128:(jj + 1) * 128, :])
                for jj in range(2):
                    pb = bank()
                    for c in range(2):
                        P.tr(pb[:, c * 128:(c + 1) * 128], cst[:, jj, c * 128:(c + 1) * 128], ident[:, :])
                    P.tr(pb[0:96, 256:384], cst[:, jj, 256:352], ident[:, :])
                    for c in range(2):
                        P.cp("dve", ckvn[:, c, T + jj * 128:T + (jj + 1) * 128], pb[:, c * 128:(c + 1) * 128])
                    P.cp("dve", krT[64:96, T + jj * 128:T + (jj + 1) * 128], pb[64:96, 256:384])
                AR.release("cst")
            stg = None
            if not ctx:
                stg = [AR.f32("stg%d" % j, [128, 288]) for j in range(2)]
            junk = AR.f32("junk", [128, 256])
            ssq1 = AR.f32("ssq1", [128, 2])
            hs1 = alloc_hs()

            def kidx(t):
                return (t // T) * S + (t % T)

            pipe1 = len(hs1) > 1
            if pipe1:
                modulate(hs1[0][0], hs1[0][1], l, m, 0, 512)
            for t0 in range(0, NT, 512):
                n = 512
                hT, sq = hs1[(t0 // 512) % len(hs1)]
                if not pipe1:
                    modulate(hT, sq, l, m, t0, n)
                nparts = []
                if pipe1 and t0 + 512 < NT:
                    nh, nsq = hs1[((t0 // 512) + 1) % 2]
                    nparts = modulate_parts(nh, nsq, l, m, t0 + 512, 512)
                for oc in range(3):
                    pb = bank()
                    for k in range(8):
                        P.mm(pb[:, :n], w1a[:, k, oc * 128:(oc + 1) * 128], hT[:, k, :n], start=(k == 0), stop=(k == 7))
                    P.act(sq[:, oc, :n], pb[:, :n], AF.Square)
                    P.ts("dve", cqn[:, oc, t0:t0 + n], pb[:, :n], vecT[:, oc, 8 + i:9 + i], ALU.mult)
                    run_part(nparts)
                pieces = [(t0, n)] if T >= 512 else [(t0 + a, T) for a in range(0, n, T)]
                for oc in range(2):
                    pb = bank()
                    for k in range(8):
                        P.mm(pb[:, :n], w1a[:, k, 384 + oc * 128:384 + (oc + 1) * 128], hT[:, k, :n],
                             start=(k == 0), stop=(k == 7))
                    P.act(sq[:, 4 + oc, :n], pb[:, :n], AF.Square)
                    for (ta, tn) in pieces:
                        P.ts("dve", ckvn[:, oc, kidx(ta):kidx(ta) + tn], pb[:, ta - t0:ta - t0 + tn],
                             vecT[:, oc, 10 + i:11 + i], ALU.mult)
                    run_part(nparts)
                pb = bank()
                for oc in range(3):
                    P.mm(pb[:, :n], ones_bf[:, :], sq[:, oc, :n], start=(oc == 0), stop=(oc == 2))
                rstd_from_ssq(pb[:, :n], n, 384, rstd2[:, :n])
                for oc in range(3):
                    P.tt("pool" if oc == 1 else "dve", cqn[:, oc, t0:t0 + n], cqn[:, oc, t0:t0 + n], rstd2[:, :n], ALU.mult)
                pb = bank()
                for k in range(8):
                    P.mm(pb[0:96, :n], w1a[:, k, 576:672], hT[:, k, :n], start=(k == 0), stop=(k == 7))
                if ctx:
                    p2 = rope_apply(pb, 96, n, 1.0, pm96, cs, t0, rb)
                    P.cp("act", krT[64:96, kidx(t0):kidx(t0) + n], p2[64:96, :n])
                else:
                    for (ta, tn) in pieces:
                        P.cp("dve", krT[64:96, kidx(ta):kidx(ta) + tn], pb[64:96, ta - t0:ta - t0 + tn])
                pb = bank()
                for oc in range(2):
                    P.mm(pb[:, :n], ones_bf[:, :], sq[:, 4 + oc, :n], start=(oc == 0), stop=(oc == 1))
                rstd_from_ssq(pb[:, :n], n, 256, rstd2[:, :n])
                for oc in range(2):
                    for (ta, tn) in pieces:
                        P.tt("pool" if oc == 1 else "dve", ckvn[:, oc, kidx(ta):kidx(ta) + tn],
                             ckvn[:, oc, kidx(ta):kidx(ta) + tn], rstd2[:, ta - t0:ta - t0 + tn], ALU.mult)
                for j in range(n // 128):
                    pb = bank()
                    for k in range(8):
                        P.mm(pb[:, :], hT[:, k, j * 128:(j + 1) * 128], w1b[:, k, :], start=(k == 0), stop=(k == 7))
                    P.cp(evac_eng(), vtok[:, (t0 // 128) + j, :], pb[:, :])
                    run_part(nparts)
                run_part(nparts, 16)
                if not ctx:
                    for j in range(n // 128):
                        tok = t0 + j * 128
                        b = tok // T
                        pos = tok % T
                        pb = bank()
                        for k in range(8):
                            P.mm(pb[:, 0:288], hT[:, k, j * 128:(j + 1) * 128], w1a[:, k, 384:672],
                                 start=(k == 0), stop=(k == 7))
                        so = stg[j % 2]
                        P.act(junk[:, :], pb[:, 0:256], AF.Square)
                        P.op("dve", lambda e: e.reduce_sum(out=ssq1[:, 0:1], in_=junk[:, :], axis=mybir.AxisListType.X),
                             reads=[junk[:, :]], writes=[ssq1[:, 0:1]])
                        P.act(ssq1[:, 1:2], ssq1[:, 0:1], AF.Ln, bias=epsT[:, 0:1], scale=1.0 / 256)
                        P.act(ssq1[:, 1:2], ssq1[:, 1:2], AF.Exp, scale=-0.5)
                        P.stt("dve", so[:, 0:256], pb[:, 0:256], ssq1[:, 1:2], gkvb[:, i, :], ALU.mult, ALU.mult)
                        P.cp("dve", so[:, 256:288], pb[:, 256:288])
                        P.dma(sckv_d[b, i, pos:pos + 128, :], so[:, 0:256], out_dma=True)
                        P.dma(skr_d[b, i, pos:pos + 128, :], so[:, 256:288], out_dma=True)
            free_hs(hs1)
            pooled = [AR.bf("pooled%d" % j, [128, 512]) for j in range(2)]
            ppend = [None]
            ncs = T // 128
            for t0 in range(0, NT, 512):
                for g in range(4):
                    pb = bank()
                    for j in range(4):
                        ch = t0 // 128 + j
                        cin = ch % ncs
                        contrib = []
                        if cin > 0:
                            contrib.append((ch - 1, 3))
                        contrib.append((ch, 0 if cin == 0 else (2 if cin == ncs - 1 else 1)))
                        if cin < ncs - 1:
                            contrib.append((ch + 1, 4))
                        for ci, (src, kind) in enumerate(contrib):
                            P.mm(pb[:, j * 128:(j + 1) * 128], vtok[:, src, g * 128:(g + 1) * 128],
                                 amats[:, g * 5 + kind, :], start=(ci == 0), stop=(ci == len(contrib) - 1))
                    pl = pooled[g % 2]
                    P.cp(evac_eng(), pl[:, :], pb[:, :])
                    if ppend[0] is not None:
                        ppend[0]()

                    def _pw(pl=pl, g=g, t0=t0):
                        pb2 = bank()
                        P.mm(pb2[:, :], wp[:, g, :], pl[:, :])
                        P.ts("dve", mbuf[:, 4 + g, t0:t0 + 512], pb2[:, :], vecT[:, g, 12 + i:13 + i], ALU.mult)
                    ppend[0] = _pw
            if ppend[0] is not None:
                ppend[0]()
                ppend[0] = None
            for nm in ("pooled0", "pooled1", "junk", "ssq1", "w1a", "w1b", "wp", "amats"):
                AR.release(nm)
            if not ctx:
                AR.release("stg0")
                AR.release("stg1")
            if stages == 1 and l == trunc_l[0]:
                raise _Trunc()
            ada_hooks = None
            if ADA_INTERLEAVE and (not ctx) and l + 1 < 4:
                ada_alloc(l + 1, 2)
                ada_issue([0, 1])

                def _h0():
                    ada_compute([0, 1])
                    ada_issue([2, 3])

                def _h1():
                    ada_compute([2, 3])
                    ada_issue([4, 5])

                def _hp():
                    ada_compute([4, 5])
                    ada_finish()
                ada_hooks = {0: _h0, 1: _h1, "post": _hp}
            wuq = AR.bf("wuq", [128, 3, 768])
            wuk = AR.bf("wuk", [128, 2, 8, 64])
            wuv = AR.bf("wuv", [128, 2, 8, 64])
            P.dma(wuq[:], wuq_d[i].rearrange("(c p) n -> p c n", p=128), eng="pool")
            wukv_v = wukv_d[i].rearrange("(c p) (h t d) -> p c h t d", p=128, h=8, t=2)
            for c in range(2):
                P.dma(wuk[:, c, :, :], wukv_v[:, c, :, 0, :], eng="pool")
                P.dma(wuv[:, c, :, :], wukv_v[:, c, :, 1, :], eng="pool")
            def load_wg():
                wg_ = AR.bf("wg", [128, 8, 1024])
                load_w(wg_[:, :, 0:512], wv_in, 672, 512)
                load_w(wg_[:, :, 512:1024], wv_in, 1696, 512)
                return wg_

            def load_wout():
                wout_ = AR.bf("wout", [128, 8, 1024])
                load_w(wout_, woe_d[i].rearrange("(c p) n -> p c n", p=128), 0, 1024)
                return wout_
            wg = load_wg()
            if not ctx:
                wout = load_wout()
            NB = 2
            TT = nseq * T
            nkt = KT // 128
            qh = [AR.bf("qh%d" % b, [96, TT]) for b in range(NB)]
            kh = [AR.bf("kh%d" % b, [96, KT]) for b in range(NB)]
            vh = [AR.bf("vh%d" % b, [128, nkt, 128]) for b in range(NB)]
            pT = [AR.bf("pT%d" % b, [128, 512]) for b in range(4)]
            rc = AR.f32("rc", [64, 512])
            for b in range(NB):
                P.memset("pool", vh[b][:, :, 64:128], 1.0)
                P.cp("pool", kh[b][64:96, :], krT[64:96, 0:KT])
            pend = [None]

            def flush_pend():
                if pend[0] is not None:
                    pend[0]()
                    pend[0] = None

            def build_steps(h, b):
                st_ = []
                for ka in range(0, KT, 512):
                    def f(ka=ka):
                        kn = min(512, KT - ka)
                        pb = bank("g")
                        for c in range(2):
                            P.mm(pb[0:64, :kn], wuk[:, c, h, :], ckvn[:, c, ka:ka + kn], start=(c == 0), stop=(c == 1))
                        P.cp("dve" if ctx else "act", kh[b][0:64, ka:ka + kn], pb[0:64, :kn])
                    st_.append(f)
                for ja in range(0, nkt, 8):
                    def f(ja=ja):
                        jn = min(8, nkt - ja)
                        pb = bank("g")
                        for jj in range(jn):
                            for c in range(2):
                                P.mm(pb[:, jj * 64:(jj + 1) * 64], ckvn[:, c, (ja + jj) * 128:(ja + jj + 1) * 128],
                                     wuv[:, c, h, :], start=(c == 0), stop=(c == 1))
                        P.cp("dve" if ctx else "act", vh[b][:, ja:ja + jn, 0:64], pb[:, 0:jn * 64].rearrange("p (j d) -> p j d", d=64))
                    st_.append(f)
                for qa in range(0, TT, 512):
                    hold = {}

                    def f(qa=qa, hold=hold):
                        pb = bank("g")
                        for c in range(3):
                            P.mm(pb[0:96, :512], wuq[:, c, h * 96:(h + 1) * 96], cqn[:, c, qa:qa + 512],
                                 start=(c == 0), stop=(c == 2))
                        if ctx:
                            hold["b"] = rope_a(pb, 96, 512, MLA_SCALE, pm96, cs, qa, rb)
                        else:
                            P.act(qh[b][:, qa:qa + 512], pb[0:96, :512], AF.Copy, scale=float(MLA_SCALE))
                    st_.append(f)
                    if ctx:
                        def f2(qa=qa, hold=hold):
                            p2 = hold["b"]()
                            P.cp("dve", qh[b][0:96, qa:qa + 512], p2[0:96, :512])
                        st_.append(f2)
                return st_

            for f in build_steps(0, 0):
                f()
            for h in range(8):
                b = h % NB
                half = h % 2
                inj = build_steps(h + 1, (h + 1) % NB) if h + 1 < 8 else []
                if ctx:
                    total_steps = (T // 512) * nkc
                    every = max(1, total_steps // (len(inj) + 1))
                    stepc = 0
                    for qa in range(0, T, 512):
                        po = bank("o")
                        sc_ps = {}

                        def issue_s(j):
                            ps_ = bank("s")
                            P.mm(ps_[:, :512], kh[b][:, j * 128:(j + 1) * 128], qh[b][:, qa:qa + 512])
                            sc_ps[j] = ps_

                        for j in range(3):
                            issue_s(j)
                        for j in range(nkc):
                            pt = pT[j % 4]
                            P.act(pt[:, :], sc_ps.pop(j)[:, :], AF.Exp)
                            if j + 3 < nkc:
                                issue_s(j + 3)
                            P.mm(po[:, :], vh[b][:, j, :], pt[:, :], start=(j == 0), stop=(j == nkc - 1))
                            stepc += 1
                            if inj and stepc % every == 0:
                                inj.pop(0)()
                        P.recip(rc[0:64, :], po[64:128, :])
                        P.tt("dve", mbuf[half * 64:(half + 1) * 64, h // 2, qa:qa + 512],
                             po[0:64, :], rc[0:64, :], ALU.mult)
                else:
                    for p in range(nseq // 2):
                        pts = []
                        for a in range(2):
                            sq_ = 2 * p + a
                            ps_ = bank("s")
                            for j in range(2):
                                P.mm(ps_[:, j * 256:(j + 1) * 256], kh[b][:, sq_ * 256 + j * 128:sq_ * 256 + (j + 1) * 128],
                                     qh[b][:, sq_ * 256:(sq_ + 1) * 256])
                            pt = pT[(2 * (p + h * (nseq // 2)) + a) % 4]
                            P.act(pt[:, :], ps_[:, :], AF.Exp)
                            pts.append(pt)
                        flush_pend()
                        for _ in range(3):
                            if inj:
                                inj.pop(0)()

                        def fin(p=p, pts=pts, b=b, h=h, half=half):
                            po = bank("o")
                            for a in range(2):
                                sq_ = 2 * p + a
                                for j in range(2):
                                    P.mm(po[:, a * 256:(a + 1) * 256], vh[b][:, 2 * sq_ + j, :], pts[a][:, j * 256:(j + 1) * 256],
                                         start=(j == 0), stop=(j == 1))
                            P.cp("dve", rc[0:64, :], po[64:128, :])
                            P.act(rc[0:64, :], rc[0:64, :], AF.Ln)
                            P.act(rc[0:64, :], rc[0:64, :], AF.Exp, scale=-1.0)
                            P.tt("dve", mbuf[half * 64:(half + 1) * 64, h // 2, p * 512:(p + 1) * 512],
                                 po[0:64, :], rc[0:64, :], ALU.mult)
                        pend[0] = fin
                while inj:
                    inj.pop(0)()
            flush_pend()
            for nm in ["qh%d" % b for b in range(NB)] + ["kh%d" % b for b in range(NB)] + ["vh%d" % b for b in range(NB)] + \
                      ["pT%d" % b for b in range(4)] + ["rc", "wuq", "wuk", "wuv", "cqn", "ckvn", "krT"]:
                AR.release(nm)
            if ctx:
                AR.release("cs")
                free_rb()
                wout = load_wout()
            if stages == 2 and l == trunc_l[0]:
                raise _Trunc()
            stage3(l, m, NT, mbuf, wg, wout, ada_hooks, nxt)
            for nm in ("wg", "wout", "mbuf"):
                AR.release(nm)

        def odd_layer(l, m, NT, nseq, T, ctx, nxt=None):
            i = l // 2
            S = T + (256 if ctx else 0)
            KT = nseq * S
            nkc = S // 128
            wv_in = wino_d[i].rearrange("(c p) n -> p c n", p=128)
            qT = AR.bf("mbuf", [128, 8, NT])
            kd = AR.bf("kd", [128, 4, KT])
            va = AR.bf("va", [128, KT // 128, 4, 128])
            wq0 = pref.pop("wq0", None)
            if wq0 is None:
                wq0 = AR.bf("wq0", [128, 8, 512])
                load_w(wq0, wv_in, 0, 512)
            wq1 = pref.pop("wq1", None)
            if wq1 is None:
                wq1 = AR.bf("wq1", [128, 8, 512])
                load_w(wq1, wv_in, 512, 512)
            wqs = [wq0, wq1]
            wk = pref.pop("wk", None)
            if wk is None:
                wk = AR.bf("wk", [128, 8, 256])
                load_w(wk, wv_in, 1024, 256)
            nkv = 256 if ctx else 512
            wkv = AR.bf("wkv", [128, 8, nkv])
            load_w(wkv, wv_in, 1536 - nkv, nkv)
            P.memset("pool", va[:, :, :, 64:128], 1.0)
            cs = None
            qs = None
            if ctx:
                cs = AR.bf("cs", [128, 2, 2048])
                P.dma(cs[:, 0, :], swacs_d[0], eng="pool")
                P.dma(cs[:, 1, :], swacs_d[1], eng="pool")
                rb = alloc_rb()
                cst = AR.f32("cst", [128, 2, 256])
                cdup = AR.f32("cdup", [128, 4, 128])
                for jj in range(2):
                    P.dma(cst[:, jj, :], cv_d[i, jj * 128:(jj + 1) * 128, :])
                for jj in range(2):
                    P.cp("dve", va[:, T // 128 + jj, :, 0:64], cst[:, jj, :].rearrange("p (h d) -> p h d", h=4))
                for jj in range(2):
                    P.dma(cst[:, jj, :], ck_d[i, jj * 128:(jj + 1) * 128, :])
                for jj in range(2):
                    P.cp("dve", cdup[:, :, 0:64], cst[:, jj, :].rearrange("p (h d) -> p h d", h=4))
                    P.cp("pool", cdup[:, :, 64:128], cst[:, jj, :].rearrange("p (h d) -> p h d", h=4))
                    pb = bank()
                    for kvh in range(4):
                        P.tr(pb[:, kvh * 128:(kvh + 1) * 128], cdup[:, kvh, :], ident[:, :])
                    for kvh in range(4):
                        P.cp("dve", kd[:, kvh, T + jj * 128:T + (jj + 1) * 128], pb[:, kvh * 128:(kvh + 1) * 128])
                AR.release("cst")
                AR.release("cdup")
            stg = None
            if not ctx:
                stg = [AR.f32("stg%d" % j, [128, 512]) for j in range(2)]
            if ctx:
                hs1 = [(AR.bf("hT", [128, 8, 512]), AR.bf("sq", [128, 4, 512])),
                       (AR.bf("hT1", [128, 8, 512]), AR.bf("sq1", [128, 4, 512]))]
            else:
                hs1 = alloc_hs()

            def kidx(t):
                return (t // T) * S + (t % T)

            pipe1 = len(hs1) > 1
            rpend = [None]
            if pipe1:
                modulate(hs1[0][0], hs1[0][1], l, m, 0, 512)
            for t0 in range(0, NT, 512):
                n = 512
                hT, sq = hs1[(t0 // 512) % len(hs1)]
                if not pipe1:
                    modulate(hT, sq, l, m, t0, n)
                pieces = [(t0, n)] if T >= 512 else [(t0 + a, T) for a in range(0, n, T)]
                nparts = []
                if pipe1 and t0 + 512 < NT:
                    nh, nsq = hs1[((t0 // 512) + 1) % 2]
                    nparts = modulate_parts(nh, nsq, l, m, t0 + 512, 512)
                for oc in range(8):
                    pb = bank()
                    for k in range(8):
                        P.mm(pb[:, :n], wqs[oc // 4][:, k, (oc % 4) * 128:(oc % 4 + 1) * 128], hT[:, k, :n],
                             start=(k == 0), stop=(k == 7))
                    if ctx:
                        pb_fn = rope_a(pb, 128, n, SWA_SCALE, pm128, cs, t0, rb)
                        if rpend[0] is not None:
                            rpend[0]()

                        def _fin_q(pb_fn=pb_fn, oc=oc, t0=t0, n=n):
                            p2 = pb_fn()
                            P.cp("act", qT[:, oc, t0:t0 + n], p2[:, :n])
                        rpend[0] = _fin_q
                    else:
                        P.ts("dve", qT[:, oc, t0:t0 + n], pb[:, :n], SWA_SCALE, ALU.mult)
                    run_part(nparts)
                for kc in range(2):
                    pb = bank()
                    for k in range(8):
                        P.mm(pb[:, :n], wk[:, k, kc * 128:(kc + 1) * 128], hT[:, k, :n], start=(k == 0), stop=(k == 7))
                    def _copies(srcs, kc=kc):
                        for (src, so, sn, ko) in srcs:
                            for hh in range(2):
                                for dh in range(2):
                                    P.cp("act" if dh != hh else "dve",
                                         kd[dh * 64:(dh + 1) * 64, 2 * kc + hh, ko:ko + sn],
                                         src[hh * 64:(hh + 1) * 64, so:so + sn])
                    if ctx:
                        pb_fn = rope_a(pb, 128, n, 1.0, pm128, cs, t0, rb)
                        if rpend[0] is not None:
                            rpend[0]()

                        def _fin_k(pb_fn=pb_fn, t0=t0, n=n, _copies=_copies):
                            p2 = pb_fn()
                            _copies([(p2, 0, n, kidx(t0))])
                        rpend[0] = _fin_k
                    else:
                        _copies([(pb, ta - t0, tn, kidx(ta)) for (ta, tn) in pieces])
                run_part(nparts, 16)
                for j in range(n // 128):
                    tok = t0 + j * 128
                    pb = bank()
                    for k in range(8):
                        P.mm(pb[:, 0:nkv], hT[:, k, j * 128:(j + 1) * 128], wkv[:, k, :], start=(k == 0), stop=(k == 7))
                    P.cp("dve", va[:, kidx(tok) // 128, :, 0:64], pb[:, nkv - 256:nkv].rearrange("p (h d) -> p h d", h=4))
                    if not ctx:
                        b = tok // T
                        pos = tok % T
                        so = stg[j % 2]
                        P.cp("act", so[:, :], pb[:, :])
                        P.dma(sk_d[b, i, pos:pos + 128, :], so[:, 0:256], out_dma=True)
                        P.dma(sv_d[b, i, pos:pos + 128, :], so[:, 256:512], out_dma=True)
                    if j == 0 and rpend[0] is not None:
                        rpend[0]()
                        rpend[0] = None
            free_hs(hs1)
            for nm in ("wq0", "wq1", "wk", "wkv"):
                AR.release(nm)
            if ctx:
                AR.release("cs")
                free_rb()
            else:
                AR.release("stg0")
                AR.release("stg1")
            if stages == 1 and l == trunc_l[0]:
                raise _Trunc()
            ada_hooks = None
            if ADA_INTERLEAVE and (not ctx) and l + 1 < 4:
                ada_alloc(l + 1, 2)
                ada_issue([0, 1])

                def _h0():
                    ada_compute([0, 1])
                    ada_issue([2, 3])

                def _h1():
                    ada_compute([2, 3])
                    ada_issue([4, 5])

                def _hp():
                    ada_compute([4, 5])
                    ada_finish()
                ada_hooks = {0: _h0, 1: _h1, "post": _hp}
            vaB = AR.bf("vaB", [128, KT // 128, 4, 128])
            wg = AR.bf("wg", [128, 8, 1024])
            wout = AR.bf("wout", [128, 8, 1024])
            load_w(wg, wv_in, 1536, 1024)
            load_w(wout, woo_d[i].rearrange("(c p) n -> p c n", p=128), 0, 1024)
            pT = [AR.bf("pT%d" % b, [128, 512]) for b in range(4)]
            rc = AR.f32("rc", [128, 512])
            P.memset("pool", vaB[:, :, :, 0:64], 1.0)
            nch_all = KT // 128
            for ja in range(0, nch_all, 6):
                jb = min(nch_all, ja + 6)
                P.cp("pool", vaB[:, ja:jb, :, 64:128], va[:, ja:jb, :, 0:64])
            mbuf = qT
            pools["o4"] = [4, 5, 6, 7]
            rr["o4"] = 0
            pend = []

            def flush_pend(keep=0):
                while len(pend) > keep:
                    pend.pop(0)()

            def finalize_pair(c, pos, t_lo):
                hA, hB = 2 * c, 2 * c + 1
                P.ts("dve", rc[0:64, :], pos[0][64:128, :], esink[64:128, i * 16 + hA:i * 16 + hA + 1], ALU.add)
                P.ts("dve", rc[64:128, :], pos[1][0:64, :], esink[0:64, i * 16 + hB:i * 16 + hB + 1], ALU.add)
                if ctx:
                    P.recip(rc[:, :], rc[:, :])
                else:
                    P.act(rc[:, :], rc[:, :], AF.Ln)
                    P.act(rc[:, :], rc[:, :], AF.Exp, scale=-1.0)
                P.tt("dve", mbuf[0:64, c, t_lo:t_lo + 512], pos[0][0:64, :], rc[0:64, :], ALU.mult)
                P.tt("dve", mbuf[64:128, c, t_lo:t_lo + 512], pos[1][64:128, :], rc[64:128, :], ALU.mult)

            ucount = 0
            for c in range(8):
                kvh = c // 2
                ntile = (nseq // 2) if not ctx else (T // 512)
                for tix in range(ntile):
                    pos = []
                    if ctx:
                        qt = tix
                        q0 = qt * 512
                        pos = [bank("o4"), bank("o4")]
                        jobs = []
                        for jj in range(2):
                            jobs.append((T // 128 + jj, 0, 512, []))
                        for j in range(4 * qt - 1, 4 * qt + 5):
                            if j < 0 or j >= T // 128:
                                continue
                            nlo = max(4 * qt, j - 1)
                            nhi = min(4 * qt + 3, j + 1)
                            mk = []
                            for nb in range(nlo, nhi + 1):
                                if nb == j - 1:
                                    mk.append((nb, 0))
                                elif nb == j + 1:
                                    mk.append((nb, 1))
                            jobs.append((j, (nlo - 4 * qt) * 128, (nhi - 4 * qt + 1) * 128, mk))
                        sc_ps = {}

                        def issue_s(ji):
                            kc, lo, hi, mk_ = jobs[ji]
                            for half in (0, 1):
                                r0, r1 = half * 64, half * 64 + 64
                                ps_ = bank("s")
                                P.mm(ps_[:, lo:hi], kd[r0:r1, kvh, kc * 128:(kc + 1) * 128], qT[r0:r1, c, q0 + lo:q0 + hi],
                                     start=True, stop=(len(mk_) == 0))
                                sc_ps[(ji, half)] = ps_
                            for half in (0, 1):
                                for mi, (nb, which) in enumerate(mk_):
                                    cl = (nb - 4 * qt) * 128
                                    P.mm(sc_ps[(ji, half)][:, cl:cl + 128], identb[:, :], masks[:, which, :],
                                         start=False, stop=(mi == len(mk_) - 1))

                        for ji in range(min(2, len(jobs))):
                            issue_s(ji)
                        for ji, (kc, lo, hi, mk) in enumerate(jobs):
                            pts = []
                            for half in (0, 1):
                                pt = pT[(2 * ji + half) % 4]
                                P.act(pt[:, lo:hi], sc_ps.pop((ji, half))[:, lo:hi], AF.Exp)
                                pts.append(pt)
                            if ji + 2 < len(jobs):
                                issue_s(ji + 2)
                            for half in (0, 1):
                                vsrc = va if half == 0 else vaB
                                P.mm(pos[half][:, lo:hi], vsrc[:, kc, kvh, :], pts[half][:, lo:hi],
                                     start=(ji == 0), stop=(ji == len(jobs) - 1))
                            if ji == 2:
                                flush_pend()
                        pend.append(lambda c=c, pos=pos, q0=q0: finalize_pair(c, pos, q0))
                        continue
                    for half in (0, 1):
                        r0, r1 = half * 64, half * 64 + 64
                        vsrc = va if half == 0 else vaB
                        p = tix
                        pts = []
                        for a in range(2):
                            sq_ = 2 * p + a
                            ps_ = bank("s")
                            for j in range(2):
                                P.mm(ps_[:, j * 256:(j + 1) * 256], kd[r0:r1, kvh, sq_ * 256 + j * 128:sq_ * 256 + (j + 1) * 128],
                                     qT[r0:r1, c, sq_ * 256:(sq_ + 1) * 256])
                            pt = pT[(2 * ucount + a) % 4]
                            P.act(pt[:, :], ps_[:, :], AF.Exp)
                            pts.append(pt)
                        ucount += 1
                        flush_pend()
                        po = bank("o4")
                        pos.append(po)

                        def pv(p=p, pts=pts, po=po, vsrc=vsrc, kvh=kvh):
                            for a in range(2):
                                sq_ = 2 * p + a
                                for j in range(2):
                                    P.mm(po[:, a * 256:(a + 1) * 256], vsrc[:, 2 * sq_ + j, kvh, :], pts[a][:, j * 256:(j + 1) * 256],
                                         start=(j == 0), stop=(j == 1))
                        pend.append(pv)
                        if half == 1:
                            pend.append(lambda c=c, pos=pos, p=p: finalize_pair(c, pos, p * 512))
            flush_pend()
            AR.release("vaB")
            for nm in ["pT%d" % b for b in range(4)] + ["rc", "kd", "va"]:
                AR.release(nm)
            if stages == 2 and l == trunc_l[0]:
                raise _Trunc()
            stage3(l, m, NT, mbuf, wg, wout, ada_hooks, nxt)
            for nm in ("wg", "wout", "mbuf"):
                AR.release(nm)

        def load_x(src, NT):
            stage = AR.f32("xstage", [128, 4, 1024])
            for t0 in range(0, NT, 512):
                for j in range(4):
                    P.dma(stage[:, j, :], src[t0 + j * 128:t0 + (j + 1) * 128, :])
                for c in range(8):
                    pb = bank()
                    for j in range(4):
                        P.tr(pb[:, j * 128:(j + 1) * 128], stage[:, j, c * 128:(c + 1) * 128], ident[:, :])
                    P.cp(evac_eng(), xT[:, c, t0:t0 + 512], pb[:, :])
            AR.release("xstage")

        def store_x(dst, NT):
            stage = AR.f32("xstage", [128, 4, 1024])
            for t0 in range(0, NT, 512):
                for j in range(4):
                    for hb in range(2):
                        pb = bank()
                        for cc in range(4):
                            c = hb * 4 + cc
                            P.tr(pb[:, cc * 128:(cc + 1) * 128], xT[:, c, t0 + j * 128:t0 + (j + 1) * 128], ident[:, :])
                        P.cp(evac_eng(), stage[:, j, hb * 512:(hb + 1) * 512], pb[:, :])
                    P.dma(dst[t0 + j * 128:t0 + (j + 1) * 128, :], stage[:, j, :], out_dma=True)
            AR.release("xstage")

        trunc_l = [-1]
        try:
          for (src, dst, m, NT, nseq, T, ctx) in ((xp_d, yp_d, 0, 1024, 4, 256, False),
                                                   (xs_d, ys_d, 1, 2048, 1, 2048, True)):
              nl = nl_p if not ctx else nl_s
              if nl < 0:
                  continue
              load_x(src, NT)
              if not ada0_done[0]:
                  ada_all(0)
                  ada0_done[0] = True
              trunc_l[0] = nl - 1 if ((ctx and nl_s >= 0) or (not ctx and nl_s < 0)) else -1
              for l in range(nl):
                  if l + 1 < nl:
                      nxt = (l + 1, ctx, not ctx)
                  elif (not ctx) and nl_s > 0:
                      nxt = (0, True, True)
                  else:
                      nxt = None
                  store_state["dst"] = dst if (l == nl - 1 and stages == 3) else None
                  store_state["done"] = False
                  if l % 2 == 0:
                      even_layer(l, m, NT, nseq, T, ctx, nxt)
                  else:
                      odd_layer(l, m, NT, nseq, T, ctx, nxt)
              if not store_state["done"]:
                  store_x(dst, NT)
              store_state["dst"] = None

        except _Trunc:
            P.dma(yp_d[0:128, 0:512], rstd[:, :], out_dma=True)

        P.finish()
        build_program.stats = {e: len(P.ops[e]) for e in ENGS}
        build_program.peak = AR.peak
    return nc


_CONSTS = None


def _consts():
    global _CONSTS
    if _CONSTS is None:
        mla, swa = _rope_tables()
        pm96, pm128 = _perm_mats()
        _CONSTS = dict(ident=np.eye(128, dtype=np.float32), pm96=pm96, pm128=pm128, masks=_masks(),
                       amats=_pool_mats().reshape(20, 128, 128), mla_cs=mla, swa_cs=swa)
    return _CONSTS


def kernel(x_prompt, x_sample, cache_ckv, cache_krope, cache_k, cache_v, c, c_ctx,
           ada_w, ada_b, norm_pre, norm_post,
           mla_w_in, mla_g_qn, mla_g_kvn, mla_w_uq, mla_w_ukv, pool_w, pool_scale, mixa_w_out,
           swa_w_in, swa_sink, swa_w_out):
    f = lambda a: np.ascontiguousarray(np.asarray(a, dtype=np.float32))
    x_prompt, x_sample = f(x_prompt), f(x_sample)
    cache_ckv, cache_krope, cache_k, cache_v = f(cache_ckv), f(cache_krope), f(cache_k), f(cache_v)
    c, c_ctx = f(c), f(c_ctx)
    vecs = np.zeros((14, 1024), np.float32)
    vecs[0:4] = f(norm_pre)
    vecs[4:8] = f(norm_post)
    vecs[8:10, :384] = f(mla_g_qn)
    vecs[10:12, :256] = f(mla_g_kvn)
    vecs[12:14, :512] = f(pool_scale)
    shared = dict(ada_w=f(ada_w), ada_b=f(ada_b), vecs=vecs, gkvn=f(mla_g_kvn), sink=f(swa_sink).reshape(32),
                  w_in_e=f(mla_w_in), w_uq=f(mla_w_uq), w_ukv=f(mla_w_ukv), w_pool=f(pool_w), w_out_e=f(mixa_w_out),
                  w_in_o=f(swa_w_in), w_out_o=f(swa_w_out))
    shared.update(_consts())
    in_maps = []
    for i in range(8):
        d = dict(shared)
        d["xp"] = x_prompt[4 * i:4 * i + 4].reshape(1024, 1024)
        d["xs"] = x_sample[i]
        d["cckv"] = cache_ckv[i]
        d["ckr"] = cache_krope[i]
        d["ck"] = cache_k[i].reshape(2, 256, 256)
        d["cv"] = cache_v[i].reshape(2, 256, 256)
        d["cc"] = np.stack([c_ctx, c[i]], axis=0)
        in_maps.append(d)
    nc = build_program()
    res = run_bass_kernel_spmd(nc, in_maps, core_ids=list(range(8)))
    R = res.results
    y_prompt = np.concatenate([r["yp"].reshape(4, 256, 1024) for r in R], axis=0)
    y_sample = np.stack([r["ys"] for r in R], axis=0)
    st_ckv = np.concatenate([r["st_ckv"] for r in R], axis=0)
    st_kr = np.concatenate([r["st_kr"] for r in R], axis=0)
    st_k = np.concatenate([r["st_k"].reshape(4, 2, 256, 4, 64) for r in R], axis=0)
    st_v = np.concatenate([r["st_v"].reshape(4, 2, 256, 4, 64) for r in R], axis=0)
    return (y_prompt.astype(np.float32), y_sample.astype(np.float32), st_ckv.astype(np.float32),
            st_kr.astype(np.float32), st_k.astype(np.float32), st_v.astype(np.float32))
```

```python
import numpy as np
from contextlib import ExitStack
import concourse.bass as bass
import concourse.mybir as mybir
from concourse.bass_utils import run_bass_kernel_spmd

F32 = mybir.dt.float32
BF16 = mybir.dt.bfloat16
AF = mybir.ActivationFunctionType
ALU = mybir.AluOpType

ENGS = ("pe", "act", "dve", "pool", "sp")
SAME_ENGINE_RAW = True
EMBED_LAST_WAIT = True
EMBED_ENGINES = ("pe", "act", "dve")
N_DMA_SEMS = 24
BUCKET = 4096

D = 1024
EPS = 1e-6
MLA_SCALE = 96 ** -0.5
SWA_SCALE = 64 ** -0.5


class Prog:
    def __init__(self, nc, stack):
        self.nc = nc
        self.stack = stack
        self.ops = {e: [] for e in ENGS}
        self.recs = {}
        self.known = {e: {} for e in ENGS}
        self.eng_sem = {e: stack.enter_context(nc.semaphore("s_" + e)) for e in ENGS}
        self.dma_sems = [stack.enter_context(nc.semaphore("s_dma%d" % i)) for i in range(N_DMA_SEMS)]
        self.dma_cnt = [0] * N_DMA_SEMS
        self.dma_rr = 0
        self.dma_rr_pool = 0
        self.out_dma_tokens = []

    def sb(self, name, shape, dtype):
        return self.stack.enter_context(self.nc.sbuf_tensor("sb_" + name, list(shape), dtype))

    def ps(self, name, shape, dtype=F32):
        return self.stack.enter_context(self.nc.psum_tensor("pp_" + name, list(shape), dtype))

    @staticmethod
    def _box(ap):
        shp = list(ap.tensor.shape)
        row = 1
        for s in shp[1:]:
            row *= s
        isz = mybir.dt.size(ap.dtype)
        off = int(ap.offset)
        dims = ap.ap
        p0 = off // row
        f0 = off % row
        pstep, pcnt = dims[0]
        if pstep == row or (pcnt == 1 and len(dims) > 1):
            p1 = p0 + pcnt
            rest = dims[1:]
        elif pstep == 0:
            p1 = p0 + 1
            rest = dims[1:]
        else:
            p1 = p0 + 1
            rest = dims
        ext = 0
        for st, cn in rest:
            ext += abs(st) * (cn - 1)
        return (p0, p1, f0 * isz, (f0 + ext + 1) * isz)

    @staticmethod
    def _tracked(ap):
        return str(ap.space).upper() in ("SB", "PSUM")

    def op(self, eng, fn, reads=(), writes=(), dma=False, out_dma=False):
        idx = len(self.ops[eng])
        waits = []
        kn = self.known[eng]

        def need(tok, raw=False):
            if tok[0] == 'e':
                if tok[1] == eng and not (raw and SAME_ENGINE_RAW and eng != "pe"):
                    return
                key = tok[1]
            else:
                key = ('d', tok[1])
            if kn.get(key, -1) >= tok[2]:
                return
            kn[key] = tok[2]
            waits.append(tok)

        if dma:
            if eng == "pool":
                k = 8 + self.dma_rr_pool
                self.dma_rr_pool = (self.dma_rr_pool + 1) % (N_DMA_SEMS - 8)
            else:
                k = self.dma_rr
                self.dma_rr = (self.dma_rr + 1) % 8
            if self.dma_cnt[k] > 0:
                need(('d', k, self.dma_cnt[k] * 16))
            self.dma_cnt[k] += 1
            mytok = ('d', k, self.dma_cnt[k] * 16)
        else:
            mytok = ('e', eng, idx)

        rb = [(ap.name, self._box(ap)) for ap in reads if self._tracked(ap) and str(ap.space).upper() != "PSUM"]
        wb = [(ap.name, self._box(ap)) for ap in writes if self._tracked(ap) and str(ap.space).upper() != "PSUM"]
        for ap in list(reads) + list(writes):
            if str(ap.space).upper() == "PSUM":
                ent = (ap.name, (0, 128, 0, 1 << 20))
                if ent not in wb:
                    wb.append(ent)
        for name, box in rb:
            tr = self.recs.get(name)
            if tr is None:
                continue
            for b in range(box[2] // BUCKET, (box[3] - 1) // BUCKET + 1):
                for r in tr.get(b, ()):
                    if r[3] and r[2]:
                        rbx = r[0]
                        if rbx[0] < box[1] and box[0] < rbx[1] and rbx[2] < box[3] and box[2] < rbx[3]:
                            need(r[1], True)
        for name, box in wb:
            tr = self.recs.get(name)
            if tr is None:
                continue
            for b in range(box[2] // BUCKET, (box[3] - 1) // BUCKET + 1):
                lst = tr.get(b)
                if not lst:
                    continue
                for r in lst:
                    if r[3]:
                        rbx = r[0]
                        if rbx[0] < box[1] and box[0] < rbx[1] and rbx[2] < box[3] and box[2] < rbx[3]:
                            need(r[1])
                            if box[0] <= rbx[0] and box[1] >= rbx[1] and box[2] <= rbx[2] and box[3] >= rbx[3]:
                                r[3] = False
                tr[b] = [r for r in lst if r[3]]
        for name, box in rb:
            tr = self.recs.setdefault(name, {})
            rec = [box, mytok, False, True]
            for b in range(box[2] // BUCKET, (box[3] - 1) // BUCKET + 1):
                lst = tr.setdefault(b, [])
                if not dma:
                    for r in lst:
                        if r[3] and (not r[2]) and r[1][0] == 'e' and r[1][1] == eng and r[0] == box:
                            r[3] = False
                lst.append(rec)
        for name, box in wb:
            tr = self.recs.setdefault(name, {})
            rec = [box, mytok, True, True]
            for b in range(box[2] // BUCKET, (box[3] - 1) // BUCKET + 1):
                tr.setdefault(b, []).append(rec)
        o = dict(fn=fn, waits=waits, tok=mytok, marked=False)
        self.ops[eng].append(o)
        if out_dma:
            self.out_dma_tokens.append(mytok)
        return o

    def dma(self, out, in_, eng="sp", out_dma=False):
        return self.op(eng, lambda e: e.dma_start(out=out, in_=in_), reads=[in_], writes=[out],
                       dma=True, out_dma=out_dma)

    def mm(self, out, lhsT, rhs, start=True, stop=True):
        return self.op("pe", lambda e: e.matmul(out, lhsT, rhs, start=start, stop=stop),
                       reads=[lhsT, rhs] + ([] if start else [out]), writes=[out])

    def tr(self, out, in_, ident):
        return self.op("pe", lambda e: e.transpose(out, in_, ident), reads=[in_, ident], writes=[out])

    def act(self, out, in_, func, bias=None, scale=None, accum=None):
        kw = {}
        rd = [in_]
        wr = [out]
        if bias is not None:
            kw["bias"] = bias
            if not isinstance(bias, (int, float)):
                rd.append(bias)
        if scale is not None:
            kw["scale"] = scale
            if not isinstance(scale, (int, float)):
                rd.append(scale)
        if accum is not None:
            kw["accum_out"] = accum
            wr.append(accum)
        return self.op("act", lambda e: e.activation(out=out, in_=in_, func=func, **kw), reads=rd, writes=wr)

    def cp(self, eng, out, in_):
        if eng == "act":
            return self.op("act", lambda e: e.copy(out=out, in_=in_), reads=[in_], writes=[out])
        return self.op(eng, lambda e: e.tensor_copy(out=out, in_=in_), reads=[in_], writes=[out])

    def tt(self, eng, out, in0, in1, op):
        return self.op(eng, lambda e: e.tensor_tensor(out=out, in0=in0, in1=in1, op=op), reads=[in0, in1], writes=[out])

    def ts(self, eng, out, in0, s1, op0, s2=None, op1=None):
        rd = [in0]
        if not isinstance(s1, (int, float)):
            rd.append(s1)
        if s2 is not None and not isinstance(s2, (int, float)):
            rd.append(s2)
        if op1 is None:
            return self.op(eng, lambda e: e.tensor_scalar(out=out, in0=in0, scalar1=s1, scalar2=None, op0=op0),
                           reads=rd, writes=[out])
        return self.op(eng, lambda e: e.tensor_scalar(out=out, in0=in0, scalar1=s1, scalar2=s2, op0=op0, op1=op1),
                       reads=rd, writes=[out])

    def stt(self, eng, out, in0, scalar, in1, op0, op1):
        rd = [in0, in1]
        if not isinstance(scalar, (int, float)):
            rd.append(scalar)
        return self.op(eng, lambda e: e.scalar_tensor_tensor(out=out, in0=in0, scalar=scalar, in1=in1, op0=op0, op1=op1),
                       reads=rd, writes=[out])

    def memset(self, eng, out, val):
        return self.op(eng, lambda e: e.memset(out, val), writes=[out])

    def recip(self, out, in_):
        return self.op("dve", lambda e: e.reciprocal(out=out, in_=in_), reads=[in_], writes=[out])

    def finish(self):
        nc = self.nc
        for e in ENGS:
            for o in self.ops[e]:
                for tok in o["waits"]:
                    if tok[0] == 'e':
                        self.ops[tok[1]][tok[2]]["marked"] = True
        cnt_at = {}
        for e in ENGS:
            c = 0
            arr = []
            for o in self.ops[e]:
                if o["marked"]:
                    c += 1
                arr.append(c)
            cnt_at[e] = arr
        fw = {}
        for tok in self.out_dma_tokens:
            fw[tok[1]] = max(fw.get(tok[1], 0), tok[2])

        with nc.Block() as block:
            def emit(ename, engobj):
                for o in self.ops[ename]:
                    ws = o["waits"]
                    emb = None
                    if EMBED_LAST_WAIT and ws and ename in EMBED_ENGINES and o["tok"][0] == 'e':
                        emb = ws[-1]
                        ws = ws[:-1]
                    for tok in ws:
                        if tok[0] == 'e':
                            engobj.wait_ge(self.eng_sem[tok[1]], cnt_at[tok[1]][tok[2]])
                        else:
                            engobj.wait_ge(self.dma_sems[tok[1]], tok[2])
                    ins = o["fn"](engobj)
                    if emb is not None:
                        if emb[0] == 'e':
                            ins._wait_ge(self.eng_sem[emb[1]], cnt_at[emb[1]][emb[2]])
                        else:
                            ins._wait_ge(self.dma_sems[emb[1]], emb[2])
                    if o["tok"][0] == 'd':
                        ins.then_inc(self.dma_sems[o["tok"][1]], 16)
                    elif o["marked"]:
                        ins.then_inc(self.eng_sem[ename], 1)
                if ename == "sp":
                    for k, v in fw.items():
                        engobj.wait_ge(self.dma_sems[k], v)

            @block.sync
            def _(sync):
                emit("sp", sync)

            @block.tensor
            def _(tensor):
                emit("pe", tensor)

            @block.scalar
            def _(scalar):
                emit("act", scalar)

            @block.vector
            def _(vector):
                emit("dve", vector)

            @block.gpsimd
            def _(gpsimd):
                emit("pool", gpsimd)


class _Trunc(Exception):
    pass


class Arena:
    def __init__(self, tensor, nelem):
        self.t = tensor
        self.n = nelem
        self.free = [(0, nelem)]
        self.live = {}
        self.peak = 0

    def alloc(self, name, nelem_bf16):
        nelem_bf16 = (nelem_bf16 + 15) // 16 * 16
        for i, (o, s) in enumerate(self.free):
            if s >= nelem_bf16:
                if s == nelem_bf16:
                    self.free.pop(i)
                else:
                    self.free[i] = (o + nelem_bf16, s - nelem_bf16)
                self.live[name] = (o, nelem_bf16)
                used = self.n - sum(s for _, s in self.free)
                self.peak = max(self.peak, used)
                return o
        raise RuntimeError("arena OOM allocating %s (%d); live=%s free=%s" % (name, nelem_bf16, self.live, self.free))

    def release(self, name):
        o, s = self.live.pop(name)
        self.free.append((o, s))
        self.free.sort()
        merged = []
        for o, s in self.free:
            if merged and merged[-1][0] + merged[-1][1] == o:
                merged[-1] = (merged[-1][0], merged[-1][1] + s)
            else:
                merged.append((o, s))
        self.free = merged

    def bf(self, name, shape):
        n = 1
        for s in shape[1:]:
            n *= s
        o = self.alloc(name, n)
        v = self.t[0:shape[0], o:o + n]
        if len(shape) == 3:
            v = v.rearrange("p (a b) -> p a b", a=shape[1])
        elif len(shape) == 4:
            v = v.rearrange("p (a b c) -> p a b c", a=shape[1], b=shape[2])
        return v

    def f32(self, name, shape):
        n = 1
        for s in shape[1:]:
            n *= s
        o = self.alloc(name, 2 * n)
        v = self.t[0:shape[0], o:o + 2 * n].bitcast(F32)
        if len(shape) == 3:
            v = v.rearrange("p (a b) -> p a b", a=shape[1])
        elif len(shape) == 4:
            v = v.rearrange("p (a b c) -> p a b c", a=shape[1], b=shape[2])
        return v


def _rope_tables():
    t = np.arange(2048)
    row = (t // 64).astype(np.float64)
    col = (t % 64).astype(np.float64)
    mla = np.zeros((2, 96, 2048), np.float32)
    mla[0, 0:64] = 1.0
    for r in range(32):
        pos = row if r < 16 else col
        i = r % 16
        f = i % 8
        inv = 10000.0 ** (-(2.0 * f) / 16.0)
        ang = pos * inv
        mla[0, 64 + r] = np.cos(ang)
        mla[1, 64 + r] = np.sin(ang) if i < 8 else -np.sin(ang)
    swa = np.zeros((2, 128, 2048), np.float32)
    for p in range(128):
        d = p % 64
        pos = row if d < 32 else col
        i = d % 32
        f = i % 16
        inv = 10000.0 ** (-(2.0 * f) / 32.0)
        ang = pos * inv
        swa[0, p] = np.cos(ang)
        swa[1, p] = np.sin(ang) if i < 16 else -np.sin(ang)
    return mla, swa


def _perm_mats():
    pm96 = np.zeros((96, 96), np.float32)
    for r in range(32):
        i = r % 16
        partner = r + 8 if i < 8 else r - 8
        pm96[64 + partner, 64 + r] = 1.0
    pm128 = np.zeros((128, 128), np.float32)
    for m in range(128):
        i = m % 32
        partner = m + 16 if i < 16 else m - 16
        pm128[partner, m] = 1.0
    return pm96, pm128


def _pool_mats():
    T = 384
    out = np.zeros((4, 5, 128, 128), np.float32)
    for g, w in enumerate((2, 4, 8, 16)):
        A = np.zeros((T, T), np.float64)
        for t in range(T):
            lo = min(max(t - w // 2, 0), T)
            hi = min(max(t + w // 2, 0), T)
            A[lo:hi, t] = 1.0 / (hi - lo)
            A[t, t] -= 1.0
        out[g, 0] = A[0:128, 0:128]
        out[g, 1] = A[128:256, 128:256]
        out[g, 2] = A[256:384, 256:384]
        out[g, 3] = A[0:128, 128:256]
        out[g, 4] = A[128:256, 0:128]
    return out


def _masks():
    k = np.arange(128)[:, None]
    q = np.arange(128)[None, :]
    m = np.zeros((2, 128, 128), np.float32)
    m[0] = np.where(k <= q, 0.0, -30000.0)
    m[1] = np.where(q <= k, 0.0, -30000.0)
    return m


def build_program(nl_p=4, nl_s=4, stages=3, dbg=None):
    nc = bass.Bass("TRN2", target_bir_lowering=False)

    def din(name, shape):
        return nc.dram_tensor(name, list(shape), F32, kind="ExternalInput").ap()

    def dout(name, shape):
        return nc.dram_tensor(name, list(shape), F32, kind="ExternalOutput").ap()

    xp_d = din("xp", [1024, 1024])
    xs_d = din("xs", [2048, 1024])
    cckv_d = din("cckv", [2, 256, 256])
    ckr_d = din("ckr", [2, 256, 32])
    ck_d = din("ck", [2, 256, 256])
    cv_d = din("cv", [2, 256, 256])
    cc_d = din("cc", [2, 1024])
    adaw_d = din("ada_w", [4, 1024, 3072])
    adab_d = din("ada_b", [4, 3072])
    vecs_d = din("vecs", [14, 1024])
    gkvn_d = din("gkvn", [2, 256])
    sink_d = din("sink", [32])
    wine_d = din("w_in_e", [2, 1024, 2208])
    wuq_d = din("w_uq", [2, 384, 768])
    wukv_d = din("w_ukv", [2, 256, 1024])
    wpool_d = din("w_pool", [2, 4, 128, 128])
    woe_d = din("w_out_e", [2, 1024, 1024])
    wino_d = din("w_in_o", [2, 1024, 2560])
    woo_d = din("w_out_o", [2, 1024, 1024])
    ident_d = din("ident", [128, 128])
    pm96_d = din("pm96", [96, 96])
    pm128_d = din("pm128", [128, 128])
    masks_d = din("masks", [2, 128, 128])
    amats_d = din("amats", [20, 128, 128])
    mlacs_d = din("mla_cs", [2, 96, 2048])
    swacs_d = din("swa_cs", [2, 128, 2048])

    yp_d = dout("yp", [1024, 1024])
    ys_d = dout("ys", [2048, 1024])
    sckv_d = dout("st_ckv", [4, 2, 256, 256])
    skr_d = dout("st_kr", [4, 2, 256, 32])
    sk_d = dout("st_k", [4, 2, 256, 256])
    sv_d = dout("st_v", [4, 2, 256, 256])

    with ExitStack() as st:
        P = Prog(nc, st)
        xT = P.sb("xT", [128, 8, 2048], F32)
        ident = P.sb("ident", [128, 128], F32)
        ones_bf = P.sb("ones_bf", [128, 128], BF16)
        identb = P.sb("identb", [128, 128], BF16)
        scb = P.sb("scb", [128, 8, 2], BF16)
        pm96 = P.sb("pm96", [96, 96], BF16)
        pm128 = P.sb("pm128", [128, 128], BF16)
        masks = P.sb("masks", [128, 2, 128], BF16)
        epsT = P.sb("epsT", [128, 1], F32)
        mod = P.sb("mod", [128, 4, 48], F32)
        vecT = P.sb("vecT", [128, 8, 32], F32)
        coefA = P.sb("coefA", [128, 4, 2, 8], F32)
        coefB = P.sb("coefB", [128, 4, 2, 8], F32)
        coefG = P.sb("coefG", [128, 4, 2, 8], F32)
        gkvb = P.sb("gkvb", [128, 2, 256], F32)
        esink = P.sb("esink", [128, 32], F32)
        rstd = P.sb("rstd", [128, 512], F32)
        rstd2 = P.sb("rstd2", [128, 512], F32)
        tmpf = [P.sb("tmpf%d" % i, [128, 512], F32) for i in range(2)]
        ARENA_N = 64 * 1024
        arena_t = P.sb("arena", [128, ARENA_N], BF16)
        AR = Arena(arena_t, ARENA_N)
        banks = [P.ps("ps%d" % i, [128, 512], F32) for i in range(8)]
        rr = {"all": 0, "s": 0, "o": 0, "g": 0}
        pools = {"all": list(range(8)), "s": [0, 1, 2, 3], "o": [4, 5], "g": [6, 7]}

        def bank(pool="all"):
            lst = pools[pool]
            b = banks[lst[rr[pool] % len(lst)]]
            rr[pool] += 1
            return b

        evac_rr = [0]

        def evac_eng():
            evac_rr[0] += 1
            return "dve" if evac_rr[0] % 2 else "act"

        P.dma(ident[:], ident_d)
        P.dma(identb[:], ident_d, eng="pool")
        P.dma(pm96[:], pm96_d, eng="pool")
        P.dma(pm128[:], pm128_d, eng="pool")
        P.dma(masks[:], masks_d.rearrange("a k q -> k a q"), eng="pool")
        P.dma(gkvb[:].rearrange("p a n -> p (a n)"), gkvn_d.rearrange("a n -> (a n)").partition_broadcast(128))
        P.dma(esink[:], sink_d.partition_broadcast(128))
        P.memset("dve", ones_bf[:], 1.0)
        P.memset("dve", epsT[:], EPS)
        P.act(esink[:], esink[:], AF.Exp)

        if dbg == "consts":
            P.dma(yp_d[0:128, 0:32], esink[:], out_dma=True)
            P.finish()
            return nc
        vst = AR.f32("vst", [32, 1024])
        P.memset("dve", vst[:], 0.0)
        P.dma(vst[0:14, :], vecs_d)
        pv = bank()
        for c in range(8):
            P.tr(pv[:, c * 32:(c + 1) * 32], vst[0:32, c * 128:(c + 1) * 128], ident[0:32, 0:32])
        for c in range(8):
            P.cp("dve", vecT[:, c, :], pv[:, c * 32:(c + 1) * 32])
        AR.release("vst")

        if dbg == "vec":
            P.dma(yp_d[0:128, 0:256], vecT[:].rearrange("p a b -> p (a b)"), out_dma=True)
            P.finish()
            return nc
        ccT = AR.f32("ccT", [128, 2, 8])
        for m in range(2):
            P.dma(ccT[:, m, :], cc_d[m].rearrange("(p c) -> p c", c=8))
        for m in range(2):
            P.act(scb[:, :, m], ccT[:, m, :], AF.Silu)
        AR.release("ccT")
        modv = mod[:].rearrange("p l (j c m) -> p l j c m", j=3, c=8)
        ada_state = {}

        def ada_alloc(l, nbuf):
            ada_state["l"] = l
            ada_state["adab"] = AR.bf("adab", [1, 3072])
            ada_state["bufs"] = [AR.bf("adw%d" % i, [128, 8, 512]) for i in range(nbuf)]
            ada_state["nbuf"] = nbuf
            P.dma(ada_state["adab"][:], adab_d[l:l + 1, :], eng="pool")

        def ada_issue(blks):
            l = ada_state["l"]
            wv = adaw_d[l].rearrange("(p c) n -> p c n", c=8)
            for blk in blks:
                wt = ada_state["bufs"][blk % ada_state["nbuf"]]
                P.dma(wt[:], wv[:, :, blk * 512:(blk + 1) * 512], eng="pool")

        def ada_compute(blks):
            l = ada_state["l"]
            adab = ada_state["adab"]
            for blk in blks:
                wt = ada_state["bufs"][blk % ada_state["nbuf"]]
                pm = bank()
                for nci in range(4):
                    nch = blk * 4 + nci
                    for c in range(8):
                        P.mm(pm[:, 2 * nci:2 * nci + 2], wt[:, c, nci * 128:(nci + 1) * 128], scb[:, c, :],
                             start=(c == 0), stop=False)
                    P.mm(pm[:, 2 * nci:2 * nci + 2], adab[0:1, nch * 128:(nch + 1) * 128], ones_bf[0:1, 0:2],
                         start=False, stop=True)
                P.cp("dve", mod[:, l, blk * 8:(blk + 1) * 8], pm[:, 0:8])

        def ada_finish():
            l = ada_state["l"]
            for m in range(2):
                P.stt("dve", coefA[:, l, m, :], modv[:, l, 1, :, m], 1.0, vecT[:, :, l], ALU.add, ALU.mult)
                P.cp("dve", coefB[:, l, m, :], modv[:, l, 0, :, m])
                P.tt("dve", coefG[:, l, m, :], modv[:, l, 2, :, m], vecT[:, :, 4 + l], ALU.mult)
            for i in range(ada_state["nbuf"]):
                AR.release("adw%d" % i)
            AR.release("adab")
            ada_state.clear()

        def ada_all(l):
            ada_alloc(l, 3)
            for blk in range(6):
                ada_issue([blk])
                ada_compute([blk])
            ada_finish()

        ADA_INTERLEAVE = nl_p >= 4
        ada0_done = [False]
        if not ADA_INTERLEAVE:
            for l in range(1, 4):
                ada_all(l)
        if dbg == "ada":
            P.dma(yp_d[0:128, 0:192], mod[:].rearrange("p a b -> p (a b)"), out_dma=True)
            P.dma(yp_d[128:256, 0:64], coefA[:].rearrange("p a b c -> p (a b c)"), out_dma=True)
            P.dma(yp_d[256:384, 0:64], coefG[:].rearrange("p a b c -> p (a b c)"), out_dma=True)
            P.finish()
            return nc
        def rstd_from_ssq(ps_ssq, n, dim, out):
            P.act(out, ps_ssq, AF.Ln, bias=epsT[:, 0:1], scale=1.0 / dim)
            P.act(out, out, AF.Exp, scale=-0.5)

        def modulate_parts(hT, sq, l, m, t0, n):
            parts = []

            ns_ = sq.shape[1]

            def stats_a():
                for c in range(8):
                    P.act(sq[:, c, :n], xT[:, c, t0:t0 + n], AF.Square)

            def stats_b():
                pb = bank()
                for c in range(8):
                    P.mm(pb[:, :n], ones_bf[:, :], sq[:, c, :n], start=(c == 0), stop=(c == 7))
                rstd_from_ssq(pb[:, :n], n, 1024, rstd[:, :n])

            def stats():
                pb = bank()
                for c0 in range(0, 8, ns_):
                    for c in range(c0, c0 + ns_):
                        P.act(sq[:, c % ns_, :n], xT[:, c, t0:t0 + n], AF.Square)
                    for c in range(c0, c0 + ns_):
                        P.mm(pb[:, :n], ones_bf[:, :], sq[:, c % ns_, :n], start=(c == 0), stop=(c == 7))
                rstd_from_ssq(pb[:, :n], n, 1024, rstd[:, :n])
            if ns_ >= 8:
                parts.extend([stats_a, (lambda: None), stats_b])
            else:
                parts.append(stats)
            for c in range(8):
                def ap(c=c):
                    tf = tmpf[c % 2]
                    P.stt("dve", tf[:, :n], xT[:, c, t0:t0 + n], coefA[:, l, m, c:c + 1], rstd[:, :n], ALU.mult, ALU.mult)
                    P.act(hT[:, c, :n], tf[:, :n], AF.Identity, bias=coefB[:, l, m, c:c + 1], scale=1.0)
                parts.append(ap)
            return parts

        def modulate(hT, sq, l, m, t0, n):
            for f in modulate_parts(hT, sq, l, m, t0, n):
                f()

        def run_part(parts, k=1):
            for _ in range(k):
                if parts:
                    parts.pop(0)()

        def load_w(dst, src_rows_view, col0, ncols):
            for c in range(dst.shape[1]):
                P.dma(dst[:, c, :], src_rows_view[:, c, col0:col0 + ncols], eng="pool")

        def alloc_hs():
            hs = [(AR.bf("hT", [128, 8, 512]), AR.bf("sq", [128, 8, 512]))]
            try:
                a = AR.bf("hT1", [128, 8, 512])
                try:
                    b = AR.bf("sq1", [128, 8, 512])
                    hs.append((a, b))
                except RuntimeError:
                    AR.release("hT1")
            except RuntimeError:
                pass
            return hs

        def free_hs(hs):
            AR.release("hT")
            AR.release("sq")
            if len(hs) > 1:
                AR.release("hT1")
                AR.release("sq1")

        store_state = {"dst": None, "done": False}

        def store_tile(dst, stage2, t0):
            for j in range(4):
                for hb in range(2):
                    pb = bank()
                    for cc in range(4):
                        c = hb * 4 + cc
                        P.tr(pb[:, cc * 128:(cc + 1) * 128], xT[:, c, t0 + j * 128:t0 + (j + 1) * 128], ident[:, :])
                    P.cp(evac_eng(), stage2[:, j % 2, hb * 512:(hb + 1) * 512], pb[:, :])
                P.dma(dst[t0 + j * 128:t0 + (j + 1) * 128, :], stage2[:, j % 2, :], out_dma=True)

        def stage3(l, m, NT, mbuf, wg, wout, hooks=None, nxt=None):
            oT = AR.f32("oT", [128, 8, 512])
            hs3 = alloc_hs()
            sg = [AR.bf("sg%d" % i, [128, 512]) for i in range(2)]
            if nxt is not None:
                prefetch_w1(*nxt)
            stage2 = None
            spend = [None]
            if store_state["dst"] is not None:
                try:
                    stage2 = AR.f32("xst2", [128, 2, 1024])
                except RuntimeError:
                    stage2 = None
            tiles = list(range(0, NT, 512))
            n = 512
            pipe = len(hs3) > 1
            if pipe:
                modulate(hs3[0][0], hs3[0][1], l, m, tiles[0], n)
            for ti, t0 in enumerate(tiles):
                if hooks and ti in hooks:
                    hooks[ti]()
                hT, sq = hs3[ti % len(hs3)]
                if not pipe:
                    modulate(hT, sq, l, m, t0, n)
                nparts = []
                if pipe and ti + 1 < len(tiles):
                    nh, nsq = hs3[(ti + 1) % 2]
                    nparts = modulate_parts(nh, nsq, l, m, tiles[ti + 1], n)
                for mc in range(8):
                    pb = bank()
                    for k in range(8):
                        P.mm(pb[:, :n], wg[:, k, mc * 128:(mc + 1) * 128], hT[:, k, :n], start=(k == 0), stop=(k == 7))
                    s_ = sg[mc % 2]
                    P.act(s_[:, :n], pb[:, :n], AF.Silu)
                    P.tt("dve", mbuf[:, mc, t0:t0 + n], mbuf[:, mc, t0:t0 + n], s_[:, :n], ALU.mult)
                    if mc == 1:
                        run_part(nparts)
                if spend[0] is not None:
                    spend[0]()
                    spend[0] = None
                for dc in range(8):
                    pb = bank()
                    for k in range(8):
                        P.mm(pb[:, :n], wout[:, k, dc * 128:(dc + 1) * 128], mbuf[:, k, t0:t0 + n],
                             start=(k == 0), stop=(k == 7))
                    P.cp("dve", oT[:, dc, :n], pb[:, :n])
                    P.act(sq[:, dc, :n], pb[:, :n], AF.Square)
                    run_part(nparts)
                run_part(nparts, 16)
                pb = bank()
                for dc in range(8):
                    P.mm(pb[:, :n], ones_bf[:, :], sq[:, dc, :n], start=(dc == 0), stop=(dc == 7))
                rstd_from_ssq(pb[:, :n], n, 1024, rstd2[:, :n])
                for dc in range(8):
                    tf = tmpf[dc % 2]
                    P.stt("dve", tf[:, :n], oT[:, dc, :n], coefG[:, l, m, dc:dc + 1], rstd2[:, :n], ALU.mult, ALU.mult)
                    P.tt("dve", xT[:, dc, t0:t0 + n], xT[:, dc, t0:t0 + n], tf[:, :n], ALU.add)
                if stage2 is not None:
                    spend[0] = (lambda t0=t0: store_tile(store_state["dst"], stage2, t0))
            if hooks and "post" in hooks:
                hooks["post"]()
            if stage2 is not None:
                if spend[0] is not None:
                    spend[0]()
                AR.release("xst2")
                store_state["done"] = True
            free_hs(hs3)
            for nm in ("oT", "sg0", "sg1"):
                AR.release(nm)

        rope_rr = [0]

        def rope_a(src_ps, pr, n, scale, pm, cs, t0, rb):
            qc, qsn = rb[rope_rr[0] % len(rb)]
            rope_rr[0] += 1
            P.stt("dve", qc[0:pr, :n], src_ps[0:pr, :n], float(scale), cs[0:pr, 0, t0:t0 + n], ALU.mult, ALU.mult)
            P.stt("dve", qsn[0:pr, :n], src_ps[0:pr, :n], float(scale), cs[0:pr, 1, t0:t0 + n], ALU.mult, ALU.mult)

            def phase_b():
                p2 = bank("g")
                P.mm(p2[0:pr, :n], identb[0:pr, 0:pr], qc[0:pr, :n], start=True, stop=False)
                P.mm(p2[0:pr, :n], pm[:, :], qsn[0:pr, :n], start=False, stop=True)
                return p2
            return phase_b

        def rope_apply(src_ps, pr, n, scale, pm, cs, t0, rb):
            return rope_a(src_ps, pr, n, scale, pm, cs, t0, rb)()

        def alloc_rb():
            return [(AR.bf("rqc%d" % i, [128, 512]), AR.bf("rqs%d" % i, [128, 512])) for i in range(2)]

        def free_rb():
            for i in range(2):
                AR.release("rqc%d" % i)
                AR.release("rqs%d" % i)

        pref = {}

        def prefetch_w1(lnext, ctx_next, full):
            i2 = lnext // 2
            if (not full) and lnext % 2 == 1:
                return
            try:
                if lnext % 2 == 0:
                    wv = wine_d[i2].rearrange("(c p) n -> p c n", p=128)
                    a = AR.bf("w1a", [128, 8, 672])
                    load_w(a, wv, 0, 672)
                    pref["w1a"] = a
                    if full:
                        b_ = AR.bf("w1b", [128, 8, 512])
                        load_w(b_, wv, 1184, 512)
                        pref["w1b"] = b_
                else:
                    wv = wino_d[i2].rearrange("(c p) n -> p c n", p=128)
                    a = AR.bf("wq0", [128, 8, 512])
                    load_w(a, wv, 0, 512)
                    pref["wq0"] = a
                    if full:
                        b_ = AR.bf("wq1", [128, 8, 512])
                        load_w(b_, wv, 512, 512)
                        pref["wq1"] = b_
                        k_ = AR.bf("wk", [128, 8, 256])
                        load_w(k_, wv, 1024, 256)
                        pref["wk"] = k_
            except RuntimeError:
                pass

        def even_layer(l, m, NT, nseq, T, ctx, nxt=None):
            i = l // 2
            S = T + (256 if ctx else 0)
            KT = nseq * S
            nkc = S // 128
            wv_in = wine_d[i].rearrange("(c p) n -> p c n", p=128)
            w1a = pref.pop("w1a", None)
            if w1a is None:
                w1a = AR.bf("w1a", [128, 8, 672])
                load_w(w1a, wv_in, 0, 672)
            w1b = pref.pop("w1b", None)
            if w1b is None:
                w1b = AR.bf("w1b", [128, 8, 512])
                load_w(w1b, wv_in, 1184, 512)
            wp = AR.bf("wp", [128, 4, 128])
            amats = AR.bf("amats", [128, 20, 128])
            P.dma(amats[:], amats_d.rearrange("a k q -> k a q"), eng="pool")
            P.dma(wp[:], wpool_d[i].rearrange("g c e -> c g e"), eng="pool")
            mbuf = AR.bf("mbuf", [128, 8, NT])
            mo = AR.live["mbuf"][0]
            vtok = arena_t[:, mo:mo + (NT // 128) * 512].rearrange("p (j q) -> p j q", q=512)
            cqn = AR.bf("cqn", [128, 3, NT])
            ckvn = AR.bf("ckvn", [128, 2, KT])
            krT = AR.bf("krT", [96, KT])
            cs = None
            if ctx:
                cs = AR.bf("cs", [96, 2, 2048])
                P.dma(cs[:, 0, :], mlacs_d[0], eng="pool")
                P.dma(cs[:, 1, :], mlacs_d[1], eng="pool")
                rb = alloc_rb()
                cst = AR.f32("cst", [128, 2, 256 + 96])
                P.memset("pool", cst[:, :, 256:320], 0.0)
                for jj in range(2):
                    P.dma(cst[:, jj, 0:256], cckv_d[i, jj * 128:(jj + 1) * 128, :])
                    P.dma(cst[:, jj, 320:352], ckr_d[i, jj * 128:(jj + 1) * 128, :])
                for jj in range(2):
                    pb = bank()
                    for c in range(2):
                        P.tr(pb[:, c * 128:(c + 1) * 128], cst[:, jj, c * 128:(c + 1) * 128], ident[:, :])
                    P.tr(pb[0:96, 256:384], cst[:, jj, 256:352], ident[:, :])
                    for c in range(2):
                        P.cp("dve", ckvn[:, c, T + jj * 128:T + (jj + 1) * 128], pb[:, c * 128:(c + 1) * 128])
                    P.cp("dve", krT[64:96, T + jj * 128:T + (jj + 1) * 128], pb[64:96, 256:384])
                AR.release("cst")
            stg = None
            if not ctx:
                stg = [AR.f32("stg%d" % j, [128, 288]) for j in range(2)]
            junk = AR.f32("junk", [128, 256])
            ssq1 = AR.f32("ssq1", [128, 2])
            hs1 = alloc_hs()

            def kidx(t):
                return (t // T) * S + (t % T)

            pipe1 = len(hs1) > 1
            if pipe1:
                modulate(hs1[0][0], hs1[0][1], l, m, 0, 512)
            for t0 in range(0, NT, 512):
                n = 512
                hT, sq = hs1[(t0 // 512) % len(hs1)]
                if not pipe1:
                    modulate(hT, sq, l, m, t0, n)
                nparts = []
                if pipe1 and t0 + 512 < NT:
                    nh, nsq = hs1[((t0 // 512) + 1) % 2]
                    nparts = modulate_parts(nh, nsq, l, m, t0 + 512, 512)
                for oc in range(3):
                    pb = bank()
                    for k in range(8):
                        P.mm(pb[:, :n], w1a[:, k, oc * 128:(oc + 1) * 128], hT[:, k, :n], start=(k == 0), stop=(k == 7))
                    P.act(sq[:, oc, :n], pb[:, :n], AF.Square)
                    P.ts("dve", cqn[:, oc, t0:t0 + n], pb[:, :n], vecT[:, oc, 8 + i:9 + i], ALU.mult)
                    run_part(nparts)
                pieces = [(t0, n)] if T >= 512 else [(t0 + a, T) for a in range(0, n, T)]
                for oc in range(2):
                    pb = bank()
                    for k in range(8):
                        P.mm(pb[:, :n], w1a[:, k, 384 + oc * 128:384 + (oc + 1) * 128], hT[:, k, :n],
                             start=(k == 0), stop=(k == 7))
                    P.act(sq[:, 4 + oc, :n], pb[:, :n], AF.Square)
                    for (ta, tn) in pieces:
                        P.ts("dve", ckvn[:, oc, kidx(ta):kidx(ta) + tn], pb[:, ta - t0:ta - t0 + tn],
                             vecT[:, oc, 10 + i:11 + i], ALU.mult)
                    run_part(nparts)
                pb = bank()
                for oc in range(3):
                    P.mm(pb[:, :n], ones_bf[:, :], sq[:, oc, :n], start=(oc == 0), stop=(oc == 2))
                rstd_from_ssq(pb[:, :n], n, 384, rstd2[:, :n])
                for oc in range(3):
                    P.tt("dve", cqn[:, oc, t0:t0 + n], cqn[:, oc, t0:t0 + n], rstd2[:, :n], ALU.mult)
                pb = bank()
                for k in range(8):
                    P.mm(pb[0:96, :n], w1a[:, k, 576:672], hT[:, k, :n], start=(k == 0), stop=(k == 7))
                if ctx:
                    p2 = rope_apply(pb, 96, n, 1.0, pm96, cs, t0, rb)
                    P.cp("act", krT[64:96, kidx(t0):kidx(t0) + n], p2[64:96, :n])
                else:
                    for (ta, tn) in pieces:
                        P.cp("dve", krT[64:96, kidx(ta):kidx(ta) + tn], pb[64:96, ta - t0:ta - t0 + tn])
                pb = bank()
                for oc in range(2):
                    P.mm(pb[:, :n], ones_bf[:, :], sq[:, 4 + oc, :n], start=(oc == 0), stop=(oc == 1))
                rstd_from_ssq(pb[:, :n], n, 256, rstd2[:, :n])
                for oc in range(2):
                    for (ta, tn) in pieces:
                        P.tt("dve", ckvn[:, oc, kidx(ta):kidx(ta) + tn],
                             ckvn[:, oc, kidx(ta):kidx(ta) + tn], rstd2[:, ta - t0:ta - t0 + tn], ALU.mult)
                for j in range(n // 128):
                    pb = bank()
                    for k in range(8):
                        P.mm(pb[:, :], hT[:, k, j * 128:(j + 1) * 128], w1b[:, k, :], start=(k == 0), stop=(k == 7))
                    P.cp(evac_eng(), vtok[:, (t0 // 128) + j, :], pb[:, :])
                    run_part(nparts)
                run_part(nparts, 16)
                if not ctx:
                    for j in range(n // 128):
                        tok = t0 + j * 128
                        b = tok // T
                        pos = tok % T
                        pb = bank()
                        for k in range(8):
                            P.mm(pb[:, 0:288], hT[:, k, j * 128:(j + 1) * 128], w1a[:, k, 384:672],
                                 start=(k == 0), stop=(k == 7))
                        so = stg[j % 2]
                        P.act(junk[:, :], pb[:, 0:256], AF.Square)
                        P.op("dve", lambda e: e.reduce_sum(out=ssq1[:, 0:1], in_=junk[:, :], axis=mybir.AxisListType.X),
                             reads=[junk[:, :]], writes=[ssq1[:, 0:1]])
                        P.act(ssq1[:, 1:2], ssq1[:, 0:1], AF.Ln, bias=epsT[:, 0:1], scale=1.0 / 256)
                        P.act(ssq1[:, 1:2], ssq1[:, 1:2], AF.Exp, scale=-0.5)
                        P.stt("dve", so[:, 0:256], pb[:, 0:256], ssq1[:, 1:2], gkvb[:, i, :], ALU.mult, ALU.mult)
                        P.cp("dve", so[:, 256:288], pb[:, 256:288])
                        P.dma(sckv_d[b, i, pos:pos + 128, :], so[:, 0:256], out_dma=True)
                        P.dma(skr_d[b, i, pos:pos + 128, :], so[:, 256:288], out_dma=True)
            free_hs(hs1)
            pooled = [AR.bf("pooled%d" % j, [128, 512]) for j in range(2)]
            ppend = [None]
            ncs = T // 128
            for t0 in range(0, NT, 512):
                for g in range(4):
                    pb = bank()
                    for j in range(4):
                        ch = t0 // 128 + j
                        cin = ch % ncs
                        contrib = []
                        if cin > 0:
                            contrib.append((ch - 1, 3))
                        contrib.append((ch, 0 if cin == 0 else (2 if cin == ncs - 1 else 1)))
                        if cin < ncs - 1:
                            contrib.append((ch + 1, 4))
                        for ci, (src, kind) in enumerate(contrib):
                            P.mm(pb[:, j * 128:(j + 1) * 128], vtok[:, src, g * 128:(g + 1) * 128],
                                 amats[:, g * 5 + kind, :], start=(ci == 0), stop=(ci == len(contrib) - 1))
                    pl = pooled[g % 2]
                    P.cp(evac_eng(), pl[:, :], pb[:, :])
                    if ppend[0] is not None:
                        ppend[0]()

                    def _pw(pl=pl, g=g, t0=t0):
                        pb2 = bank()
                        P.mm(pb2[:, :], wp[:, g, :], pl[:, :])
                        P.ts("dve", mbuf[:, 4 + g, t0:t0 + 512], pb2[:, :], vecT[:, g, 12 + i:13 + i], ALU.mult)
                    ppend[0] = _pw
            if ppend[0] is not None:
                ppend[0]()
                ppend[0] = None
            for nm in ("pooled0", "pooled1", "junk", "ssq1", "w1a", "w1b", "wp", "amats"):
                AR.release(nm)
            if not ctx:
                AR.release("stg0")
                AR.release("stg1")
            if stages == 1 and l == trunc_l[0]:
                raise _Trunc()
            ada_hooks = None
            if ADA_INTERLEAVE and (not ctx) and l + 1 < 4:
                ada_alloc(l + 1, 2)
                ada_issue([0, 1])

                def _h0():
                    ada_compute([0, 1])
                    ada_issue([2, 3])

                def _h1():
                    ada_compute([2, 3])
                    ada_issue([4, 5])

                def _hp():
                    ada_compute([4, 5])
                    ada_finish()
                ada_hooks = {0: _h0, 1: _h1, "post": _hp}
            wuq = AR.bf("wuq", [128, 3, 768])
            wuk = AR.bf("wuk", [128, 2, 8, 64])
            wuv = AR.bf("wuv", [128, 2, 8, 64])
            P.dma(wuq[:], wuq_d[i].rearrange("(c p) n -> p c n", p=128), eng="pool")
            wukv_v = wukv_d[i].rearrange("(c p) (h t d) -> p c h t d", p=128, h=8, t=2)
            for c in range(2):
                P.dma(wuk[:, c, :, :], wukv_v[:, c, :, 0, :], eng="pool")
                P.dma(wuv[:, c, :, :], wukv_v[:, c, :, 1, :], eng="pool")
            def load_wg():
                wg_ = AR.bf("wg", [128, 8, 1024])
                load_w(wg_[:, :, 0:512], wv_in, 672, 512)
                load_w(wg_[:, :, 512:1024], wv_in, 1696, 512)
                return wg_

            def load_wout():
                wout_ = AR.bf("wout", [128, 8, 1024])
                load_w(wout_, woe_d[i].rearrange("(c p) n -> p c n", p=128), 0, 1024)
                return wout_
            wg = load_wg()
            if not ctx:
                wout = load_wout()
            NB = 2
            TT = nseq * T
            nkt = KT // 128
            qh = [AR.bf("qh%d" % b, [96, TT]) for b in range(NB)]
            kh = [AR.bf("kh%d" % b, [96, KT]) for b in range(NB)]
            vh = [AR.bf("vh%d" % b, [128, nkt, 128]) for b in range(NB)]
            pT = [AR.bf("pT%d" % b, [128, 512]) for b in range(4)]
            rc = AR.f32("rc", [64, 512])
            for b in range(NB):
                P.memset("pool", vh[b][:, :, 64:128], 1.0)
                P.cp("pool", kh[b][64:96, :], krT[64:96, 0:KT])
            pend = [None]

            def flush_pend():
                if pend[0] is not None:
                    pend[0]()
                    pend[0] = None

            def build_steps(h, b):
                st_ = []
                for ka in range(0, KT, 512):
                    def f(ka=ka):
                        kn = min(512, KT - ka)
                        pb = bank("g")
                        for c in range(2):
                            P.mm(pb[0:64, :kn], wuk[:, c, h, :], ckvn[:, c, ka:ka + kn], start=(c == 0), stop=(c == 1))
                        P.cp("dve" if ctx else "act", kh[b][0:64, ka:ka + kn], pb[0:64, :kn])
                    st_.append(f)
                for ja in range(0, nkt, 8):
                    def f(ja=ja):
                        jn = min(8, nkt - ja)
                        pb = bank("g")
                        for jj in range(jn):
                            for c in range(2):
                                P.mm(pb[:, jj * 64:(jj + 1) * 64], ckvn[:, c, (ja + jj) * 128:(ja + jj + 1) * 128],
                                     wuv[:, c, h, :], start=(c == 0), stop=(c == 1))
                        P.cp("dve" if ctx else "act", vh[b][:, ja:ja + jn, 0:64], pb[:, 0:jn * 64].rearrange("p (j d) -> p j d", d=64))
                    st_.append(f)
                for qa in range(0, TT, 512):
                    hold = {}

                    def f(qa=qa, hold=hold):
                        pb = bank("g")
                        for c in range(3):
                            P.mm(pb[0:96, :512], wuq[:, c, h * 96:(h + 1) * 96], cqn[:, c, qa:qa + 512],
                                 start=(c == 0), stop=(c == 2))
                        if ctx:
                            hold["b"] = rope_a(pb, 96, 512, MLA_SCALE, pm96, cs, qa, rb)
                        else:
                            P.act(qh[b][:, qa:qa + 512], pb[0:96, :512], AF.Copy, scale=float(MLA_SCALE))
                    st_.append(f)
                    if ctx:
                        def f2(qa=qa, hold=hold):
                            p2 = hold["b"]()
                            P.cp("dve", qh[b][0:96, qa:qa + 512], p2[0:96, :512])
                        st_.append(f2)
                return st_

            for f in build_steps(0, 0):
                f()
            for h in range(8):
                b = h % NB
                half = h % 2
                inj = build_steps(h + 1, (h + 1) % NB) if h + 1 < 8 else []
                if ctx:
                    total_steps = (T // 512) * nkc
                    every = max(1, total_steps // (len(inj) + 1))
                    stepc = 0
                    for qa in range(0, T, 512):
                        po = bank("o")
                        sc_ps = {}

                        def issue_s(j):
                            ps_ = bank("s")
                            P.mm(ps_[:, :512], kh[b][:, j * 128:(j + 1) * 128], qh[b][:, qa:qa + 512])
                            sc_ps[j] = ps_

                        for j in range(3):
                            issue_s(j)
                        for j in range(nkc):
                            pt = pT[j % 4]
                            P.act(pt[:, :], sc_ps.pop(j)[:, :], AF.Exp)
                            if j + 3 < nkc:
                                issue_s(j + 3)
                            P.mm(po[:, :], vh[b][:, j, :], pt[:, :], start=(j == 0), stop=(j == nkc - 1))
                            stepc += 1
                            if inj and stepc % every == 0:
                                inj.pop(0)()
                        P.recip(rc[0:64, :], po[64:128, :])
                        P.tt("dve", mbuf[half * 64:(half + 1) * 64, h // 2, qa:qa + 512],
                             po[0:64, :], rc[0:64, :], ALU.mult)
                else:
                    for p in range(nseq // 2):
                        pts = []
                        for a in range(2):
                            sq_ = 2 * p + a
                            ps_ = bank("s")
                            for j in range(2):
                                P.mm(ps_[:, j * 256:(j + 1) * 256], kh[b][:, sq_ * 256 + j * 128:sq_ * 256 + (j + 1) * 128],
                                     qh[b][:, sq_ * 256:(sq_ + 1) * 256])
                            pt = pT[(2 * (p + h * (nseq // 2)) + a) % 4]
                            P.act(pt[:, :], ps_[:, :], AF.Exp)
                            pts.append(pt)
                        flush_pend()
                        for _ in range(3):
                            if inj:
                                inj.pop(0)()

                        def fin(p=p, pts=pts, b=b, h=h, half=half):
                            po = bank("o")
                            for a in range(2):
                                sq_ = 2 * p + a
                                for j in range(2):
                                    P.mm(po[:, a * 256:(a + 1) * 256], vh[b][:, 2 * sq_ + j, :], pts[a][:, j * 256:(j + 1) * 256],
                                         start=(j == 0), stop=(j == 1))
                            P.cp("dve", rc[0:64, :], po[64:128, :])
                            P.act(rc[0:64, :], rc[0:64, :], AF.Ln)
                            P.act(rc[0:64, :], rc[0:64, :], AF.Exp, scale=-1.0)
                            P.tt("dve", mbuf[half * 64:(half + 1) * 64, h // 2, p * 512:(p + 1) * 512],
                                 po[0:64, :], rc[0:64, :], ALU.mult)
                        pend[0] = fin
                while inj:
                    inj.pop(0)()
            flush_pend()
            for nm in ["qh%d" % b for b in range(NB)] + ["kh%d" % b for b in range(NB)] + ["vh%d" % b for b in range(NB)] + \
                      ["pT%d" % b for b in range(4)] + ["rc", "wuq", "wuk", "wuv", "cqn", "ckvn", "krT"]:
                AR.release(nm)
            if ctx:
                AR.release("cs")
                free_rb()
                wout = load_wout()
            if stages == 2 and l == trunc_l[0]:
                raise _Trunc()
            stage3(l, m, NT, mbuf, wg, wout, ada_hooks, nxt)
            for nm in ("wg", "wout", "mbuf"):
                AR.release(nm)

        def odd_layer(l, m, NT, nseq, T, ctx, nxt=None):
            i = l // 2
            S = T + (256 if ctx else 0)
            KT = nseq * S
            nkc = S // 128
            wv_in = wino_d[i].rearrange("(c p) n -> p c n", p=128)
            qT = AR.bf("mbuf", [128, 8, NT])
            kd = AR.bf("kd", [128, 4, KT])
            va = AR.bf("va", [128, KT // 128, 4, 128])
            wq0 = pref.pop("wq0", None)
            if wq0 is None:
                wq0 = AR.bf("wq0", [128, 8, 512])
                load_w(wq0, wv_in, 0, 512)
            wq1 = pref.pop("wq1", None)
            if wq1 is None:
                wq1 = AR.bf("wq1", [128, 8, 512])
                load_w(wq1, wv_in, 512, 512)
            wqs = [wq0, wq1]
            wk = pref.pop("wk", None)
            if wk is None:
                wk = AR.bf("wk", [128, 8, 256])
                load_w(wk, wv_in, 1024, 256)
            nkv = 256 if ctx else 512
            wkv = AR.bf("wkv", [128, 8, nkv])
            load_w(wkv, wv_in, 1536 - nkv, nkv)
            P.memset("pool", va[:, :, :, 64:128], 1.0)
            cs = None
            qs = None
            if ctx:
                cs = AR.bf("cs", [128, 2, 2048])
                P.dma(cs[:, 0, :], swacs_d[0], eng="pool")
                P.dma(cs[:, 1, :], swacs_d[1], eng="pool")
                rb = alloc_rb()
                cst = AR.f32("cst", [128, 2, 256])
                cdup = AR.f32("cdup", [128, 4, 128])
                for jj in range(2):
                    P.dma(cst[:, jj, :], cv_d[i, jj * 128:(jj + 1) * 128, :])
                for jj in range(2):
                    P.cp("dve", va[:, T // 128 + jj, :, 0:64], cst[:, jj, :].rearrange("p (h d) -> p h d", h=4))
                for jj in range(2):
                    P.dma(cst[:, jj, :], ck_d[i, jj * 128:(jj + 1) * 128, :])
                for jj in range(2):
                    P.cp("dve", cdup[:, :, 0:64], cst[:, jj, :].rearrange("p (h d) -> p h d", h=4))
                    P.cp("pool", cdup[:, :, 64:128], cst[:, jj, :].rearrange("p (h d) -> p h d", h=4))
                    pb = bank()
                    for kvh in range(4):
                        P.tr(pb[:, kvh * 128:(kvh + 1) * 128], cdup[:, kvh, :], ident[:, :])
                    for kvh in range(4):
                        P.cp("dve", kd[:, kvh, T + jj * 128:T + (jj + 1) * 128], pb[:, kvh * 128:(kvh + 1) * 128])
                AR.release("cst")
                AR.release("cdup")
            stg = None
            if not ctx:
                stg = [AR.f32("stg%d" % j, [128, 512]) for j in range(2)]
            if ctx:
                hs1 = [(AR.bf("hT", [128, 8, 512]), AR.bf("sq", [128, 4, 512])),
                       (AR.bf("hT1", [128, 8, 512]), AR.bf("sq1", [128, 4, 512]))]
            else:
                hs1 = alloc_hs()

            def kidx(t):
                return (t // T) * S + (t % T)

            pipe1 = len(hs1) > 1
            rpend = [None]
            if pipe1:
                modulate(hs1[0][0], hs1[0][1], l, m, 0, 512)
            for t0 in range(0, NT, 512):
                n = 512
                hT, sq = hs1[(t0 // 512) % len(hs1)]
                if not pipe1:
                    modulate(hT, sq, l, m, t0, n)
                pieces = [(t0, n)] if T >= 512 else [(t0 + a, T) for a in range(0, n, T)]
                nparts = []
                if pipe1 and t0 + 512 < NT:
                    nh, nsq = hs1[((t0 // 512) + 1) % 2]
                    nparts = modulate_parts(nh, nsq, l, m, t0 + 512, 512)
                for oc in range(8):
                    pb = bank()
                    for k in range(8):
                        P.mm(pb[:, :n], wqs[oc // 4][:, k, (oc % 4) * 128:(oc % 4 + 1) * 128], hT[:, k, :n],
                             start=(k == 0), stop=(k == 7))
                    if ctx:
                        pb_fn = rope_a(pb, 128, n, SWA_SCALE, pm128, cs, t0, rb)
                        if rpend[0] is not None:
                            rpend[0]()

                        def _fin_q(pb_fn=pb_fn, oc=oc, t0=t0, n=n):
                            p2 = pb_fn()
                            P.cp("act", qT[:, oc, t0:t0 + n], p2[:, :n])
                        rpend[0] = _fin_q
                    else:
                        P.ts("dve", qT[:, oc, t0:t0 + n], pb[:, :n], SWA_SCALE, ALU.mult)
                    run_part(nparts)
                for kc in range(2):
                    pb = bank()
                    for k in range(8):
                        P.mm(pb[:, :n], wk[:, k, kc * 128:(kc + 1) * 128], hT[:, k, :n], start=(k == 0), stop=(k == 7))
                    def _copies(srcs, kc=kc):
                        for (src, so, sn, ko) in srcs:
                            for hh in range(2):
                                for dh in range(2):
                                    P.cp("act" if dh != hh else "dve",
                                         kd[dh * 64:(dh + 1) * 64, 2 * kc + hh, ko:ko + sn],
                                         src[hh * 64:(hh + 1) * 64, so:so + sn])
                    if ctx:
                        pb_fn = rope_a(pb, 128, n, 1.0, pm128, cs, t0, rb)
                        if rpend[0] is not None:
                            rpend[0]()

                        def _fin_k(pb_fn=pb_fn, t0=t0, n=n, _copies=_copies):
                            p2 = pb_fn()
                            _copies([(p2, 0, n, kidx(t0))])
                        rpend[0] = _fin_k
                    else:
                        _copies([(pb, ta - t0, tn, kidx(ta)) for (ta, tn) in pieces])
                run_part(nparts, 16)
                for j in range(n // 128):
                    tok = t0 + j * 128
                    pb = bank()
                    for k in range(8):
                        P.mm(pb[:, 0:nkv], hT[:, k, j * 128:(j + 1) * 128], wkv[:, k, :], start=(k == 0), stop=(k == 7))
                    P.cp("dve", va[:, kidx(tok) // 128, :, 0:64], pb[:, nkv - 256:nkv].rearrange("p (h d) -> p h d", h=4))
                    if not ctx:
                        b = tok // T
                        pos = tok % T
                        so = stg[j % 2]
                        P.cp("act", so[:, :], pb[:, :])
                        P.dma(sk_d[b, i, pos:pos + 128, :], so[:, 0:256], out_dma=True)
                        P.dma(sv_d[b, i, pos:pos + 128, :], so[:, 256:512], out_dma=True)
                    if j == 0 and rpend[0] is not None:
                        rpend[0]()
                        rpend[0] = None
            free_hs(hs1)
            for nm in ("wq0", "wq1", "wk", "wkv"):
                AR.release(nm)
            if ctx:
                AR.release("cs")
                free_rb()
            else:
                AR.release("stg0")
                AR.release("stg1")
            if stages == 1 and l == trunc_l[0]:
                raise _Trunc()
            ada_hooks = None
            if ADA_INTERLEAVE and (not ctx) and l + 1 < 4:
                ada_alloc(l + 1, 2)
                ada_issue([0, 1])

                def _h0():
                    ada_compute([0, 1])
                    ada_issue([2, 3])

                def _h1():
                    ada_compute([2, 3])
                    ada_issue([4, 5])

                def _hp():
                    ada_compute([4, 5])
                    ada_finish()
                ada_hooks = {0: _h0, 1: _h1, "post": _hp}
            vaB = AR.bf("vaB", [128, KT // 128, 4, 128])
            wg = AR.bf("wg", [128, 8, 1024])
            wout = AR.bf("wout", [128, 8, 1024])
            load_w(wg, wv_in, 1536, 1024)
            load_w(wout, woo_d[i].rearrange("(c p) n -> p c n", p=128), 0, 1024)
            pT = [AR.bf("pT%d" % b, [128, 512]) for b in range(4)]
            rc = AR.f32("rc", [128, 512])
            P.memset("pool", vaB[:, :, :, 0:64], 1.0)
            nch_all = KT // 128
            for ja in range(0, nch_all, 6):
                jb = min(nch_all, ja + 6)
                P.cp("pool", vaB[:, ja:jb, :, 64:128], va[:, ja:jb, :, 0:64])
            mbuf = qT
            pools["o4"] = [4, 5, 6, 7]
            rr["o4"] = 0
            pend = []

            def flush_pend(keep=0):
                while len(pend) > keep:
                    pend.pop(0)()

            def finalize_pair(c, pos, t_lo):
                hA, hB = 2 * c, 2 * c + 1
                P.ts("dve", rc[0:64, :], pos[0][64:128, :], esink[64:128, i * 16 + hA:i * 16 + hA + 1], ALU.add)
                P.ts("dve", rc[64:128, :], pos[1][0:64, :], esink[0:64, i * 16 + hB:i * 16 + hB + 1], ALU.add)
                if ctx:
                    P.recip(rc[:, :], rc[:, :])
                else:
                    P.act(rc[:, :], rc[:, :], AF.Ln)
                    P.act(rc[:, :], rc[:, :], AF.Exp, scale=-1.0)
                P.tt("dve", mbuf[0:64, c, t_lo:t_lo + 512], pos[0][0:64, :], rc[0:64, :], ALU.mult)
                P.tt("dve", mbuf[64:128, c, t_lo:t_lo + 512], pos[1][64:128, :], rc[64:128, :], ALU.mult)

            ucount = 0
            for c in range(8):
                kvh = c // 2
                ntile = (nseq // 2) if not ctx else (T // 512)
                for tix in range(ntile):
                    pos = []
                    if ctx:
                        qt = tix
                        q0 = qt * 512
                        pos = [bank("o4"), bank("o4")]
                        jobs = []
                        for jj in range(2):
                            jobs.append((T // 128 + jj, 0, 512, []))
                        for j in range(4 * qt - 1, 4 * qt + 5):
                            if j < 0 or j >= T // 128:
                                continue
                            nlo = max(4 * qt, j - 1)
                            nhi = min(4 * qt + 3, j + 1)
                            mk = []
                            for nb in range(nlo, nhi + 1):
                                if nb == j - 1:
                                    mk.append((nb, 0))
                                elif nb == j + 1:
                                    mk.append((nb, 1))
                            jobs.append((j, (nlo - 4 * qt) * 128, (nhi - 4 * qt + 1) * 128, mk))
                        sc_ps = {}

                        def issue_s(ji):
                            kc, lo, hi, mk_ = jobs[ji]
                            for half in (0, 1):
                                r0, r1 = half * 64, half * 64 + 64
                                ps_ = bank("s")
                                P.mm(ps_[:, lo:hi], kd[r0:r1, kvh, kc * 128:(kc + 1) * 128], qT[r0:r1, c, q0 + lo:q0 + hi],
                                     start=True, stop=(len(mk_) == 0))
                                sc_ps[(ji, half)] = ps_
                            for half in (0, 1):
                                for mi, (nb, which) in enumerate(mk_):
                                    cl = (nb - 4 * qt) * 128
                                    P.mm(sc_ps[(ji, half)][:, cl:cl + 128], identb[:, :], masks[:, which, :],
                                         start=False, stop=(mi == len(mk_) - 1))

                        for ji in range(min(2, len(jobs))):
                            issue_s(ji)
                        for ji, (kc, lo, hi, mk) in enumerate(jobs):
                            pts = []
                            for half in (0, 1):
                                pt = pT[(2 * ji + half) % 4]
                                P.act(pt[:, lo:hi], sc_ps.pop((ji, half))[:, lo:hi], AF.Exp)
                                pts.append(pt)
                            if ji + 2 < len(jobs):
                                issue_s(ji + 2)
                            for half in (0, 1):
                                vsrc = va if half == 0 else vaB
                                P.mm(pos[half][:, lo:hi], vsrc[:, kc, kvh, :], pts[half][:, lo:hi],
                                     start=(ji == 0), stop=(ji == len(jobs) - 1))
                            if ji == 2:
                                flush_pend()
                        pend.append(lambda c=c, pos=pos, q0=q0: finalize_pair(c, pos, q0))
                        continue
                    for half in (0, 1):
                        r0, r1 = half * 64, half * 64 + 64
                        vsrc = va if half == 0 else vaB
                        p = tix
                        pts = []
                        for a in range(2):
                            sq_ = 2 * p + a
                            ps_ = bank("s")
                            for j in range(2):
                                P.mm(ps_[:, j * 256:(j + 1) * 256], kd[r0:r1, kvh, sq_ * 256 + j * 128:sq_ * 256 + (j + 1) * 128],
                                     qT[r0:r1, c, sq_ * 256:(sq_ + 1) * 256])
                            pt = pT[(2 * ucount + a) % 4]
                            P.act(pt[:, :], ps_[:, :], AF.Exp)
                            pts.append(pt)
                        ucount += 1
                        flush_pend()
                        po = bank("o4")
                        pos.append(po)

                        def pv(p=p, pts=pts, po=po, vsrc=vsrc, kvh=kvh):
                            for a in range(2):
                                sq_ = 2 * p + a
                                for j in range(2):
                                    P.mm(po[:, a * 256:(a + 1) * 256], vsrc[:, 2 * sq_ + j, kvh, :], pts[a][:, j * 256:(j + 1) * 256],
                                         start=(j == 0), stop=(j == 1))
                        pend.append(pv)
                        if half == 1:
                            pend.append(lambda c=c, pos=pos, p=p: finalize_pair(c, pos, p * 512))
            flush_pend()
            AR.release("vaB")
            for nm in ["pT%d" % b for b in range(4)] + ["rc", "kd", "va"]:
                AR.release(nm)
            if stages == 2 and l == trunc_l[0]:
                raise _Trunc()
            stage3(l, m, NT, mbuf, wg, wout, ada_hooks, nxt)
            for nm in ("wg", "wout", "mbuf"):
                AR.release(nm)

        def load_x(src, NT):
            stage = AR.f32("xstage", [128, 4, 1024])
            for t0 in range(0, NT, 512):
                for j in range(4):
                    P.dma(stage[:, j, :], src[t0 + j * 128:t0 + (j + 1) * 128, :])
                for c in range(8):
                    pb = bank()
                    for j in range(4):
                        P.tr(pb[:, j * 128:(j + 1) * 128], stage[:, j, c * 128:(c + 1) * 128], ident[:, :])
                    P.cp(evac_eng(), xT[:, c, t0:t0 + 512], pb[:, :])
            AR.release("xstage")

        def store_x(dst, NT):
            stage = AR.f32("xstage", [128, 4, 1024])
            for t0 in range(0, NT, 512):
                for j in range(4):
                    for hb in range(2):
                        pb = bank()
                        for cc in range(4):
                            c = hb * 4 + cc
                            P.tr(pb[:, cc * 128:(cc + 1) * 128], xT[:, c, t0 + j * 128:t0 + (j + 1) * 128], ident[:, :])
                        P.cp(evac_eng(), stage[:, j, hb * 512:(hb + 1) * 512], pb[:, :])
                    P.dma(dst[t0 + j * 128:t0 + (j + 1) * 128, :], stage[:, j, :], out_dma=True)
            AR.release("xstage")

        trunc_l = [-1]
        try:
          for (src, dst, m, NT, nseq, T, ctx) in ((xp_d, yp_d, 0, 1024, 4, 256, False),
                                                   (xs_d, ys_d, 1, 2048, 1, 2048, True)):
              nl = nl_p if not ctx else nl_s
              if nl < 0:
                  continue
              load_x(src, NT)
              if not ada0_done[0]:
                  ada_all(0)
                  ada0_done[0] = True
              trunc_l[0] = nl - 1 if ((ctx and nl_s >= 0) or (not ctx and nl_s < 0)) else -1
              for l in range(nl):
                  if l + 1 < nl:
                      nxt = (l + 1, ctx, not ctx)
                  elif (not ctx) and nl_s > 0:
                      nxt = (0, True, True)
                  else:
                      nxt = None
                  store_state["dst"] = dst if (l == nl - 1 and stages == 3) else None
                  store_state["done"] = False
                  if l % 2 == 0:
                      even_layer(l, m, NT, nseq, T, ctx, nxt)
                  else:
                      odd_layer(l, m, NT, nseq, T, ctx, nxt)
              if not store_state["done"]:
                  store_x(dst, NT)
              store_state["dst"] = None

        except _Trunc:
            P.dma(yp_d[0:128, 0:512], rstd[:, :], out_dma=True)

        P.finish()
        build_program.stats = {e: len(P.ops[e]) for e in ENGS}
        build_program.peak = AR.peak
    return nc


_CONSTS = None


def _consts():
    global _CONSTS
    if _CONSTS is None:
        mla, swa = _rope_tables()
        pm96, pm128 = _perm_mats()
        _CONSTS = dict(ident=np.eye(128, dtype=np.float32), pm96=pm96, pm128=pm128, masks=_masks(),
                       amats=_pool_mats().reshape(20, 128, 128), mla_cs=mla, swa_cs=swa)
    return _CONSTS


def kernel(x_prompt, x_sample, cache_ckv, cache_krope, cache_k, cache_v, c, c_ctx,
           ada_w, ada_b, norm_pre, norm_post,
           mla_w_in, mla_g_qn, mla_g_kvn, mla_w_uq, mla_w_ukv, pool_w, pool_scale, mixa_w_out,
           swa_w_in, swa_sink, swa_w_out):
    f = lambda a: np.ascontiguousarray(np.asarray(a, dtype=np.float32))
    x_prompt, x_sample = f(x_prompt), f(x_sample)
    cache_ckv, cache_krope, cache_k, cache_v = f(cache_ckv), f(cache_krope), f(cache_k), f(cache_v)
    c, c_ctx = f(c), f(c_ctx)
    vecs = np.zeros((14, 1024), np.float32)
    vecs[0:4] = f(norm_pre)
    vecs[4:8] = f(norm_post)
    vecs[8:10, :384] = f(mla_g_qn)
    vecs[10:12, :256] = f(mla_g_kvn)
    vecs[12:14, :512] = f(pool_scale)
    shared = dict(ada_w=f(ada_w), ada_b=f(ada_b), vecs=vecs, gkvn=f(mla_g_kvn), sink=f(swa_sink).reshape(32),
                  w_in_e=f(mla_w_in), w_uq=f(mla_w_uq), w_ukv=f(mla_w_ukv), w_pool=f(pool_w), w_out_e=f(mixa_w_out),
                  w_in_o=f(swa_w_in), w_out_o=f(swa_w_out))
    shared.update(_consts())
    in_maps = []
    for i in range(8):
        d = dict(shared)
        d["xp"] = x_prompt[4 * i:4 * i + 4].reshape(1024, 1024)
        d["xs"] = x_sample[i]
        d["cckv"] = cache_ckv[i]
        d["ckr"] = cache_krope[i]
        d["ck"] = cache_k[i].reshape(2, 256, 256)
        d["cv"] = cache_v[i].reshape(2, 256, 256)
        d["cc"] = np.stack([c_ctx, c[i]], axis=0)
        in_maps.append(d)
    nc = build_program()
    res = run_bass_kernel_spmd(nc, in_maps, core_ids=list(range(8)))
    R = res.results
    y_prompt = np.concatenate([r["yp"].reshape(4, 256, 1024) for r in R], axis=0)
    y_sample = np.stack([r["ys"] for r in R], axis=0)
    st_ckv = np.concatenate([r["st_ckv"] for r in R], axis=0)
    st_kr = np.concatenate([r["st_kr"] for r in R], axis=0)
    st_k = np.concatenate([r["st_k"].reshape(4, 2, 256, 4, 64) for r in R], axis=0)
    st_v = np.concatenate([r["st_v"].reshape(4, 2, 256, 4, 64) for r in R], axis=0)
    return (y_prompt.astype(np.float32), y_sample.astype(np.float32), st_ckv.astype(np.float32),
            st_kr.astype(np.float32), st_k.astype(np.float32), st_v.astype(np.float32))
```

```python
import numpy as np
from contextlib import ExitStack
import concourse.bass as bass
import concourse.mybir as mybir
from concourse.bass_utils import run_bass_kernel_spmd

F32 = mybir.dt.float32
BF16 = mybir.dt.bfloat16
AF = mybir.ActivationFunctionType
ALU = mybir.AluOpType

ENGS = ("pe", "act", "dve", "pool", "sp")
SAME_ENGINE_RAW = True
EMBED_LAST_WAIT = True
EMBED_ENGINES = ("pe", "act", "dve")
N_DMA_SEMS = 24
BUCKET = 4096

D = 1024
EPS = 1e-6
MLA_SCALE = 96 ** -0.5
SWA_SCALE = 64 ** -0.5


class Prog:
    def __init__(self, nc, stack):
        self.nc = nc
        self.stack = stack
        self.ops = {e: [] for e in ENGS}
        self.recs = {}
        self.known = {e: {} for e in ENGS}
        self.eng_sem = {e: stack.enter_context(nc.semaphore("s_" + e)) for e in ENGS}
        self.dma_sems = [stack.enter_context(nc.semaphore("s_dma%d" % i)) for i in range(N_DMA_SEMS)]
        self.dma_cnt = [0] * N_DMA_SEMS
        self.dma_rr = 0
        self.dma_rr_pool = 0
        self.out_dma_tokens = []

    def sb(self, name, shape, dtype):
        return self.stack.enter_context(self.nc.sbuf_tensor("sb_" + name, list(shape), dtype))

    def ps(self, name, shape, dtype=F32):
        return self.stack.enter_context(self.nc.psum_tensor("pp_" + name, list(shape), dtype))

    @staticmethod
    def _box(ap):
        shp = list(ap.tensor.shape)
        row = 1
        for s in shp[1:]:
            row *= s
        isz = mybir.dt.size(ap.dtype)
        off = int(ap.offset)
        dims = ap.ap
        p0 = off // row
        f0 = off % row
        pstep, pcnt = dims[0]
        if pstep == row or (pcnt == 1 and len(dims) > 1):
            p1 = p0 + pcnt
            rest = dims[1:]
        elif pstep == 0:
            p1 = p0 + 1
            rest = dims[1:]
        else:
            p1 = p0 + 1
            rest = dims
        ext = 0
        for st, cn in rest:
            ext += abs(st) * (cn - 1)
        return (p0, p1, f0 * isz, (f0 + ext + 1) * isz)

    @staticmethod
    def _tracked(ap):
        return str(ap.space).upper() in ("SB", "PSUM")

    def op(self, eng, fn, reads=(), writes=(), dma=False, out_dma=False):
        idx = len(self.ops[eng])
        waits = []
        kn = self.known[eng]

        def need(tok, raw=False):
            if tok[0] == 'e':
                if tok[1] == eng and not (raw and SAME_ENGINE_RAW and eng != "pe"):
                    return
                key = tok[1]
            else:
                key = ('d', tok[1])
            if kn.get(key, -1) >= tok[2]:
                return
            kn[key] = tok[2]
            waits.append(tok)

        if dma:
            if eng == "pool":
                k = 8 + self.dma_rr_pool
                self.dma_rr_pool = (self.dma_rr_pool + 1) % (N_DMA_SEMS - 8)
            else:
                k = self.dma_rr
                self.dma_rr = (self.dma_rr + 1) % 8
            if self.dma_cnt[k] > 0:
                need(('d', k, self.dma_cnt[k] * 16))
            self.dma_cnt[k] += 1
            mytok = ('d', k, self.dma_cnt[k] * 16)
        else:
            mytok = ('e', eng, idx)

        rb = [(ap.name, self._box(ap)) for ap in reads if self._tracked(ap) and str(ap.space).upper() != "PSUM"]
        wb = [(ap.name, self._box(ap)) for ap in writes if self._tracked(ap) and str(ap.space).upper() != "PSUM"]
        for ap in list(reads) + list(writes):
            if str(ap.space).upper() == "PSUM":
                ent = (ap.name, (0, 128, 0, 1 << 20))
                if ent not in wb:
                    wb.append(ent)
        for name, box in rb:
            tr = self.recs.get(name)
            if tr is None:
                continue
            for b in range(box[2] // BUCKET, (box[3] - 1) // BUCKET + 1):
                for r in tr.get(b, ()):
                    if r[3] and r[2]:
                        rbx = r[0]
                        if rbx[0] < box[1] and box[0] < rbx[1] and rbx[2] < box[3] and box[2] < rbx[3]:
                            need(r[1], True)
        for name, box in wb:
            tr = self.recs.get(name)
            if tr is None:
                continue
            for b in range(box[2] // BUCKET, (box[3] - 1) // BUCKET + 1):
                lst = tr.get(b)
                if not lst:
                    continue
                for r in lst:
                    if r[3]:
                        rbx = r[0]
                        if rbx[0] < box[1] and box[0] < rbx[1] and rbx[2] < box[3] and box[2] < rbx[3]:
                            need(r[1])
                            if box[0] <= rbx[0] and box[1] >= rbx[1] and box[2] <= rbx[2] and box[3] >= rbx[3]:
                                r[3] = False
                tr[b] = [r for r in lst if r[3]]
        for name, box in rb:
            tr = self.recs.setdefault(name, {})
            rec = [box, mytok, False, True]
            for b in range(box[2] // BUCKET, (box[3] - 1) // BUCKET + 1):
                lst = tr.setdefault(b, [])
                if not dma:
                    for r in lst:
                        if r[3] and (not r[2]) and r[1][0] == 'e' and r[1][1] == eng and r[0] == box:
                            r[3] = False
                lst.append(rec)
        for name, box in wb:
            tr = self.recs.setdefault(name, {})
            rec = [box, mytok, True, True]
            for b in range(box[2] // BUCKET, (box[3] - 1) // BUCKET + 1):
                tr.setdefault(b, []).append(rec)
        o = dict(fn=fn, waits=waits, tok=mytok, marked=False)
        self.ops[eng].append(o)
        if out_dma:
            self.out_dma_tokens.append(mytok)
        return o

    def dma(self, out, in_, eng="sp", out_dma=False):
        return self.op(eng, lambda e: e.dma_start(out=out, in_=in_), reads=[in_], writes=[out],
                       dma=True, out_dma=out_dma)

    def mm(self, out, lhsT, rhs, start=True, stop=True):
        return self.op("pe", lambda e: e.matmul(out, lhsT, rhs, start=start, stop=stop),
                       reads=[lhsT, rhs] + ([] if start else [out]), writes=[out])

    def tr(self, out, in_, ident):
        return self.op("pe", lambda e: e.transpose(out, in_, ident), reads=[in_, ident], writes=[out])

    def act(self, out, in_, func, bias=None, scale=None, accum=None):
        kw = {}
        rd = [in_]
        wr = [out]
        if bias is not None:
            kw["bias"] = bias
            if not isinstance(bias, (int, float)):
                rd.append(bias)
        if scale is not None:
            kw["scale"] = scale
            if not isinstance(scale, (int, float)):
                rd.append(scale)
        if accum is not None:
            kw["accum_out"] = accum
            wr.append(accum)
        return self.op("act", lambda e: e.activation(out=out, in_=in_, func=func, **kw), reads=rd, writes=wr)

    def cp(self, eng, out, in_):
        if eng == "act":
            return self.op("act", lambda e: e.copy(out=out, in_=in_), reads=[in_], writes=[out])
        return self.op(eng, lambda e: e.tensor_copy(out=out, in_=in_), reads=[in_], writes=[out])

    def tt(self, eng, out, in0, in1, op):
        return self.op(eng, lambda e: e.tensor_tensor(out=out, in0=in0, in1=in1, op=op), reads=[in0, in1], writes=[out])

    def ts(self, eng, out, in0, s1, op0, s2=None, op1=None):
        rd = [in0]
        if not isinstance(s1, (int, float)):
            rd.append(s1)
        if s2 is not None and not isinstance(s2, (int, float)):
            rd.append(s2)
        if op1 is None:
            return self.op(eng, lambda e: e.tensor_scalar(out=out, in0=in0, scalar1=s1, scalar2=None, op0=op0),
                           reads=rd, writes=[out])
        return self.op(eng, lambda e: e.tensor_scalar(out=out, in0=in0, scalar1=s1, scalar2=s2, op0=op0, op1=op1),
                       reads=rd, writes=[out])

    def stt(self, eng, out, in0, scalar, in1, op0, op1):
        rd = [in0, in1]
        if not isinstance(scalar, (int, float)):
            rd.append(scalar)
        return self.op(eng, lambda e: e.scalar_tensor_tensor(out=out, in0=in0, scalar=scalar, in1=in1, op0=op0, op1=op1),
                       reads=rd, writes=[out])

    def memset(self, eng, out, val):
        return self.op(eng, lambda e: e.memset(out, val), writes=[out])

    def recip(self, out, in_):
        return self.op("dve", lambda e: e.reciprocal(out=out, in_=in_), reads=[in_], writes=[out])

    def finish(self):
        nc = self.nc
        for e in ENGS:
            for o in self.ops[e]:
                for tok in o["waits"]:
                    if tok[0] == 'e':
                        self.ops[tok[1]][tok[2]]["marked"] = True
        cnt_at = {}
        for e in ENGS:
            c = 0
            arr = []
            for o in self.ops[e]:
                if o["marked"]:
                    c += 1
                arr.append(c)
            cnt_at[e] = arr
        fw = {}
        for tok in self.out_dma_tokens:
            fw[tok[1]] = max(fw.get(tok[1], 0), tok[2])

        with nc.Block() as block:
            def emit(ename, engobj):
                for o in self.ops[ename]:
                    ws = o["waits"]
                    emb = None
                    if EMBED_LAST_WAIT and ws and ename in EMBED_ENGINES and o["tok"][0] == 'e':
                        emb = ws[-1]
                        ws = ws[:-1]
                    for tok in ws:
                        if tok[0] == 'e':
                            engobj.wait_ge(self.eng_sem[tok[1]], cnt_at[tok[1]][tok[2]])
                        else:
                            engobj.wait_ge(self.dma_sems[tok[1]], tok[2])
                    ins = o["fn"](engobj)
                    if emb is not None:
                        if emb[0] == 'e':
                            ins._wait_ge(self.eng_sem[emb[1]], cnt_at[emb[1]][emb[2]])
                        else:
                            ins._wait_ge(self.dma_sems[emb[1]], emb[2])
                    if o["tok"][0] == 'd':
                        ins.then_inc(self.dma_sems[o["tok"][1]], 16)
                    elif o["marked"]:
                        ins.then_inc(self.eng_sem[ename], 1)
                if ename == "sp":
                    for k, v in fw.items():
                        engobj.wait_ge(self.dma_sems[k], v)

            @block.sync
            def _(sync):
                emit("sp", sync)

            @block.tensor
            def _(tensor):
                emit("pe", tensor)

            @block.scalar
            def _(scalar):
                emit("act", scalar)

            @block.vector
            def _(vector):
                emit("dve", vector)

            @block.gpsimd
            def _(gpsimd):
                emit("pool", gpsimd)


class _Trunc(Exception):
    pass


class Arena:
    def __init__(self, tensor, nelem):
        self.t = tensor
        self.n = nelem
        self.free = [(0, nelem)]
        self.live = {}
        self.peak = 0

    def alloc(self, name, nelem_bf16):
        nelem_bf16 = (nelem_bf16 + 15) // 16 * 16
        for i, (o, s) in enumerate(self.free):
            if s >= nelem_bf16:
                if s == nelem_bf16:
                    self.free.pop(i)
                else:
                    self.free[i] = (o + nelem_bf16, s - nelem_bf16)
                self.live[name] = (o, nelem_bf16)
                used = self.n - sum(s for _, s in self.free)
                self.peak = max(self.peak, used)
                return o
        raise RuntimeError("arena OOM allocating %s (%d); live=%s free=%s" % (name, nelem_bf16, self.live, self.free))

    def release(self, name):
        o, s = self.live.pop(name)
        self.free.append((o, s))
        self.free.sort()
        merged = []
        for o, s in self.free:
            if merged and merged[-1][0] + merged[-1][1] == o:
                merged[-1] = (merged[-1][0], merged[-1][1] + s)
            else:
                merged.append((o, s))
        self.free = merged

    def bf(self, name, shape):
        n = 1
        for s in shape[1:]:
            n *= s
        o = self.alloc(name, n)
        v = self.t[0:shape[0], o:o + n]
        if len(shape) == 3:
            v = v.rearrange("p (a b) -> p a b", a=shape[1])
        elif len(shape) == 4:
            v = v.rearrange("p (a b c) -> p a b c", a=shape[1], b=shape[2])
        return v

    def f32(self, name, shape):
        n = 1
        for s in shape[1:]:
            n *= s
        o = self.alloc(name, 2 * n)
        v = self.t[0:shape[0], o:o + 2 * n].bitcast(F32)
        if len(shape) == 3:
            v = v.rearrange("p (a b) -> p a b", a=shape[1])
        elif len(shape) == 4:
            v = v.rearrange("p (a b c) -> p a b c", a=shape[1], b=shape[2])
        return v


def _rope_tables():
    t = np.arange(2048)
    row = (t // 64).astype(np.float64)
    col = (t % 64).astype(np.float64)
    mla = np.zeros((2, 96, 2048), np.float32)
    mla[0, 0:64] = 1.0
    for r in range(32):
        pos = row if r < 16 else col
        i = r % 16
        f = i % 8
        inv = 10000.0 ** (-(2.0 * f) / 16.0)
        ang = pos * inv
        mla[0, 64 + r] = np.cos(ang)
        mla[1, 64 + r] = np.sin(ang) if i < 8 else -np.sin(ang)
    swa = np.zeros((2, 128, 2048), np.float32)
    for p in range(128):
        d = p % 64
        pos = row if d < 32 else col
        i = d % 32
        f = i % 16
        inv = 10000.0 ** (-(2.0 * f) / 32.0)
        ang = pos * inv
        swa[0, p] = np.cos(ang)
        swa[1, p] = np.sin(ang) if i < 16 else -np.sin(ang)
    return mla, swa


def _perm_mats():
    pm96 = np.zeros((96, 96), np.float32)
    for r in range(32):
        i = r % 16
        partner = r + 8 if i < 8 else r - 8
        pm96[64 + partner, 64 + r] = 1.0
    pm128 = np.zeros((128, 128), np.float32)
    for m in range(128):
        i = m % 32
        partner = m + 16 if i < 16 else m - 16
        pm128[partner, m] = 1.0
    return pm96, pm128


def _pool_mats():
    T = 384
    out = np.zeros((4, 5, 128, 128), np.float32)
    for g, w in enumerate((2, 4, 8, 16)):
        A = np.zeros((T, T), np.float64)
        for t in range(T):
            lo = min(max(t - w // 2, 0), T)
            hi = min(max(t + w // 2, 0), T)
            A[lo:hi, t] = 1.0 / (hi - lo)
            A[t, t] -= 1.0
        out[g, 0] = A[0:128, 0:128]
        out[g, 1] = A[128:256, 128:256]
        out[g, 2] = A[256:384, 256:384]
        out[g, 3] = A[0:128, 128:256]
        out[g, 4] = A[128:256, 0:128]
    return out


def _masks():
    k = np.arange(128)[:, None]
    q = np.arange(128)[None, :]
    m = np.zeros((2, 128, 128), np.float32)
    m[0] = np.where(k <= q, 0.0, -30000.0)
    m[1] = np.where(q <= k, 0.0, -30000.0)
    return m


def build_program(nl_p=4, nl_s=4, stages=3, dbg=None):
    nc = bass.Bass("TRN2", target_bir_lowering=False)

    def din(name, shape):
        return nc.dram_tensor(name, list(shape), F32, kind="ExternalInput").ap()

    def dout(name, shape):
        return nc.dram_tensor(name, list(shape), F32, kind="ExternalOutput").ap()

    xp_d = din("xp", [1024, 1024])
    xs_d = din("xs", [2048, 1024])
    cckv_d = din("cckv", [2, 256, 256])
    ckr_d = din("ckr", [2, 256, 32])
    ck_d = din("ck", [2, 256, 256])
    cv_d = din("cv", [2, 256, 256])
    cc_d = din("cc", [2, 1024])
    adaw_d = din("ada_w", [4, 1024, 3072])
    adab_d = din("ada_b", [4, 3072])
    vecs_d = din("vecs", [14, 1024])
    gkvn_d = din("gkvn", [2, 256])
    sink_d = din("sink", [32])
    wine_d = din("w_in_e", [2, 1024, 2208])
    wuq_d = din("w_uq", [2, 384, 768])
    wukv_d = din("w_ukv", [2, 256, 1024])
    wpool_d = din("w_pool", [2, 4, 128, 128])
    woe_d = din("w_out_e", [2, 1024, 1024])
    wino_d = din("w_in_o", [2, 1024, 2560])
    woo_d = din("w_out_o", [2, 1024, 1024])
    ident_d = din("ident", [128, 128])
    pm96_d = din("pm96", [96, 96])
    pm128_d = din("pm128", [128, 128])
    masks_d = din("masks", [2, 128, 128])
    amats_d = din("amats", [20, 128, 128])
    mlacs_d = din("mla_cs", [2, 96, 2048])
    swacs_d = din("swa_cs", [2, 128, 2048])

    yp_d = dout("yp", [1024, 1024])
    ys_d = dout("ys", [2048, 1024])
    sckv_d = dout("st_ckv", [4, 2, 256, 256])
    skr_d = dout("st_kr", [4, 2, 256, 32])
    sk_d = dout("st_k", [4, 2, 256, 256])
    sv_d = dout("st_v", [4, 2, 256, 256])

    with ExitStack() as st:
        P = Prog(nc, st)
        xT = P.sb("xT", [128, 8, 2048], F32)
        ident = P.sb("ident", [128, 128], F32)
        ones_bf = P.sb("ones_bf", [128, 128], BF16)
        identb = P.sb("identb", [128, 128], BF16)
        scb = P.sb("scb", [128, 8, 2], BF16)
        pm96 = P.sb("pm96", [96, 96], BF16)
        pm128 = P.sb("pm128", [128, 128], BF16)
        masks = P.sb("masks", [128, 2, 128], BF16)
        epsT = P.sb("epsT", [128, 1], F32)
        mod = P.sb("mod", [128, 4, 48], F32)
        vecT = P.sb("vecT", [128, 8, 32], F32)
        coefA = P.sb("coefA", [128, 4, 2, 8], F32)
        coefB = P.sb("coefB", [128, 4, 2, 8], F32)
        coefG = P.sb("coefG", [128, 4, 2, 8], F32)
        gkvb = P.sb("gkvb", [128, 2, 256], F32)
        esink = P.sb("esink", [128, 32], F32)
        rstd = P.sb("rstd", [128, 512], F32)
        rstd2 = P.sb("rstd2", [128, 512], F32)
        tmpf = [P.sb("tmpf%d" % i, [128, 512], F32) for i in range(2)]
        ARENA_N = 64 * 1024
        arena_t = P.sb("arena", [128, ARENA_N], BF16)
        AR = Arena(arena_t, ARENA_N)
        banks = [P.ps("ps%d" % i, [128, 512], F32) for i in range(8)]
        rr = {"all": 0, "s": 0, "o": 0, "g": 0}
        pools = {"all": list(range(8)), "s": [0, 1, 2, 3], "o": [4, 5], "g": [6, 7]}

        def bank(pool="all"):
            lst = pools[pool]
            b = banks[lst[rr[pool] % len(lst)]]
            rr[pool] += 1
            return b

        evac_rr = [0]

        def evac_eng():
            evac_rr[0] += 1
            return "dve" if evac_rr[0] % 2 else "act"

        P.dma(ident[:], ident_d)
        P.dma(identb[:], ident_d, eng="pool")
        P.dma(pm96[:], pm96_d, eng="pool")
        P.dma(pm128[:], pm128_d, eng="pool")
        P.dma(masks[:], masks_d.rearrange("a k q -> k a q"), eng="pool")
        P.dma(gkvb[:].rearrange("p a n -> p (a n)"), gkvn_d.rearrange("a n -> (a n)").partition_broadcast(128))
        P.dma(esink[:], sink_d.partition_broadcast(128))
        P.memset("dve", ones_bf[:], 1.0)
        P.memset("dve", epsT[:], EPS)
        P.act(esink[:], esink[:], AF.Exp)

        if dbg == "consts":
            P.dma(yp_d[0:128, 0:32], esink[:], out_dma=True)
            P.finish()
            return nc
        vst = AR.f32("vst", [32, 1024])
        P.memset("dve", vst[:], 0.0)
        P.dma(vst[0:14, :], vecs_d)
        pv = bank()
        for c in range(8):
            P.tr(pv[:, c * 32:(c + 1) * 32], vst[0:32, c * 128:(c + 1) * 128], ident[0:32, 0:32])
        for c in range(8):
            P.cp("dve", vecT[:, c, :], pv[:, c * 32:(c + 1) * 32])
        AR.release("vst")

        if dbg == "vec":
            P.dma(yp_d[0:128, 0:256], vecT[:].rearrange("p a b -> p (a b)"), out_dma=True)
            P.finish()
            return nc
        ccT = AR.f32("ccT", [128, 2, 8])
        for m in range(2):
            P.dma(ccT[:, m, :], cc_d[m].rearrange("(p c) -> p c", c=8))
        for m in range(2):
            P.act(scb[:, :, m], ccT[:, m, :], AF.Silu)
        AR.release("ccT")
        modv = mod[:].rearrange("p l (j c m) -> p l j c m", j=3, c=8)
        ada_state = {}

        def ada_alloc(l, nbuf):
            ada_state["l"] = l
            ada_state["adab"] = AR.bf("adab", [1, 3072])
            ada_state["bufs"] = [AR.bf("adw%d" % i, [128, 8, 512]) for i in range(nbuf)]
            ada_state["nbuf"] = nbuf
            P.dma(ada_state["adab"][:], adab_d[l:l + 1, :], eng="pool")

        def ada_issue(blks):
            l = ada_state["l"]
            wv = adaw_d[l].rearrange("(p c) n -> p c n", c=8)
            for blk in blks:
                wt = ada_state["bufs"][blk % ada_state["nbuf"]]
                P.dma(wt[:], wv[:, :, blk * 512:(blk + 1) * 512], eng="pool")

        def ada_compute(blks):
            l = ada_state["l"]
            adab = ada_state["adab"]
            for blk in blks:
                wt = ada_state["bufs"][blk % ada_state["nbuf"]]
                pm = bank()
                for nci in range(4):
                    nch = blk * 4 + nci
                    for c in range(8):
                        P.mm(pm[:, 2 * nci:2 * nci + 2], wt[:, c, nci * 128:(nci + 1) * 128], scb[:, c, :],
                             start=(c == 0), stop=False)
                    P.mm(pm[:, 2 * nci:2 * nci + 2], adab[0:1, nch * 128:(nch + 1) * 128], ones_bf[0:1, 0:2],
                         start=False, stop=True)
                P.cp("dve", mod[:, l, blk * 8:(blk + 1) * 8], pm[:, 0:8])

        def ada_finish():
            l = ada_state["l"]
            for m in range(2):
                P.stt("dve", coefA[:, l, m, :], modv[:, l, 1, :, m], 1.0, vecT[:, :, l], ALU.add, ALU.mult)
                P.cp("dve", coefB[:, l, m, :], modv[:, l, 0, :, m])
                P.tt("dve", coefG[:, l, m, :], modv[:, l, 2, :, m], vecT[:, :, 4 + l], ALU.mult)
            for i in range(ada_state["nbuf"]):
                AR.release("adw%d" % i)
            AR.release("adab")
            ada_state.clear()

        def ada_all(l):
            ada_alloc(l, 3)
            for blk in range(6):
                ada_issue([blk])
                ada_compute([blk])
            ada_finish()

        ADA_INTERLEAVE = nl_p >= 4
        ada0_done = [False]
        if not ADA_INTERLEAVE:
            for l in range(1, 4):
                ada_all(l)
        if dbg == "ada":
            P.dma(yp_d[0:128, 0:192], mod[:].rearrange("p a b -> p (a b)"), out_dma=True)
            P.dma(yp_d[128:256, 0:64], coefA[:].rearrange("p a b c -> p (a b c)"), out_dma=True)
            P.dma(yp_d[256:384, 0:64], coefG[:].rearrange("p a b c -> p (a b c)"), out_dma=True)
            P.finish()
            return nc
        def rstd_from_ssq(ps_ssq, n, dim, out):
            P.act(out, ps_ssq, AF.Ln, bias=epsT[:, 0:1], scale=1.0 / dim)
            P.act(out, out, AF.Exp, scale=-0.5)

        def modulate_parts(hT, sq, l, m, t0, n):
            parts = []

            ns_ = sq.shape[1]

            def stats_a():
                for c in range(8):
                    P.act(sq[:, c, :n], xT[:, c, t0:t0 + n], AF.Square)

            def stats_b():
                pb = bank()
                for c in range(8):
                    P.mm(pb[:, :n], ones_bf[:, :], sq[:, c, :n], start=(c == 0), stop=(c == 7))
                rstd_from_ssq(pb[:, :n], n, 1024, rstd[:, :n])

            def stats():
                pb = bank()
                for c0 in range(0, 8, ns_):
                    for c in range(c0, c0 + ns_):
                        P.act(sq[:, c % ns_, :n], xT[:, c, t0:t0 + n], AF.Square)
                    for c in range(c0, c0 + ns_):
                        P.mm(pb[:, :n], ones_bf[:, :], sq[:, c % ns_, :n], start=(c == 0), stop=(c == 7))
                rstd_from_ssq(pb[:, :n], n, 1024, rstd[:, :n])
            if ns_ >= 8:
                parts.extend([stats_a, (lambda: None), stats_b])
            else:
                parts.append(stats)
            for c in range(8):
                def ap(c=c):
                    tf = tmpf[c % 2]
                    P.stt("dve", tf[:, :n], xT[:, c, t0:t0 + n], coefA[:, l, m, c:c + 1], rstd[:, :n], ALU.mult, ALU.mult)
                    P.act(hT[:, c, :n], tf[:, :n], AF.Identity, bias=coefB[:, l, m, c:c + 1], scale=1.0)
                parts.append(ap)
            return parts

        def modulate(hT, sq, l, m, t0, n):
            for f in modulate_parts(hT, sq, l, m, t0, n):
                f()

        def run_part(parts, k=1):
            for _ in range(k):
                if parts:
                    parts.pop(0)()

        def load_w(dst, src_rows_view, col0, ncols):
            for c in range(dst.shape[1]):
                P.dma(dst[:, c, :], src_rows_view[:, c, col0:col0 + ncols], eng="pool")

        def alloc_hs():
            hs = [(AR.bf("hT", [128, 8, 512]), AR.bf("sq", [128, 8, 512]))]
            try:
                a = AR.bf("hT1", [128, 8, 512])
                try:
                    b = AR.bf("sq1", [128, 8, 512])
                    hs.append((a, b))
                except RuntimeError:
                    AR.release("hT1")
            except RuntimeError:
                pass
            return hs

        def free_hs(hs):
            AR.release("hT")
            AR.release("sq")
            if len(hs) > 1:
                AR.release("hT1")
                AR.release("sq1")

        store_state = {"dst": None, "done": False}

        def store_tile(dst, stage2, t0):
            for j in range(4):
                for hb in range(2):
                    pb = bank()
                    for cc in range(4):
                        c = hb * 4 + cc
                        P.tr(pb[:, cc * 128:(cc + 1) * 128], xT[:, c, t0 + j * 128:t0 + (j + 1) * 128], ident[:, :])
                    P.cp(evac_eng(), stage2[:, j % 2, hb * 512:(hb + 1) * 512], pb[:, :])
                P.dma(dst[t0 + j * 128:t0 + (j + 1) * 128, :], stage2[:, j % 2, :], out_dma=True)

        def stage3(l, m, NT, mbuf, wg, wout, hooks=None, nxt=None):
            oT = AR.f32("oT", [128, 8, 512])
            hs3 = alloc_hs()
            sg = [AR.bf("sg%d" % i, [128, 512]) for i in range(2)]
            if nxt is not None:
                prefetch_w1(*nxt)
            stage2 = None
            spend = [None]
            if store_state["dst"] is not None:
                try:
                    stage2 = AR.f32("xst2", [128, 2, 1024])
                except RuntimeError:
                    stage2 = None
            tiles = list(range(0, NT, 512))
            n = 512
            pipe = len(hs3) > 1
            if pipe:
                modulate(hs3[0][0], hs3[0][1], l, m, tiles[0], n)
            for ti, t0 in enumerate(tiles):
                if hooks and ti in hooks:
                    hooks[ti]()
                hT, sq = hs3[ti % len(hs3)]
                if not pipe:
                    modulate(hT, sq, l, m, t0, n)
                nparts = []
                if pipe and ti + 1 < len(tiles):
                    nh, nsq = hs3[(ti + 1) % 2]
                    nparts = modulate_parts(nh, nsq, l, m, tiles[ti + 1], n)
                for mc in range(8):
                    pb = bank()
                    for k in range(8):
                        P.mm(pb[:, :n], wg[:, k, mc * 128:(mc + 1) * 128], hT[:, k, :n], start=(k == 0), stop=(k == 7))
                    s_ = sg[mc % 2]
                    P.act(s_[:, :n], pb[:, :n], AF.Silu)
                    P.tt("dve", mbuf[:, mc, t0:t0 + n], mbuf[:, mc, t0:t0 + n], s_[:, :n], ALU.mult)
                    if mc == 1:
                        run_part(nparts)
                if spend[0] is not None:
                    spend[0]()
                    spend[0] = None
                for dc in range(8):
                    pb = bank()
                    for k in range(8):
                        P.mm(pb[:, :n], wout[:, k, dc * 128:(dc + 1) * 128], mbuf[:, k, t0:t0 + n],
                             start=(k == 0), stop=(k == 7))
                    P.cp("dve", oT[:, dc, :n], pb[:, :n])
                    P.act(sq[:, dc, :n], pb[:, :n], AF.Square)
                    run_part(nparts)
                run_part(nparts, 16)
                pb = bank()
                for dc in range(8):
                    P.mm(pb[:, :n], ones_bf[:, :], sq[:, dc, :n], start=(dc == 0), stop=(dc == 7))
                rstd_from_ssq(pb[:, :n], n, 1024, rstd2[:, :n])
                for dc in range(8):
                    tf = tmpf[dc % 2]
                    P.stt("dve", tf[:, :n], oT[:, dc, :n], coefG[:, l, m, dc:dc + 1], rstd2[:, :n], ALU.mult, ALU.mult)
                    P.tt("dve", xT[:, dc, t0:t0 + n], xT[:, dc, t0:t0 + n], tf[:, :n], ALU.add)
                if stage2 is not None:
                    spend[0] = (lambda t0=t0: store_tile(store_state["dst"], stage2, t0))
            if hooks and "post" in hooks:
                hooks["post"]()
            if stage2 is not None:
                if spend[0] is not None:
                    spend[0]()
                AR.release("xst2")
                store_state["done"] = True
            free_hs(hs3)
            for nm in ("oT", "sg0", "sg1"):
                AR.release(nm)

        rope_rr = [0]

        def rope_a(src_ps, pr, n, scale, pm, cs, t0, rb):
            qc, qsn = rb[rope_rr[0] % len(rb)]
            rope_rr[0] += 1
            P.stt("dve", qc[0:pr, :n], src_ps[0:pr, :n], float(scale), cs[0:pr, 0, t0:t0 + n], ALU.mult, ALU.mult)
            P.stt("dve", qsn[0:pr, :n], src_ps[0:pr, :n], float(scale), cs[0:pr, 1, t0:t0 + n], ALU.mult, ALU.mult)

            def phase_b():
                p2 = bank("g")
                P.mm(p2[0:pr, :n], identb[0:pr, 0:pr], qc[0:pr, :n], start=True, stop=False)
                P.mm(p2[0:pr, :n], pm[:, :], qsn[0:pr, :n], start=False, stop=True)
                return p2
            return phase_b

        def rope_apply(src_ps, pr, n, scale, pm, cs, t0, rb):
            return rope_a(src_ps, pr, n, scale, pm, cs, t0, rb)()

        def alloc_rb():
            return [(AR.bf("rqc%d" % i, [128, 512]), AR.bf("rqs%d" % i, [128, 512])) for i in range(2)]

        def free_rb():
            for i in range(2):
                AR.release("rqc%d" % i)
                AR.release("rqs%d" % i)

        pref = {}

        def prefetch_w1(lnext, ctx_next, full):
            i2 = lnext // 2
            if (not full) and lnext % 2 == 1:
                return
            try:
                if lnext % 2 == 0:
                    wv = wine_d[i2].rearrange("(c p) n -> p c n", p=128)
                    a = AR.bf("w1a", [128, 8, 672])
                    load_w(a, wv, 0, 672)
                    pref["w1a"] = a
                    if full:
                        b_ = AR.bf("w1b", [128, 8, 512])
                        load_w(b_, wv, 1184, 512)
                        pref["w1b"] = b_
                else:
                    wv = wino_d[i2].rearrange("(c p) n -> p c n", p=128)
                    a = AR.bf("wq0", [128, 8, 512])
                    load_w(a, wv, 0, 512)
                    pref["wq0"] = a
                    if full:
                        b_ = AR.bf("wq1", [128, 8, 512])
                        load_w(b_, wv, 512, 512)
                        pref["wq1"] = b_
                        k_ = AR.bf("wk", [128, 8, 256])
                        load_w(k_, wv, 1024, 256)
                        pref["wk"] = k_
            except RuntimeError:
                pass

        def even_layer(l, m, NT, nseq, T, ctx, nxt=None):
            i = l // 2
            S = T + (256 if ctx else 0)
            KT = nseq * S
            nkc = S // 128
            wv_in = wine_d[i].rearrange("(c p) n -> p c n", p=128)
            w1a = pref.pop("w1a", None)
            if w1a is None:
                w1a = AR.bf("w1a", [128, 8, 672])
                load_w(w1a, wv_in, 0, 672)
            w1b = pref.pop("w1b", None)
            if w1b is None:
                w1b = AR.bf("w1b", [128, 8, 512])
                load_w(w1b, wv_in, 1184, 512)
            wp = AR.bf("wp", [128, 4, 128])
            amats = AR.bf("amats", [128, 20, 128])
            P.dma(amats[:], amats_d.rearrange("a k q -> k a q"), eng="pool")
            P.dma(wp[:], wpool_d[i].rearrange("g c e -> c g e"), eng="pool")
            mbuf = AR.bf("mbuf", [128, 8, NT])
            mo = AR.live["mbuf"][0]
            vtok = arena_t[:, mo:mo + (NT // 128) * 512].rearrange("p (j q) -> p j q", q=512)
            cqn = AR.bf("cqn", [128, 3, NT])
            ckvn = AR.bf("ckvn", [128, 2, KT])
            krT = AR.bf("krT", [96, KT])
            cs = None
            if ctx:
                cs = AR.bf("cs", [96, 2, 2048])
                P.dma(cs[:, 0, :], mlacs_d[0], eng="pool")
                P.dma(cs[:, 1, :], mlacs_d[1], eng="pool")
                rb = alloc_rb()
                cst = AR.f32("cst", [128, 2, 256 + 96])
                P.memset("pool", cst[:, :, 256:320], 0.0)
                for jj in range(2):
                    P.dma(cst[:, jj, 0:256], cckv_d[i, jj * 128:(jj + 1) * 128, :])
                    P.dma(cst[:, jj, 320:352], ckr_d[i, jj * 128:(jj + 1) * 128, :])
                for jj in range(2):
                    pb = bank()
                    for c in range(2):
                        P.tr(pb[:, c * 128:(c + 1) * 128], cst[:, jj, c * 128:(c + 1) * 128], ident[:, :])
                    P.tr(pb[0:96, 256:384], cst[:, jj, 256:352], ident[:, :])
                    for c in range(2):
                        P.cp("dve", ckvn[:, c, T + jj * 128:T + (jj + 1) * 128], pb[:, c * 128:(c + 1) * 128])
                    P.cp("dve", krT[64:96, T + jj * 128:T + (jj + 1) * 128], pb[64:96, 256:384])
                AR.release("cst")
            stg = None
            if not ctx:
                stg = [AR.f32("stg%d" % j, [128, 288]) for j in range(2)]
            junk = AR.f32("junk", [128, 256])
            ssq1 = AR.f32("ssq1", [128, 2])
            hs1 = alloc_hs()

            def kidx(t):
                return (t // T) * S + (t % T)

            pipe1 = len(hs1) > 1
            if pipe1:
                modulate(hs1[0][0], hs1[0][1], l, m, 0, 512)
            for t0 in range(0, NT, 512):
                n = 512
                hT, sq = hs1[(t0 // 512) % len(hs1)]
                if not pipe1:
                    modulate(hT, sq, l, m, t0, n)
                nparts = []
                if pipe1 and t0 + 512 < NT:
                    nh, nsq = hs1[((t0 // 512) + 1) % 2]
                    nparts = modulate_parts(nh, nsq, l, m, t0 + 512, 512)
                for oc in range(3):
                    pb = bank()
                    for k in range(8):
                        P.mm(pb[:, :n], w1a[:, k, oc * 128:(oc + 1) * 128], hT[:, k, :n], start=(k == 0), stop=(k == 7))
                    P.act(sq[:, oc, :n], pb[:, :n], AF.Square)
                    P.ts("dve", cqn[:, oc, t0:t0 + n], pb[:, :n], vecT[:, oc, 8 + i:9 + i], ALU.mult)
                    run_part(nparts)
                pieces = [(t0, n)] if T >= 512 else [(t0 + a, T) for a in range(0, n, T)]
                for oc in range(2):
                    pb = bank()
                    for k in range(8):
                        P.mm(pb[:, :n], w1a[:, k, 384 + oc * 128:384 + (oc + 1) * 128], hT[:, k, :n],
                             start=(k == 0), stop=(k == 7))
                    P.act(sq[:, 4 + oc, :n], pb[:, :n], AF.Square)
                    for (ta, tn) in pieces:
                        P.ts("dve", ckvn[:, oc, kidx(ta):kidx(ta) + tn], pb[:, ta - t0:ta - t0 + tn],
                             vecT[:, oc, 10 + i:11 + i], ALU.mult)
                    run_part(nparts)
                pb = bank()
                for oc in range(3):
                    P.mm(pb[:, :n], ones_bf[:, :], sq[:, oc, :n], start=(oc == 0), stop=(oc == 2))
                rstd_from_ssq(pb[:, :n], n, 384, rstd2[:, :n])
                for oc in range(3):
                    P.tt("dve", cqn[:, oc, t0:t0 + n], cqn[:, oc, t0:t0 + n], rstd2[:, :n], ALU.mult)
                pb = bank()
                for k in range(8):
                    P.mm(pb[0:96, :n], w1a[:, k, 576:672], hT[:, k, :n], start=(k == 0), stop=(k == 7))
                if ctx:
                    p2 = rope_apply(pb, 96, n, 1.0, pm96, cs, t0, rb)
                    P.cp("act", krT[64:96, kidx(t0):kidx(t0) + n], p2[64:96, :n])
                else:
                    for (ta, tn) in pieces:
                        P.cp("dve", krT[64:96, kidx(ta):kidx(ta) + tn], pb[64:96, ta - t0:ta - t0 + tn])
                pb = bank()
                for oc in range(2):
                    P.mm(pb[:, :n], ones_bf[:, :], sq[:, 4 + oc, :n], start=(oc == 0), stop=(oc == 1))
                rstd_from_ssq(pb[:, :n], n, 256, rstd2[:, :n])
                for oc in range(2):
                    for (ta, tn) in pieces:
                        P.tt("dve", ckvn[:, oc, kidx(ta):kidx(ta) + tn],
                             ckvn[:, oc, kidx(ta):kidx(ta) + tn], rstd2[:, ta - t0:ta - t0 + tn], ALU.mult)
                for j in range(n // 128):
                    pb = bank()
                    for k in range(8):
                        P.mm(pb[:, :], hT[:, k, j * 128:(j + 1) * 128], w1b[:, k, :], start=(k == 0), stop=(k == 7))
                    P.cp(evac_eng(), vtok[:, (t0 // 128) + j, :], pb[:, :])
                    run_part(nparts)
                run_part(nparts, 16)
                if not ctx:
                    for j in range(n // 128):
                        tok = t0 + j * 128
                        b = tok // T
                        pos = tok % T
                        pb = bank()
                        for k in range(8):
                            P.mm(pb[:, 0:288], hT[:, k, j * 128:(j + 1) * 128], w1a[:, k, 384:672],
                                 start=(k == 0), stop=(k == 7))
                        so = stg[j % 2]
                        P.act(junk[:, :], pb[:, 0:256], AF.Square)
                        P.op("dve", lambda e: e.reduce_sum(out=ssq1[:, 0:1], in_=junk[:, :], axis=mybir.AxisListType.X),
                             reads=[junk[:, :]], writes=[ssq1[:, 0:1]])
                        P.act(ssq1[:, 1:2], ssq1[:, 0:1], AF.Ln, bias=epsT[:, 0:1], scale=1.0 / 256)
                        P.act(ssq1[:, 1:2], ssq1[:, 1:2], AF.Exp, scale=-0.5)
                        P.stt("dve", so[:, 0:256], pb[:, 0:256], ssq1[:, 1:2], gkvb[:, i, :], ALU.mult, ALU.mult)
                        P.cp("dve", so[:, 256:288], pb[:, 256:288])
                        P.dma(sckv_d[b, i, pos:pos + 128, :], so[:, 0:256], out_dma=True)
                        P.dma(skr_d[b, i, pos:pos + 128, :], so[:, 256:288], out_dma=True)
            free_hs(hs1)
            pooled = [AR.bf("pooled%d" % j, [128, 512]) for j in range(2)]
            ppend = [None]
            ncs = T // 128
            for t0 in range(0, NT, 512):
                for g in range(4):
                    pb = bank()
                    for j in range(4):
                        ch = t0 // 128 + j
                        cin = ch % ncs
                        contrib = []
                        if cin > 0:
                            contrib.append((ch - 1, 3))
                        contrib.append((ch, 0 if cin == 0 else (2 if cin == ncs - 1 else 1)))
                        if cin < ncs - 1:
                            contrib.append((ch + 1, 4))
                        for ci, (src, kind) in enumerate(contrib):
                            P.mm(pb[:, j * 128:(j + 1) * 128], vtok[:, src, g * 128:(g + 1) * 128],
                                 amats[:, g * 5 + kind, :], start=(ci == 0), stop=(ci == len(contrib) - 1))
                    pl = pooled[g % 2]
                    P.cp(evac_eng(), pl[:, :], pb[:, :])
                    if ppend[0] is not None:
                        ppend[0]()

                    def _pw(pl=pl, g=g, t0=t0):
                        pb2 = bank()
                        P.mm(pb2[:, :], wp[:, g, :], pl[:, :])
                        P.ts("dve", mbuf[:, 4 + g, t0:t0 + 512], pb2[:, :], vecT[:, g, 12 + i:13 + i], ALU.mult)
                    ppend[0] = _pw
            if ppend[0] is not None:
                ppend[0]()
                ppend[0] = None
            for nm in ("pooled0", "pooled1", "junk", "ssq1", "w1a", "w1b", "wp", "amats"):
                AR.release(nm)
            if not ctx:
                AR.release("stg0")
                AR.release("stg1")
            if stages == 1 and l == trunc_l[0]:
                raise _Trunc()
            ada_hooks = None
            if ADA_INTERLEAVE and (not ctx) and l + 1 < 4:
                ada_alloc(l + 1, 2)
                ada_issue([0, 1])

                def _h0():
                    ada_compute([0, 1])
                    ada_issue([2, 3])

                def _h1():
                    ada_compute([2, 3])
                    ada_issue([4, 5])

                def _hp():
                    ada_compute([4, 5])
                    ada_finish()
                ada_hooks = {0: _h0, 1: _h1, "post": _hp}
            wuq = AR.bf("wuq", [128, 3, 768])
            wuk = AR.bf("wuk", [128, 2, 8, 64])
            wuv = AR.bf("wuv", [128, 2, 8, 64])
            P.dma(wuq[:], wuq_d[i].rearrange("(c p) n -> p c n", p=128), eng="pool")
            wukv_v = wukv_d[i].rearrange("(c p) (h t d) -> p c h t d", p=128, h=8, t=2)
            for c in range(2):
                P.dma(wuk[:, c, :, :], wukv_v[:, c, :, 0, :], eng="pool")
                P.dma(wuv[:, c, :, :], wukv_v[:, c, :, 1, :], eng="pool")
            def load_wg():
                wg_ = AR.bf("wg", [128, 8, 1024])
                load_w(wg_[:, :, 0:512], wv_in, 672, 512)
                load_w(wg_[:, :, 512:1024], wv_in, 1696, 512)
                return wg_

            def load_wout():
                wout_ = AR.bf("wout", [128, 8, 1024])
                load_w(wout_, woe_d[i].rearrange("(c p) n -> p c n", p=128), 0, 1024)
                return wout_
            wg = load_wg()
            if not ctx:
                wout = load_wout()
            NB = 2
            TT = nseq * T
            nkt = KT // 128
            qh = [AR.bf("qh%d" % b, [96, TT]) for b in range(NB)]
            kh = [AR.bf("kh%d" % b, [96, KT]) for b in range(NB)]
            vh = [AR.bf("vh%d" % b, [128, nkt, 128]) for b in range(NB)]
            pT = [AR.bf("pT%d" % b, [128, 512]) for b in range(4)]
            rc = AR.f32("rc", [64, 512])
            for b in range(NB):
                P.memset("pool", vh[b][:, :, 64:128], 1.0)
                P.cp("pool", kh[b][64:96, :], krT[64:96, 0:KT])
            pend = [None]

            def flush_pend():
                if pend[0] is not None:
                    pend[0]()
                    pend[0] = None

            def build_steps(h, b):
                st_ = []
                for ka in range(0, KT, 512):
                    def f(ka=ka):
                        kn = min(512, KT - ka)
                        pb = bank("g")
                        for c in range(2):
                            P.mm(pb[0:64, :kn], wuk[:, c, h, :], ckvn[:, c, ka:ka + kn], start=(c == 0), stop=(c == 1))
                        P.cp("dve" if ctx else "act", kh[b][0:64, ka:ka + kn], pb[0:64, :kn])
                    st_.append(f)
                for ja in range(0, nkt, 8):
                    def f(ja=ja):
                        jn = min(8, nkt - ja)
                        pb = bank("g")
                        for jj in range(jn):
                            for c in range(2):
                                P.mm(pb[:, jj * 64:(jj + 1) * 64], ckvn[:, c, (ja + jj) * 128:(ja + jj + 1) * 128],
                                     wuv[:, c, h, :], start=(c == 0), stop=(c == 1))
                        P.cp("dve" if ctx else "act", vh[b][:, ja:ja + jn, 0:64], pb[:, 0:jn * 64].rearrange("p (j d) -> p j d", d=64))
                    st_.append(f)
                for qa in range(0, TT, 512):
                    hold = {}

                    def f(qa=qa, hold=hold):
                        pb = bank("g")
                        for c in range(3):
                            P.mm(pb[0:96, :512], wuq[:, c, h * 96:(h + 1) * 96], cqn[:, c, qa:qa + 512],
                                 start=(c == 0), stop=(c == 2))
                        if ctx:
                            hold["b"] = rope_a(pb, 96, 512, MLA_SCALE, pm96, cs, qa, rb)
                        else:
                            P.act(qh[b][:, qa:qa + 512], pb[0:96, :512], AF.Copy, scale=float(MLA_SCALE))
                    st_.append(f)
                    if ctx:
                        def f2(qa=qa, hold=hold):
                            p2 = hold["b"]()
                            P.cp("dve", qh[b][0:96, qa:qa + 512], p2[0:96, :512])
                        st_.append(f2)
                return st_

            for f in build_steps(0, 0):
                f()
            for h in range(8):
                b = h % NB
                half = h % 2
                inj = build_steps(h + 1, (h + 1) % NB) if h + 1 < 8 else []
                if ctx:
                    total_steps = (T // 512) * nkc
                    every = max(1, total_steps // (len(inj) + 1))
                    stepc = 0
                    for qa in range(0, T, 512):
                        po = bank("o")
                        sc_ps = {}

                        def issue_s(j):
                            ps_ = bank("s")
                            P.mm(ps_[:, :512], kh[b][:, j * 128:(j + 1) * 128], qh[b][:, qa:qa + 512])
                            sc_ps[j] = ps_

                        for j in range(3):
                            issue_s(j)
                        for j in range(nkc):
                            pt = pT[j % 4]
                            P.act(pt[:, :], sc_ps.pop(j)[:, :], AF.Exp)
                            if j + 3 < nkc:
                                issue_s(j + 3)
                            P.mm(po[:, :], vh[b][:, j, :], pt[:, :], start=(j == 0), stop=(j == nkc - 1))
                            stepc += 1
                            if inj and stepc % every == 0:
                                inj.pop(0)()
                        P.recip(rc[0:64, :], po[64:128, :])
                        P.tt("dve", mbuf[half * 64:(half + 1) * 64, h // 2, qa:qa + 512],
                             po[0:64, :], rc[0:64, :], ALU.mult)
                else:
                    for p in range(nseq // 2):
                        pts = []
                        for a in range(2):
                            sq_ = 2 * p + a
                            ps_ = bank("s")
                            for j in range(2):
                                P.mm(ps_[:, j * 256:(j + 1) * 256], kh[b][:, sq_ * 256 + j * 128:sq_ * 256 + (j + 1) * 128],
                                     qh[b][:, sq_ * 256:(sq_ + 1) * 256])
                            pt = pT[(2 * (p + h * (nseq // 2)) + a) % 4]
                            P.act(pt[:, :], ps_[:, :], AF.Exp)
                            pts.append(pt)
                        flush_pend()
                        for _ in range(3):
                            if inj:
                                inj.pop(0)()

                        def fin(p=p, pts=pts, b=b, h=h, half=half):
                            po = bank("o")
                            for a in range(2):
                                sq_ = 2 * p + a
                                for j in range(2):
                                    P.mm(po[:, a * 256:(a + 1) * 256], vh[b][:, 2 * sq_ + j, :], pts[a][:, j * 256:(j + 1) * 256],
                                         start=(j == 0), stop=(j == 1))
                            P.cp("dve", rc[0:64, :], po[64:128, :])
                            P.act(rc[0:64, :], rc[0:64, :], AF.Ln)
                            P.act(rc[0:64, :], rc[0:64, :], AF.Exp, scale=-1.0)
                            P.tt("dve", mbuf[half * 64:(half + 1) * 64, h // 2, p * 512:(p + 1) * 512],
                                 po[0:64, :], rc[0:64, :], ALU.mult)
                        pend[0] = fin
                while inj:
                    inj.pop(0)()
            flush_pend()
            for nm in ["qh%d" % b for b in range(NB)] + ["kh%d" % b for b in range(NB)] + ["vh%d" % b for b in range(NB)] + \
                      ["pT%d" % b for b in range(4)] + ["rc", "wuq", "wuk", "wuv", "cqn", "ckvn", "krT"]:
                AR.release(nm)
            if ctx:
                AR.release("cs")
                free_rb()
                wout = load_wout()
            if stages == 2 and l == trunc_l[0]:
                raise _Trunc()
            stage3(l, m, NT, mbuf, wg, wout, ada_hooks, nxt)
            for nm in ("wg", "wout", "mbuf"):
                AR.release(nm)

        def odd_layer(l, m, NT, nseq, T, ctx, nxt=None):
            i = l // 2
            S = T + (256 if ctx else 0)
            KT = nseq * S
            nkc = S // 128
            wv_in = wino_d[i].rearrange("(c p) n -> p c n", p=128)
            qT = AR.bf("mbuf", [128, 8, NT])
            kd = AR.bf("kd", [128, 4, KT])
            va = AR.bf("va", [128, KT // 128, 4, 128])
            wq0 = pref.pop("wq0", None)
            if wq0 is None:
                wq0 = AR.bf("wq0", [128, 8, 512])
                load_w(wq0, wv_in, 0, 512)
            wq1 = pref.pop("wq1", None)
            if wq1 is None:
                wq1 = AR.bf("wq1", [128, 8, 512])
                load_w(wq1, wv_in, 512, 512)
            wqs = [wq0, wq1]
            wk = pref.pop("wk", None)
            if wk is None:
                wk = AR.bf("wk", [128, 8, 256])
                load_w(wk, wv_in, 1024, 256)
            nkv = 256 if ctx else 512
            wkv = AR.bf("wkv", [128, 8, nkv])
            load_w(wkv, wv_in, 1536 - nkv, nkv)
            P.memset("pool", va[:, :, :, 64:128], 1.0)
            cs = None
            qs = None
            if ctx:
                cs = AR.bf("cs", [128, 2, 2048])
                P.dma(cs[:, 0, :], swacs_d[0], eng="pool")
                P.dma(cs[:, 1, :], swacs_d[1], eng="pool")
                rb = alloc_rb()
                cst = AR.f32("cst", [128, 2, 256])
                cdup = AR.f32("cdup", [128, 4, 128])
                for jj in range(2):
                    P.dma(cst[:, jj, :], cv_d[i, jj * 128:(jj + 1) * 128, :])
                for jj in range(2):
                    P.cp("dve", va[:, T // 128 + jj, :, 0:64], cst[:, jj, :].rearrange("p (h d) -> p h d", h=4))
                for jj in range(2):
                    P.dma(cst[:, jj, :], ck_d[i, jj * 128:(jj + 1) * 128, :])
                for jj in range(2):
                    P.cp("dve", cdup[:, :, 0:64], cst[:, jj, :].rearrange("p (h d) -> p h d", h=4))
                    P.cp("pool", cdup[:, :, 64:128], cst[:, jj, :].rearrange("p (h d) -> p h d", h=4))
                    pb = bank()
                    for kvh in range(4):
                        P.tr(pb[:, kvh * 128:(kvh + 1) * 128], cdup[:, kvh, :], ident[:, :])
                    for kvh in range(4):
                        P.cp("dve", kd[:, kvh, T + jj * 128:T + (jj + 1) * 128], pb[:, kvh * 128:(kvh + 1) * 128])
                AR.release("cst")
                AR.release("cdup")
            stg = None
            if not ctx:
                stg = [AR.f32("stg%d" % j, [128, 512]) for j in range(2)]
            if ctx:
                hs1 = [(AR.bf("hT", [128, 8, 512]), AR.bf("sq", [128, 4, 512])),
                       (AR.bf("hT1", [128, 8, 512]), AR.bf("sq1", [128, 4, 512]))]
            else:
                hs1 = alloc_hs()

            def kidx(t):
                return (t // T) * S + (t % T)

            pipe1 = len(hs1) > 1
            rpend = [None]
            if pipe1:
                modulate(hs1[0][0], hs1[0][1], l, m, 0, 512)
            for t0 in range(0, NT, 512):
                n = 512
                hT, sq = hs1[(t0 // 512) % len(hs1)]
                if not pipe1:
                    modulate(hT, sq, l, m, t0, n)
                pieces = [(t0, n)] if T >= 512 else [(t0 + a, T) for a in range(0, n, T)]
                nparts = []
                if pipe1 and t0 + 512 < NT:
                    nh, nsq = hs1[((t0 // 512) + 1) % 2]
                    nparts = modulate_parts(nh, nsq, l, m, t0 + 512, 512)
                for oc in range(8):
                    pb = bank()
                    for k in range(8):
                        P.mm(pb[:, :n], wqs[oc // 4][:, k, (oc % 4) * 128:(oc % 4 + 1) * 128], hT[:, k, :n],
                             start=(k == 0), stop=(k == 7))
                    if ctx:
                        pb_fn = rope_a(pb, 128, n, SWA_SCALE, pm128, cs, t0, rb)
                        if rpend[0] is not None:
                            rpend[0]()

                        def _fin_q(pb_fn=pb_fn, oc=oc, t0=t0, n=n):
                            p2 = pb_fn()
                            P.cp("act", qT[:, oc, t0:t0 + n], p2[:, :n])
                        rpend[0] = _fin_q
                    else:
                        P.ts("dve", qT[:, oc, t0:t0 + n], pb[:, :n], SWA_SCALE, ALU.mult)
                    run_part(nparts)
                for kc in range(2):
                    pb = bank()
                    for k in range(8):
                        P.mm(pb[:, :n], wk[:, k, kc * 128:(kc + 1) * 128], hT[:, k, :n], start=(k == 0), stop=(k == 7))
                    def _copies(srcs, kc=kc):
                        for (src, so, sn, ko) in srcs:
                            for hh in range(2):
                                for dh in range(2):
                                    P.cp("act" if dh != hh else "dve",
                                         kd[dh * 64:(dh + 1) * 64, 2 * kc + hh, ko:ko + sn],
                                         src[hh * 64:(hh + 1) * 64, so:so + sn])
                    if ctx:
                        pb_fn = rope_a(pb, 128, n, 1.0, pm128, cs, t0, rb)
                        if rpend[0] is not None:
                            rpend[0]()

                        def _fin_k(pb_fn=pb_fn, t0=t0, n=n, _copies=_copies):
                            p2 = pb_fn()
                            _copies([(p2, 0, n, kidx(t0))])
                        rpend[0] = _fin_k
                    else:
                        _copies([(pb, ta - t0, tn, kidx(ta)) for (ta, tn) in pieces])
                run_part(nparts, 16)
                for j in range(n // 128):
                    tok = t0 + j * 128
                    pb = bank()
                    for k in range(8):
                        P.mm(pb[:, 0:nkv], hT[:, k, j * 128:(j + 1) * 128], wkv[:, k, :], start=(k == 0), stop=(k == 7))
                    P.cp("dve", va[:, kidx(tok) // 128, :, 0:64], pb[:, nkv - 256:nkv].rearrange("p (h d) -> p h d", h=4))
                    if not ctx:
                        b = tok // T
                        pos = tok % T
                        so = stg[j % 2]
                        P.cp("act", so[:, :], pb[:, :])
                        P.dma(sk_d[b, i, pos:pos + 128, :], so[:, 0:256], out_dma=True)
                        P.dma(sv_d[b, i, pos:pos + 128, :], so[:, 256:512], out_dma=True)
                    if j == 0 and rpend[0] is not None:
                        rpend[0]()
                        rpend[0] = None
            free_hs(hs1)
            for nm in ("wq0", "wq1", "wk", "wkv"):
                AR.release(nm)
            if ctx:
                AR.release("cs")
                free_rb()
            else:
                AR.release("stg0")
                AR.release("stg1")
            if stages == 1 and l == trunc_l[0]:
                raise _Trunc()
            ada_hooks = None
            if ADA_INTERLEAVE and (not ctx) and l + 1 < 4:
                ada_alloc(l + 1, 2)
                ada_issue([0, 1])

                def _h0():
                    ada_compute([0, 1])
                    ada_issue([2, 3])

                def _h1():
                    ada_compute([2, 3])
                    ada_issue([4, 5])

                def _hp():
                    ada_compute([4, 5])
                    ada_finish()
                ada_hooks = {0: _h0, 1: _h1, "post": _hp}
            vaB = AR.bf("vaB", [128, KT // 128, 4, 128])
            wg = AR.bf("wg", [128, 8, 1024])
            wout = AR.bf("wout", [128, 8, 1024])
            load_w(wg, wv_in, 1536, 1024)
            load_w(wout, woo_d[i].rearrange("(c p) n -> p c n", p=128), 0, 1024)
            pT = [AR.bf("pT%d" % b, [128, 512]) for b in range(4)]
            rc = AR.f32("rc", [128, 512])
            P.memset("dve", vaB[:, :, :, 0:64], 1.0)
            nch_all = KT // 128
            for ja in range(0, nch_all, 6):
                jb = min(nch_all, ja + 6)
                P.cp("dve" if (ja // 6) % 2 == 0 else "act", vaB[:, ja:jb, :, 64:128], va[:, ja:jb, :, 0:64])
            mbuf = qT
            pools["o4"] = [4, 5, 6, 7]
            rr["o4"] = 0
            pend = []

            def flush_pend(keep=0):
                while len(pend) > keep:
                    pend.pop(0)()

            def finalize_pair(c, pos, t_lo):
                hA, hB = 2 * c, 2 * c + 1
                P.ts("dve", rc[0:64, :], pos[0][64:128, :], esink[64:128, i * 16 + hA:i * 16 + hA + 1], ALU.add)
                P.ts("dve", rc[64:128, :], pos[1][0:64, :], esink[0:64, i * 16 + hB:i * 16 + hB + 1], ALU.add)
                if ctx:
                    P.recip(rc[:, :], rc[:, :])
                else:
                    P.act(rc[:, :], rc[:, :], AF.Ln)
                    P.act(rc[:, :], rc[:, :], AF.Exp, scale=-1.0)
                P.tt("dve", mbuf[0:64, c, t_lo:t_lo + 512], pos[0][0:64, :], rc[0:64, :], ALU.mult)
                P.tt("dve", mbuf[64:128, c, t_lo:t_lo + 512], pos[1][64:128, :], rc[64:128, :], ALU.mult)

            ucount = 0
            for c in range(8):
                kvh = c // 2
                ntile = (nseq // 2) if not ctx else (T // 512)
                for tix in range(ntile):
                    pos = []
                    if ctx:
                        qt = tix
                        q0 = qt * 512
                        pos = [bank("o4"), bank("o4")]
                        jobs = []
                        for jj in range(2):
                            jobs.append((T // 128 + jj, 0, 512, []))
                        for j in range(4 * qt - 1, 4 * qt + 5):
                            if j < 0 or j >= T // 128:
                                continue
                            nlo = max(4 * qt, j - 1)
                            nhi = min(4 * qt + 3, j + 1)
                            mk = []
                            for nb in range(nlo, nhi + 1):
                                if nb == j - 1:
                                    mk.append((nb, 0))
                                elif nb == j + 1:
                                    mk.append((nb, 1))
                            jobs.append((j, (nlo - 4 * qt) * 128, (nhi - 4 * qt + 1) * 128, mk))
                        sc_ps = {}

                        def issue_s(ji):
                            kc, lo, hi, mk_ = jobs[ji]
                            for half in (0, 1):
                                r0, r1 = half * 64, half * 64 + 64
                                ps_ = bank("s")
                                P.mm(ps_[:, lo:hi], kd[r0:r1, kvh, kc * 128:(kc + 1) * 128], qT[r0:r1, c, q0 + lo:q0 + hi],
                                     start=True, stop=(len(mk_) == 0))
                                sc_ps[(ji, half)] = ps_
                            for half in (0, 1):
                                for mi, (nb, which) in enumerate(mk_):
                                    cl = (nb - 4 * qt) * 128
                                    P.mm(sc_ps[(ji, half)][:, cl:cl + 128], identb[:, :], masks[:, which, :],
                                         start=False, stop=(mi == len(mk_) - 1))

                        for ji in range(min(2, len(jobs))):
                            issue_s(ji)
                        for ji, (kc, lo, hi, mk) in enumerate(jobs):
                            pts = []
                            for half in (0, 1):
                                pt = pT[(2 * ji + half) % 4]
                                P.act(pt[:, lo:hi], sc_ps.pop((ji, half))[:, lo:hi], AF.Exp)
                                pts.append(pt)
                            if ji + 2 < len(jobs):
                                issue_s(ji + 2)
                            for half in (0, 1):
                                vsrc = va if half == 0 else vaB
                                P.mm(pos[half][:, lo:hi], vsrc[:, kc, kvh, :], pts[half][:, lo:hi],
                                     start=(ji == 0), stop=(ji == len(jobs) - 1))
                            if ji == 2:
                                flush_pend()
                        pend.append(lambda c=c, pos=pos, q0=q0: finalize_pair(c, pos, q0))
                        continue
                    for half in (0, 1):
                        r0, r1 = half * 64, half * 64 + 64
                        vsrc = va if half == 0 else vaB
                        p = tix
                        pts = []
                        for a in range(2):
                            sq_ = 2 * p + a
                            ps_ = bank("s")
                            for j in range(2):
                                P.mm(ps_[:, j * 256:(j + 1) * 256], kd[r0:r1, kvh, sq_ * 256 + j * 128:sq_ * 256 + (j + 1) * 128],
                                     qT[r0:r1, c, sq_ * 256:(sq_ + 1) * 256])
                            pt = pT[(2 * ucount + a) % 4]
                            P.act(pt[:, :], ps_[:, :], AF.Exp)
                            pts.append(pt)
                        ucount += 1
                        flush_pend()
                        po = bank("o4")
                        pos.append(po)

                        def pv(p=p, pts=pts, po=po, vsrc=vsrc, kvh=kvh):
                            for a in range(2):
                                sq_ = 2 * p + a
                                for j in range(2):
                                    P.mm(po[:, a * 256:(a + 1) * 256], vsrc[:, 2 * sq_ + j, kvh, :], pts[a][:, j * 256:(j + 1) * 256],
                                         start=(j == 0), stop=(j == 1))
                        pend.append(pv)
                        if half == 1:
                            pend.append(lambda c=c, pos=pos, p=p: finalize_pair(c, pos, p * 512))
            flush_pend()
            AR.release("vaB")
            for nm in ["pT%d" % b for b in range(4)] + ["rc", "kd", "va"]:
                AR.release(nm)
            if stages == 2 and l == trunc_l[0]:
                raise _Trunc()
            stage3(l, m, NT, mbuf, wg, wout, ada_hooks, nxt)
            for nm in ("wg", "wout", "mbuf"):
                AR.release(nm)

        def load_x(src, NT):
            stage = AR.f32("xstage", [128, 4, 1024])
            for t0 in range(0, NT, 512):
                for j in range(4):
                    P.dma(stage[:, j, :], src[t0 + j * 128:t0 + (j + 1) * 128, :])
                for c in range(8):
                    pb = bank()
                    for j in range(4):
                        P.tr(pb[:, j * 128:(j + 1) * 128], stage[:, j, c * 128:(c + 1) * 128], ident[:, :])
                    P.cp(evac_eng(), xT[:, c, t0:t0 + 512], pb[:, :])
            AR.release("xstage")

        def store_x(dst, NT):
            stage = AR.f32("xstage", [128, 4, 1024])
            for t0 in range(0, NT, 512):
                for j in range(4):
                    for hb in range(2):
                        pb = bank()
                        for cc in range(4):
                            c = hb * 4 + cc
                            P.tr(pb[:, cc * 128:(cc + 1) * 128], xT[:, c, t0 + j * 128:t0 + (j + 1) * 128], ident[:, :])
                        P.cp(evac_eng(), stage[:, j, hb * 512:(hb + 1) * 512], pb[:, :])
                    P.dma(dst[t0 + j * 128:t0 + (j + 1) * 128, :], stage[:, j, :], out_dma=True)
            AR.release("xstage")

        trunc_l = [-1]
        try:
          for (src, dst, m, NT, nseq, T, ctx) in ((xp_d, yp_d, 0, 1024, 4, 256, False),
                                                   (xs_d, ys_d, 1, 2048, 1, 2048, True)):
              nl = nl_p if not ctx else nl_s
              if nl < 0:
                  continue
              load_x(src, NT)
              if not ada0_done[0]:
                  ada_all(0)
                  ada0_done[0] = True
              trunc_l[0] = nl - 1 if ((ctx and nl_s >= 0) or (not ctx and nl_s < 0)) else -1
              for l in range(nl):
                  if l + 1 < nl:
                      nxt = (l + 1, ctx, not ctx)
                  elif (not ctx) and nl_s > 0:
                      nxt = (0, True, True)
                  else:
                      nxt = None
                  store_state["dst"] = dst if (l == nl - 1 and stages == 3) else None
                  store_state["done"] = False
                  if l % 2 == 0:
                      even_layer(l, m, NT, nseq, T, ctx, nxt)
                  else:
                      odd_layer(l, m, NT, nseq, T, ctx, nxt)
              if not store_state["done"]:
                  store_x(dst, NT)
              store_state["dst"] = None

        except _Trunc:
            P.dma(yp_d[0:128, 0:512], rstd[:, :], out_dma=True)

        P.finish()
        build_program.stats = {e: len(P.ops[e]) for e in ENGS}
        build_program.peak = AR.peak
    return nc


_CONSTS = None


def _consts():
    global _CONSTS
    if _CONSTS is None:
        mla, swa = _rope_tables()
        pm96, pm128 = _perm_mats()
        _CONSTS = dict(ident=np.eye(128, dtype=np.float32), pm96=pm96, pm128=pm128, masks=_masks(),
                       amats=_pool_mats().reshape(20, 128, 128), mla_cs=mla, swa_cs=swa)
    return _CONSTS


def kernel(x_prompt, x_sample, cache_ckv, cache_krope, cache_k, cache_v, c, c_ctx,
           ada_w, ada_b, norm_pre, norm_post,
           mla_w_in, mla_g_qn, mla_g_kvn, mla_w_uq, mla_w_ukv, pool_w, pool_scale, mixa_w_out,
           swa_w_in, swa_sink, swa_w_out):
    f = lambda a: np.ascontiguousarray(np.asarray(a, dtype=np.float32))
    x_prompt, x_sample = f(x_prompt), f(x_sample)
    cache_ckv, cache_krope, cache_k, cache_v = f(cache_ckv), f(cache_krope), f(cache_k), f(cache_v)
    c, c_ctx = f(c), f(c_ctx)
    vecs = np.zeros((14, 1024), np.float32)
    vecs[0:4] = f(norm_pre)
    vecs[4:8] = f(norm_post)
    vecs[8:10, :384] = f(mla_g_qn)
    vecs[10:12, :256] = f(mla_g_kvn)
    vecs[12:14, :512] = f(pool_scale)
    shared = dict(ada_w=f(ada_w), ada_b=f(ada_b), vecs=vecs, gkvn=f(mla_g_kvn), sink=f(swa_sink).reshape(32),
                  w_in_e=f(mla_w_in), w_uq=f(mla_w_uq), w_ukv=f(mla_w_ukv), w_pool=f(pool_w), w_out_e=f(mixa_w_out),
                  w_in_o=f(swa_w_in), w_out_o=f(swa_w_out))
    shared.update(_consts())
    in_maps = []
    for i in range(8):
        d = dict(shared)
        d["xp"] = x_prompt[4 * i:4 * i + 4].reshape(1024, 1024)
        d["xs"] = x_sample[i]
        d["cckv"] = cache_ckv[i]
        d["ckr"] = cache_krope[i]
        d["ck"] = cache_k[i].reshape(2, 256, 256)
        d["cv"] = cache_v[i].reshape(2, 256, 256)
        d["cc"] = np.stack([c_ctx, c[i]], axis=0)
        in_maps.append(d)
    nc = build_program()
    res = run_bass_kernel_spmd(nc, in_maps, core_ids=list(range(8)))
    R = res.results
    y_prompt = np.concatenate([r["yp"].reshape(4, 256, 1024) for r in R], axis=0)
    y_sample = np.stack([r["ys"] for r in R], axis=0)
    st_ckv = np.concatenate([r["st_ckv"] for r in R], axis=0)
    st_kr = np.concatenate([r["st_kr"] for r in R], axis=0)
    st_k = np.concatenate([r["st_k"].reshape(4, 2, 256, 4, 64) for r in R], axis=0)
    st_v = np.concatenate([r["st_v"].reshape(4, 2, 256, 4, 64) for r in R], axis=0)
    return (y_prompt.astype(np.float32), y_sample.astype(np.float32), st_ckv.astype(np.float32),
            st_kr.astype(np.float32), st_k.astype(np.float32), st_v.astype(np.float32))
```
